# Optimizing a Trainium2 kernel written in Bass

```python
import jax, jax.numpy as jnp
from jax import lax
import numpy as np

D_MODEL = 1024
BATCH = 8
SEQ = 4096
DEPTH = 1

HEAD_DIM = 64
RWKV_HEADS = 8
FOX_HEADS = 8
RWKV_DIM = RWKV_HEADS * HEAD_DIM
FOX_DIM = FOX_HEADS * HEAD_DIM
MIX_DIM = RWKV_DIM + FOX_DIM
DECAY_LORA = 64
AAA_LORA = 64
GATE_LORA = 128
RWKV_COLS = 3 * RWKV_DIM + DECAY_LORA + AAA_LORA + GATE_LORA
FOX_COLS = 4 * FOX_DIM + FOX_HEADS
IN_COLS = RWKV_COLS + FOX_COLS
D_FF = 2816
CONV_WIDTH = 3
Q_BLOCK = 128
NORM_EPS = 1e-6
LNX_EPS = 64e-5

kernel_name = "hymba_rwkv7_fox_convffn"


def rmsnorm(x, w, eps=NORM_EPS):
    x32 = x.astype(jnp.float32)
    y = x32 * lax.rsqrt(jnp.mean(x32 * x32, axis=-1, keepdims=True) + eps)
    return y * w.astype(jnp.float32)


def wkv7_scan(r, w, k, v, a, b):
    B, T, H, N = r.shape
    xs = tuple(jnp.moveaxis(t.astype(jnp.float32), 1, 0) for t in (r, w, k, v, a, b))

    def step(S, inp):
        r_t, w_t, k_t, v_t, a_t, b_t = inp
        sa = jnp.einsum('bhvk,bhk->bhv', S, a_t)
        S = S * w_t[:, :, None, :] + sa[..., None] * b_t[:, :, None, :] + v_t[..., None] * k_t[:, :, None, :]
        y = jnp.einsum('bhvk,bhk->bhv', S, r_t)
        return S, y

    S0 = jnp.zeros((B, H, N, N), jnp.float32)
    _, ys = lax.scan(step, S0, xs)
    return jnp.moveaxis(ys, 0, 1)


def rwkv7_group(p, mu, w0, w2, a0, a2, g2, k_k, k_a, r_k, lnx_w, lnx_b):
    B, T, _ = p.shape
    p = p.astype(jnp.float32)
    prev = jnp.pad(p, ((0, 0), (1, 0), (0, 0)))[:, :T]
    p = p + (prev - p) * mu.astype(jnp.float32)
    r, k, v, wl, al, gl = jnp.split(
        p, [RWKV_DIM, 2 * RWKV_DIM, 3 * RWKV_DIM, 3 * RWKV_DIM + DECAY_LORA,
            3 * RWKV_DIM + DECAY_LORA + AAA_LORA], axis=-1)
    w = -jax.nn.softplus(-(w0 + jnp.tanh(wl) @ w2)) - 0.5
    decay = jnp.exp(-jnp.exp(w))
    a = jax.nn.sigmoid(a0 + al @ a2)
    g = jax.nn.sigmoid(gl) @ g2
    heads = lambda t: t.reshape(B, T, RWKV_HEADS, HEAD_DIM)
    kk = heads(k * k_k)
    kk = kk * lax.rsqrt(jnp.maximum(jnp.sum(kk * kk, axis=-1, keepdims=True), 1e-24))
    k = k * (1.0 + (a - 1.0) * k_a)
    r_h, k_h, v_h, a_h, w_h = heads(r), heads(k), heads(v), heads(a), heads(decay)
    y = wkv7_scan(r_h, w_h, k_h, v_h, -kk, kk * a_h)
    mean = jnp.mean(y, axis=-1, keepdims=True)
    var = jnp.mean(jnp.square(y - mean), axis=-1, keepdims=True)
    y = ((y - mean) * lax.rsqrt(var + LNX_EPS)).reshape(B, T, RWKV_DIM) * lnx_w + lnx_b
    bonus = jnp.sum(r_h * k_h * r_k, axis=-1, keepdims=True) * v_h
    y = y + bonus.reshape(B, T, RWKV_DIM)
    return y * g


def fox_group(p, f_bias, q_norm_w, k_norm_w, o_norm_w):
    B, T, _ = p.shape
    p = p.astype(jnp.float32)
    q, k, v, og, fl = jnp.split(p, [FOX_DIM, 2 * FOX_DIM, 3 * FOX_DIM, 4 * FOX_DIM], axis=-1)
    heads = lambda t: t.reshape(B, T, FOX_HEADS, HEAD_DIM).transpose(0, 2, 1, 3)
    q = heads(rmsnorm(q.reshape(B, T, FOX_HEADS, HEAD_DIM), q_norm_w).reshape(B, T, FOX_DIM))
    k = heads(rmsnorm(k.reshape(B, T, FOX_HEADS, HEAD_DIM), k_norm_w).reshape(B, T, FOX_DIM))
    v = heads(v)
    log_f = jax.nn.log_sigmoid(fl + f_bias.astype(jnp.float32))
    c = jnp.cumsum(log_f, axis=1).transpose(0, 2, 1)
    n_blocks = T // Q_BLOCK
    qb = q.reshape(B, FOX_HEADS, n_blocks, Q_BLOCK, HEAD_DIM).transpose(2, 0, 1, 3, 4)
    cb = c.reshape(B, FOX_HEADS, n_blocks, Q_BLOCK).transpose(2, 0, 1, 3)
    starts = jnp.arange(n_blocks, dtype=jnp.int32) * Q_BLOCK
    key_pos = jnp.arange(T, dtype=jnp.int32)
    scale = HEAD_DIM ** -0.5

    def block(args):
        q_i, c_i, s0 = args
        s = jnp.einsum('bhqd,bhkd->bhqk', q_i, k) * scale + c_i[..., None] - c[:, :, None, :]
        mask = (s0 + jnp.arange(Q_BLOCK, dtype=jnp.int32))[:, None] >= key_pos[None, :]
        s = jnp.where(mask, s, -jnp.inf)
        return jnp.einsum('bhqk,bhkd->bhqd', jax.nn.softmax(s, axis=-1), v)

    o = lax.map(block, (qb, cb, starts))
    o = o.transpose(1, 0, 3, 2, 4).reshape(B, T, FOX_HEADS, HEAD_DIM)
    o = rmsnorm(o, o_norm_w).reshape(B, T, FOX_DIM)
    return o * jax.nn.sigmoid(og)


def conv_glu_ffn(h, w_up, conv_w, conv_b, w_down):
    u = h @ w_up
    u = lax.conv_general_dilated(
        u, conv_w[:, None, :].astype(u.dtype), window_strides=(1,),
        padding=[(CONV_WIDTH - 1, 0)], dimension_numbers=('NWC', 'WIO', 'NWC'),
        feature_group_count=2 * D_FF) + conv_b
    gate, val = jnp.split(u, 2, axis=-1)
    return (jax.nn.silu(gate) * val) @ w_down


def setup_inputs(seed: int = 0) -> dict:
    key = jax.random.key(seed)
    ks = jax.random.split(key, 26)
    L = DEPTH
    nrm = lambda k, shape, s: jax.random.normal(k, shape, jnp.float32) * s
    uni = lambda k, shape, lo, hi: jax.random.uniform(k, shape, jnp.float32, minval=lo, maxval=hi)
    last_tap = (jnp.arange(CONV_WIDTH) == CONV_WIDTH - 1).astype(jnp.float32)[None, :, None]
    return {
        "x": nrm(ks[0], (BATCH, SEQ, D_MODEL), 1.0),
        "norm_mix_w": 1.0 + nrm(ks[1], (L, D_MODEL), 0.02),
        "w_in": nrm(ks[2], (L, D_MODEL, IN_COLS), D_MODEL ** -0.5),
        "rwkv_mu": uni(ks[3], (L, RWKV_COLS), 0.0, 1.0),
        "rwkv_w0": uni(ks[4], (L, RWKV_DIM), -6.5, -1.5),
        "rwkv_w2": nrm(ks[5], (L, DECAY_LORA, RWKV_DIM), DECAY_LORA ** -0.5),
        "rwkv_a0": nrm(ks[6], (L, RWKV_DIM), 0.1),
        "rwkv_a2": nrm(ks[7], (L, AAA_LORA, RWKV_DIM), AAA_LORA ** -0.5),
        "rwkv_g2": nrm(ks[8], (L, GATE_LORA, RWKV_DIM), GATE_LORA ** -0.5),
        "rwkv_k_k": 0.85 + nrm(ks[9], (L, RWKV_DIM), 0.02),
        "rwkv_k_a": 1.0 + nrm(ks[10], (L, RWKV_DIM), 0.02),
        "rwkv_r_k": nrm(ks[11], (L, RWKV_HEADS, HEAD_DIM), 0.1),
        "rwkv_lnx_w": 1.0 + nrm(ks[12], (L, RWKV_DIM), 0.02),
        "rwkv_lnx_b": nrm(ks[13], (L, RWKV_DIM), 0.02),
        "fox_f_bias": uni(ks[14], (L, FOX_HEADS), 1.0, 6.0),
        "fox_q_norm_w": 1.0 + nrm(ks[15], (L, HEAD_DIM), 0.02),
        "fox_k_norm_w": 1.0 + nrm(ks[16], (L, HEAD_DIM), 0.02),
        "fox_o_norm_w": 1.0 + nrm(ks[17], (L, HEAD_DIM), 0.02),
        "w_out": nrm(ks[18], (L, MIX_DIM, D_MODEL), MIX_DIM ** -0.5),
        "norm_ffn_w": 1.0 + nrm(ks[19], (L, D_MODEL), 0.02),
        "ffn_w_up": nrm(ks[20], (L, D_MODEL, 2 * D_FF), D_MODEL ** -0.5),
        "ffn_conv_w": nrm(ks[21], (L, CONV_WIDTH, 2 * D_FF), 0.2) + last_tap,
        "ffn_conv_b": nrm(ks[22], (L, 2 * D_FF), 0.02),
        "ffn_w_down": nrm(ks[23], (L, D_FF, D_MODEL), D_FF ** -0.5),
        "norm_final_w": 1.0 + nrm(ks[24], (D_MODEL,), 0.02),
    }


def reference(x, norm_mix_w, w_in, rwkv_mu, rwkv_w0, rwkv_w2, rwkv_a0, rwkv_a2, rwkv_g2,
              rwkv_k_k, rwkv_k_a, rwkv_r_k, rwkv_lnx_w, rwkv_lnx_b, fox_f_bias, fox_q_norm_w,
              fox_k_norm_w, fox_o_norm_w, w_out, norm_ffn_w, ffn_w_up, ffn_conv_w, ffn_conv_b,
              ffn_w_down, norm_final_w):
    in_dtype = x.dtype
    for l in range(DEPTH):
        h = rmsnorm(x, norm_mix_w[l])
        p = h @ w_in[l].astype(jnp.float32)
        y_rwkv = rwkv7_group(p[..., :RWKV_COLS], rwkv_mu[l], rwkv_w0[l], rwkv_w2[l], rwkv_a0[l],
                             rwkv_a2[l], rwkv_g2[l], rwkv_k_k[l], rwkv_k_a[l], rwkv_r_k[l],
                             rwkv_lnx_w[l], rwkv_lnx_b[l])
        y_fox = fox_group(p[..., RWKV_COLS:], fox_f_bias[l], fox_q_norm_w[l], fox_k_norm_w[l],
                          fox_o_norm_w[l])
        x = x + jnp.concatenate([y_rwkv, y_fox], axis=-1) @ w_out[l].astype(jnp.float32)
        h = rmsnorm(x, norm_ffn_w[l])
        x = x + conv_glu_ffn(h, ffn_w_up[l].astype(jnp.float32), ffn_conv_w[l], ffn_conv_b[l],
                             ffn_w_down[l].astype(jnp.float32))
    return rmsnorm(x, norm_final_w).astype(in_dtype)
```

```python
import numpy as np
import concourse.bass as bass
import concourse.mybir as mybir
from concourse.bass_utils import run_bass_kernel_spmd

F32 = mybir.dt.float32
BF16 = mybir.dt.bfloat16
AF = mybir.ActivationFunctionType
ALU = mybir.AluOpType
AX = mybir.AxisListType

ENGS = ('pe', 'act', 'dve', 'pool', 'sp')


class Sched:
    def __init__(self, nc, esems, dsems):
        self.nc = nc
        self.esem = dict(zip(ENGS, esems))
        self.dsems = dsems
        self.cnt = {e: 0 for e in ENGS}
        self.stream = {e: [] for e in ENGS}
        self.seen = {e: {} for e in ENGS}
        self.dcount = [0] * len(dsems)
        self.dn = {'sp': 0, 'pool': 0, 'act': 0}
        nq = len(dsems) // 3
        self.dq = {'sp': list(range(0, nq)), 'pool': list(range(nq, 2 * nq)), 'act': list(range(2 * nq, 3 * nq))}
        self.res = {}

    def _deps(self, reads, writes):
        deps = {}
        def add(tok):
            if tok is None:
                return
            k, v = tok
            if deps.get(k, 0) < v:
                deps[k] = v
        for r in reads:
            st = self.res.get(r)
            if st:
                add(st[0])
        for w in writes:
            st = self.res.get(w)
            if st:
                add(st[0])
                for k, v in st[1].items():
                    add((k, v))
        return deps

    def _commit(self, tok, reads, writes):
        for r in reads:
            st = self.res.setdefault(r, [None, {}])
            if st[1].get(tok[0], 0) < tok[1]:
                st[1][tok[0]] = tok[1]
        for w in writes:
            self.res[w] = [tok, {}]

    def op(self, eng, fn, reads=(), writes=()):
        deps = self._deps(reads, writes)
        waits = []
        seen = self.seen[eng]
        for k, v in deps.items():
            if k == 'pe' and eng == 'pe':
                continue
            if seen.get(k, 0) >= v:
                continue
            seen[k] = v
            waits.append((k, v))
        self.cnt[eng] += 1
        tok = (eng, self.cnt[eng])
        self.stream[eng].append((waits, fn, (eng, 1)))
        self._commit(tok, reads, writes)

    def dma(self, eng, fn, reads=(), writes=()):
        deps = self._deps(reads, writes)
        q = self.dq[eng]
        k = q[self.dn[eng] % len(q)]
        self.dn[eng] += 1
        prev = 16 * self.dcount[k]
        self.dcount[k] += 1
        key = ('d', k)
        if prev > 0:
            if deps.get(key, 0) < prev:
                deps[key] = prev
        waits = []
        seen = self.seen[eng]
        for kk, v in deps.items():
            if seen.get(kk, 0) >= v:
                continue
            seen[kk] = v
            waits.append((kk, v))
        tok = (key, prev + 16)
        self.stream[eng].append((waits, fn, (key, 16)))
        self._commit(tok, reads, writes)

    def wait_all(self, eng):
        waits = []
        for e in ENGS:
            if self.cnt[e] > 0 and e != eng:
                waits.append((e, self.cnt[e]))
        for k in range(len(self.dsems)):
            if self.dcount[k] > 0:
                waits.append((('d', k), 16 * self.dcount[k]))
        self.stream[eng].append((waits, None, None))

    def _sem(self, key):
        if isinstance(key, tuple):
            return self.dsems[key[1]]
        return self.esem[key]

    def emit(self, eng, engine):
        for waits, fn, inc in self.stream[eng]:
            for k, v in waits:
                engine.wait_ge(self._sem(k), v)
            if fn is None:
                continue
            inst = fn(engine)
            inst.then_inc(self._sem(inc[0]), inc[1])

    def flush(self):
        self.emit_all()
        self.stream = {e: [] for e in ENGS}

    def emit_all(self):
        nc = self.nc
        with nc.Block() as block:
            @block.tensor
            def _(e):
                self.emit('pe', e)

            @block.scalar
            def _(e):
                self.emit('act', e)

            @block.vector
            def _(e):
                self.emit('dve', e)

            @block.gpsimd
            def _(e):
                self.emit('pool', e)

            @block.sync
            def _(e):
                self.emit('sp', e)


def _mk(eng):
    def f(self, fn, reads=(), writes=()):
        return self.op(eng, fn, reads, writes)
    return f


for _e in ('pe', 'act', 'dve', 'pool'):
    setattr(Sched, _e, _mk(_e))

import contextlib

D = 1024
RW = 1792
FOXC = 2056
DFF = 2816
NEG_E05 = -0.6065306597126334


class K:
    def __init__(self, S):
        self.S = S

    def tt(self, eng, out, in0, in1, op, r, w):
        self.S.op(eng, lambda e: e.tensor_tensor(out=out, in0=in0, in1=in1, op=op), r, w)

    def ts(self, eng, out, in0, s1, s2, op0, op1, r, w):
        if s2 is None:
            self.S.op(eng, lambda e: e.tensor_scalar(out=out, in0=in0, scalar1=s1, scalar2=None, op0=op0), r, w)
        else:
            self.S.op(eng, lambda e: e.tensor_scalar(out=out, in0=in0, scalar1=s1, scalar2=s2, op0=op0, op1=op1), r, w)

    def stt(self, out, in0, scalar, in1, op0, op1, r, w):
        self.S.op('dve', lambda e: e.scalar_tensor_tensor(out=out, in0=in0, scalar=scalar, in1=in1, op0=op0, op1=op1), r, w)

    def act(self, out, in_, func, r, w, bias=None, scale=None, accum_out=None):
        kw = {}
        if bias is not None:
            kw['bias'] = bias
        if scale is not None:
            kw['scale'] = scale
        if accum_out is not None:
            kw['accum_out'] = accum_out
        self.S.op('act', lambda e: e.activation(out=out, in_=in_, func=func, **kw), r, w)

    def cp(self, eng, out, in_, r, w):
        if eng == 'act':
            self.S.op('act', lambda e: e.copy(out=out, in_=in_), r, w)
        else:
            self.S.op(eng, lambda e: e.tensor_copy(out=out, in_=in_), r, w)

    def mm(self, out, lhsT, rhs, r, w, start=True, stop=True):
        self.S.op('pe', lambda e: e.matmul(out, lhsT=lhsT, rhs=rhs, start=start, stop=stop), r, w)

    def tr(self, out, in_, ident, r, w):
        self.S.op('pe', lambda e: e.transpose(out, in_, ident), r, w)

    def dma(self, q, out, in_, r, w, **kw):
        self.S.dma(q, lambda e: e.dma_start(out=out, in_=in_, **kw), r, w)

    def memset(self, eng, ap, val, w):
        self.S.op(eng, lambda e: e.memset(ap, val), [], w)

    def recip(self, out, in_, r, w):
        self.S.op('dve', lambda e: e.reciprocal(out=out, in_=in_), r, w)

    def asel(self, out, in_, pattern, op, base, cm, r, w):
        self.S.op('pool', lambda e: e.affine_select(out=out, in_=in_, pattern=pattern, compare_op=op,
                                                    fill=0.0, base=base, channel_multiplier=cm), r, w)

    def scan(self, out, d0, d1, r, w):
        self.S.op('dve', lambda e: e.tensor_tensor_scan(out=out, data0=d0, data1=d1, initial=0.0,
                                                        op0=ALU.mult, op1=ALU.add), r, w)


def build(T=4096, dbg=False, phases=('A1', 'A2', 'B', 'C1', 'C2')):
    nc = bass.Bass("TRN2", target_bir_lowering=False)
    NT = T // 128
    NB = T // 512
    din = {}

    def DI(name, shape):
        din[name] = nc.dram_tensor(name, list(shape), F32, kind="ExternalInput").ap()
        return din[name]

    x = DI("x", [T, D])
    norm_mix_w = DI("norm_mix_w", [1, D])
    w_in = DI("w_in", [D, RW + FOXC])
    rwkv_mu = DI("rwkv_mu", [RW])
    rwkv_w0 = DI("rwkv_w0", [512])
    rwkv_w2 = DI("rwkv_w2", [64, 512])
    rwkv_a0 = DI("rwkv_a0", [512])
    rwkv_a2 = DI("rwkv_a2", [64, 512])
    rwkv_g2 = DI("rwkv_g2", [128, 512])
    rwkv_k_k = DI("rwkv_k_k", [512])
    rwkv_k_a = DI("rwkv_k_a", [512])
    rwkv_r_k = DI("rwkv_r_k", [512])
    rwkv_lnx_w = DI("rwkv_lnx_w", [512])
    rwkv_lnx_b = DI("rwkv_lnx_b", [512])
    fox_f_bias = DI("fox_f_bias", [1, 8])
    fox_q_norm_w = DI("fox_q_norm_w", [64])
    fox_k_norm_w = DI("fox_k_norm_w", [64])
    fox_o_norm_w = DI("fox_o_norm_w", [1, 64])
    w_out = DI("w_out", [D, D])
    norm_ffn_w = DI("norm_ffn_w", [1, D])
    ffn_w_up = DI("ffn_w_up", [D, 2 * DFF])
    ffn_conv_w = DI("ffn_conv_w", [3, 2 * DFF])
    ffn_conv_b = DI("ffn_conv_b", [2 * DFF])
    ffn_w_down = DI("ffn_w_down", [DFF, D])
    norm_final_w = DI("norm_final_w", [1, D])
    out = nc.dram_tensor("out", [T, D], F32, kind="ExternalOutput").ap()

    okind = "ExternalOutput" if dbg else "Internal"
    YR = nc.dram_tensor("yr", [4, 128, T], BF16, kind=okind).ap()
    YF = nc.dram_tensor("yf", [4, 128, T], BF16, kind=okind).ap()
    SGd = nc.dram_tensor("sgd", [T, 512], BF16, kind="Internal").ap()
    X1 = nc.dram_tensor("x1", [T, D], F32, kind=okind).ap()

    with contextlib.ExitStack() as es0:
        esems = [es0.enter_context(nc.semaphore(f"es{i}")) for i in range(5)]
        dsems = [es0.enter_context(nc.semaphore(f"ds{i}")) for i in range(24)]
        S = Sched(nc, esems, dsems)
        k = K(S)

        uid = [0]

        def SB(es, name, shape, dt=F32):
            uid[0] += 1
            return es.enter_context(nc.sbuf_tensor(f"{name}_u{uid[0]}", list(shape), dt))

        pst = es0.enter_context(nc.psum_tensor("pst", [128, 1024], BF16))
        pbig = [es0.enter_context(nc.psum_tensor(f"pb{i}", [128, 512], F32)) for i in range(7)]
        st = {'big': 0, 'q': 0}

        def PB():
            i = st['big'] % 7
            st['big'] += 1
            return pbig[i], f"pb{i}"

        def PQ():
            i = st['q'] % 16
            st['q'] += 1
            return pqb[i // 4][:, (i % 4) * 128:(i % 4) * 128 + 128], f"pq{i}"

        def PQbank():
            return PQ()

        ones = SB(es0, "ones", [128, 128])
        ident = SB(es0, "ident", [128, 128])
        identb = SB(es0, "identb", [128, 128], BF16)
        mST = SB(es0, "mST", [128, 128])
        mIT = SB(es0, "mIT", [128, 128])
        mS = SB(es0, "mS", [128, 128])
        bones = SB(es0, "bones", [128, 128])
        k.memset('pool', ones[:], 1.0, ['ones'])
        k.asel(ident[:], ones[:], [[-1, 128]], ALU.is_equal, 0, 1, ['ones'], ['ident'])
        k.cp('dve', identb[:], ident[:], ['ident'], ['identb'])
        for m_, nm in ((mST, 'mST'), (mIT, 'mIT'), (mS, 'mS'), (bones, 'bones')):
            k.memset('pool', m_[:], 0.0, [nm])
        for b in range(2):
            sl = slice(64 * b, 64 * b + 64)
            k.asel(mST[sl, sl], ones[sl, sl], [[1, 64]], ALU.is_ge, -1, -1, ['ones', 'mST'], ['mST'])
            k.asel(mIT[sl, sl], ones[sl, sl], [[1, 64]], ALU.is_ge, 0, -1, ['ones', 'mIT'], ['mIT'])
            k.asel(mS[sl, sl], ones[sl, sl], [[-1, 64]], ALU.is_ge, -1, 1, ['ones', 'mS'], ['mS'])
            k.cp('pool', bones[sl, sl], ones[sl, sl], ['ones', 'bones'], ['bones'])

        nwb = SB(es0, "nwb", [128, D])

        def rms_to_hT(es, xsrc, t0, hT, nwname, tagp, ntile=4, xtl=None, nwt=None):
            nwt_ = nwb if nwt is None else nwt
            for i in range(ntile):
                if xtl is None:
                    par = i % 2
                    xt, xtn = xts[par], f'xt{par}'
                else:
                    xt, xtn = xtl[i]
                k.dma('sp', xt[:], xsrc[t0 + 128 * i:t0 + 128 * i + 128, :], [], [xtn])
                k.act(junk[:], xt[:], AF.Square, [xtn], ['junk', 'ss'], accum_out=ss[:])
                k.act(ss[:], ss[:], AF.Sqrt, ['ss'], ['ss'], bias=1e-6, scale=1.0 / D)
                k.recip(rs[:], ss[:], ['ss'], ['rs'])
                k.stt(xn[:], xt[:], rs[:, 0:1], nwt_[:], ALU.mult, ALU.mult, [xtn, 'rs', nwname], ['xn'])
                for kk_ in range(8):
                    k.tr(pst[:, 128 * kk_:128 * kk_ + 128], xn[:, 128 * kk_:128 * kk_ + 128], identb[:],
                         ['xn', 'identb'], ['pst'])
                k.cp('act', hT[:, :, 128 * i:128 * i + 128], pst[:].rearrange("p (k t) -> p k t", t=128),
                     ['pst'], ['hT'])

        if 'A1' in phases:
            with contextlib.ExitStack() as es:
                mST4 = SB(es, "mST4", [128, 4, 128])
                mIT4 = SB(es, "mIT4", [128, 4, 128])
                mS4 = SB(es, "mS4", [128, 4, 128])
                ident4 = SB(es, "ident4", [128, 4, 128])
                for i in range(4):
                    for src_, sn, dst_, dn_ in ((mST, 'mST', mST4, 'mST4'), (mIT, 'mIT', mIT4, 'mIT4'), (mS, 'mS', mS4, 'mS4'),
                                                (ident, 'ident', ident4, 'ident4')):
                        k.cp('pool', dst_[:, i, :], src_[:], [sn, dn_], [dn_])
                k.dma('sp', nwb[:], norm_mix_w.partition_broadcast(128), [], ['nwb'])
                Win = SB(es, "WinR", [128, 8, RW], BF16)
                for kc in range(8):
                    k.dma('pool', Win[:, kc, :], w_in[128 * kc:128 * kc + 128, 0:RW], [], ['Win'])
                xts = [SB(es, f"xt{i}", [128, D]) for i in range(2)]
                junk = SB(es, "junk", [128, D], BF16)
                ss = SB(es, "ss", [128, 1])
                rs = SB(es, "rs", [128, 1])
                xn = SB(es, "xn", [128, D], BF16)
                hT = SB(es, "hT", [128, 8, 512], BF16)
                T1 = [SB(es, f"T1_{i}", [128, 512]) for i in range(2)]
                carry = SB(es, "carry", [128, 14])
                X = SB(es, "X", [128, 14, 512])
                mu_cm = SB(es, "mu_cm", [128, 14])
                omm_cm = SB(es, "omm_cm", [128, 14])
                prm = SB(es, "prm", [128, 7, 4])
                omka = SB(es, "omka", [128, 4])
                W2Z = SB(es, "W2Z", [128, 512])
                A2Z = SB(es, "A2Z", [128, 512])
                G2 = SB(es, "G2", [128, 512])
                rmask = SB(es, "rmask", [128, 512])
                k.dma('sp', mu_cm[:], rwkv_mu.rearrange("(t p) -> p t", p=128), [], ['mu_cm'], allow_slow_non_contiguous=True)
                for i, prm_in in enumerate((rwkv_w0, rwkv_a0, rwkv_k_k, rwkv_k_a, rwkv_r_k, rwkv_lnx_w, rwkv_lnx_b)):
                    k.dma('sp', prm[:, i, :], prm_in.rearrange("(t p) -> p t", p=128), [], ['prm'], allow_slow_non_contiguous=True)
                k.ts('dve', omm_cm[:], mu_cm[:], -1.0, 1.0, ALU.mult, ALU.add, ['mu_cm'], ['omm_cm'])
                k.ts('dve', omka[:], prm[:, 3, :], -1.0, 1.0, ALU.mult, ALU.add, ['prm'], ['omka'])
                k.memset('pool', W2Z[:], 0.0, ['W2Z'])
                k.memset('pool', A2Z[:], 0.0, ['A2Z'])
                k.dma('sp', W2Z[0:64, :], rwkv_w2, [], ['W2Z'])
                k.dma('sp', A2Z[64:128, :], rwkv_a2, [], ['A2Z'])
                k.dma('sp', G2[:], rwkv_g2, [], ['G2'])
                k.memset('pool', rmask[:], 1.0, ['rmask'])
                k.memset('pool', rmask[:].rearrange("p (c t) -> p c t", t=64)[:, :, 0:1], 0.0, ['rmask'])
                k.memset('pool', carry[:], 0.0, [f'carry{c_}' for c_ in range(14)])

                TA = SB(es, "TA", [128, 512])
                SGg = SB(es, "SGg", [128, 512])
                tmp = [SB(es, f"tmp{i}", [128, 512]) for i in range(9)]
                Gt = SB(es, "Gt", [128, 512])
                BSt = SB(es, "BSt", [128, 512])
                PCt = SB(es, "PCt", [128, 8])
                BDn = ('RT', 'AT', 'BT', 'KT', 'BH', 'KH', 'VT')
                BD = {n: SB(es, f"BD_{n}", [128, 8, 128]) for n in BDn}
                for n in BDn:
                    k.memset('pool', BD[n][:], 0.0, [f'BD_{n}'])
                TMn = ('A', 'BH', 'KH', 'V')
                TM = {n: SB(es, f"TM_{n}", [128, 4, 128]) for n in TMn}
                CMn = ('NTa', 'NTb', 'Na', 'Nb', 'ST', 'AKT', 'MRBT', 'MRKT', 'ApT', 'W2', 'Vp', 'U')
                CM = {n: SB(es, f"CM_{n}", [128, 4, 128]) for n in CMn}
                H = [SB(es, f"H{p}", [128, 128]) for p in range(4)]
                for p in range(4):
                    k.memset('pool', H[p][:], 0.0, [f'H{p}'])
                YT = SB(es, "YT", [128, 512])
                YO = SB(es, "YO", [128, 512], BF16)

                def bdwrite(eng, name, in0, in1, op, r):
                    for hh in range(2):
                        ps_ = slice(64 * hh, 64 * hh + 64)
                        o = BD[name][ps_, :, 64 * hh:64 * hh + 64]
                        a0 = in0[ps_, :].rearrange("p (c t) -> p c t", t=64)
                        if in1 is None:
                            k.cp(eng, o, a0, r, [f'BD_{name}'])
                        else:
                            a1 = in1[ps_, :].rearrange("p (c t) -> p c t", t=64)
                            k.tt(eng, o, a0, a1, op, r, [f'BD_{name}'])

                for b in range(NB):
                    t0 = 512 * b
                    rms_to_hT(es, x, t0, hT, 'nwb', 'a1')
                    for ct in range(14):
                        pb, pbn = PB()
                        for kc in range(8):
                            k.mm(pb[:], Win[:, kc, 128 * ct:128 * ct + 128], hT[:, kc, :], ['Win', 'hT'], [pbn],
                                 start=(kc == 0), stop=(kc == 7))
                        t1 = T1[ct % 2]
                        t1n = f'T1_{ct % 2}'
                        k.act(t1[:], pb[:], AF.Identity, [pbn, 'mu_cm'], [t1n], scale=mu_cm[:, ct:ct + 1])
                        xn_ = f'X{ct}'
                        k.stt(X[:, ct, 1:512], pb[:, 1:512], omm_cm[:, ct:ct + 1], t1[:, 0:511], ALU.mult, ALU.add,
                              [pbn, t1n, 'omm_cm'], [xn_ + 'a'])
                        k.stt(X[:, ct, 0:1], pb[:, 0:1], omm_cm[:, ct:ct + 1], carry[:, ct:ct + 1], ALU.mult, ALU.add,
                              [pbn, f'carry{ct}', 'omm_cm'], [xn_ + 'b'])
                        k.cp('pool', carry[:, ct:ct + 1], t1[:, 511:512], [t1n], [f'carry{ct}'])

                    def XR(ct):
                        return [f'X{ct}a', f'X{ct}b']

                    k.act(TA[0:64, :], X[0:64, 12, :], AF.Tanh, XR(12), ['TAa'])
                    k.cp('pool', TA[64:128, :], X[64:128, 12, :], XR(12), ['TAb'])
                    k.act(SGg[:], X[:, 13, :], AF.Sigmoid, XR(13), ['SGg'])
                    for pr in range(4):
                        cs = slice(128 * pr, 128 * pr + 128)
                        Xr, Xk, Xv = X[:, pr, :], X[:, 4 + pr, :], X[:, 8 + pr, :]
                        rXr, rXk, rXv = XR(pr), XR(4 + pr), XR(8 + pr)
                        SIG, A_, KKN, KP, BV, L, E1, E2, E3 = tmp
                        tn = [f'tmp{i}' for i in range(9)]
                        pb, pbn = PB()
                        k.mm(pb[:], W2Z[:, cs], TA[:], ['W2Z', 'TAa', 'TAb'], [pbn])
                        k.act(SIG[:], pb[:], AF.Sigmoid, [pbn, 'prm'], [tn[0]], bias=prm[:, 0, pr:pr + 1])
                        pb, pbn = PB()
                        k.mm(pb[:], A2Z[:, cs], TA[:], ['A2Z', 'TAa', 'TAb'], [pbn])
                        k.act(A_[:], pb[:], AF.Sigmoid, [pbn, 'prm'], [tn[1]], bias=prm[:, 1, pr:pr + 1])
                        pb, pbn = PB()
                        k.mm(pb[:], G2[:, cs], SGg[:], ['G2', 'SGg'], [pbn])
                        k.cp('act', Gt[:], pb[:], [pbn], ['Gt'])
                        k.ts('dve', KKN[:], Xk, prm[:, 2, pr:pr + 1], None, ALU.mult, None, rXk + ['prm'], [tn[2]])
                        k.act(E1[:], KKN[:], AF.Square, [tn[2]], [tn[6]])
                        pb, pbn = PB()
                        k.mm(pb[:], bones[:], E1[:], ['bones', tn[6]], [pbn])
                        k.ts('dve', E1[:], pb[:], 1e-24, None, ALU.max, None, [pbn], [tn[6]])
                        k.act(E1[:], E1[:], AF.Sqrt, [tn[6]], [tn[6]])
                        k.recip(E1[:], E1[:], [tn[6]], [tn[6]])
                        k.tt('dve', KKN[:], KKN[:], E1[:], ALU.mult, [tn[2], tn[6]], [tn[2]])
                        k.ts('dve', KP[:], A_[:], prm[:, 3, pr:pr + 1], omka[:, pr:pr + 1], ALU.mult, ALU.add,
                             [tn[1], 'prm', 'omka'], [tn[3]])
                        k.tt('dve', KP[:], KP[:], Xk, ALU.mult, [tn[3]] + rXk, [tn[3]])
                        k.tt('pool', BV[:], KKN[:], A_[:], ALU.mult, [tn[2], tn[1]], [tn[4]])
                        k.stt(E1[:], Xr, prm[:, 4, pr:pr + 1], KP[:], ALU.mult, ALU.mult, rXr + ['prm', tn[3]], [tn[6]])
                        pb, pbn = PB()
                        k.mm(pb[:], bones[:], E1[:], ['bones', tn[6]], [pbn])
                        k.tt('dve', BSt[:], pb[:], Xv, ALU.mult, [pbn] + rXv, ['BSt'])
                        k.ts('dve', SIG[:], SIG[:], NEG_E05, None, ALU.mult, None, [tn[0]], [tn[0]])
                        k.scan(L[:], rmask[:], SIG[:], ['rmask', tn[0]], [tn[5]])
                        k.act(E1[:], L[:], AF.Exp, [tn[5]], [tn[6]])
                        bdwrite('dve', 'RT', Xr, E1, ALU.mult, rXr + [tn[6]])
                        k.tt('pool', E2[:], L[:], SIG[:], ALU.subtract, [tn[5], tn[0]], [tn[7]])
                        k.act(E2[:], E2[:], AF.Exp, [tn[7]], [tn[7]])
                        k.ts('dve', E2[:], E2[:], -1.0, None, ALU.mult, None, [tn[7]], [tn[7]])
                        bdwrite('dve', 'AT', KKN, E2, ALU.mult, [tn[2], tn[7]])
                        k.act(E3[:], L[:], AF.Exp, [tn[5]], [tn[8]], scale=-1.0)
                        bdwrite('dve', 'BT', BV, E3, ALU.mult, [tn[4], tn[8]])
                        bdwrite('pool', 'KT', KP, E3, ALU.mult, [tn[3], tn[8]])
                        L3 = L[:].rearrange("p (c t) -> p c t", t=64)
                        k.tt('dve', E1[:].rearrange("p (c t) -> p c t", t=64), L3,
                             L3[:, :, 63:64].to_broadcast([128, 8, 64]), ALU.subtract, [tn[5]], [tn[6]])
                        k.act(E1[:], E1[:], AF.Exp, [tn[6]], [tn[6]], scale=-1.0)
                        bdwrite('dve', 'BH', BV, E1, ALU.mult, [tn[4], tn[6]])
                        bdwrite('pool', 'KH', KP, E1, ALU.mult, [tn[3], tn[6]])
                        k.act(PCt[:], L3[:, :, 63], AF.Exp, [tn[5]], ['PCt'])
                        bdwrite('pool', 'VT', Xv, None, None, rXv)

                        Hp, Hn = H[pr], f'H{pr}'
                        for g in range(2):
                            cl = [4 * g + i for i in range(4)]

                            def q4(pb_, i):
                                return pb_[:, 128 * i:128 * i + 128]

                            def v4(pb_):
                                return pb_[:].rearrange("p (i t) -> p i t", t=128)
                            for (ln, rn, mk, mkn, on) in (('BT', 'AT', mST4, 'mST4', 'NTa'), ('AT', 'BT', mS4, 'mS4', 'Na'),
                                                          ('KT', 'AT', mST4, 'mST4', 'AKT'), ('BT', 'RT', mIT4, 'mIT4', 'MRBT'),
                                                          ('KT', 'RT', mIT4, 'mIT4', 'MRKT')):
                                pb, pbn = PB()
                                for i, c in enumerate(cl):
                                    k.mm(q4(pb, i), BD[ln][:, c, :], BD[rn][:, c, :], [f'BD_{ln}', f'BD_{rn}'], [pbn])
                                k.tt('dve', CM[on][:], v4(pb), mk[:], ALU.mult, [pbn, mkn], ['CM_' + on])
                            for (src, dst) in (('AT', 'A'), ('BH', 'BH'), ('KH', 'KH'), ('VT', 'V')):
                                pb, pbn = PB()
                                for i, c in enumerate(cl):
                                    k.tr(q4(pb, i), BD[src][:, c, :], ident[:], [f'BD_{src}', 'ident'], [pbn])
                                k.cp('act', TM[dst][:], v4(pb), [pbn], ['TM_' + dst])
                            k.tt('pool', CM['ST'][:], CM['NTa'][:], ident4[:], ALU.add, ['CM_NTa', 'ident4'], ['CM_ST'])
                            curN, curNT = 'Na', 'NTa'
                            for lev in range(1, 6):
                                nxtN = 'Nb' if curN == 'Na' else 'Na'
                                nxtNT = 'NTb' if curNT == 'NTa' else 'NTa'
                                pb, pbn = PB()
                                for i in range(4):
                                    k.mm(q4(pb, i), CM[curNT][:, i, :], CM[curN][:, i, :], ['CM_' + curNT, 'CM_' + curN], [pbn])
                                k.cp('act', CM[nxtN][:], v4(pb), [pbn], ['CM_' + nxtN])
                                if lev < 5:
                                    pb, pbn = PB()
                                    for i in range(4):
                                        k.mm(q4(pb, i), CM[curN][:, i, :], CM[curNT][:, i, :], ['CM_' + curNT, 'CM_' + curN], [pbn])
                                    k.cp('act', CM[nxtNT][:], v4(pb), [pbn], ['CM_' + nxtNT])
                                pb, pbn = PB()
                                for i in range(4):
                                    k.mm(q4(pb, i), CM[nxtN][:, i, :], CM['ST'][:, i, :], ['CM_' + nxtN, 'CM_ST'], [pbn])
                                k.tt('dve', CM['ST'][:], v4(pb), CM['ST'][:], ALU.add, [pbn, 'CM_ST'], ['CM_ST'])
                                curN, curNT = nxtN, nxtNT
                            pb, pbn = PB()
                            for i in range(4):
                                k.mm(q4(pb, i), TM['A'][:, i, :], CM['ST'][:, i, :], ['TM_A', 'CM_ST'], [pbn])
                            k.cp('act', CM['ApT'][:], v4(pb), [pbn], ['CM_ApT'])
                            pb, pbn = PB()
                            for i in range(4):
                                k.mm(q4(pb, i), CM['AKT'][:, i, :], TM['V'][:, i, :], ['CM_AKT', 'TM_V'], [pbn])
                            k.cp('act', CM['W2'][:], v4(pb), [pbn], ['CM_W2'])
                            pb, pbn = PB()
                            for i in range(4):
                                k.mm(q4(pb, i), CM['ST'][:, i, :], CM['W2'][:, i, :], ['CM_ST', 'CM_W2'], [pbn])
                            k.cp('act', CM['Vp'][:], v4(pb), [pbn], ['CM_Vp'])
                            for i, c in enumerate(cl):
                                un = f'CM_U{i}'
                                pb, pbn = PB()
                                k.mm(q4(pb, 0), CM['ApT'][:, i, :], Hp[:], ['CM_ApT', Hn], [pbn])
                                k.tt('dve', CM['U'][:, i, :], q4(pb, 0), CM['Vp'][:, i, :], ALU.add, [pbn, 'CM_Vp'], [un])
                                pb, pbn = PB()
                                k.mm(q4(pb, 0), Hp[:], BD['RT'][:, c, :], [Hn, 'BD_RT'], [pbn], start=True, stop=False)
                                k.mm(q4(pb, 0), CM['U'][:, i, :], CM['MRBT'][:, i, :], [un, 'CM_MRBT'], [pbn],
                                     start=False, stop=False)
                                k.mm(q4(pb, 0), TM['V'][:, i, :], CM['MRKT'][:, i, :], ['TM_V', 'CM_MRKT'], [pbn],
                                     start=False, stop=True)
                                for hh in range(2):
                                    ps_ = slice(64 * hh, 64 * hh + 64)
                                    k.cp('act', YT[ps_, 64 * c:64 * c + 64], pb[ps_, 64 * hh:64 * hh + 64], [pbn], [f'YT{hh}'])
                                pb, pbn = PB()
                                k.mm(q4(pb, 0), TM['BH'][:, i, :], CM['U'][:, i, :], ['TM_BH', un], [pbn],
                                     start=True, stop=False)
                                k.mm(q4(pb, 0), TM['KH'][:, i, :], TM['V'][:, i, :], ['TM_KH', 'TM_V'], [pbn],
                                     start=False, stop=True)
                                k.stt(Hp[:], Hp[:], PCt[:, c:c + 1], q4(pb, 0), ALU.mult, ALU.add, [Hn, 'PCt', pbn], [Hn])
                        rYT = ['YT0', 'YT1']
                        pb, pbn = PB()
                        k.mm(pb[:], bones[:], YT[:], ['bones'] + rYT, [pbn])
                        k.stt(E1[:], pb[:], -1.0 / 64, YT[:], ALU.mult, ALU.add, [pbn] + rYT, [tn[6]])
                        k.act(E2[:], E1[:], AF.Square, [tn[6]], [tn[7]])
                        pb, pbn = PB()
                        k.mm(pb[:], bones[:], E2[:], ['bones', tn[7]], [pbn])
                        k.act(E2[:], pb[:], AF.Sqrt, [pbn], [tn[7]], bias=64e-5, scale=1.0 / 64)
                        k.recip(E2[:], E2[:], [tn[7]], [tn[7]])
                        k.tt('dve', E1[:], E1[:], E2[:], ALU.mult, [tn[6], tn[7]], [tn[6]])
                        k.ts('dve', E1[:], E1[:], prm[:, 5, pr:pr + 1], prm[:, 6, pr:pr + 1], ALU.mult, ALU.add,
                             [tn[6], 'prm'], [tn[6]])
                        k.tt('dve', E1[:], E1[:], BSt[:], ALU.add, [tn[6], 'BSt'], [tn[6]])
                        k.tt('dve', YO[:], E1[:], Gt[:], ALU.mult, [tn[6], 'Gt'], ['YO'])
                        k.dma('sp', YR[pr, :, t0:t0 + 512], YO[:], ['YO'], ['YR'])
                S.wait_all('sp')
                S.flush()


        if 'A2' in phases:
            with contextlib.ExitStack() as esAB:
                QT = SB(esAB, "QT", [128, 4, T], BF16)
                KT_ = SB(esAB, "KTf", [128, 4, T], BF16)
                V1 = SB(esAB, "V1", [128, NT, 8, 65], BF16)
                LFs = SB(esAB, "LFs", [128, NT, 8])
                k.memset('pool', V1[:], 1.0, ['V1'])
                with contextlib.ExitStack() as es:
                    k.dma('sp', nwb[:], norm_mix_w.partition_broadcast(128), [], ['nwb'])
                    Wf = SB(es, "Wf", [128, 8, FOXC], BF16)
                    for kc in range(8):
                        k.dma('pool', Wf[:, kc, 0:1024], w_in[128 * kc:128 * kc + 128, RW:RW + 1024], [], ['Wf'])
                        k.dma('pool', Wf[:, kc, 1024:FOXC], w_in[128 * kc:128 * kc + 128, RW + 1024:RW + FOXC], [], ['Wf'])
                    xts = [SB(es, f"xt{i}", [128, D]) for i in range(2)]
                    junk = SB(es, "junk", [128, D], BF16)
                    ss = SB(es, "ss", [128, 1])
                    rs = SB(es, "rs", [128, 1])
                    xn = SB(es, "xn", [128, D], BF16)
                    hT = SB(es, "hT", [128, 8, 512], BF16)
                    qkw = SB(es, "qkw", [128, 2])
                    fbb = SB(es, "fbb", [128, 8])
                    sq = SB(es, "sq", [128, 512])
                    rq = SB(es, "rq", [128, 512])
                    SGt = [SB(es, f"SGt{i}", [128, 512], BF16) for i in range(2)]
                    zt = SB(es, "zt", [128, 8])
                    for hh in range(2):
                        k.dma('sp', qkw[64 * hh:64 * hh + 64, 0:1], fox_q_norm_w.rearrange("(p o) -> p o", o=1), [], ['qkw'])
                        k.dma('sp', qkw[64 * hh:64 * hh + 64, 1:2], fox_k_norm_w.rearrange("(p o) -> p o", o=1), [], ['qkw'])
                    k.dma('sp', fbb[:], fox_f_bias.partition_broadcast(128), [], ['fbb'])
                    for b in range(NB):
                        t0 = 512 * b
                        rms_to_hT(es, x, t0, hT, 'nwb', 'a2')
                        for ct in range(8):
                            pb, pbn = PB()
                            for kc in range(8):
                                k.mm(pb[:], Wf[:, kc, 128 * ct:128 * ct + 128], hT[:, kc, :], ['Wf', 'hT'], [pbn],
                                     start=(kc == 0), stop=(kc == 7))
                            k.act(sq[:], pb[:], AF.Square, [pbn], ['sq'])
                            pb2, pbn2 = PB()
                            k.mm(pb2[:], bones[:], sq[:], ['bones', 'sq'], [pbn2])
                            k.act(rq[:], pb2[:], AF.Sqrt, [pbn2], ['rq'], bias=1e-6, scale=1.0 / 64)
                            k.recip(rq[:], rq[:], ['rq'], ['rq'])
                            dst = QT if ct < 4 else KT_
                            dn_ = 'QT' if ct < 4 else 'KTf'
                            k.stt(dst[:, ct % 4, t0:t0 + 512], pb[:], qkw[:, (ct // 4):(ct // 4) + 1], rq[:], ALU.mult, ALU.mult,
                                  [pbn, 'qkw', 'rq'], [dn_])
                        for i in range(4):
                            ti = 4 * b + i
                            tsl = slice(128 * i, 128 * i + 128)
                            pb, pbn = PB()
                            for kc in range(8):
                                k.mm(pb[:], hT[:, kc, tsl], Wf[:, kc, 1024:1536], ['Wf', 'hT'], [pbn], start=(kc == 0), stop=(kc == 7))
                            k.cp('act', V1[:, ti, :, 0:64], pb[:].rearrange("p (h d) -> p h d", d=64), [pbn], ['V1'])
                            pb, pbn = PB()
                            for kc in range(8):
                                k.mm(pb[:], hT[:, kc, tsl], Wf[:, kc, 1536:2048], ['Wf', 'hT'], [pbn], start=(kc == 0), stop=(kc == 7))
                            sg, sgn = SGt[i % 2], f'SGt{i % 2}'
                            k.act(sg[:], pb[:], AF.Sigmoid, [pbn], [sgn])
                            k.dma('sp', SGd[t0 + 128 * i:t0 + 128 * i + 128, :], sg[:], [sgn], ['SGd'])
                            pb, pbn = PB()
                            for kc in range(8):
                                k.mm(pb[:, 0:8], hT[:, kc, tsl], Wf[:, kc, 2048:2056], ['Wf', 'hT'], [pbn], start=(kc == 0), stop=(kc == 7))
                            k.tt('dve', zt[:], pb[:, 0:8], fbb[:], ALU.add, [pbn, 'fbb'], ['zt'])
                            k.act(zt[:], zt[:], AF.Exp, ['zt'], ['zt'], scale=-1.0)
                            k.act(LFs[:, ti, :], zt[:], AF.Ln, ['zt'], ['LFs'], bias=1.0)
                    S.wait_all('sp')
                S.flush()
                with contextlib.ExitStack() as es:
                    tri = SB(es, "tri", [128, 128])
                    cmask = SB(es, "cmask", [128, 128], BF16)
                    NCk = SB(es, "NCk", [128, NT, 8])
                    TOT = SB(es, "TOT", [128, NT, 8])
                    CAR = SB(es, "CAR", [128, NT, 8])
                    NBq = SB(es, "NBq", [128, NT, 8])
                    ownb = SB(es, "ownb", [128, 64])
                    k.asel(tri[:], ones[:], [[1, 128]], ALU.is_ge, 0, -1, ['ones'], ['tri'])
                    k.cp('dve', cmask[:], tri[:], ['tri'], ['cmask'])
                    k.dma('sp', ownb[:], fox_o_norm_w.partition_broadcast(128), [], ['ownb'])
                    LF2 = LFs[:].rearrange("p t h -> p (t h)")
                    nchunk = (NT * 8 + 511) // 512
                    for cc in range(nchunk):
                        c0 = 512 * cc
                        c1 = min(NT * 8, c0 + 512)
                        pb, pbn = PB()
                        k.mm(pb[:, 0:c1 - c0], tri[:], LF2[:, c0:c1], ['tri', 'LFs'], [pbn])
                        k.cp('act', NCk[:].rearrange("p t h -> p (t h)")[:, c0:c1], pb[:, 0:c1 - c0], [pbn], ['NCk'])
                        pb, pbn = PB()
                        k.mm(pb[:, 0:c1 - c0], ones[:], LF2[:, c0:c1], ['ones', 'LFs'], [pbn])
                        k.cp('act', TOT[:].rearrange("p t h -> p (t h)")[:, c0:c1], pb[:, 0:c1 - c0], [pbn], ['TOT'])
                    k.memset('pool', CAR[:, 0, :], 0.0, ['CAR'])
                    for ti in range(1, NT):
                        k.tt('dve', CAR[:, ti, :], CAR[:, ti - 1, :], TOT[:, ti - 1, :], ALU.add, ['CAR', 'TOT'], ['CAR'])
                    k.tt('dve', NCk[:], NCk[:], CAR[:], ALU.add, ['NCk', 'CAR'], ['NCk'])
                    k.stt(NBq[:], TOT[:], 0.5, CAR[:], ALU.mult, ALU.add, ['TOT', 'CAR'], ['NBq'])
                    biasT = [SB(es, f"biasT{i}", [128, NT]) for i in range(2)]
                    PT = [SB(es, f"PT{i}", [128, 128], BF16) for i in range(8)]
                    RL = SB(es, "RL", [128, 8])
                    Ot = SB(es, "Ot", [128, 8, 64])
                    O2 = SB(es, "O2", [128, 8, 64])
                    ssq = SB(es, "ssq", [128, 8])
                    SGl = SB(es, "SGl", [128, 512], BF16)
                    YFt = SB(es, "YFt", [128, 512], BF16)
                    YFo = SB(es, "YFo", [128, 4, 128], BF16)
                    pS = [(pbig[i], f'pb{i}') for i in range(3)]
                    pO = [(pbig[3 + i], f'pb{3 + i}') for i in range(4)]
                    nS = 0
                    nP = 0
                    for qt in range(NT):
                        qsl = slice(128 * qt, 128 * qt + 128)
                        for h in range(8):
                            pr, r0 = h // 2, 64 * (h % 2)
                            bt, btn = biasT[h % 2], f'biasT{h % 2}'
                            k.ts('dve', bt[:, 0:qt + 1], NCk[:, 0:qt + 1, h], NBq[:, qt, h:h + 1], None, ALU.subtract, None,
                                 ['NCk', 'NBq'], [btn])
                            po, pon = pO[2 * (qt % 2) + h // 4]
                            pocol = 65 * (h % 4)
                            for kt0 in range(0, qt + 1, 4):
                                kts = list(range(kt0, min(qt + 1, kt0 + 4)))
                                pb, pbn = pS[nS % 3]
                                nS += 1
                                for j, kt in enumerate(kts):
                                    k.mm(pb[:, 128 * j:128 * j + 128], KT_[r0:r0 + 64, pr, 128 * kt:128 * kt + 128],
                                         QT[r0:r0 + 64, pr, qsl], ['KTf', 'QT'], [pbn])
                                for j, kt in enumerate(kts):
                                    pt, ptn = PT[nP % 8], f'PT{nP % 8}'
                                    nP += 1
                                    k.act(pt[:], pb[:, 128 * j:128 * j + 128], AF.Exp, [pbn, btn], [ptn],
                                          bias=bt[:, kt:kt + 1], scale=0.125)
                                    if kt == qt:
                                        k.tt('pool', pt[:], pt[:], cmask[:], ALU.mult, [ptn, 'cmask'], [ptn])
                                    k.mm(po[:, pocol:pocol + 65], pt[:], V1[:, kt, h, :], [ptn, 'V1'], [pon],
                                         start=(kt == 0), stop=(kt == qt))
                        for hf in range(2):
                            po, pon = pO[2 * (qt % 2) + hf]
                            po3 = po[:, 0:260].rearrange("p (h d) -> p h d", d=65)
                            k.recip(RL[:, 4 * hf:4 * hf + 4], po3[:, :, 64], [pon], [f'RL{hf}'])
                            k.tt('dve', Ot[:, 4 * hf:4 * hf + 4, :], po3[:, :, 0:64],
                                 RL[:, 4 * hf:4 * hf + 4].unsqueeze(2).to_broadcast([128, 4, 64]), ALU.mult,
                                 [pon, f'RL{hf}'], [f'Ot{hf}'])
                        rOt = ['Ot0', 'Ot1']
                        k.act(O2[:], Ot[:], AF.Square, rOt, ['O2'])
                        S.op('dve', lambda e: e.tensor_reduce(out=ssq[:], in_=O2[:], axis=AX.X, op=ALU.add), ['O2'], ['ssq'])
                        k.act(ssq[:], ssq[:], AF.Sqrt, ['ssq'], ['ssq'], bias=1e-6, scale=1.0 / 64)
                        k.recip(ssq[:], ssq[:], ['ssq'], ['ssq'])
                        k.tt('dve', O2[:], Ot[:], ssq[:].unsqueeze(2).to_broadcast([128, 8, 64]), ALU.mult, rOt + ['ssq'], ['O2'])
                        k.tt('pool', O2[:], O2[:], ownb[:].unsqueeze(1).to_broadcast([128, 8, 64]), ALU.mult, ['O2', 'ownb'], ['O2'])
                        k.dma('sp', SGl[:], SGd[qsl, :], ['SGd'], ['SGl'])
                        k.tt('dve', YFt[:], O2[:].rearrange("p h d -> p (h d)"), SGl[:], ALU.mult, ['O2', 'SGl'], ['YFt'])
                        for c4 in range(4):
                            k.tr(pst[:, 128 * c4:128 * c4 + 128], YFt[:, 128 * c4:128 * c4 + 128], identb[:], ['YFt', 'identb'], ['pst'])
                        k.cp('act', YFo[:], pst[:, 0:512].rearrange("p (c t) -> p c t", t=128), ['pst'], ['YFo'])
                        k.dma('sp', YF[:, :, qsl].rearrange("c p t -> p c t"), YFo[:], ['YFo'], ['YF'])
                    S.wait_all('sp')
                S.flush()

        if 'C1' in phases:
            with contextlib.ExitStack() as es:
                Wo = SB(es, "Wo", [128, 8, D], BF16)
                for kc in range(8):
                    k.dma('pool', Wo[:, kc, :], w_out[128 * kc:128 * kc + 128, :], [], ['Wo'])
                Yb = [SB(es, f"Yb{i}", [128, 8, 512], BF16) for i in range(2)]
                xts = [SB(es, f"xt{i}", [128, D]) for i in range(2)]
                x1t = [SB(es, f"x1t{i}", [128, D]) for i in range(2)]
                for b in range(NB):
                    t0 = 512 * b
                    yb, ybn = Yb[b % 2], f'Yb{b % 2}'
                    k.dma('sp', yb[:, 0:4, :], YR[:, :, t0:t0 + 512].rearrange("c p t -> p c t"), ['YR'], [ybn])
                    k.dma('sp', yb[:, 4:8, :], YF[:, :, t0:t0 + 512].rearrange("c p t -> p c t"), ['YF'], [ybn])
                    for i in range(4):
                        par = i % 2
                        tsl = slice(128 * i, 128 * i + 128)
                        k.dma('sp', xts[par][:], x[t0 + 128 * i:t0 + 128 * i + 128, :], [], [f'xt{par}'])
                        for hf in range(2):
                            pb, pbn = PB()
                            for kc in range(8):
                                k.mm(pb[:], yb[:, kc, tsl], Wo[:, kc, 512 * hf:512 * hf + 512], [ybn, 'Wo'], [pbn],
                                     start=(kc == 0), stop=(kc == 7))
                            k.tt('dve', x1t[par][:, 512 * hf:512 * hf + 512], pb[:], xts[par][:, 512 * hf:512 * hf + 512], ALU.add,
                                 [pbn, f'xt{par}'], [f'x1t{par}'])
                        k.dma('sp', X1[t0 + 128 * i:t0 + 128 * i + 128, :], x1t[par][:], [f'x1t{par}'], ['X1'])
                S.wait_all('sp')
                S.flush()

        if 'C2' in phases:
            with contextlib.ExitStack() as es:
                NF = 2 * DFF // 128
                Wup = SB(es, "Wup", [128, 8, 2 * DFF], BF16)
                Wd = SB(es, "Wd", [128, 22, D], BF16)
                for kc in range(8):
                    for c0 in range(0, 2 * DFF, 2048):
                        c1 = min(2 * DFF, c0 + 2048)
                        k.dma('pool', Wup[:, kc, c0:c1], ffn_w_up[128 * kc:128 * kc + 128, c0:c1], [], ['Wup'])
                for ft in range(22):
                    k.dma('pool', Wd[:, ft, :], ffn_w_down[128 * ft:128 * ft + 128, :], [], ['Wd'])
                nw2 = SB(es, "nw2", [128, D])
                nwf = SB(es, "nwf", [128, D])
                k.dma('sp', nw2[:], norm_ffn_w.partition_broadcast(128), [], ['nw2'])
                k.dma('sp', nwf[:], norm_final_w.partition_broadcast(128), [], ['nwf'])
                craw = SB(es, "craw", [128, 128])
                craw2 = SB(es, "craw2", [48, 128])
                cw = SB(es, "cw", [128, 176])
                k.dma('sp', craw[:], ffn_conv_w.rearrange("a (f p) -> (a f) p", p=128)[0:128, :], [], ['craw'])
                k.dma('sp', craw2[0:4, :], ffn_conv_w.rearrange("a (f p) -> (a f) p", p=128)[128:132, :], [], ['craw2'])
                k.dma('sp', craw2[4:48, :], ffn_conv_b.rearrange("(f p) -> f p", p=128), [], ['craw2'])
                pb, pbn = PB()
                k.tr(pb[:, 0:128], craw[:], ident[:], ['craw', 'ident'], [pbn])
                k.tr(pb[:, 128:176], craw2[:], ident[0:48, 0:48], ['craw2', 'ident'], [pbn])
                k.cp('act', cw[:], pb[:, 0:176], [pbn], ['cw'])
                x1s = [(SB(es, f"x1s{i}", [128, D]), f'x1s{i}') for i in range(2)]
                junk = SB(es, "junk", [128, D], BF16)
                ss = SB(es, "ss", [128, 1])
                rs = SB(es, "rs", [128, 1])
                xn = SB(es, "xn", [128, D], BF16)
                h2T = SB(es, "h2T", [128, 8, 256], BF16)
                Ub = [SB(es, f"Ub{i}", [128, 258]) for i in range(2)]
                HL = SB(es, "HL", [128, NF, 2])
                k.memset('pool', HL[:], 0.0, [f'HL{f}' for f in range(NF)])
                cva = [SB(es, f"cva{i}", [128, 256]) for i in range(2)]
                cvb = [SB(es, f"cvb{i}", [128, 256]) for i in range(2)]
                gsl = SB(es, "gsl", [128, 256])
                GT = SB(es, "GT", [128, 22, 256], BF16)
                x2 = SB(es, "x2", [128, D])
                ot = [SB(es, "ot", [128, D])] * 2

                def conv_tile(ft, slot):
                    pb, pbn = PB()
                    for kc in range(8):
                        k.mm(pb[:, 0:256], Wup[:, kc, 128 * ft:128 * ft + 128], h2T[:, kc, :], ['Wup', 'hT'], [pbn],
                             start=(kc == 0), stop=(kc == 7))
                    ub, ubn = Ub[slot], f'Ub{slot}'
                    k.cp('act', ub[:, 2:258], pb[:, 0:256], [pbn], [ubn + 'm'])
                    k.cp('pool', ub[:, 0:2], HL[:, ft, :], [f'HL{ft}'], [ubn + 'h'])
                    ur = [ubn + 'm', ubn + 'h']
                    ca, can = cva[slot], f'cva{slot}'
                    cb_, cbn = cvb[slot], f'cvb{slot}'
                    k.ts('dve', ca[:], ub[:, 2:258], cw[:, 88 + ft:89 + ft], cw[:, 132 + ft:133 + ft], ALU.mult, ALU.add,
                         ur + ['cw'], [can])
                    k.stt(cb_[:], ub[:, 1:257], cw[:, 44 + ft:45 + ft], ca[:], ALU.mult, ALU.add, ur + ['cw', can], [cbn])
                    k.stt(ca[:], ub[:, 0:256], cw[:, ft:ft + 1], cb_[:], ALU.mult, ALU.add, ur + ['cw', cbn], [can])
                    k.cp('pool', HL[:, ft, :], ub[:, 256:258], ur, [f'HL{ft}'])
                    return ca, can

                for b2 in range(T // 256):
                    t0 = 256 * b2
                    rms_to_hT(es, X1, t0, h2T, 'nw2', 'c2', ntile=2, xtl=x1s, nwt=nw2)
                    for ft in range(22):
                        ga, gan = conv_tile(ft, 0)
                        va, van = conv_tile(22 + ft, 1)
                        k.act(gsl[:], ga[:], AF.Silu, [gan], ['gsl'])
                        k.tt('dve', GT[:, ft, :], gsl[:], va[:], ALU.mult, ['gsl', van], ['GT'])
                    for i in range(2):
                        xs_, xsn = x1s[i]
                        for hf in range(2):
                            pb, pbn = PB()
                            for ft in range(22):
                                k.mm(pb[:], GT[:, ft, 128 * i:128 * i + 128], Wd[:, ft, 512 * hf:512 * hf + 512], ['GT', 'Wd'], [pbn],
                                     start=(ft == 0), stop=(ft == 21))
                            k.tt('dve', x2[:, 512 * hf:512 * hf + 512], pb[:], xs_[:, 512 * hf:512 * hf + 512], ALU.add,
                                 [pbn, xsn], [f'x2{hf}'])
                        rx2 = ['x20', 'x21']
                        k.act(junk[:], x2[:], AF.Square, rx2, ['junk', 'ss'], accum_out=ss[:])
                        k.act(ss[:], ss[:], AF.Sqrt, ['ss'], ['ss'], bias=1e-6, scale=1.0 / D)
                        k.recip(rs[:], ss[:], ['ss'], ['rs'])
                        k.stt(ot[i][:], x2[:], rs[:, 0:1], nwf[:], ALU.mult, ALU.mult, rx2 + ['rs', 'nwf'], ['ot'])
                        k.dma('sp', out[t0 + 128 * i:t0 + 128 * i + 128, :], ot[i][:], ['ot'], ['out'])
                S.wait_all('sp')
                S.flush()

        S.wait_all('sp')
        S.flush()
    return nc


_IN_NAMES = ["x", "norm_mix_w", "w_in", "rwkv_mu", "rwkv_w0", "rwkv_w2", "rwkv_a0", "rwkv_a2", "rwkv_g2",
             "rwkv_k_k", "rwkv_k_a", "rwkv_r_k", "rwkv_lnx_w", "rwkv_lnx_b", "fox_f_bias", "fox_q_norm_w",
             "fox_k_norm_w", "fox_o_norm_w", "w_out", "norm_ffn_w", "ffn_w_up", "ffn_conv_w", "ffn_conv_b",
             "ffn_w_down", "norm_final_w"]

_SHAPES = {"norm_mix_w": (1, D), "w_in": (D, RW + FOXC), "rwkv_mu": (RW,), "rwkv_w0": (512,), "rwkv_w2": (64, 512),
           "rwkv_a0": (512,), "rwkv_a2": (64, 512), "rwkv_g2": (128, 512), "rwkv_k_k": (512,), "rwkv_k_a": (512,),
           "rwkv_r_k": (512,), "rwkv_lnx_w": (512,), "rwkv_lnx_b": (512,), "fox_f_bias": (1, 8),
           "fox_q_norm_w": (64,), "fox_k_norm_w": (64,), "fox_o_norm_w": (1, 64), "w_out": (D, D),
           "norm_ffn_w": (1, D), "ffn_w_up": (D, 2 * DFF), "ffn_conv_w": (3, 2 * DFF), "ffn_conv_b": (2 * DFF,),
           "ffn_w_down": (DFF, D), "norm_final_w": (1, D)}


def make_in_maps(inputs, T, ncores):
    shared = {n: np.ascontiguousarray(np.asarray(inputs[n], dtype=np.float32).reshape(_SHAPES[n])) for n in _SHAPES}
    xs = np.asarray(inputs["x"], dtype=np.float32)
    maps = []
    for c in range(ncores):
        m = dict(shared)
        m["x"] = np.ascontiguousarray(xs[c, :T])
        maps.append(m)
    return maps


def kernel(**inputs):
    T = inputs["x"].shape[1]
    B = inputs["x"].shape[0]
    nc = build(T=T)
    res = run_bass_kernel_spmd(nc, make_in_maps(inputs, T, B), core_ids=list(range(B)))
    return np.stack([r["out"] for r in res.results], axis=0).astype(np.float32)
```

```python
import numpy as np
import concourse.bass as bass
import concourse.mybir as mybir
from concourse.bass_utils import run_bass_kernel_spmd

F32 = mybir.dt.float32
BF16 = mybir.dt.bfloat16
AF = mybir.ActivationFunctionType
ALU = mybir.AluOpType
AX = mybir.AxisListType

ENGS = ('pe', 'act', 'dve', 'pool', 'sp')


class Sched:
    def __init__(self, nc, esems, dsems):
        self.nc = nc
        self.esem = dict(zip(ENGS, esems))
        self.dsems = dsems
        self.cnt = {e: 0 for e in ENGS}
        self.stream = {e: [] for e in ENGS}
        self.seen = {e: {} for e in ENGS}
        self.dcount = [0] * len(dsems)
        self.dn = {'sp': 0, 'pool': 0, 'act': 0}
        nq = len(dsems) // 3
        self.dq = {'sp': list(range(0, nq)), 'pool': list(range(nq, 2 * nq)), 'act': list(range(2 * nq, 3 * nq))}
        self.res = {}

    def _deps(self, reads, writes):
        deps = {}
        def add(tok):
            if tok is None:
                return
            k, v = tok
            if deps.get(k, 0) < v:
                deps[k] = v
        for r in reads:
            st = self.res.get(r)
            if st:
                add(st[0])
        for w in writes:
            st = self.res.get(w)
            if st:
                add(st[0])
                for k, v in st[1].items():
                    add((k, v))
        return deps

    def _commit(self, tok, reads, writes):
        for r in reads:
            st = self.res.setdefault(r, [None, {}])
            if st[1].get(tok[0], 0) < tok[1]:
                st[1][tok[0]] = tok[1]
        for w in writes:
            self.res[w] = [tok, {}]

    def op(self, eng, fn, reads=(), writes=()):
        deps = self._deps(reads, writes)
        waits = []
        seen = self.seen[eng]
        for k, v in deps.items():
            if k == 'pe' and eng == 'pe':
                continue
            if seen.get(k, 0) >= v:
                continue
            seen[k] = v
            waits.append((k, v))
        self.cnt[eng] += 1
        tok = (eng, self.cnt[eng])
        self.stream[eng].append((waits, fn, (eng, 1)))
        self._commit(tok, reads, writes)

    def dma(self, eng, fn, reads=(), writes=()):
        deps = self._deps(reads, writes)
        q = self.dq[eng]
        k = q[self.dn[eng] % len(q)]
        self.dn[eng] += 1
        prev = 16 * self.dcount[k]
        self.dcount[k] += 1
        key = ('d', k)
        if prev > 0:
            if deps.get(key, 0) < prev:
                deps[key] = prev
        waits = []
        seen = self.seen[eng]
        for kk, v in deps.items():
            if seen.get(kk, 0) >= v:
                continue
            seen[kk] = v
            waits.append((kk, v))
        tok = (key, prev + 16)
        self.stream[eng].append((waits, fn, (key, 16)))
        self._commit(tok, reads, writes)

    def wait_all(self, eng):
        waits = []
        for e in ENGS:
            if self.cnt[e] > 0 and e != eng:
                waits.append((e, self.cnt[e]))
        for k in range(len(self.dsems)):
            if self.dcount[k] > 0:
                waits.append((('d', k), 16 * self.dcount[k]))
        self.stream[eng].append((waits, None, None))

    def _sem(self, key):
        if isinstance(key, tuple):
            return self.dsems[key[1]]
        return self.esem[key]

    def emit(self, eng, engine):
        for waits, fn, inc in self.stream[eng]:
            for k, v in waits:
                engine.wait_ge(self._sem(k), v)
            if fn is None:
                continue
            inst = fn(engine)
            inst.then_inc(self._sem(inc[0]), inc[1])

    def flush(self):
        self.emit_all()
        self.stream = {e: [] for e in ENGS}

    def emit_all(self):
        nc = self.nc
        with nc.Block() as block:
            @block.tensor
            def _(e):
                self.emit('pe', e)

            @block.scalar
            def _(e):
                self.emit('act', e)

            @block.vector
            def _(e):
                self.emit('dve', e)

            @block.gpsimd
            def _(e):
                self.emit('pool', e)

            @block.sync
            def _(e):
                self.emit('sp', e)


def _mk(eng):
    def f(self, fn, reads=(), writes=()):
        return self.op(eng, fn, reads, writes)
    return f


for _e in ('pe', 'act', 'dve', 'pool'):
    setattr(Sched, _e, _mk(_e))

import contextlib

D = 1024
RW = 1792
FOXC = 2056
DFF = 2816
NEG_E05 = -0.6065306597126334
F32R = mybir.dt.float32r


def R(ap):
    return ap.bitcast(F32R)


class K:
    def __init__(self, S):
        self.S = S

    def tt(self, eng, out, in0, in1, op, r, w):
        self.S.op(eng, lambda e: e.tensor_tensor(out=out, in0=in0, in1=in1, op=op), r, w)

    def ts(self, eng, out, in0, s1, s2, op0, op1, r, w):
        if s2 is None:
            self.S.op(eng, lambda e: e.tensor_scalar(out=out, in0=in0, scalar1=s1, scalar2=None, op0=op0), r, w)
        else:
            self.S.op(eng, lambda e: e.tensor_scalar(out=out, in0=in0, scalar1=s1, scalar2=s2, op0=op0, op1=op1), r, w)

    def stt(self, out, in0, scalar, in1, op0, op1, r, w):
        self.S.op('dve', lambda e: e.scalar_tensor_tensor(out=out, in0=in0, scalar=scalar, in1=in1, op0=op0, op1=op1), r, w)

    def act(self, out, in_, func, r, w, bias=None, scale=None, accum_out=None):
        kw = {}
        if bias is not None:
            kw['bias'] = bias
        if scale is not None:
            kw['scale'] = scale
        if accum_out is not None:
            kw['accum_out'] = accum_out
        self.S.op('act', lambda e: e.activation(out=out, in_=in_, func=func, **kw), r, w)

    def cp(self, eng, out, in_, r, w):
        if eng == 'act':
            self.S.op('act', lambda e: e.copy(out=out, in_=in_), r, w)
        else:
            self.S.op(eng, lambda e: e.tensor_copy(out=out, in_=in_), r, w)

    def mm(self, out, lhsT, rhs, r, w, start=True, stop=True, r32=False):
        if r32:
            lhsT, rhs = R(lhsT), R(rhs)
        self.S.op('pe', lambda e: e.matmul(out, lhsT=lhsT, rhs=rhs, start=start, stop=stop), r, w)

    def tr(self, out, in_, ident, r, w):
        self.S.op('pe', lambda e: e.transpose(out, in_, ident), r, w)

    def dma(self, q, out, in_, r, w, **kw):
        self.S.dma(q, lambda e: e.dma_start(out=out, in_=in_, **kw), r, w)

    def memset(self, eng, ap, val, w):
        self.S.op(eng, lambda e: e.memset(ap, val), [], w)

    def recip(self, out, in_, r, w):
        self.S.op('dve', lambda e: e.reciprocal(out=out, in_=in_), r, w)

    def asel(self, out, in_, pattern, op, base, cm, r, w):
        self.S.op('pool', lambda e: e.affine_select(out=out, in_=in_, pattern=pattern, compare_op=op,
                                                    fill=0.0, base=base, channel_multiplier=cm), r, w)

    def scan(self, out, d0, d1, r, w):
        self.S.op('dve', lambda e: e.tensor_tensor_scan(out=out, data0=d0, data1=d1, initial=0.0,
                                                        op0=ALU.mult, op1=ALU.add), r, w)


def build(T=4096, dbg=False, phases=('A1', 'A2', 'B', 'C1', 'C2')):
    nc = bass.Bass("TRN2", target_bir_lowering=False)
    NT = T // 128
    NB = T // 512
    din = {}

    def DI(name, shape):
        din[name] = nc.dram_tensor(name, list(shape), F32, kind="ExternalInput").ap()
        return din[name]

    x = DI("x", [T, D])
    norm_mix_w = DI("norm_mix_w", [1, D])
    w_in = DI("w_in", [D, RW + FOXC])
    rwkv_mu = DI("rwkv_mu", [RW])
    rwkv_w0 = DI("rwkv_w0", [512])
    rwkv_w2 = DI("rwkv_w2", [64, 512])
    rwkv_a0 = DI("rwkv_a0", [512])
    rwkv_a2 = DI("rwkv_a2", [64, 512])
    rwkv_g2 = DI("rwkv_g2", [128, 512])
    rwkv_k_k = DI("rwkv_k_k", [512])
    rwkv_k_a = DI("rwkv_k_a", [512])
    rwkv_r_k = DI("rwkv_r_k", [512])
    rwkv_lnx_w = DI("rwkv_lnx_w", [512])
    rwkv_lnx_b = DI("rwkv_lnx_b", [512])
    fox_f_bias = DI("fox_f_bias", [1, 8])
    fox_q_norm_w = DI("fox_q_norm_w", [64])
    fox_k_norm_w = DI("fox_k_norm_w", [64])
    fox_o_norm_w = DI("fox_o_norm_w", [1, 64])
    w_out = DI("w_out", [D, D])
    norm_ffn_w = DI("norm_ffn_w", [1, D])
    ffn_w_up = DI("ffn_w_up", [D, 2 * DFF])
    ffn_conv_w = DI("ffn_conv_w", [3, 2 * DFF])
    ffn_conv_b = DI("ffn_conv_b", [2 * DFF])
    ffn_w_down = DI("ffn_w_down", [DFF, D])
    norm_final_w = DI("norm_final_w", [1, D])
    out = nc.dram_tensor("out", [T, D], F32, kind="ExternalOutput").ap()

    okind = "ExternalOutput" if dbg else "Internal"
    YR = nc.dram_tensor("yr", [4, 128, T], BF16, kind=okind).ap()
    YF = nc.dram_tensor("yf", [4, 128, T], BF16, kind=okind).ap()
    SGd = nc.dram_tensor("sgd", [T, 512], BF16, kind="Internal").ap()
    X1 = nc.dram_tensor("x1", [T, D], F32, kind=okind).ap()

    with contextlib.ExitStack() as es0:
        esems = [es0.enter_context(nc.semaphore(f"es{i}")) for i in range(5)]
        dsems = [es0.enter_context(nc.semaphore(f"ds{i}")) for i in range(24)]
        S = Sched(nc, esems, dsems)
        k = K(S)

        uid = [0]

        def SB(es, name, shape, dt=F32):
            uid[0] += 1
            return es.enter_context(nc.sbuf_tensor(f"{name}_u{uid[0]}", list(shape), dt))

        pst = es0.enter_context(nc.psum_tensor("pst", [128, 1024], BF16))
        pbig = [es0.enter_context(nc.psum_tensor(f"pb{i}", [128, 512], F32)) for i in range(7)]
        st = {'big': 0, 'q': 0}

        def PB():
            i = st['big'] % 7
            st['big'] += 1
            return pbig[i], f"pb{i}"

        def PQ():
            i = st['q'] % 16
            st['q'] += 1
            return pqb[i // 4][:, (i % 4) * 128:(i % 4) * 128 + 128], f"pq{i}"

        def PQbank():
            return PQ()

        ones = SB(es0, "ones", [128, 128])
        ident = SB(es0, "ident", [128, 128])
        identb = SB(es0, "identb", [128, 128], BF16)
        mST = SB(es0, "mST", [128, 128])
        mIT = SB(es0, "mIT", [128, 128])
        mS = SB(es0, "mS", [128, 128])
        bones_raw = SB(es0, "bones_raw", [128, 128])
        bones = SB(es0, "bones", [128, 128])
        k.memset('pool', ones[:], 1.0, ['ones'])
        k.asel(ident[:], ones[:], [[-1, 128]], ALU.is_equal, 0, 1, ['ones'], ['ident'])
        k.cp('dve', identb[:], ident[:], ['ident'], ['identb'])
        for m_, nm in ((mST, 'mST'), (mIT, 'mIT'), (mS, 'mS'), (bones_raw, 'bones_raw')):
            k.memset('pool', m_[:], 0.0, [nm])
        for b in range(2):
            sl = slice(64 * b, 64 * b + 64)
            k.asel(mST[sl, sl], ones[sl, sl], [[1, 64]], ALU.is_ge, -1, -1, ['ones', 'mST'], ['mST'])
            k.asel(mIT[sl, sl], ones[sl, sl], [[1, 64]], ALU.is_ge, 0, -1, ['ones', 'mIT'], ['mIT'])
            k.asel(mS[sl, sl], ones[sl, sl], [[-1, 64]], ALU.is_ge, -1, 1, ['ones', 'mS'], ['mS'])
            k.cp('pool', bones_raw[sl, sl], ones[sl, sl], ['ones', 'bones_raw'], ['bones_raw'])

        k.cp('dve', R(bones[:]), bones_raw[:], ['bones_raw'], ['bones'])
        nwb = SB(es0, "nwb", [128, D])

        def rms_to_hT(es, xsrc, t0, hT, nwname, tagp, ntile=4, xtl=None, nwt=None):
            nwt_ = nwb if nwt is None else nwt
            for i in range(ntile):
                if xtl is None:
                    par = i % 2
                    xt, xtn = xts[par], f'xt{par}'
                else:
                    xt, xtn = xtl[i]
                k.dma('sp', xt[:], xsrc[t0 + 128 * i:t0 + 128 * i + 128, :], [], [xtn])
                k.act(junk[:], xt[:], AF.Square, [xtn], ['junk', 'ss'], accum_out=ss[:])
                k.act(ss[:], ss[:], AF.Sqrt, ['ss'], ['ss'], bias=1e-6, scale=1.0 / D)
                k.recip(rs[:], ss[:], ['ss'], ['rs'])
                k.stt(xn[:], xt[:], rs[:, 0:1], nwt_[:], ALU.mult, ALU.mult, [xtn, 'rs', nwname], ['xn'])
                for kk_ in range(8):
                    k.tr(pst[:, 128 * kk_:128 * kk_ + 128], xn[:, 128 * kk_:128 * kk_ + 128], identb[:],
                         ['xn', 'identb'], ['pst'])
                k.cp('act', hT[:, :, 128 * i:128 * i + 128], pst[:].rearrange("p (k t) -> p k t", t=128),
                     ['pst'], ['hT'])

        if 'A1' in phases:
            with contextlib.ExitStack() as es:
                mST4 = SB(es, "mST4", [128, 4, 128])
                mIT4 = SB(es, "mIT4", [128, 4, 128])
                mS4 = SB(es, "mS4", [128, 4, 128])
                ident4 = SB(es, "ident4", [128, 4, 128])
                for i in range(4):
                    for src_, sn, dst_, dn_ in ((mST, 'mST', mST4, 'mST4'), (mIT, 'mIT', mIT4, 'mIT4'), (mS, 'mS', mS4, 'mS4'),
                                                (ident, 'ident', ident4, 'ident4')):
                        k.cp('pool', dst_[:, i, :], src_[:], [sn, dn_], [dn_])
                k.dma('sp', nwb[:], norm_mix_w.partition_broadcast(128), [], ['nwb'])
                Win = SB(es, "WinR", [128, 8, RW], BF16)
                for kc in range(8):
                    k.dma('pool', Win[:, kc, :], w_in[128 * kc:128 * kc + 128, 0:RW], [], ['Win'])
                xts = [SB(es, f"xt{i}", [128, D]) for i in range(2)]
                junk = SB(es, "junk", [128, D], BF16)
                ss = SB(es, "ss", [128, 1])
                rs = SB(es, "rs", [128, 1])
                xn = SB(es, "xn", [128, D], BF16)
                hT = SB(es, "hT", [128, 8, 512], BF16)
                T1 = [SB(es, f"T1_{i}", [128, 512]) for i in range(2)]
                carry = SB(es, "carry", [128, 14])
                X = SB(es, "X", [128, 14, 512])
                mu_cm = SB(es, "mu_cm", [128, 14])
                omm_cm = SB(es, "omm_cm", [128, 14])
                prm = SB(es, "prm", [128, 7, 4])
                omka = SB(es, "omka", [128, 4])
                W2Z = SB(es, "W2Z", [128, 512])
                A2Z = SB(es, "A2Z", [128, 512])
                G2 = SB(es, "G2", [128, 512])
                Wraw = [SB(es, f"Wraw{i}", [128, 512]) for i in range(3)]
                rmask = SB(es, "rmask", [128, 512])
                k.dma('sp', mu_cm[:], rwkv_mu.rearrange("(t p) -> p t", p=128), [], ['mu_cm'], allow_slow_non_contiguous=True)
                for i, prm_in in enumerate((rwkv_w0, rwkv_a0, rwkv_k_k, rwkv_k_a, rwkv_r_k, rwkv_lnx_w, rwkv_lnx_b)):
                    k.dma('sp', prm[:, i, :], prm_in.rearrange("(t p) -> p t", p=128), [], ['prm'], allow_slow_non_contiguous=True)
                k.ts('dve', omm_cm[:], mu_cm[:], -1.0, 1.0, ALU.mult, ALU.add, ['mu_cm'], ['omm_cm'])
                k.ts('dve', omka[:], prm[:, 3, :], -1.0, 1.0, ALU.mult, ALU.add, ['prm'], ['omka'])
                k.memset('pool', Wraw[0][:], 0.0, ['Wraw0'])
                k.memset('pool', Wraw[1][:], 0.0, ['Wraw1'])
                k.dma('sp', Wraw[0][0:64, :], rwkv_w2, [], ['Wraw0'])
                k.dma('sp', Wraw[1][64:128, :], rwkv_a2, [], ['Wraw1'])
                k.dma('sp', Wraw[2][:], rwkv_g2, [], ['Wraw2'])
                for i_, (w_, wn_) in enumerate(((W2Z, 'W2Z'), (A2Z, 'A2Z'), (G2, 'G2'))):
                    k.cp('dve', R(w_[:]), Wraw[i_][:], [f'Wraw{i_}'], [wn_])
                k.memset('pool', rmask[:], 1.0, ['rmask'])
                k.memset('pool', rmask[:].rearrange("p (c t) -> p c t", t=64)[:, :, 0:1], 0.0, ['rmask'])
                k.memset('pool', carry[:], 0.0, [f'carry{c_}' for c_ in range(14)])

                TA = SB(es, "TA", [128, 512])
                SGg = SB(es, "SGg", [128, 512])
                tmp = [SB(es, f"tmp{i}", [128, 512]) for i in range(9)]
                Gt = SB(es, "Gt", [128, 512])
                SQr = SB(es, "SQr", [128, 512])
                BSt = SB(es, "BSt", [128, 512])
                PCt = SB(es, "PCt", [128, 8])
                BDn = ('RT', 'AT', 'BT', 'KT', 'BH', 'KH', 'VT')
                BD = {n: SB(es, f"BD_{n}", [128, 8, 128]) for n in BDn}
                for n in BDn:
                    k.memset('pool', BD[n][:], 0.0, [f'BD_{n}'])
                    k.cp('dve', R(BD[n][:]), BD[n][:], [f'BD_{n}'], [f'BD_{n}'])
                TMn = ('A', 'BH', 'KH', 'V')
                TM = {n: SB(es, f"TM_{n}", [128, 4, 128]) for n in TMn}
                CMn = ('NTa', 'NTb', 'Na', 'Nb', 'ST', 'AKT', 'MRBT', 'MRKT', 'ApT', 'W2', 'Vp', 'U')
                CM = {n: SB(es, f"CM_{n}", [128, 4, 128]) for n in CMn}
                H = [SB(es, f"H{p}", [128, 128]) for p in range(4)]
                for p in range(4):
                    k.memset('pool', H[p][:], 0.0, [f'H{p}'])
                    k.cp('dve', R(H[p][:]), H[p][:], [f'H{p}'], [f'H{p}'])
                YT = SB(es, "YT", [128, 512])
                YO = SB(es, "YO", [128, 512], BF16)

                def bdwrite(eng, name, in0, in1, op, r):
                    for hh in range(2):
                        ps_ = slice(64 * hh, 64 * hh + 64)
                        o = R(BD[name][ps_, :, 64 * hh:64 * hh + 64])
                        a0 = in0[ps_, :].rearrange("p (c t) -> p c t", t=64)
                        if in1 is None:
                            k.cp(eng, o, a0, r, [f'BD_{name}'])
                        else:
                            a1 = in1[ps_, :].rearrange("p (c t) -> p c t", t=64)
                            k.tt(eng, o, a0, a1, op, r, [f'BD_{name}'])

                for b in range(NB):
                    t0 = 512 * b
                    rms_to_hT(es, x, t0, hT, 'nwb', 'a1')
                    for ct in range(14):
                        pb, pbn = PB()
                        for kc in range(8):
                            k.mm(pb[:], Win[:, kc, 128 * ct:128 * ct + 128], hT[:, kc, :], ['Win', 'hT'], [pbn],
                                 start=(kc == 0), stop=(kc == 7))
                        t1 = T1[ct % 2]
                        t1n = f'T1_{ct % 2}'
                        k.act(t1[:], pb[:], AF.Identity, [pbn, 'mu_cm'], [t1n], scale=mu_cm[:, ct:ct + 1])
                        xn_ = f'X{ct}'
                        k.stt(X[:, ct, 1:512], pb[:, 1:512], omm_cm[:, ct:ct + 1], t1[:, 0:511], ALU.mult, ALU.add,
                              [pbn, t1n, 'omm_cm'], [xn_ + 'a'])
                        k.stt(X[:, ct, 0:1], pb[:, 0:1], omm_cm[:, ct:ct + 1], carry[:, ct:ct + 1], ALU.mult, ALU.add,
                              [pbn, f'carry{ct}', 'omm_cm'], [xn_ + 'b'])
                        k.cp('pool', carry[:, ct:ct + 1], t1[:, 511:512], [t1n], [f'carry{ct}'])

                    def XR(ct):
                        return [f'X{ct}a', f'X{ct}b']

                    k.act(R(TA[0:64, :]), X[0:64, 12, :], AF.Tanh, XR(12), ['TAa'])
                    k.cp('pool', R(TA[64:128, :]), X[64:128, 12, :], XR(12), ['TAb'])
                    k.act(R(SGg[:]), X[:, 13, :], AF.Sigmoid, XR(13), ['SGg'])
                    for pr in range(4):
                        cs = slice(128 * pr, 128 * pr + 128)
                        Xr, Xk, Xv = X[:, pr, :], X[:, 4 + pr, :], X[:, 8 + pr, :]
                        rXr, rXk, rXv = XR(pr), XR(4 + pr), XR(8 + pr)
                        SIG, A_, KKN, KP, BV, L, E1, E2, E3 = tmp
                        tn = [f'tmp{i}' for i in range(9)]
                        pb, pbn = PB()
                        k.mm(pb[:], W2Z[:, cs], TA[:], ['W2Z', 'TAa', 'TAb'], [pbn], r32=True)
                        k.act(SIG[:], pb[:], AF.Sigmoid, [pbn, 'prm'], [tn[0]], bias=prm[:, 0, pr:pr + 1])
                        pb, pbn = PB()
                        k.mm(pb[:], A2Z[:, cs], TA[:], ['A2Z', 'TAa', 'TAb'], [pbn], r32=True)
                        k.act(A_[:], pb[:], AF.Sigmoid, [pbn, 'prm'], [tn[1]], bias=prm[:, 1, pr:pr + 1])
                        pb, pbn = PB()
                        k.mm(pb[:], G2[:, cs], SGg[:], ['G2', 'SGg'], [pbn], r32=True)
                        k.cp('act', Gt[:], pb[:], [pbn], ['Gt'])
                        k.ts('dve', KKN[:], Xk, prm[:, 2, pr:pr + 1], None, ALU.mult, None, rXk + ['prm'], [tn[2]])
                        k.act(R(SQr[:]), KKN[:], AF.Square, [tn[2]], ['SQr'])
                        pb, pbn = PB()
                        k.mm(pb[:], bones[:], SQr[:], ['bones', 'SQr'], [pbn], r32=True)
                        k.ts('dve', E1[:], pb[:], 1e-24, None, ALU.max, None, [pbn], [tn[6]])
                        k.act(E1[:], E1[:], AF.Sqrt, [tn[6]], [tn[6]])
                        k.recip(E1[:], E1[:], [tn[6]], [tn[6]])
                        k.tt('dve', KKN[:], KKN[:], E1[:], ALU.mult, [tn[2], tn[6]], [tn[2]])
                        k.ts('dve', KP[:], A_[:], prm[:, 3, pr:pr + 1], omka[:, pr:pr + 1], ALU.mult, ALU.add,
                             [tn[1], 'prm', 'omka'], [tn[3]])
                        k.tt('dve', KP[:], KP[:], Xk, ALU.mult, [tn[3]] + rXk, [tn[3]])
                        k.tt('pool', BV[:], KKN[:], A_[:], ALU.mult, [tn[2], tn[1]], [tn[4]])
                        k.stt(R(SQr[:]), Xr, prm[:, 4, pr:pr + 1], KP[:], ALU.mult, ALU.mult, rXr + ['prm', tn[3]], ['SQr'])
                        pb, pbn = PB()
                        k.mm(pb[:], bones[:], SQr[:], ['bones', 'SQr'], [pbn], r32=True)
                        k.tt('dve', BSt[:], pb[:], Xv, ALU.mult, [pbn] + rXv, ['BSt'])
                        k.ts('dve', SIG[:], SIG[:], NEG_E05, None, ALU.mult, None, [tn[0]], [tn[0]])
                        k.scan(L[:], rmask[:], SIG[:], ['rmask', tn[0]], [tn[5]])
                        k.act(E1[:], L[:], AF.Exp, [tn[5]], [tn[6]])
                        bdwrite('dve', 'RT', Xr, E1, ALU.mult, rXr + [tn[6]])
                        k.tt('pool', E2[:], L[:], SIG[:], ALU.subtract, [tn[5], tn[0]], [tn[7]])
                        k.act(E2[:], E2[:], AF.Exp, [tn[7]], [tn[7]])
                        k.ts('dve', E2[:], E2[:], -1.0, None, ALU.mult, None, [tn[7]], [tn[7]])
                        bdwrite('dve', 'AT', KKN, E2, ALU.mult, [tn[2], tn[7]])
                        k.act(E3[:], L[:], AF.Exp, [tn[5]], [tn[8]], scale=-1.0)
                        bdwrite('dve', 'BT', BV, E3, ALU.mult, [tn[4], tn[8]])
                        bdwrite('pool', 'KT', KP, E3, ALU.mult, [tn[3], tn[8]])
                        L3 = L[:].rearrange("p (c t) -> p c t", t=64)
                        k.tt('dve', E1[:].rearrange("p (c t) -> p c t", t=64), L3,
                             L3[:, :, 63:64].to_broadcast([128, 8, 64]), ALU.subtract, [tn[5]], [tn[6]])
                        k.act(E1[:], E1[:], AF.Exp, [tn[6]], [tn[6]], scale=-1.0)
                        bdwrite('dve', 'BH', BV, E1, ALU.mult, [tn[4], tn[6]])
                        bdwrite('pool', 'KH', KP, E1, ALU.mult, [tn[3], tn[6]])
                        k.act(PCt[:], L3[:, :, 63], AF.Exp, [tn[5]], ['PCt'])
                        bdwrite('pool', 'VT', Xv, None, None, rXv)

                        Hp, Hn = H[pr], f'H{pr}'
                        for g in range(2):
                            cl = [4 * g + i for i in range(4)]

                            def q4(pb_, i):
                                return pb_[:, 128 * i:128 * i + 128]

                            def v4(pb_):
                                return pb_[:].rearrange("p (i t) -> p i t", t=128)
                            for (ln, rn, mk, mkn, on) in (('BT', 'AT', mST4, 'mST4', 'NTa'), ('AT', 'BT', mS4, 'mS4', 'Na'),
                                                          ('KT', 'AT', mST4, 'mST4', 'AKT'), ('BT', 'RT', mIT4, 'mIT4', 'MRBT'),
                                                          ('KT', 'RT', mIT4, 'mIT4', 'MRKT')):
                                pb, pbn = PB()
                                for i, c in enumerate(cl):
                                    k.mm(q4(pb, i), BD[ln][:, c, :], BD[rn][:, c, :], [f'BD_{ln}', f'BD_{rn}'], [pbn], r32=True)
                                k.tt('dve', R(CM[on][:]), v4(pb), mk[:], ALU.mult, [pbn, mkn], ['CM_' + on])
                            for (src, dst) in (('AT', 'A'), ('BH', 'BH'), ('KH', 'KH'), ('VT', 'V')):
                                pb, pbn = PB()
                                for i, c in enumerate(cl):
                                    k.tr(q4(pb, i), BD[src][:, c, :], ident[:], [f'BD_{src}', 'ident'], [pbn])
                                k.cp('act', R(TM[dst][:]), v4(pb), [pbn], ['TM_' + dst])
                            k.tt('pool', R(CM['ST'][:]), CM['NTa'][:], ident4[:], ALU.add, ['CM_NTa', 'ident4'], ['CM_ST'])
                            curN, curNT = 'Na', 'NTa'
                            for lev in range(1, 6):
                                nxtN = 'Nb' if curN == 'Na' else 'Na'
                                nxtNT = 'NTb' if curNT == 'NTa' else 'NTa'
                                pb, pbn = PB()
                                for i in range(4):
                                    k.mm(q4(pb, i), CM[curNT][:, i, :], CM[curN][:, i, :], ['CM_' + curNT, 'CM_' + curN], [pbn], r32=True)
                                k.cp('act', R(CM[nxtN][:]), v4(pb), [pbn], ['CM_' + nxtN])
                                if lev < 5:
                                    pb, pbn = PB()
                                    for i in range(4):
                                        k.mm(q4(pb, i), CM[curN][:, i, :], CM[curNT][:, i, :], ['CM_' + curNT, 'CM_' + curN], [pbn], r32=True)
                                    k.cp('act', R(CM[nxtNT][:]), v4(pb), [pbn], ['CM_' + nxtNT])
                                pb, pbn = PB()
                                for i in range(4):
                                    k.mm(q4(pb, i), CM[nxtN][:, i, :], CM['ST'][:, i, :], ['CM_' + nxtN, 'CM_ST'], [pbn], r32=True)
                                k.tt('dve', R(CM['ST'][:]), v4(pb), CM['ST'][:], ALU.add, [pbn, 'CM_ST'], ['CM_ST'])
                                curN, curNT = nxtN, nxtNT
                            pb, pbn = PB()
                            for i in range(4):
                                k.mm(q4(pb, i), TM['A'][:, i, :], CM['ST'][:, i, :], ['TM_A', 'CM_ST'], [pbn], r32=True)
                            k.cp('act', R(CM['ApT'][:]), v4(pb), [pbn], ['CM_ApT'])
                            pb, pbn = PB()
                            for i in range(4):
                                k.mm(q4(pb, i), CM['AKT'][:, i, :], TM['V'][:, i, :], ['CM_AKT', 'TM_V'], [pbn], r32=True)
                            k.cp('act', R(CM['W2'][:]), v4(pb), [pbn], ['CM_W2'])
                            pb, pbn = PB()
                            for i in range(4):
                                k.mm(q4(pb, i), CM['ST'][:, i, :], CM['W2'][:, i, :], ['CM_ST', 'CM_W2'], [pbn], r32=True)
                            k.cp('act', R(CM['Vp'][:]), v4(pb), [pbn], ['CM_Vp'])
                            for i, c in enumerate(cl):
                                un = f'CM_U{i}'
                                pb, pbn = PB()
                                k.mm(q4(pb, 0), CM['ApT'][:, i, :], Hp[:], ['CM_ApT', Hn], [pbn], r32=True)
                                k.tt('dve', R(CM['U'][:, i, :]), q4(pb, 0), CM['Vp'][:, i, :], ALU.add, [pbn, 'CM_Vp'], [un])
                                pb, pbn = PB()
                                k.mm(q4(pb, 0), Hp[:], BD['RT'][:, c, :], [Hn, 'BD_RT'], [pbn], r32=True, start=True, stop=False)
                                k.mm(q4(pb, 0), CM['U'][:, i, :], CM['MRBT'][:, i, :], [un, 'CM_MRBT'], [pbn], r32=True,
                                     start=False, stop=False)
                                k.mm(q4(pb, 0), TM['V'][:, i, :], CM['MRKT'][:, i, :], ['TM_V', 'CM_MRKT'], [pbn], r32=True,
                                     start=False, stop=True)
                                for hh in range(2):
                                    ps_ = slice(64 * hh, 64 * hh + 64)
                                    k.cp('act', R(YT[ps_, 64 * c:64 * c + 64]), pb[ps_, 64 * hh:64 * hh + 64], [pbn], [f'YT{hh}'])
                                pb, pbn = PB()
                                k.mm(q4(pb, 0), TM['BH'][:, i, :], CM['U'][:, i, :], ['TM_BH', un], [pbn], r32=True,
                                     start=True, stop=False)
                                k.mm(q4(pb, 0), TM['KH'][:, i, :], TM['V'][:, i, :], ['TM_KH', 'TM_V'], [pbn], r32=True,
                                     start=False, stop=True)
                                k.stt(R(Hp[:]), Hp[:], PCt[:, c:c + 1], q4(pb, 0), ALU.mult, ALU.add, [Hn, 'PCt', pbn], [Hn])
                        rYT = ['YT0', 'YT1']
                        pb, pbn = PB()
                        k.mm(pb[:], bones[:], YT[:], ['bones'] + rYT, [pbn], r32=True)
                        k.stt(E1[:], pb[:], -1.0 / 64, YT[:], ALU.mult, ALU.add, [pbn] + rYT, [tn[6]])
                        k.act(R(SQr[:]), E1[:], AF.Square, [tn[6]], ['SQr'])
                        pb, pbn = PB()
                        k.mm(pb[:], bones[:], SQr[:], ['bones', 'SQr'], [pbn], r32=True)
                        k.act(E2[:], pb[:], AF.Sqrt, [pbn], [tn[7]], bias=64e-5, scale=1.0 / 64)
                        k.recip(E2[:], E2[:], [tn[7]], [tn[7]])
                        k.tt('dve', E1[:], E1[:], E2[:], ALU.mult, [tn[6], tn[7]], [tn[6]])
                        k.ts('dve', E1[:], E1[:], prm[:, 5, pr:pr + 1], prm[:, 6, pr:pr + 1], ALU.mult, ALU.add,
                             [tn[6], 'prm'], [tn[6]])
                        k.tt('dve', E1[:], E1[:], BSt[:], ALU.add, [tn[6], 'BSt'], [tn[6]])
                        k.tt('dve', YO[:], E1[:], Gt[:], ALU.mult, [tn[6], 'Gt'], ['YO'])
                        k.dma('sp', YR[pr, :, t0:t0 + 512], YO[:], ['YO'], ['YR'])
                S.wait_all('sp')
                S.flush()


        if 'A2' in phases:
            with contextlib.ExitStack() as esAB:
                QT = SB(esAB, "QT", [128, 4, T], BF16)
                KT_ = SB(esAB, "KTf", [128, 4, T], BF16)
                V1 = SB(esAB, "V1", [128, NT, 8, 65], BF16)
                LFs = SB(esAB, "LFs", [128, NT, 8])
                k.memset('pool', V1[:], 1.0, ['V1'])
                with contextlib.ExitStack() as es:
                    k.dma('sp', nwb[:], norm_mix_w.partition_broadcast(128), [], ['nwb'])
                    Wf = SB(es, "Wf", [128, 8, FOXC], BF16)
                    for kc in range(8):
                        k.dma('pool', Wf[:, kc, 0:1024], w_in[128 * kc:128 * kc + 128, RW:RW + 1024], [], ['Wf'])
                        k.dma('pool', Wf[:, kc, 1024:FOXC], w_in[128 * kc:128 * kc + 128, RW + 1024:RW + FOXC], [], ['Wf'])
                    xts = [SB(es, f"xt{i}", [128, D]) for i in range(2)]
                    junk = SB(es, "junk", [128, D], BF16)
                    ss = SB(es, "ss", [128, 1])
                    rs = SB(es, "rs", [128, 1])
                    xn = SB(es, "xn", [128, D], BF16)
                    hT = SB(es, "hT", [128, 8, 512], BF16)
                    qkw = SB(es, "qkw", [128, 2])
                    fbb = SB(es, "fbb", [128, 8])
                    sq = SB(es, "sq", [128, 512])
                    rq = SB(es, "rq", [128, 512])
                    SGt = [SB(es, f"SGt{i}", [128, 512], BF16) for i in range(2)]
                    zt = SB(es, "zt", [128, 8])
                    for hh in range(2):
                        k.dma('sp', qkw[64 * hh:64 * hh + 64, 0:1], fox_q_norm_w.rearrange("(p o) -> p o", o=1), [], ['qkw'])
                        k.dma('sp', qkw[64 * hh:64 * hh + 64, 1:2], fox_k_norm_w.rearrange("(p o) -> p o", o=1), [], ['qkw'])
                    k.dma('sp', fbb[:], fox_f_bias.partition_broadcast(128), [], ['fbb'])
                    for b in range(NB):
                        t0 = 512 * b
                        rms_to_hT(es, x, t0, hT, 'nwb', 'a2')
                        for ct in range(8):
                            pb, pbn = PB()
                            for kc in range(8):
                                k.mm(pb[:], Wf[:, kc, 128 * ct:128 * ct + 128], hT[:, kc, :], ['Wf', 'hT'], [pbn],
                                     start=(kc == 0), stop=(kc == 7))
                            k.act(R(sq[:]), pb[:], AF.Square, [pbn], ['sq'])
                            pb2, pbn2 = PB()
                            k.mm(pb2[:], bones[:], sq[:], ['bones', 'sq'], [pbn2], r32=True)
                            k.act(rq[:], pb2[:], AF.Sqrt, [pbn2], ['rq'], bias=1e-6, scale=1.0 / 64)
                            k.recip(rq[:], rq[:], ['rq'], ['rq'])
                            dst = QT if ct < 4 else KT_
                            dn_ = 'QT' if ct < 4 else 'KTf'
                            k.stt(dst[:, ct % 4, t0:t0 + 512], pb[:], qkw[:, (ct // 4):(ct // 4) + 1], rq[:], ALU.mult, ALU.mult,
                                  [pbn, 'qkw', 'rq'], [dn_])
                        for i in range(4):
                            ti = 4 * b + i
                            tsl = slice(128 * i, 128 * i + 128)
                            pb, pbn = PB()
                            for kc in range(8):
                                k.mm(pb[:], hT[:, kc, tsl], Wf[:, kc, 1024:1536], ['Wf', 'hT'], [pbn], start=(kc == 0), stop=(kc == 7))
                            k.cp('act', V1[:, ti, :, 0:64], pb[:].rearrange("p (h d) -> p h d", d=64), [pbn], ['V1'])
                            pb, pbn = PB()
                            for kc in range(8):
                                k.mm(pb[:], hT[:, kc, tsl], Wf[:, kc, 1536:2048], ['Wf', 'hT'], [pbn], start=(kc == 0), stop=(kc == 7))
                            sg, sgn = SGt[i % 2], f'SGt{i % 2}'
                            k.act(sg[:], pb[:], AF.Sigmoid, [pbn], [sgn])
                            k.dma('sp', SGd[t0 + 128 * i:t0 + 128 * i + 128, :], sg[:], [sgn], ['SGd'])
                            pb, pbn = PB()
                            for kc in range(8):
                                k.mm(pb[:, 0:8], hT[:, kc, tsl], Wf[:, kc, 2048:2056], ['Wf', 'hT'], [pbn], start=(kc == 0), stop=(kc == 7))
                            k.tt('dve', zt[:], pb[:, 0:8], fbb[:], ALU.add, [pbn, 'fbb'], ['zt'])
                            k.act(zt[:], zt[:], AF.Exp, ['zt'], ['zt'], scale=-1.0)
                            k.act(LFs[:, ti, :], zt[:], AF.Ln, ['zt'], ['LFs'], bias=1.0)
                    S.wait_all('sp')
                S.flush()
                with contextlib.ExitStack() as es:
                    tri = SB(es, "tri", [128, 128])
                    cmask = SB(es, "cmask", [128, 128], BF16)
                    NCk = SB(es, "NCk", [128, NT, 8])
                    TOT = SB(es, "TOT", [128, NT, 8])
                    CAR = SB(es, "CAR", [128, NT, 8])
                    NBq = SB(es, "NBq", [128, NT, 8])
                    ownb = SB(es, "ownb", [128, 64])
                    k.asel(tri[:], ones[:], [[1, 128]], ALU.is_ge, 0, -1, ['ones'], ['tri'])
                    k.cp('dve', cmask[:], tri[:], ['tri'], ['cmask'])
                    k.dma('sp', ownb[:], fox_o_norm_w.partition_broadcast(128), [], ['ownb'])
                    LF2 = LFs[:].rearrange("p t h -> p (t h)")
                    nchunk = (NT * 8 + 511) // 512
                    for cc in range(nchunk):
                        c0 = 512 * cc
                        c1 = min(NT * 8, c0 + 512)
                        pb, pbn = PB()
                        k.mm(pb[:, 0:c1 - c0], tri[:], LF2[:, c0:c1], ['tri', 'LFs'], [pbn])
                        k.cp('act', NCk[:].rearrange("p t h -> p (t h)")[:, c0:c1], pb[:, 0:c1 - c0], [pbn], ['NCk'])
                        pb, pbn = PB()
                        k.mm(pb[:, 0:c1 - c0], ones[:], LF2[:, c0:c1], ['ones', 'LFs'], [pbn])
                        k.cp('act', TOT[:].rearrange("p t h -> p (t h)")[:, c0:c1], pb[:, 0:c1 - c0], [pbn], ['TOT'])
                    k.memset('pool', CAR[:, 0, :], 0.0, ['CAR'])
                    for ti in range(1, NT):
                        k.tt('dve', CAR[:, ti, :], CAR[:, ti - 1, :], TOT[:, ti - 1, :], ALU.add, ['CAR', 'TOT'], ['CAR'])
                    k.tt('dve', NCk[:], NCk[:], CAR[:], ALU.add, ['NCk', 'CAR'], ['NCk'])
                    k.stt(NBq[:], TOT[:], 0.5, CAR[:], ALU.mult, ALU.add, ['TOT', 'CAR'], ['NBq'])
                    biasT = [SB(es, f"biasT{i}", [128, NT]) for i in range(2)]
                    PT = [SB(es, f"PT{i}", [128, 128], BF16) for i in range(8)]
                    RL = SB(es, "RL", [128, 8])
                    Ot = SB(es, "Ot", [128, 8, 64])
                    O2 = SB(es, "O2", [128, 8, 64])
                    ssq = SB(es, "ssq", [128, 8])
                    SGl = SB(es, "SGl", [128, 512], BF16)
                    YFt = SB(es, "YFt", [128, 512], BF16)
                    YFo = SB(es, "YFo", [128, 4, 128], BF16)
                    pS = [(pbig[i], f'pb{i}') for i in range(3)]
                    pO = [(pbig[3 + i], f'pb{3 + i}') for i in range(4)]
                    nS = 0
                    nP = 0
                    items = []
                    for qt in range(NT):
                        for h in range(8):
                            for kt0 in range(0, qt + 1, 4):
                                items.append((qt, h, list(range(kt0, min(qt + 1, kt0 + 4)))))

                    def emit_st(it):
                        qt, h, kts = it
                        qsl = slice(128 * qt, 128 * qt + 128)
                        pr, r0 = h // 2, 64 * (h % 2)
                        bt, btn = biasT[h % 2], f'biasT{h % 2}'
                        if kts[0] == 0:
                            k.ts('dve', bt[:, 0:qt + 1], NCk[:, 0:qt + 1, h], NBq[:, qt, h:h + 1], None, ALU.subtract, None,
                                 ['NCk', 'NBq'], [btn])
                        pb, pbn = pS[st['big'] % 3]
                        st['big'] += 1
                        for j, kt in enumerate(kts):
                            k.mm(pb[:, 128 * j:128 * j + 128], KT_[r0:r0 + 64, pr, 128 * kt:128 * kt + 128],
                                 QT[r0:r0 + 64, pr, qsl], ['KTf', 'QT'], [pbn])
                        return pb, pbn

                    def emit_pv(it, pb, pbn):
                        qt, h, kts = it
                        bt, btn = biasT[h % 2], f'biasT{h % 2}'
                        po, pon = pO[2 * (qt % 2) + h // 4]
                        pocol = 65 * (h % 4)
                        for j, kt in enumerate(kts):
                            pt, ptn = PT[st['q'] % 8], f"PT{st['q'] % 8}"
                            st['q'] += 1
                            k.act(pt[:], pb[:, 128 * j:128 * j + 128], AF.Exp, [pbn, btn], [ptn],
                                  bias=bt[:, kt:kt + 1], scale=0.125)
                            if kt == qt:
                                k.tt('pool', pt[:], pt[:], cmask[:], ALU.mult, [ptn, 'cmask'], [ptn])
                            k.mm(po[:, pocol:pocol + 65], pt[:], V1[:, kt, h, :], [ptn, 'V1'], [pon],
                                 start=(kt == 0), stop=(kt == qt))

                    def epilogue(qt):
                        qsl = slice(128 * qt, 128 * qt + 128)
                        for hf in range(2):
                            po, pon = pO[2 * (qt % 2) + hf]
                            po3 = po[:, 0:260].rearrange("p (h d) -> p h d", d=65)
                            k.recip(RL[:, 4 * hf:4 * hf + 4], po3[:, :, 64], [pon], [f'RL{hf}'])
                            k.tt('dve', Ot[:, 4 * hf:4 * hf + 4, :], po3[:, :, 0:64],
                                 RL[:, 4 * hf:4 * hf + 4].unsqueeze(2).to_broadcast([128, 4, 64]), ALU.mult,
                                 [pon, f'RL{hf}'], [f'Ot{hf}'])
                        rOt = ['Ot0', 'Ot1']
                        k.act(O2[:], Ot[:], AF.Square, rOt, ['O2'])
                        S.op('dve', lambda e: e.tensor_reduce(out=ssq[:], in_=O2[:], axis=AX.X, op=ALU.add), ['O2'], ['ssq'])
                        k.act(ssq[:], ssq[:], AF.Sqrt, ['ssq'], ['ssq'], bias=1e-6, scale=1.0 / 64)
                        k.recip(ssq[:], ssq[:], ['ssq'], ['ssq'])
                        k.tt('dve', O2[:], Ot[:], ssq[:].unsqueeze(2).to_broadcast([128, 8, 64]), ALU.mult, rOt + ['ssq'], ['O2'])
                        k.tt('pool', O2[:], O2[:], ownb[:].unsqueeze(1).to_broadcast([128, 8, 64]), ALU.mult, ['O2', 'ownb'], ['O2'])
                        k.dma('sp', SGl[:], SGd[qsl, :], ['SGd'], ['SGl'])
                        k.tt('dve', YFt[:], O2[:].rearrange("p h d -> p (h d)"), SGl[:], ALU.mult, ['O2', 'SGl'], ['YFt'])
                        for c4 in range(4):
                            k.tr(pst[:, 128 * c4:128 * c4 + 128], YFt[:, 128 * c4:128 * c4 + 128], identb[:], ['YFt', 'identb'], ['pst'])
                        k.cp('act', YFo[:], pst[:, 0:512].rearrange("p (c t) -> p c t", t=128), ['pst'], ['YFo'])
                        k.dma('sp', YF[:, :, qsl].rearrange("c p t -> p c t"), YFo[:], ['YFo'], ['YF'])
                    cur = emit_st(items[0])
                    for ii, it in enumerate(items):
                        nxt = emit_st(items[ii + 1]) if ii + 1 < len(items) else None
                        emit_pv(it, *cur)
                        cur = nxt
                        if it[1] == 7 and it[2][-1] == it[0]:
                            epilogue(it[0])
                    S.wait_all('sp')
                S.flush()

        if 'C1' in phases:
            with contextlib.ExitStack() as es:
                Wo = SB(es, "Wo", [128, 8, D], BF16)
                for kc in range(8):
                    k.dma('pool', Wo[:, kc, :], w_out[128 * kc:128 * kc + 128, :], [], ['Wo'])
                Yb = [SB(es, f"Yb{i}", [128, 8, 512], BF16) for i in range(2)]
                xts = [SB(es, f"xt{i}", [128, D]) for i in range(2)]
                x1t = [SB(es, f"x1t{i}", [128, D]) for i in range(2)]
                for b in range(NB):
                    t0 = 512 * b
                    yb, ybn = Yb[b % 2], f'Yb{b % 2}'
                    k.dma('sp', yb[:, 0:4, :], YR[:, :, t0:t0 + 512].rearrange("c p t -> p c t"), ['YR'], [ybn])
                    k.dma('sp', yb[:, 4:8, :], YF[:, :, t0:t0 + 512].rearrange("c p t -> p c t"), ['YF'], [ybn])
                    for i in range(4):
                        par = i % 2
                        tsl = slice(128 * i, 128 * i + 128)
                        k.dma('sp', xts[par][:], x[t0 + 128 * i:t0 + 128 * i + 128, :], [], [f'xt{par}'])
                        for hf in range(2):
                            pb, pbn = PB()
                            for kc in range(8):
                                k.mm(pb[:], yb[:, kc, tsl], Wo[:, kc, 512 * hf:512 * hf + 512], [ybn, 'Wo'], [pbn],
                                     start=(kc == 0), stop=(kc == 7))
                            k.tt('dve', x1t[par][:, 512 * hf:512 * hf + 512], pb[:], xts[par][:, 512 * hf:512 * hf + 512], ALU.add,
                                 [pbn, f'xt{par}'], [f'x1t{par}'])
                        k.dma('sp', X1[t0 + 128 * i:t0 + 128 * i + 128, :], x1t[par][:], [f'x1t{par}'], ['X1'])
                S.wait_all('sp')
                S.flush()

        if 'C2' in phases:
            with contextlib.ExitStack() as es:
                NF = 2 * DFF // 128
                Wup = SB(es, "Wup", [128, 8, 2 * DFF], BF16)
                Wd = SB(es, "Wd", [128, 22, D], BF16)
                for kc in range(8):
                    for c0 in range(0, 2 * DFF, 2048):
                        c1 = min(2 * DFF, c0 + 2048)
                        k.dma('pool', Wup[:, kc, c0:c1], ffn_w_up[128 * kc:128 * kc + 128, c0:c1], [], ['Wup'])
                for ft in range(22):
                    k.dma('pool', Wd[:, ft, :], ffn_w_down[128 * ft:128 * ft + 128, :], [], ['Wd'])
                nw2 = SB(es, "nw2", [128, D])
                nwf = SB(es, "nwf", [128, D])
                k.dma('sp', nw2[:], norm_ffn_w.partition_broadcast(128), [], ['nw2'])
                k.dma('sp', nwf[:], norm_final_w.partition_broadcast(128), [], ['nwf'])
                craw = SB(es, "craw", [128, 128])
                craw2 = SB(es, "craw2", [48, 128])
                cw = SB(es, "cw", [128, 176])
                k.dma('sp', craw[:], ffn_conv_w.rearrange("a (f p) -> (a f) p", p=128)[0:128, :], [], ['craw'])
                k.dma('sp', craw2[0:4, :], ffn_conv_w.rearrange("a (f p) -> (a f) p", p=128)[128:132, :], [], ['craw2'])
                k.dma('sp', craw2[4:48, :], ffn_conv_b.rearrange("(f p) -> f p", p=128), [], ['craw2'])
                pb, pbn = PB()
                k.tr(pb[:, 0:128], craw[:], ident[:], ['craw', 'ident'], [pbn])
                k.tr(pb[:, 128:176], craw2[:], ident[0:48, 0:48], ['craw2', 'ident'], [pbn])
                k.cp('act', cw[:], pb[:, 0:176], [pbn], ['cw'])
                x1s = [(SB(es, f"x1s{i}", [128, D]), f'x1s{i}') for i in range(2)]
                junk = SB(es, "junk", [128, D], BF16)
                ss = SB(es, "ss", [128, 1])
                rs = SB(es, "rs", [128, 1])
                xn = SB(es, "xn", [128, D], BF16)
                h2T = SB(es, "h2T", [128, 8, 256], BF16)
                Ub = [SB(es, f"Ub{i}", [128, 258]) for i in range(2)]
                HL = SB(es, "HL", [128, NF, 2])
                k.memset('pool', HL[:], 0.0, [f'HL{f}' for f in range(NF)])
                cva = [SB(es, f"cva{i}", [128, 256]) for i in range(2)]
                cvb = [SB(es, f"cvb{i}", [128, 256]) for i in range(2)]
                gsl = SB(es, "gsl", [128, 256])
                GT = SB(es, "GT", [128, 22, 256], BF16)
                x2 = SB(es, "x2", [128, D])
                ot = [SB(es, "ot", [128, D])] * 2

                def conv_tile(ft, slot):
                    pb, pbn = PB()
                    for kc in range(8):
                        k.mm(pb[:, 0:256], Wup[:, kc, 128 * ft:128 * ft + 128], h2T[:, kc, :], ['Wup', 'hT'], [pbn],
                             start=(kc == 0), stop=(kc == 7))
                    ub, ubn = Ub[slot], f'Ub{slot}'
                    k.cp('act', ub[:, 2:258], pb[:, 0:256], [pbn], [ubn + 'm'])
                    k.cp('pool', ub[:, 0:2], HL[:, ft, :], [f'HL{ft}'], [ubn + 'h'])
                    ur = [ubn + 'm', ubn + 'h']
                    ca, can = cva[slot], f'cva{slot}'
                    cb_, cbn = cvb[slot], f'cvb{slot}'
                    k.ts('dve', ca[:], ub[:, 2:258], cw[:, 88 + ft:89 + ft], cw[:, 132 + ft:133 + ft], ALU.mult, ALU.add,
                         ur + ['cw'], [can])
                    k.stt(cb_[:], ub[:, 1:257], cw[:, 44 + ft:45 + ft], ca[:], ALU.mult, ALU.add, ur + ['cw', can], [cbn])
                    k.stt(ca[:], ub[:, 0:256], cw[:, ft:ft + 1], cb_[:], ALU.mult, ALU.add, ur + ['cw', cbn], [can])
                    k.cp('pool', HL[:, ft, :], ub[:, 256:258], ur, [f'HL{ft}'])
                    return ca, can

                for b2 in range(T // 256):
                    t0 = 256 * b2
                    rms_to_hT(es, X1, t0, h2T, 'nw2', 'c2', ntile=2, xtl=x1s, nwt=nw2)
                    for ft in range(22):
                        ga, gan = conv_tile(ft, 0)
                        va, van = conv_tile(22 + ft, 1)
                        k.act(gsl[:], ga[:], AF.Silu, [gan], ['gsl'])
                        k.tt('dve', GT[:, ft, :], gsl[:], va[:], ALU.mult, ['gsl', van], ['GT'])
                    for i in range(2):
                        xs_, xsn = x1s[i]
                        for hf in range(2):
                            pb, pbn = PB()
                            for ft in range(22):
                                k.mm(pb[:], GT[:, ft, 128 * i:128 * i + 128], Wd[:, ft, 512 * hf:512 * hf + 512], ['GT', 'Wd'], [pbn],
                                     start=(ft == 0), stop=(ft == 21))
                            k.tt('dve', x2[:, 512 * hf:512 * hf + 512], pb[:], xs_[:, 512 * hf:512 * hf + 512], ALU.add,
                                 [pbn, xsn], [f'x2{hf}'])
                        rx2 = ['x20', 'x21']
                        k.act(junk[:], x2[:], AF.Square, rx2, ['junk', 'ss'], accum_out=ss[:])
                        k.act(ss[:], ss[:], AF.Sqrt, ['ss'], ['ss'], bias=1e-6, scale=1.0 / D)
                        k.recip(rs[:], ss[:], ['ss'], ['rs'])
                        k.stt(ot[i][:], x2[:], rs[:, 0:1], nwf[:], ALU.mult, ALU.mult, rx2 + ['rs', 'nwf'], ['ot'])
                        k.dma('sp', out[t0 + 128 * i:t0 + 128 * i + 128, :], ot[i][:], ['ot'], ['out'])
                S.wait_all('sp')
                S.flush()

        S.wait_all('sp')
        S.flush()
    return nc


_IN_NAMES = ["x", "norm_mix_w", "w_in", "rwkv_mu", "rwkv_w0", "rwkv_w2", "rwkv_a0", "rwkv_a2", "rwkv_g2",
             "rwkv_k_k", "rwkv_k_a", "rwkv_r_k", "rwkv_lnx_w", "rwkv_lnx_b", "fox_f_bias", "fox_q_norm_w",
             "fox_k_norm_w", "fox_o_norm_w", "w_out", "norm_ffn_w", "ffn_w_up", "ffn_conv_w", "ffn_conv_b",
             "ffn_w_down", "norm_final_w"]

_SHAPES = {"norm_mix_w": (1, D), "w_in": (D, RW + FOXC), "rwkv_mu": (RW,), "rwkv_w0": (512,), "rwkv_w2": (64, 512),
           "rwkv_a0": (512,), "rwkv_a2": (64, 512), "rwkv_g2": (128, 512), "rwkv_k_k": (512,), "rwkv_k_a": (512,),
           "rwkv_r_k": (512,), "rwkv_lnx_w": (512,), "rwkv_lnx_b": (512,), "fox_f_bias": (1, 8),
           "fox_q_norm_w": (64,), "fox_k_norm_w": (64,), "fox_o_norm_w": (1, 64), "w_out": (D, D),
           "norm_ffn_w": (1, D), "ffn_w_up": (D, 2 * DFF), "ffn_conv_w": (3, 2 * DFF), "ffn_conv_b": (2 * DFF,),
           "ffn_w_down": (DFF, D), "norm_final_w": (1, D)}


def make_in_maps(inputs, T, ncores):
    shared = {n: np.ascontiguousarray(np.asarray(inputs[n], dtype=np.float32).reshape(_SHAPES[n])) for n in _SHAPES}
    xs = np.asarray(inputs["x"], dtype=np.float32)
    maps = []
    for c in range(ncores):
        m = dict(shared)
        m["x"] = np.ascontiguousarray(xs[c, :T])
        maps.append(m)
    return maps


def kernel(**inputs):
    T = inputs["x"].shape[1]
    B = inputs["x"].shape[0]
    nc = build(T=T)
    res = run_bass_kernel_spmd(nc, make_in_maps(inputs, T, B), core_ids=list(range(B)))
    return np.stack([r["out"] for r in res.results], axis=0).astype(np.float32)
```

```python
import numpy as np
import concourse.bass as bass
import concourse.mybir as mybir
from concourse.bass_utils import run_bass_kernel_spmd

F32 = mybir.dt.float32
BF16 = mybir.dt.bfloat16
AF = mybir.ActivationFunctionType
ALU = mybir.AluOpType
AX = mybir.AxisListType

ENGS = ('pe', 'act', 'dve', 'pool', 'sp')


class Sched:
    def __init__(self, nc, esems, dsems):
        self.nc = nc
        self.esem = dict(zip(ENGS, esems))
        self.dsems = dsems
        self.cnt = {e: 0 for e in ENGS}
        self.stream = {e: [] for e in ENGS}
        self.seen = {e: {} for e in ENGS}
        self.dcount = [0] * len(dsems)
        self.dn = {'sp': 0, 'pool': 0, 'act': 0}
        nq = len(dsems) // 3
        self.dq = {'sp': list(range(0, nq)), 'pool': list(range(nq, 2 * nq)), 'act': list(range(2 * nq, 3 * nq))}
        self.res = {}

    def _deps(self, reads, writes):
        deps = {}
        def add(tok):
            if tok is None:
                return
            k, v = tok
            if deps.get(k, 0) < v:
                deps[k] = v
        for r in reads:
            st = self.res.get(r)
            if st:
                add(st[0])
        for w in writes:
            st = self.res.get(w)
            if st:
                add(st[0])
                for k, v in st[1].items():
                    add((k, v))
        return deps

    def _commit(self, tok, reads, writes):
        for r in reads:
            st = self.res.setdefault(r, [None, {}])
            if st[1].get(tok[0], 0) < tok[1]:
                st[1][tok[0]] = tok[1]
        for w in writes:
            self.res[w] = [tok, {}]

    def op(self, eng, fn, reads=(), writes=()):
        deps = self._deps(reads, writes)
        waits = []
        seen = self.seen[eng]
        for k, v in deps.items():
            if k == 'pe' and eng == 'pe':
                continue
            if seen.get(k, 0) >= v:
                continue
            seen[k] = v
            waits.append((k, v))
        self.cnt[eng] += 1
        tok = (eng, self.cnt[eng])
        self.stream[eng].append((waits, fn, (eng, 1)))
        self._commit(tok, reads, writes)

    def dma(self, eng, fn, reads=(), writes=()):
        deps = self._deps(reads, writes)
        q = self.dq[eng]
        k = q[self.dn[eng] % len(q)]
        self.dn[eng] += 1
        prev = 16 * self.dcount[k]
        self.dcount[k] += 1
        key = ('d', k)
        if prev > 0:
            if deps.get(key, 0) < prev:
                deps[key] = prev
        waits = []
        seen = self.seen[eng]
        for kk, v in deps.items():
            if seen.get(kk, 0) >= v:
                continue
            seen[kk] = v
            waits.append((kk, v))
        tok = (key, prev + 16)
        self.stream[eng].append((waits, fn, (key, 16)))
        self._commit(tok, reads, writes)

    def wait_all(self, eng):
        waits = []
        for e in ENGS:
            if self.cnt[e] > 0 and e != eng:
                waits.append((e, self.cnt[e]))
        for k in range(len(self.dsems)):
            if self.dcount[k] > 0:
                waits.append((('d', k), 16 * self.dcount[k]))
        self.stream[eng].append((waits, None, None))

    def _sem(self, key):
        if isinstance(key, tuple):
            return self.dsems[key[1]]
        return self.esem[key]

    def emit(self, eng, engine):
        for waits, fn, inc in self.stream[eng]:
            for k, v in waits:
                engine.wait_ge(self._sem(k), v)
            if fn is None:
                continue
            inst = fn(engine)
            inst.then_inc(self._sem(inc[0]), inc[1])

    def flush(self):
        self.emit_all()
        self.stream = {e: [] for e in ENGS}

    def emit_all(self):
        nc = self.nc
        with nc.Block() as block:
            @block.tensor
            def _(e):
                self.emit('pe', e)

            @block.scalar
            def _(e):
                self.emit('act', e)

            @block.vector
            def _(e):
                self.emit('dve', e)

            @block.gpsimd
            def _(e):
                self.emit('pool', e)

            @block.sync
            def _(e):
                self.emit('sp', e)


def _mk(eng):
    def f(self, fn, reads=(), writes=()):
        return self.op(eng, fn, reads, writes)
    return f


for _e in ('pe', 'act', 'dve', 'pool'):
    setattr(Sched, _e, _mk(_e))

import contextlib

D = 1024
RW = 1792
FOXC = 2056
DFF = 2816
NEG_E05 = -0.6065306597126334
F32R = mybir.dt.float32r


def R(ap):
    return ap.bitcast(F32R)


class K:
    def __init__(self, S):
        self.S = S

    def tt(self, eng, out, in0, in1, op, r, w):
        self.S.op(eng, lambda e: e.tensor_tensor(out=out, in0=in0, in1=in1, op=op), r, w)

    def ts(self, eng, out, in0, s1, s2, op0, op1, r, w):
        if s2 is None:
            self.S.op(eng, lambda e: e.tensor_scalar(out=out, in0=in0, scalar1=s1, scalar2=None, op0=op0), r, w)
        else:
            self.S.op(eng, lambda e: e.tensor_scalar(out=out, in0=in0, scalar1=s1, scalar2=s2, op0=op0, op1=op1), r, w)

    def stt(self, out, in0, scalar, in1, op0, op1, r, w):
        self.S.op('dve', lambda e: e.scalar_tensor_tensor(out=out, in0=in0, scalar=scalar, in1=in1, op0=op0, op1=op1), r, w)

    def act(self, out, in_, func, r, w, bias=None, scale=None, accum_out=None):
        kw = {}
        if bias is not None:
            kw['bias'] = bias
        if scale is not None:
            kw['scale'] = scale
        if accum_out is not None:
            kw['accum_out'] = accum_out
        self.S.op('act', lambda e: e.activation(out=out, in_=in_, func=func, **kw), r, w)

    def cp(self, eng, out, in_, r, w):
        if eng == 'act':
            self.S.op('act', lambda e: e.copy(out=out, in_=in_), r, w)
        else:
            self.S.op(eng, lambda e: e.tensor_copy(out=out, in_=in_), r, w)

    def mm(self, out, lhsT, rhs, r, w, start=True, stop=True, r32=False):
        if r32:
            lhsT, rhs = R(lhsT), R(rhs)
        self.S.op('pe', lambda e: e.matmul(out, lhsT=lhsT, rhs=rhs, start=start, stop=stop), r, w)

    def tr(self, out, in_, ident, r, w):
        self.S.op('pe', lambda e: e.transpose(out, in_, ident), r, w)

    def dma(self, q, out, in_, r, w, **kw):
        self.S.dma(q, lambda e: e.dma_start(out=out, in_=in_, **kw), r, w)

    def memset(self, eng, ap, val, w):
        self.S.op(eng, lambda e: e.memset(ap, val), [], w)

    def recip(self, out, in_, r, w):
        self.S.op('dve', lambda e: e.reciprocal(out=out, in_=in_), r, w)

    def asel(self, out, in_, pattern, op, base, cm, r, w):
        self.S.op('pool', lambda e: e.affine_select(out=out, in_=in_, pattern=pattern, compare_op=op,
                                                    fill=0.0, base=base, channel_multiplier=cm), r, w)

    def scan(self, out, d0, d1, r, w):
        self.S.op('dve', lambda e: e.tensor_tensor_scan(out=out, data0=d0, data1=d1, initial=0.0,
                                                        op0=ALU.mult, op1=ALU.add), r, w)


def build(T=4096, dbg=False, phases=('A1', 'A2', 'B', 'C1', 'C2')):
    nc = bass.Bass("TRN2", target_bir_lowering=False)
    NT = T // 128
    NB = T // 512
    din = {}

    def DI(name, shape):
        din[name] = nc.dram_tensor(name, list(shape), F32, kind="ExternalInput").ap()
        return din[name]

    x = DI("x", [T, D])
    norm_mix_w = DI("norm_mix_w", [1, D])
    w_in = DI("w_in", [D, RW + FOXC])
    rwkv_mu = DI("rwkv_mu", [RW])
    rwkv_w0 = DI("rwkv_w0", [512])
    rwkv_w2 = DI("rwkv_w2", [64, 512])
    rwkv_a0 = DI("rwkv_a0", [512])
    rwkv_a2 = DI("rwkv_a2", [64, 512])
    rwkv_g2 = DI("rwkv_g2", [128, 512])
    rwkv_k_k = DI("rwkv_k_k", [512])
    rwkv_k_a = DI("rwkv_k_a", [512])
    rwkv_r_k = DI("rwkv_r_k", [512])
    rwkv_lnx_w = DI("rwkv_lnx_w", [512])
    rwkv_lnx_b = DI("rwkv_lnx_b", [512])
    fox_f_bias = DI("fox_f_bias", [1, 8])
    fox_q_norm_w = DI("fox_q_norm_w", [64])
    fox_k_norm_w = DI("fox_k_norm_w", [64])
    fox_o_norm_w = DI("fox_o_norm_w", [1, 64])
    w_out = DI("w_out", [D, D])
    norm_ffn_w = DI("norm_ffn_w", [1, D])
    ffn_w_up = DI("ffn_w_up", [D, 2 * DFF])
    ffn_conv_w = DI("ffn_conv_w", [3, 2 * DFF])
    ffn_conv_b = DI("ffn_conv_b", [2 * DFF])
    ffn_w_down = DI("ffn_w_down", [DFF, D])
    norm_final_w = DI("norm_final_w", [1, D])
    out = nc.dram_tensor("out", [T, D], F32, kind="ExternalOutput").ap()

    okind = "ExternalOutput" if dbg else "Internal"
    YR = nc.dram_tensor("yr", [4, 128, T], BF16, kind=okind).ap()
    YF = nc.dram_tensor("yf", [4, 128, T], BF16, kind=okind).ap()
    SGd = nc.dram_tensor("sgd", [T, 512], BF16, kind="Internal").ap()
    X1 = nc.dram_tensor("x1", [T, D], F32, kind=okind).ap()

    with contextlib.ExitStack() as es0:
        esems = [es0.enter_context(nc.semaphore(f"es{i}")) for i in range(5)]
        dsems = [es0.enter_context(nc.semaphore(f"ds{i}")) for i in range(24)]
        S = Sched(nc, esems, dsems)
        k = K(S)

        uid = [0]

        def SB(es, name, shape, dt=F32):
            uid[0] += 1
            return es.enter_context(nc.sbuf_tensor(f"{name}_u{uid[0]}", list(shape), dt))

        pst = es0.enter_context(nc.psum_tensor("pst", [128, 1024], BF16))
        pbig = [es0.enter_context(nc.psum_tensor(f"pb{i}", [128, 512], F32)) for i in range(7)]
        st = {'big': 0, 'q': 0}

        def PB():
            i = st['big'] % 7
            st['big'] += 1
            return pbig[i], f"pb{i}"

        def PQ():
            i = st['q'] % 16
            st['q'] += 1
            return pqb[i // 4][:, (i % 4) * 128:(i % 4) * 128 + 128], f"pq{i}"

        def PQbank():
            return PQ()

        ones = SB(es0, "ones", [128, 128])
        ident = SB(es0, "ident", [128, 128])
        identb = SB(es0, "identb", [128, 128], BF16)
        mST = SB(es0, "mST", [128, 128])
        mIT = SB(es0, "mIT", [128, 128])
        mS = SB(es0, "mS", [128, 128])
        bones_raw = SB(es0, "bones_raw", [128, 128])
        bones = SB(es0, "bones", [128, 128])
        k.memset('pool', ones[:], 1.0, ['ones'])
        k.asel(ident[:], ones[:], [[-1, 128]], ALU.is_equal, 0, 1, ['ones'], ['ident'])
        k.cp('dve', identb[:], ident[:], ['ident'], ['identb'])
        for m_, nm in ((mST, 'mST'), (mIT, 'mIT'), (mS, 'mS'), (bones_raw, 'bones_raw')):
            k.memset('pool', m_[:], 0.0, [nm])
        for b in range(2):
            sl = slice(64 * b, 64 * b + 64)
            k.asel(mST[sl, sl], ones[sl, sl], [[1, 64]], ALU.is_ge, -1, -1, ['ones', 'mST'], ['mST'])
            k.asel(mIT[sl, sl], ones[sl, sl], [[1, 64]], ALU.is_ge, 0, -1, ['ones', 'mIT'], ['mIT'])
            k.asel(mS[sl, sl], ones[sl, sl], [[-1, 64]], ALU.is_ge, -1, 1, ['ones', 'mS'], ['mS'])
            k.cp('pool', bones_raw[sl, sl], ones[sl, sl], ['ones', 'bones_raw'], ['bones_raw'])

        k.cp('dve', R(bones[:]), bones_raw[:], ['bones_raw'], ['bones'])
        nwb = SB(es0, "nwb", [128, D])

        def rms_to_hT(es, xsrc, t0, hT, nwname, tagp, ntile=4, xtl=None, nwt=None, hTn='hT'):
            nwt_ = nwb if nwt is None else nwt
            for i in range(ntile):
                if xtl is None:
                    par = i % 2
                    xt, xtn = xts[par], f'xt{par}'
                else:
                    xt, xtn = xtl[i]
                k.dma('sp', xt[:], xsrc[t0 + 128 * i:t0 + 128 * i + 128, :], [], [xtn])
                k.act(xn[:], xt[:], AF.Square, [xtn], ['xn', 'ss'], accum_out=ss[:])
                k.act(ss[:], ss[:], AF.Sqrt, ['ss'], ['ss'], bias=1e-6, scale=1.0 / D)
                k.recip(rs[:], ss[:], ['ss'], ['rs'])
                k.stt(xn[:], xt[:], rs[:, 0:1], nwt_[:], ALU.mult, ALU.mult, [xtn, 'rs', nwname], ['xn'])
                for kk_ in range(8):
                    k.tr(pst[:, 128 * kk_:128 * kk_ + 128], xn[:, 128 * kk_:128 * kk_ + 128], identb[:],
                         ['xn', 'identb'], ['pst'])
                k.cp('act', hT[:, :, 128 * i:128 * i + 128], pst[:].rearrange("p (k t) -> p k t", t=128),
                     ['pst'], [hTn])

        if 'A1' in phases:
            with contextlib.ExitStack() as es:
                mST4 = SB(es, "mST4", [128, 4, 128])
                mIT4 = SB(es, "mIT4", [128, 4, 128])
                mS4 = SB(es, "mS4", [128, 4, 128])
                ident4 = SB(es, "ident4", [128, 4, 128])
                for i in range(4):
                    for src_, sn, dst_, dn_ in ((mST, 'mST', mST4, 'mST4'), (mIT, 'mIT', mIT4, 'mIT4'), (mS, 'mS', mS4, 'mS4'),
                                                (ident, 'ident', ident4, 'ident4')):
                        k.cp('pool', dst_[:, i, :], src_[:], [sn, dn_], [dn_])
                k.dma('sp', nwb[:], norm_mix_w.partition_broadcast(128), [], ['nwb'])
                Win = SB(es, "WinR", [128, 8, RW], BF16)
                for kc in range(8):
                    k.dma('pool', Win[:, kc, :], w_in[128 * kc:128 * kc + 128, 0:RW], [], ['Win'])
                xts = [SB(es, f"xt{i}", [128, D]) for i in range(2)]
                ss = SB(es, "ss", [128, 1])
                rs = SB(es, "rs", [128, 1])
                xn = SB(es, "xn", [128, D], BF16)
                hT = SB(es, "hT", [128, 8, 512], BF16)
                T1 = [SB(es, f"T1_{i}", [128, 512]) for i in range(2)]
                carry = SB(es, "carry", [128, 14])
                X = SB(es, "X", [128, 14, 512])
                mu_cm = SB(es, "mu_cm", [128, 14])
                omm_cm = SB(es, "omm_cm", [128, 14])
                prm = SB(es, "prm", [128, 7, 4])
                omka = SB(es, "omka", [128, 4])
                W2Z = SB(es, "W2Z", [128, 512])
                A2Z = SB(es, "A2Z", [128, 512])
                G2 = SB(es, "G2", [128, 512])
                Wraw = [SB(es, f"Wraw{i}", [128, 512]) for i in range(3)]
                rmask = SB(es, "rmask", [128, 512])
                k.dma('sp', mu_cm[:], rwkv_mu.rearrange("(t p) -> p t", p=128), [], ['mu_cm'], allow_slow_non_contiguous=True)
                for i, prm_in in enumerate((rwkv_w0, rwkv_a0, rwkv_k_k, rwkv_k_a, rwkv_r_k, rwkv_lnx_w, rwkv_lnx_b)):
                    k.dma('sp', prm[:, i, :], prm_in.rearrange("(t p) -> p t", p=128), [], ['prm'], allow_slow_non_contiguous=True)
                k.ts('dve', omm_cm[:], mu_cm[:], -1.0, 1.0, ALU.mult, ALU.add, ['mu_cm'], ['omm_cm'])
                k.ts('dve', omka[:], prm[:, 3, :], -1.0, 1.0, ALU.mult, ALU.add, ['prm'], ['omka'])
                k.memset('pool', Wraw[0][:], 0.0, ['Wraw0'])
                k.memset('pool', Wraw[1][:], 0.0, ['Wraw1'])
                k.dma('sp', Wraw[0][0:64, :], rwkv_w2, [], ['Wraw0'])
                k.dma('sp', Wraw[1][64:128, :], rwkv_a2, [], ['Wraw1'])
                k.dma('sp', Wraw[2][:], rwkv_g2, [], ['Wraw2'])
                for i_, (w_, wn_) in enumerate(((W2Z, 'W2Z'), (A2Z, 'A2Z'), (G2, 'G2'))):
                    k.cp('dve', R(w_[:]), Wraw[i_][:], [f'Wraw{i_}'], [wn_])
                k.memset('pool', rmask[:], 1.0, ['rmask'])
                k.memset('pool', rmask[:].rearrange("p (c t) -> p c t", t=64)[:, :, 0:1], 0.0, ['rmask'])
                k.memset('pool', carry[:], 0.0, [f'carry{c_}' for c_ in range(14)])

                TA = SB(es, "TA", [128, 512])
                SGg = SB(es, "SGg", [128, 512])
                tmp = [SB(es, f"tmp{i}", [128, 512]) for i in range(9)]
                Gt = SB(es, "Gt", [128, 512])
                SQr = SB(es, "SQr", [128, 512])
                BSt = SB(es, "BSt", [128, 512])
                PCt = SB(es, "PCt", [128, 8])
                BDn = ('RT', 'AT', 'BT', 'KT', 'BH', 'KH', 'VT')
                BD = {n: SB(es, f"BD_{n}", [128, 8, 128]) for n in BDn}
                for n in BDn:
                    k.memset('pool', BD[n][:], 0.0, [f'BD_{n}'])
                    k.cp('dve', R(BD[n][:]), BD[n][:], [f'BD_{n}'], [f'BD_{n}'])
                TMn = ('A', 'BH', 'KH', 'V')
                TM = {n: SB(es, f"TM_{n}", [128, 4, 128]) for n in TMn}
                CMn = ('NTa', 'NTb', 'Na', 'Nb', 'ST', 'AKT', 'MRBT', 'MRKT', 'ApT', 'W2', 'Vp', 'U')
                CM = {n: SB(es, f"CM_{n}", [128, 4, 128]) for n in CMn}
                H = [SB(es, f"H{p}", [128, 128]) for p in range(4)]
                for p in range(4):
                    k.memset('pool', H[p][:], 0.0, [f'H{p}'])
                    k.cp('dve', R(H[p][:]), H[p][:], [f'H{p}'], [f'H{p}'])
                YT = SB(es, "YT", [128, 512])
                YO = SB(es, "YO", [128, 512], BF16)

                def bdwrite(eng, name, in0, in1, op, r):
                    for hh in range(2):
                        ps_ = slice(64 * hh, 64 * hh + 64)
                        o = R(BD[name][ps_, :, 64 * hh:64 * hh + 64])
                        a0 = in0[ps_, :].rearrange("p (c t) -> p c t", t=64)
                        if in1 is None:
                            k.cp(eng, o, a0, r, [f'BD_{name}'])
                        else:
                            a1 = in1[ps_, :].rearrange("p (c t) -> p c t", t=64)
                            k.tt(eng, o, a0, a1, op, r, [f'BD_{name}'])

                for b in range(NB):
                    t0 = 512 * b
                    rms_to_hT(es, x, t0, hT, 'nwb', 'a1')
                    for ct in range(14):
                        pb, pbn = PB()
                        for kc in range(8):
                            k.mm(pb[:], Win[:, kc, 128 * ct:128 * ct + 128], hT[:, kc, :], ['Win', 'hT'], [pbn],
                                 start=(kc == 0), stop=(kc == 7))
                        t1 = T1[ct % 2]
                        t1n = f'T1_{ct % 2}'
                        k.act(t1[:], pb[:], AF.Identity, [pbn, 'mu_cm'], [t1n], scale=mu_cm[:, ct:ct + 1])
                        xn_ = f'X{ct}'
                        k.stt(X[:, ct, 1:512], pb[:, 1:512], omm_cm[:, ct:ct + 1], t1[:, 0:511], ALU.mult, ALU.add,
                              [pbn, t1n, 'omm_cm'], [xn_ + 'a'])
                        k.stt(X[:, ct, 0:1], pb[:, 0:1], omm_cm[:, ct:ct + 1], carry[:, ct:ct + 1], ALU.mult, ALU.add,
                              [pbn, f'carry{ct}', 'omm_cm'], [xn_ + 'b'])
                        k.cp('pool', carry[:, ct:ct + 1], t1[:, 511:512], [t1n], [f'carry{ct}'])

                    def XR(ct):
                        return [f'X{ct}a', f'X{ct}b']

                    k.act(R(TA[0:64, :]), X[0:64, 12, :], AF.Tanh, XR(12), ['TAa'])
                    k.cp('pool', R(TA[64:128, :]), X[64:128, 12, :], XR(12), ['TAb'])
                    k.act(R(SGg[:]), X[:, 13, :], AF.Sigmoid, XR(13), ['SGg'])
                    for pr in range(4):
                        cs = slice(128 * pr, 128 * pr + 128)
                        Xr, Xk, Xv = X[:, pr, :], X[:, 4 + pr, :], X[:, 8 + pr, :]
                        rXr, rXk, rXv = XR(pr), XR(4 + pr), XR(8 + pr)
                        SIG, A_, KKN, KP, BV, L, E1, E2, E3 = tmp
                        tn = [f'tmp{i}' for i in range(9)]
                        pb, pbn = PB()
                        k.mm(pb[:], W2Z[:, cs], TA[:], ['W2Z', 'TAa', 'TAb'], [pbn], r32=True)
                        k.act(SIG[:], pb[:], AF.Sigmoid, [pbn, 'prm'], [tn[0]], bias=prm[:, 0, pr:pr + 1])
                        pb, pbn = PB()
                        k.mm(pb[:], A2Z[:, cs], TA[:], ['A2Z', 'TAa', 'TAb'], [pbn], r32=True)
                        k.act(A_[:], pb[:], AF.Sigmoid, [pbn, 'prm'], [tn[1]], bias=prm[:, 1, pr:pr + 1])
                        pb, pbn = PB()
                        k.mm(pb[:], G2[:, cs], SGg[:], ['G2', 'SGg'], [pbn], r32=True)
                        k.cp('act', Gt[:], pb[:], [pbn], ['Gt'])
                        k.ts('dve', KKN[:], Xk, prm[:, 2, pr:pr + 1], None, ALU.mult, None, rXk + ['prm'], [tn[2]])
                        k.act(R(SQr[:]), KKN[:], AF.Square, [tn[2]], ['SQr'])
                        pb, pbn = PB()
                        k.mm(pb[:], bones[:], SQr[:], ['bones', 'SQr'], [pbn], r32=True)
                        k.ts('dve', E1[:], pb[:], 1e-24, None, ALU.max, None, [pbn], [tn[6]])
                        k.act(E1[:], E1[:], AF.Sqrt, [tn[6]], [tn[6]])
                        k.recip(E1[:], E1[:], [tn[6]], [tn[6]])
                        k.tt('dve', KKN[:], KKN[:], E1[:], ALU.mult, [tn[2], tn[6]], [tn[2]])
                        k.ts('dve', KP[:], A_[:], prm[:, 3, pr:pr + 1], omka[:, pr:pr + 1], ALU.mult, ALU.add,
                             [tn[1], 'prm', 'omka'], [tn[3]])
                        k.tt('dve', KP[:], KP[:], Xk, ALU.mult, [tn[3]] + rXk, [tn[3]])
                        k.tt('pool', BV[:], KKN[:], A_[:], ALU.mult, [tn[2], tn[1]], [tn[4]])
                        k.stt(R(SQr[:]), Xr, prm[:, 4, pr:pr + 1], KP[:], ALU.mult, ALU.mult, rXr + ['prm', tn[3]], ['SQr'])
                        pb, pbn = PB()
                        k.mm(pb[:], bones[:], SQr[:], ['bones', 'SQr'], [pbn], r32=True)
                        k.tt('dve', BSt[:], pb[:], Xv, ALU.mult, [pbn] + rXv, ['BSt'])
                        k.ts('dve', SIG[:], SIG[:], NEG_E05, None, ALU.mult, None, [tn[0]], [tn[0]])
                        k.scan(L[:], rmask[:], SIG[:], ['rmask', tn[0]], [tn[5]])
                        k.act(E1[:], L[:], AF.Exp, [tn[5]], [tn[6]])
                        bdwrite('dve', 'RT', Xr, E1, ALU.mult, rXr + [tn[6]])
                        k.tt('pool', E2[:], L[:], SIG[:], ALU.subtract, [tn[5], tn[0]], [tn[7]])
                        k.act(E2[:], E2[:], AF.Exp, [tn[7]], [tn[7]])
                        k.ts('dve', E2[:], E2[:], -1.0, None, ALU.mult, None, [tn[7]], [tn[7]])
                        bdwrite('dve', 'AT', KKN, E2, ALU.mult, [tn[2], tn[7]])
                        k.act(E3[:], L[:], AF.Exp, [tn[5]], [tn[8]], scale=-1.0)
                        bdwrite('dve', 'BT', BV, E3, ALU.mult, [tn[4], tn[8]])
                        bdwrite('pool', 'KT', KP, E3, ALU.mult, [tn[3], tn[8]])
                        L3 = L[:].rearrange("p (c t) -> p c t", t=64)
                        k.tt('dve', E1[:].rearrange("p (c t) -> p c t", t=64), L3,
                             L3[:, :, 63:64].to_broadcast([128, 8, 64]), ALU.subtract, [tn[5]], [tn[6]])
                        k.act(E1[:], E1[:], AF.Exp, [tn[6]], [tn[6]], scale=-1.0)
                        bdwrite('dve', 'BH', BV, E1, ALU.mult, [tn[4], tn[6]])
                        bdwrite('pool', 'KH', KP, E1, ALU.mult, [tn[3], tn[6]])
                        k.act(PCt[:], L3[:, :, 63], AF.Exp, [tn[5]], ['PCt'])
                        bdwrite('pool', 'VT', Xv, None, None, rXv)

                        Hp, Hn = H[pr], f'H{pr}'
                        for g in range(2):
                            cl = [4 * g + i for i in range(4)]

                            def q4(pb_, i):
                                return pb_[:, 128 * i:128 * i + 128]

                            def v4(pb_):
                                return pb_[:].rearrange("p (i t) -> p i t", t=128)
                            for (ln, rn, mk, mkn, on) in (('BT', 'AT', mST4, 'mST4', 'NTa'), ('AT', 'BT', mS4, 'mS4', 'Na'),
                                                          ('KT', 'AT', mST4, 'mST4', 'AKT'), ('BT', 'RT', mIT4, 'mIT4', 'MRBT'),
                                                          ('KT', 'RT', mIT4, 'mIT4', 'MRKT')):
                                pb, pbn = PB()
                                for i, c in enumerate(cl):
                                    k.mm(q4(pb, i), BD[ln][:, c, :], BD[rn][:, c, :], [f'BD_{ln}', f'BD_{rn}'], [pbn], r32=True)
                                k.tt('dve', R(CM[on][:]), v4(pb), mk[:], ALU.mult, [pbn, mkn], ['CM_' + on])
                            for (src, dst) in (('AT', 'A'), ('BH', 'BH'), ('KH', 'KH'), ('VT', 'V')):
                                pb, pbn = PB()
                                for i, c in enumerate(cl):
                                    k.tr(q4(pb, i), BD[src][:, c, :], ident[:], [f'BD_{src}', 'ident'], [pbn])
                                k.cp('act', R(TM[dst][:]), v4(pb), [pbn], ['TM_' + dst])
                            k.tt('pool', R(CM['ST'][:]), CM['NTa'][:], ident4[:], ALU.add, ['CM_NTa', 'ident4'], ['CM_ST'])
                            curN, curNT = 'Na', 'NTa'
                            for lev in range(1, 6):
                                nxtN = 'Nb' if curN == 'Na' else 'Na'
                                nxtNT = 'NTb' if curNT == 'NTa' else 'NTa'
                                pb, pbn = PB()
                                for i in range(4):
                                    k.mm(q4(pb, i), CM[curNT][:, i, :], CM[curN][:, i, :], ['CM_' + curNT, 'CM_' + curN], [pbn], r32=True)
                                k.cp('act', R(CM[nxtN][:]), v4(pb), [pbn], ['CM_' + nxtN])
                                if lev < 5:
                                    pb, pbn = PB()
                                    for i in range(4):
                                        k.mm(q4(pb, i), CM[curN][:, i, :], CM[curNT][:, i, :], ['CM_' + curNT, 'CM_' + curN], [pbn], r32=True)
                                    k.cp('act', R(CM[nxtNT][:]), v4(pb), [pbn], ['CM_' + nxtNT])
                                pb, pbn = PB()
                                for i in range(4):
                                    k.mm(q4(pb, i), CM[nxtN][:, i, :], CM['ST'][:, i, :], ['CM_' + nxtN, 'CM_ST'], [pbn], r32=True)
                                k.tt('dve', R(CM['ST'][:]), v4(pb), CM['ST'][:], ALU.add, [pbn, 'CM_ST'], ['CM_ST'])
                                curN, curNT = nxtN, nxtNT
                            pb, pbn = PB()
                            for i in range(4):
                                k.mm(q4(pb, i), TM['A'][:, i, :], CM['ST'][:, i, :], ['TM_A', 'CM_ST'], [pbn], r32=True)
                            k.cp('act', R(CM['ApT'][:]), v4(pb), [pbn], ['CM_ApT'])
                            pb, pbn = PB()
                            for i in range(4):
                                k.mm(q4(pb, i), CM['AKT'][:, i, :], TM['V'][:, i, :], ['CM_AKT', 'TM_V'], [pbn], r32=True)
                            k.cp('act', R(CM['W2'][:]), v4(pb), [pbn], ['CM_W2'])
                            pb, pbn = PB()
                            for i in range(4):
                                k.mm(q4(pb, i), CM['ST'][:, i, :], CM['W2'][:, i, :], ['CM_ST', 'CM_W2'], [pbn], r32=True)
                            k.cp('act', R(CM['Vp'][:]), v4(pb), [pbn], ['CM_Vp'])
                            for i, c in enumerate(cl):
                                un = f'CM_U{i}'
                                pb, pbn = PB()
                                k.mm(q4(pb, 0), CM['ApT'][:, i, :], Hp[:], ['CM_ApT', Hn], [pbn], r32=True)
                                k.tt('dve', R(CM['U'][:, i, :]), q4(pb, 0), CM['Vp'][:, i, :], ALU.add, [pbn, 'CM_Vp'], [un])
                                pb, pbn = PB()
                                k.mm(q4(pb, 0), Hp[:], BD['RT'][:, c, :], [Hn, 'BD_RT'], [pbn], r32=True, start=True, stop=False)
                                k.mm(q4(pb, 0), CM['U'][:, i, :], CM['MRBT'][:, i, :], [un, 'CM_MRBT'], [pbn], r32=True,
                                     start=False, stop=False)
                                k.mm(q4(pb, 0), TM['V'][:, i, :], CM['MRKT'][:, i, :], ['TM_V', 'CM_MRKT'], [pbn], r32=True,
                                     start=False, stop=True)
                                for hh in range(2):
                                    ps_ = slice(64 * hh, 64 * hh + 64)
                                    k.cp('act', R(YT[ps_, 64 * c:64 * c + 64]), pb[ps_, 64 * hh:64 * hh + 64], [pbn], [f'YT{hh}'])
                                pb, pbn = PB()
                                k.mm(q4(pb, 0), TM['BH'][:, i, :], CM['U'][:, i, :], ['TM_BH', un], [pbn], r32=True,
                                     start=True, stop=False)
                                k.mm(q4(pb, 0), TM['KH'][:, i, :], TM['V'][:, i, :], ['TM_KH', 'TM_V'], [pbn], r32=True,
                                     start=False, stop=True)
                                k.stt(R(Hp[:]), Hp[:], PCt[:, c:c + 1], q4(pb, 0), ALU.mult, ALU.add, [Hn, 'PCt', pbn], [Hn])
                        rYT = ['YT0', 'YT1']
                        pb, pbn = PB()
                        k.mm(pb[:], bones[:], YT[:], ['bones'] + rYT, [pbn], r32=True)
                        k.stt(E1[:], pb[:], -1.0 / 64, YT[:], ALU.mult, ALU.add, [pbn] + rYT, [tn[6]])
                        k.act(R(SQr[:]), E1[:], AF.Square, [tn[6]], ['SQr'])
                        pb, pbn = PB()
                        k.mm(pb[:], bones[:], SQr[:], ['bones', 'SQr'], [pbn], r32=True)
                        k.act(E2[:], pb[:], AF.Sqrt, [pbn], [tn[7]], bias=64e-5, scale=1.0 / 64)
                        k.recip(E2[:], E2[:], [tn[7]], [tn[7]])
                        k.tt('dve', E1[:], E1[:], E2[:], ALU.mult, [tn[6], tn[7]], [tn[6]])
                        k.ts('dve', E1[:], E1[:], prm[:, 5, pr:pr + 1], prm[:, 6, pr:pr + 1], ALU.mult, ALU.add,
                             [tn[6], 'prm'], [tn[6]])
                        k.tt('dve', E1[:], E1[:], BSt[:], ALU.add, [tn[6], 'BSt'], [tn[6]])
                        k.tt('dve', YO[:], E1[:], Gt[:], ALU.mult, [tn[6], 'Gt'], ['YO'])
                        k.dma('sp', YR[pr, :, t0:t0 + 512], YO[:], ['YO'], ['YR'])
                S.wait_all('sp')
                S.flush()


        if 'A2' in phases:
            with contextlib.ExitStack() as esAB:
                QT = SB(esAB, "QT", [128, 4, T], BF16)
                KT_ = SB(esAB, "KTf", [128, 4, T], BF16)
                V1 = SB(esAB, "V1", [128, NT, 8, 65], BF16)
                LFs = SB(esAB, "LFs", [128, NT, 8])
                k.memset('pool', V1[:], 1.0, ['V1'])
                with contextlib.ExitStack() as es:
                    k.dma('sp', nwb[:], norm_mix_w.partition_broadcast(128), [], ['nwb'])
                    Wf = SB(es, "Wf", [128, 8, FOXC], BF16)
                    for kc in range(8):
                        k.dma('pool', Wf[:, kc, 0:1024], w_in[128 * kc:128 * kc + 128, RW:RW + 1024], [], ['Wf'])
                        k.dma('pool', Wf[:, kc, 1024:FOXC], w_in[128 * kc:128 * kc + 128, RW + 1024:RW + FOXC], [], ['Wf'])
                    xts = [SB(es, f"xt{i}", [128, D]) for i in range(2)]
                    ss = SB(es, "ss", [128, 1])
                    rs = SB(es, "rs", [128, 1])
                    xn = SB(es, "xn", [128, D], BF16)
                    hT = SB(es, "hT", [128, 8, 512], BF16)
                    qkw = SB(es, "qkw", [128, 2])
                    fbb = SB(es, "fbb", [128, 8])
                    sq = SB(es, "sq", [128, 512])
                    rq = SB(es, "rq", [128, 512])
                    SGt = [SB(es, f"SGt{i}", [128, 512], BF16) for i in range(2)]
                    zt = SB(es, "zt", [128, 8])
                    for hh in range(2):
                        k.dma('sp', qkw[64 * hh:64 * hh + 64, 0:1], fox_q_norm_w.rearrange("(p o) -> p o", o=1), [], ['qkw'])
                        k.dma('sp', qkw[64 * hh:64 * hh + 64, 1:2], fox_k_norm_w.rearrange("(p o) -> p o", o=1), [], ['qkw'])
                    k.dma('sp', fbb[:], fox_f_bias.partition_broadcast(128), [], ['fbb'])
                    for b in range(NB):
                        t0 = 512 * b
                        rms_to_hT(es, x, t0, hT, 'nwb', 'a2')
                        for ct in range(8):
                            pb, pbn = PB()
                            for kc in range(8):
                                k.mm(pb[:], Wf[:, kc, 128 * ct:128 * ct + 128], hT[:, kc, :], ['Wf', 'hT'], [pbn],
                                     start=(kc == 0), stop=(kc == 7))
                            k.act(R(sq[:]), pb[:], AF.Square, [pbn], ['sq'])
                            pb2, pbn2 = PB()
                            k.mm(pb2[:], bones[:], sq[:], ['bones', 'sq'], [pbn2], r32=True)
                            k.act(rq[:], pb2[:], AF.Sqrt, [pbn2], ['rq'], bias=1e-6, scale=1.0 / 64)
                            k.recip(rq[:], rq[:], ['rq'], ['rq'])
                            dst = QT if ct < 4 else KT_
                            dn_ = 'QT' if ct < 4 else 'KTf'
                            k.stt(dst[:, ct % 4, t0:t0 + 512], pb[:], qkw[:, (ct // 4):(ct // 4) + 1], rq[:], ALU.mult, ALU.mult,
                                  [pbn, 'qkw', 'rq'], [dn_])
                        for i in range(4):
                            ti = 4 * b + i
                            tsl = slice(128 * i, 128 * i + 128)
                            pb, pbn = PB()
                            for kc in range(8):
                                k.mm(pb[:], hT[:, kc, tsl], Wf[:, kc, 1024:1536], ['Wf', 'hT'], [pbn], start=(kc == 0), stop=(kc == 7))
                            k.cp('act', V1[:, ti, :, 0:64], pb[:].rearrange("p (h d) -> p h d", d=64), [pbn], ['V1'])
                            pb, pbn = PB()
                            for kc in range(8):
                                k.mm(pb[:], hT[:, kc, tsl], Wf[:, kc, 1536:2048], ['Wf', 'hT'], [pbn], start=(kc == 0), stop=(kc == 7))
                            sg, sgn = SGt[i % 2], f'SGt{i % 2}'
                            k.act(sg[:], pb[:], AF.Sigmoid, [pbn], [sgn])
                            k.dma('sp', SGd[t0 + 128 * i:t0 + 128 * i + 128, :], sg[:], [sgn], ['SGd'])
                            pb, pbn = PB()
                            for kc in range(8):
                                k.mm(pb[:, 0:8], hT[:, kc, tsl], Wf[:, kc, 2048:2056], ['Wf', 'hT'], [pbn], start=(kc == 0), stop=(kc == 7))
                            k.tt('dve', zt[:], pb[:, 0:8], fbb[:], ALU.add, [pbn, 'fbb'], ['zt'])
                            k.act(zt[:], zt[:], AF.Exp, ['zt'], ['zt'], scale=-1.0)
                            k.act(LFs[:, ti, :], zt[:], AF.Ln, ['zt'], ['LFs'], bias=1.0)
                    S.wait_all('sp')
                S.flush()
                with contextlib.ExitStack() as es:
                    tri = SB(es, "tri", [128, 128])
                    cmask = SB(es, "cmask", [128, 128], BF16)
                    NCk = SB(es, "NCk", [128, NT, 8])
                    TOT = SB(es, "TOT", [128, NT, 8])
                    CAR = SB(es, "CAR", [128, NT, 8])
                    NBq = SB(es, "NBq", [128, NT, 8])
                    ownb = SB(es, "ownb", [128, 64])
                    k.asel(tri[:], ones[:], [[1, 128]], ALU.is_ge, 0, -1, ['ones'], ['tri'])
                    k.cp('dve', cmask[:], tri[:], ['tri'], ['cmask'])
                    k.dma('sp', ownb[:], fox_o_norm_w.partition_broadcast(128), [], ['ownb'])
                    LF2 = LFs[:].rearrange("p t h -> p (t h)")
                    nchunk = (NT * 8 + 511) // 512
                    for cc in range(nchunk):
                        c0 = 512 * cc
                        c1 = min(NT * 8, c0 + 512)
                        pb, pbn = PB()
                        k.mm(pb[:, 0:c1 - c0], tri[:], LF2[:, c0:c1], ['tri', 'LFs'], [pbn])
                        k.cp('act', NCk[:].rearrange("p t h -> p (t h)")[:, c0:c1], pb[:, 0:c1 - c0], [pbn], ['NCk'])
                        pb, pbn = PB()
                        k.mm(pb[:, 0:c1 - c0], ones[:], LF2[:, c0:c1], ['ones', 'LFs'], [pbn])
                        k.cp('act', TOT[:].rearrange("p t h -> p (t h)")[:, c0:c1], pb[:, 0:c1 - c0], [pbn], ['TOT'])
                    k.memset('pool', CAR[:, 0, :], 0.0, ['CAR'])
                    for ti in range(1, NT):
                        k.tt('dve', CAR[:, ti, :], CAR[:, ti - 1, :], TOT[:, ti - 1, :], ALU.add, ['CAR', 'TOT'], ['CAR'])
                    k.tt('dve', NCk[:], NCk[:], CAR[:], ALU.add, ['NCk', 'CAR'], ['NCk'])
                    k.stt(NBq[:], TOT[:], 0.5, CAR[:], ALU.mult, ALU.add, ['TOT', 'CAR'], ['NBq'])
                    biasT = [SB(es, f"biasT{i}", [128, NT]) for i in range(2)]
                    PT = [SB(es, f"PT{i}", [128, 128], BF16) for i in range(8)]
                    RL = SB(es, "RL", [128, 8])
                    Ot = SB(es, "Ot", [128, 8, 64])
                    O2 = SB(es, "O2", [128, 8, 64])
                    ssq = SB(es, "ssq", [128, 8])
                    SGl = SB(es, "SGl", [128, 512], BF16)
                    YFt = SB(es, "YFt", [128, 512], BF16)
                    YFo = SB(es, "YFo", [128, 4, 128], BF16)
                    pS = [(pbig[i], f'pb{i}') for i in range(3)]
                    pO = [(pbig[3 + i], f'pb{3 + i}') for i in range(4)]
                    nS = 0
                    nP = 0
                    items = []
                    for qt in range(NT):
                        for h in range(8):
                            for kt0 in range(0, qt + 1, 4):
                                items.append((qt, h, list(range(kt0, min(qt + 1, kt0 + 4)))))

                    def emit_st(it):
                        qt, h, kts = it
                        qsl = slice(128 * qt, 128 * qt + 128)
                        pr, r0 = h // 2, 64 * (h % 2)
                        bt, btn = biasT[h % 2], f'biasT{h % 2}'
                        if kts[0] == 0:
                            k.ts('dve', bt[:, 0:qt + 1], NCk[:, 0:qt + 1, h], NBq[:, qt, h:h + 1], None, ALU.subtract, None,
                                 ['NCk', 'NBq'], [btn])
                        pb, pbn = pS[st['big'] % 3]
                        st['big'] += 1
                        for j, kt in enumerate(kts):
                            k.mm(pb[:, 128 * j:128 * j + 128], KT_[r0:r0 + 64, pr, 128 * kt:128 * kt + 128],
                                 QT[r0:r0 + 64, pr, qsl], ['KTf', 'QT'], [pbn])
                        return pb, pbn

                    def emit_pv(it, pb, pbn):
                        qt, h, kts = it
                        bt, btn = biasT[h % 2], f'biasT{h % 2}'
                        po, pon = pO[2 * (qt % 2) + h // 4]
                        pocol = 65 * (h % 4)
                        for j, kt in enumerate(kts):
                            pt, ptn = PT[st['q'] % 8], f"PT{st['q'] % 8}"
                            st['q'] += 1
                            k.act(pt[:], pb[:, 128 * j:128 * j + 128], AF.Exp, [pbn, btn], [ptn],
                                  bias=bt[:, kt:kt + 1], scale=0.125)
                            if kt == qt:
                                k.tt('pool', pt[:], pt[:], cmask[:], ALU.mult, [ptn, 'cmask'], [ptn])
                            k.mm(po[:, pocol:pocol + 65], pt[:], V1[:, kt, h, :], [ptn, 'V1'], [pon],
                                 start=(kt == 0), stop=(kt == qt))

                    def epilogue(qt):
                        qsl = slice(128 * qt, 128 * qt + 128)
                        for hf in range(2):
                            po, pon = pO[2 * (qt % 2) + hf]
                            po3 = po[:, 0:260].rearrange("p (h d) -> p h d", d=65)
                            k.recip(RL[:, 4 * hf:4 * hf + 4], po3[:, :, 64], [pon], [f'RL{hf}'])
                            k.tt('dve', Ot[:, 4 * hf:4 * hf + 4, :], po3[:, :, 0:64],
                                 RL[:, 4 * hf:4 * hf + 4].unsqueeze(2).to_broadcast([128, 4, 64]), ALU.mult,
                                 [pon, f'RL{hf}'], [f'Ot{hf}'])
                        rOt = ['Ot0', 'Ot1']
                        k.act(O2[:], Ot[:], AF.Square, rOt, ['O2'])
                        S.op('dve', lambda e: e.tensor_reduce(out=ssq[:], in_=O2[:], axis=AX.X, op=ALU.add), ['O2'], ['ssq'])
                        k.act(ssq[:], ssq[:], AF.Sqrt, ['ssq'], ['ssq'], bias=1e-6, scale=1.0 / 64)
                        k.recip(ssq[:], ssq[:], ['ssq'], ['ssq'])
                        k.tt('dve', O2[:], Ot[:], ssq[:].unsqueeze(2).to_broadcast([128, 8, 64]), ALU.mult, rOt + ['ssq'], ['O2'])
                        k.tt('pool', O2[:], O2[:], ownb[:].unsqueeze(1).to_broadcast([128, 8, 64]), ALU.mult, ['O2', 'ownb'], ['O2'])
                        k.dma('sp', SGl[:], SGd[qsl, :], ['SGd'], ['SGl'])
                        k.tt('dve', YFt[:], O2[:].rearrange("p h d -> p (h d)"), SGl[:], ALU.mult, ['O2', 'SGl'], ['YFt'])
                        for c4 in range(4):
                            k.tr(pst[:, 128 * c4:128 * c4 + 128], YFt[:, 128 * c4:128 * c4 + 128], identb[:], ['YFt', 'identb'], ['pst'])
                        k.cp('act', YFo[:], pst[:, 0:512].rearrange("p (c t) -> p c t", t=128), ['pst'], ['YFo'])
                        k.dma('sp', YF[:, :, qsl].rearrange("c p t -> p c t"), YFo[:], ['YFo'], ['YF'])
                    cur = emit_st(items[0])
                    for ii, it in enumerate(items):
                        nxt = emit_st(items[ii + 1]) if ii + 1 < len(items) else None
                        emit_pv(it, *cur)
                        cur = nxt
                        if it[1] == 7 and it[2][-1] == it[0]:
                            epilogue(it[0])
                    S.wait_all('sp')
                S.flush()

        esC = es0.enter_context(contextlib.ExitStack())
        if 'C2' in phases:
            Wup = SB(esC, "Wup", [128, 8, 2 * DFF], BF16)
            Wd = SB(esC, "Wd", [128, 22, D], BF16)
        if 'C1' in phases:
            with contextlib.ExitStack() as es:
                Wo = SB(es, "Wo", [128, 8, D], BF16)
                for kc in range(8):
                    k.dma('pool', Wo[:, kc, :], w_out[128 * kc:128 * kc + 128, :], [], ['Wo'])
                if 'C2' in phases:
                    for kc in range(8):
                        for c0 in range(0, 2 * DFF, 2048):
                            c1 = min(2 * DFF, c0 + 2048)
                            k.dma('pool', Wup[:, kc, c0:c1], ffn_w_up[128 * kc:128 * kc + 128, c0:c1], [], ['Wup'])
                    for ft in range(22):
                        k.dma('pool', Wd[:, ft, :], ffn_w_down[128 * ft:128 * ft + 128, :], [], ['Wd'])
                Yb = [SB(es, f"Yb{i}", [128, 8, 512], BF16) for i in range(2)]
                xts = [SB(es, f"xt{i}", [128, D]) for i in range(2)]
                x1t = [SB(es, f"x1t{i}", [128, D]) for i in range(2)]
                for b in range(NB):
                    t0 = 512 * b
                    yb, ybn = Yb[b % 2], f'Yb{b % 2}'
                    k.dma('sp', yb[:, 0:4, :], YR[:, :, t0:t0 + 512].rearrange("c p t -> p c t"), ['YR'], [ybn])
                    k.dma('sp', yb[:, 4:8, :], YF[:, :, t0:t0 + 512].rearrange("c p t -> p c t"), ['YF'], [ybn])
                    for i in range(4):
                        par = i % 2
                        tsl = slice(128 * i, 128 * i + 128)
                        k.dma('sp', xts[par][:], x[t0 + 128 * i:t0 + 128 * i + 128, :], [], [f'xt{par}'])
                        for hf in range(2):
                            pb, pbn = PB()
                            for kc in range(8):
                                k.mm(pb[:], yb[:, kc, tsl], Wo[:, kc, 512 * hf:512 * hf + 512], [ybn, 'Wo'], [pbn],
                                     start=(kc == 0), stop=(kc == 7))
                            k.tt('dve', x1t[par][:, 512 * hf:512 * hf + 512], pb[:], xts[par][:, 512 * hf:512 * hf + 512], ALU.add,
                                 [pbn, f'xt{par}'], [f'x1t{par}'])
                        k.dma('sp', X1[t0 + 128 * i:t0 + 128 * i + 128, :], x1t[par][:], [f'x1t{par}'], ['X1'])
                S.wait_all('sp')
                S.flush()

        if 'C2' in phases:
            with contextlib.ExitStack() as es:
                NF = 2 * DFF // 128
                if 'C1' not in phases:
                    for kc in range(8):
                        for c0 in range(0, 2 * DFF, 2048):
                            c1 = min(2 * DFF, c0 + 2048)
                            k.dma('pool', Wup[:, kc, c0:c1], ffn_w_up[128 * kc:128 * kc + 128, c0:c1], [], ['Wup'])
                    for ft in range(22):
                        k.dma('pool', Wd[:, ft, :], ffn_w_down[128 * ft:128 * ft + 128, :], [], ['Wd'])
                nw2 = SB(es, "nw2", [128, D])
                nwf = SB(es, "nwf", [128, D])
                k.dma('sp', nw2[:], norm_ffn_w.partition_broadcast(128), [], ['nw2'])
                k.dma('sp', nwf[:], norm_final_w.partition_broadcast(128), [], ['nwf'])
                craw = SB(es, "craw", [128, 128])
                craw2 = SB(es, "craw2", [48, 128])
                cw = SB(es, "cw", [128, 176])
                k.dma('sp', craw[:], ffn_conv_w.rearrange("a (f p) -> (a f) p", p=128)[0:128, :], [], ['craw'])
                k.dma('sp', craw2[0:4, :], ffn_conv_w.rearrange("a (f p) -> (a f) p", p=128)[128:132, :], [], ['craw2'])
                k.dma('sp', craw2[4:48, :], ffn_conv_b.rearrange("(f p) -> f p", p=128), [], ['craw2'])
                pb, pbn = PB()
                k.tr(pb[:, 0:128], craw[:], ident[:], ['craw', 'ident'], [pbn])
                k.tr(pb[:, 128:176], craw2[:], ident[0:48, 0:48], ['craw2', 'ident'], [pbn])
                k.cp('act', cw[:], pb[:, 0:176], [pbn], ['cw'])
                x1sA = [[(SB(es, f"x1s{p}{i}", [128, D]), f'x1s{p}{i}') for i in range(2)] for p in range(2)]
                ss = SB(es, "ss", [128, 1])
                rs = SB(es, "rs", [128, 1])
                xn = SB(es, "xn", [128, D], BF16)
                h2Ts = [SB(es, f"h2T{p}", [128, 8, 256], BF16) for p in range(2)]
                Ub = [SB(es, f"Ub{i}", [128, 258]) for i in range(2)]
                HL = SB(es, "HL", [128, NF, 2])
                k.memset('pool', HL[:], 0.0, [f'HL{f}' for f in range(NF)])
                cva = [SB(es, f"cva{i}", [128, 256]) for i in range(4)]
                cvb = [SB(es, f"cvb{i}", [128, 256]) for i in range(2)]
                gsl = [SB(es, f"gsl{i}", [128, 256]) for i in range(2)]
                GT = SB(es, "GT", [128, 22, 256], BF16)
                ot = [SB(es, "ot", [128, D])] * 2

                pU = [(pbig[i], f'pb{i}') for i in range(3)]
                pD = [(pbig[3 + i], f'pb{3 + i}') for i in range(4)]
                cnt = {'u': 0}
                cur = {}

                def conv_tile(ft, slot):
                    pb, pbn = pU[cnt['u'] % 3]
                    cnt['u'] += 1
                    for kc in range(8):
                        k.mm(pb[:, 0:256], Wup[:, kc, 128 * ft:128 * ft + 128], cur['h2T'][:, kc, :], ['Wup', cur['hTn']], [pbn],
                             start=(kc == 0), stop=(kc == 7))
                    ub, ubn = Ub[slot], f'Ub{slot}'
                    ca, can = cva[slot + 2 * (ft % 2)], f'cva{slot + 2 * (ft % 2)}'
                    cb_, cbn = cvb[slot], f'cvb{slot}'
                    k.cp('act', ub[:, 2:258], pb[:, 0:256], [pbn], [ubn + 'm'])
                    k.act(ca[:], pb[:, 0:256], AF.Identity, [pbn, 'cw'], [can], bias=cw[:, 132 + ft:133 + ft],
                          scale=cw[:, 88 + ft:89 + ft])
                    k.cp('pool', ub[:, 0:2], HL[:, ft, :], [f'HL{ft}'], [ubn + 'h'])
                    ur = [ubn + 'm', ubn + 'h']
                    k.stt(cb_[:], ub[:, 1:257], cw[:, 44 + ft:45 + ft], ca[:], ALU.mult, ALU.add, ur + ['cw', can], [cbn])
                    k.stt(ca[:], ub[:, 0:256], cw[:, ft:ft + 1], cb_[:], ALU.mult, ALU.add, ur + ['cw', cbn], [can])
                    k.cp('pool', HL[:, ft, :], ub[:, 256:258], ur, [f'HL{ft}'])
                    return ca, can

                def down_mm(ft):
                    for i in range(2):
                        for hf in range(2):
                            pd, pdn = pD[2 * i + hf]
                            k.mm(pd[:], GT[:, ft, 128 * i:128 * i + 128], Wd[:, ft, 512 * hf:512 * hf + 512], [f'GT{ft}', 'Wd'], [pdn],
                                 start=(ft == 0), stop=(ft == 21))

                DLY = 2
                NB2 = T // 256

                def prep(b2):
                    p = b2 % 2
                    rms_to_hT(es, X1, 256 * b2, h2Ts[p], 'nw2', 'c2', ntile=2, xtl=x1sA[p], nwt=nw2, hTn=f'h2T{p}')

                prep(0)
                for b2 in range(NB2):
                    t0 = 256 * b2
                    cur['h2T'], cur['hTn'] = h2Ts[b2 % 2], f'h2T{b2 % 2}'
                    x1s = x1sA[b2 % 2]
                    pend = None

                    def finish(pn):
                        ft_, ga, gan, va, van = pn
                        g_, gn_ = gsl[ft_ % 2], f'gsl{ft_ % 2}'
                        k.act(g_[:], ga[:], AF.Silu, [gan], [gn_])
                        k.tt('pool', GT[:, ft_, :], g_[:], va[:], ALU.mult, [gn_, van], [f'GT{ft_}'])

                    for ft in range(22):
                        ga, gan = conv_tile(ft, 0)
                        va, van = conv_tile(22 + ft, 1)
                        if pend is not None:
                            finish(pend)
                        pend = (ft, ga, gan, va, van)
                        if ft >= DLY + 1:
                            down_mm(ft - DLY - 1)
                        if ft == 10 and b2 + 1 < NB2:
                            prep(b2 + 1)
                    finish(pend)
                    for ft in range(22 - DLY - 1, 22):
                        down_mm(ft)
                    for i in range(2):
                        xs_, xsn = x1s[i]
                        for hf in range(2):
                            pd, pdn = pD[2 * i + hf]
                            k.tt('dve', xs_[:, 512 * hf:512 * hf + 512], pd[:], xs_[:, 512 * hf:512 * hf + 512], ALU.add,
                                 [pdn, xsn], [xsn])
                        k.act(xn[:], xs_[:], AF.Square, [xsn], ['xn', 'ss'], accum_out=ss[:])
                        k.act(ss[:], ss[:], AF.Sqrt, ['ss'], ['ss'], bias=1e-6, scale=1.0 / D)
                        k.recip(rs[:], ss[:], ['ss'], ['rs'])
                        k.stt(ot[i][:], xs_[:], rs[:, 0:1], nwf[:], ALU.mult, ALU.mult, [xsn, 'rs', 'nwf'], ['ot'])
                        k.dma('sp', out[t0 + 128 * i:t0 + 128 * i + 128, :], ot[i][:], ['ot'], ['out'])
                S.wait_all('sp')
                S.flush()

        S.wait_all('sp')
        S.flush()
    return nc


_IN_NAMES = ["x", "norm_mix_w", "w_in", "rwkv_mu", "rwkv_w0", "rwkv_w2", "rwkv_a0", "rwkv_a2", "rwkv_g2",
             "rwkv_k_k", "rwkv_k_a", "rwkv_r_k", "rwkv_lnx_w", "rwkv_lnx_b", "fox_f_bias", "fox_q_norm_w",
             "fox_k_norm_w", "fox_o_norm_w", "w_out", "norm_ffn_w", "ffn_w_up", "ffn_conv_w", "ffn_conv_b",
             "ffn_w_down", "norm_final_w"]

_SHAPES = {"norm_mix_w": (1, D), "w_in": (D, RW + FOXC), "rwkv_mu": (RW,), "rwkv_w0": (512,), "rwkv_w2": (64, 512),
           "rwkv_a0": (512,), "rwkv_a2": (64, 512), "rwkv_g2": (128, 512), "rwkv_k_k": (512,), "rwkv_k_a": (512,),
           "rwkv_r_k": (512,), "rwkv_lnx_w": (512,), "rwkv_lnx_b": (512,), "fox_f_bias": (1, 8),
           "fox_q_norm_w": (64,), "fox_k_norm_w": (64,), "fox_o_norm_w": (1, 64), "w_out": (D, D),
           "norm_ffn_w": (1, D), "ffn_w_up": (D, 2 * DFF), "ffn_conv_w": (3, 2 * DFF), "ffn_conv_b": (2 * DFF,),
           "ffn_w_down": (DFF, D), "norm_final_w": (1, D)}


def make_in_maps(inputs, T, ncores):
    shared = {n: np.ascontiguousarray(np.asarray(inputs[n], dtype=np.float32).reshape(_SHAPES[n])) for n in _SHAPES}
    xs = np.asarray(inputs["x"], dtype=np.float32)
    maps = []
    for c in range(ncores):
        m = dict(shared)
        m["x"] = np.ascontiguousarray(xs[c, :T])
        maps.append(m)
    return maps


def kernel(**inputs):
    T = inputs["x"].shape[1]
    B = inputs["x"].shape[0]
    nc = build(T=T)
    res = run_bass_kernel_spmd(nc, make_in_maps(inputs, T, B), core_ids=list(range(B)))
    return np.stack([r["out"] for r in res.results], axis=0).astype(np.float32)
```

```python
import numpy as np
import concourse.bass as bass
import concourse.mybir as mybir
from concourse.bass_utils import run_bass_kernel_spmd

F32 = mybir.dt.float32
BF16 = mybir.dt.bfloat16
AF = mybir.ActivationFunctionType
ALU = mybir.AluOpType
AX = mybir.AxisListType

ENGS = ('pe', 'act', 'dve', 'pool', 'sp')


class Sched:
    def __init__(self, nc, esems, dsems):
        self.nc = nc
        self.esem = dict(zip(ENGS, esems))
        self.dsems = dsems
        self.cnt = {e: 0 for e in ENGS}
        self.stream = {e: [] for e in ENGS}
        self.seen = {e: {} for e in ENGS}
        self.dcount = [0] * len(dsems)
        self.dn = {'sp': 0, 'pool': 0, 'act': 0}
        nq = len(dsems) // 3
        self.dq = {'sp': list(range(0, nq)), 'pool': list(range(nq, 2 * nq)), 'act': list(range(2 * nq, 3 * nq))}
        self.res = {}

    def _deps(self, reads, writes):
        deps = {}
        def add(tok):
            if tok is None:
                return
            k, v = tok
            if deps.get(k, 0) < v:
                deps[k] = v
        for r in reads:
            st = self.res.get(r)
            if st:
                add(st[0])
        for w in writes:
            st = self.res.get(w)
            if st:
                add(st[0])
                for k, v in st[1].items():
                    add((k, v))
        return deps

    def _commit(self, tok, reads, writes):
        for r in reads:
            st = self.res.setdefault(r, [None, {}])
            if st[1].get(tok[0], 0) < tok[1]:
                st[1][tok[0]] = tok[1]
        for w in writes:
            self.res[w] = [tok, {}]

    def op(self, eng, fn, reads=(), writes=()):
        deps = self._deps(reads, writes)
        waits = []
        seen = self.seen[eng]
        for k, v in deps.items():
            if k == 'pe' and eng == 'pe':
                continue
            if seen.get(k, 0) >= v:
                continue
            seen[k] = v
            waits.append((k, v))
        self.cnt[eng] += 1
        tok = (eng, self.cnt[eng])
        self.stream[eng].append((waits, fn, (eng, 1)))
        self._commit(tok, reads, writes)

    def dma(self, eng, fn, reads=(), writes=()):
        deps = self._deps(reads, writes)
        q = self.dq[eng]
        k = q[self.dn[eng] % len(q)]
        self.dn[eng] += 1
        prev = 16 * self.dcount[k]
        self.dcount[k] += 1
        key = ('d', k)
        if prev > 0:
            if deps.get(key, 0) < prev:
                deps[key] = prev
        waits = []
        seen = self.seen[eng]
        for kk, v in deps.items():
            if seen.get(kk, 0) >= v:
                continue
            seen[kk] = v
            waits.append((kk, v))
        tok = (key, prev + 16)
        self.stream[eng].append((waits, fn, (key, 16)))
        self._commit(tok, reads, writes)

    def wait_all(self, eng):
        waits = []
        for e in ENGS:
            if self.cnt[e] > 0 and e != eng:
                waits.append((e, self.cnt[e]))
        for k in range(len(self.dsems)):
            if self.dcount[k] > 0:
                waits.append((('d', k), 16 * self.dcount[k]))
        self.stream[eng].append((waits, None, None))

    def _sem(self, key):
        if isinstance(key, tuple):
            return self.dsems[key[1]]
        return self.esem[key]

    def emit(self, eng, engine):
        for waits, fn, inc in self.stream[eng]:
            for k, v in waits:
                engine.wait_ge(self._sem(k), v)
            if fn is None:
                continue
            inst = fn(engine)
            inst.then_inc(self._sem(inc[0]), inc[1])

    def flush(self):
        self.emit_all()
        self.stream = {e: [] for e in ENGS}

    def emit_all(self):
        nc = self.nc
        with nc.Block() as block:
            @block.tensor
            def _(e):
                self.emit('pe', e)

            @block.scalar
            def _(e):
                self.emit('act', e)

            @block.vector
            def _(e):
                self.emit('dve', e)

            @block.gpsimd
            def _(e):
                self.emit('pool', e)

            @block.sync
            def _(e):
                self.emit('sp', e)


def _mk(eng):
    def f(self, fn, reads=(), writes=()):
        return self.op(eng, fn, reads, writes)
    return f


for _e in ('pe', 'act', 'dve', 'pool'):
    setattr(Sched, _e, _mk(_e))

import contextlib

D = 1024
RW = 1792
FOXC = 2056
DFF = 2816
NEG_E05 = -0.6065306597126334
F32R = mybir.dt.float32r


def R(ap):
    return ap.bitcast(F32R)


class K:
    def __init__(self, S):
        self.S = S

    def tt(self, eng, out, in0, in1, op, r, w):
        self.S.op(eng, lambda e: e.tensor_tensor(out=out, in0=in0, in1=in1, op=op), r, w)

    def ts(self, eng, out, in0, s1, s2, op0, op1, r, w):
        if s2 is None:
            self.S.op(eng, lambda e: e.tensor_scalar(out=out, in0=in0, scalar1=s1, scalar2=None, op0=op0), r, w)
        else:
            self.S.op(eng, lambda e: e.tensor_scalar(out=out, in0=in0, scalar1=s1, scalar2=s2, op0=op0, op1=op1), r, w)

    def stt(self, out, in0, scalar, in1, op0, op1, r, w):
        self.S.op('dve', lambda e: e.scalar_tensor_tensor(out=out, in0=in0, scalar=scalar, in1=in1, op0=op0, op1=op1), r, w)

    def act(self, out, in_, func, r, w, bias=None, scale=None, accum_out=None):
        kw = {}
        if bias is not None:
            kw['bias'] = bias
        if scale is not None:
            kw['scale'] = scale
        if accum_out is not None:
            kw['accum_out'] = accum_out
        self.S.op('act', lambda e: e.activation(out=out, in_=in_, func=func, **kw), r, w)

    def cp(self, eng, out, in_, r, w):
        if eng == 'act':
            self.S.op('act', lambda e: e.copy(out=out, in_=in_), r, w)
        else:
            self.S.op(eng, lambda e: e.tensor_copy(out=out, in_=in_), r, w)

    def mm(self, out, lhsT, rhs, r, w, start=True, stop=True, r32=False):
        if r32:
            lhsT, rhs = R(lhsT), R(rhs)
        self.S.op('pe', lambda e: e.matmul(out, lhsT=lhsT, rhs=rhs, start=start, stop=stop), r, w)

    def tr(self, out, in_, ident, r, w):
        self.S.op('pe', lambda e: e.transpose(out, in_, ident), r, w)

    def dma(self, q, out, in_, r, w, **kw):
        self.S.dma(q, lambda e: e.dma_start(out=out, in_=in_, **kw), r, w)

    def memset(self, eng, ap, val, w):
        self.S.op(eng, lambda e: e.memset(ap, val), [], w)

    def recip(self, out, in_, r, w):
        self.S.op('dve', lambda e: e.reciprocal(out=out, in_=in_), r, w)

    def asel(self, out, in_, pattern, op, base, cm, r, w):
        self.S.op('pool', lambda e: e.affine_select(out=out, in_=in_, pattern=pattern, compare_op=op,
                                                    fill=0.0, base=base, channel_multiplier=cm), r, w)

    def scan(self, out, d0, d1, r, w):
        self.S.op('dve', lambda e: e.tensor_tensor_scan(out=out, data0=d0, data1=d1, initial=0.0,
                                                        op0=ALU.mult, op1=ALU.add), r, w)


def build(T=4096, dbg=False, phases=('A1', 'A2', 'B', 'C1', 'C2')):
    nc = bass.Bass("TRN2", target_bir_lowering=False)
    NT = T // 128
    NB = T // 512
    din = {}

    def DI(name, shape):
        din[name] = nc.dram_tensor(name, list(shape), F32, kind="ExternalInput").ap()
        return din[name]

    x = DI("x", [T, D])
    norm_mix_w = DI("norm_mix_w", [1, D])
    w_in = DI("w_in", [D, RW + FOXC])
    rwkv_mu = DI("rwkv_mu", [RW])
    rwkv_w0 = DI("rwkv_w0", [512])
    rwkv_w2 = DI("rwkv_w2", [64, 512])
    rwkv_a0 = DI("rwkv_a0", [512])
    rwkv_a2 = DI("rwkv_a2", [64, 512])
    rwkv_g2 = DI("rwkv_g2", [128, 512])
    rwkv_k_k = DI("rwkv_k_k", [512])
    rwkv_k_a = DI("rwkv_k_a", [512])
    rwkv_r_k = DI("rwkv_r_k", [512])
    rwkv_lnx_w = DI("rwkv_lnx_w", [512])
    rwkv_lnx_b = DI("rwkv_lnx_b", [512])
    fox_f_bias = DI("fox_f_bias", [1, 8])
    fox_q_norm_w = DI("fox_q_norm_w", [64])
    fox_k_norm_w = DI("fox_k_norm_w", [64])
    fox_o_norm_w = DI("fox_o_norm_w", [1, 64])
    w_out = DI("w_out", [D, D])
    norm_ffn_w = DI("norm_ffn_w", [1, D])
    ffn_w_up = DI("ffn_w_up", [D, 2 * DFF])
    ffn_conv_w = DI("ffn_conv_w", [3, 2 * DFF])
    ffn_conv_b = DI("ffn_conv_b", [2 * DFF])
    ffn_w_down = DI("ffn_w_down", [DFF, D])
    norm_final_w = DI("norm_final_w", [1, D])
    out = nc.dram_tensor("out", [T, D], F32, kind="ExternalOutput").ap()

    okind = "ExternalOutput" if dbg else "Internal"
    YR = nc.dram_tensor("yr", [4, 128, T], BF16, kind=okind).ap()
    YF = nc.dram_tensor("yf", [4, 128, T], BF16, kind=okind).ap()
    SGd = nc.dram_tensor("sgd", [T, 512], BF16, kind="Internal").ap()
    X1 = nc.dram_tensor("x1", [T, D], F32, kind=okind).ap()

    with contextlib.ExitStack() as es0:
        esems = [es0.enter_context(nc.semaphore(f"es{i}")) for i in range(5)]
        dsems = [es0.enter_context(nc.semaphore(f"ds{i}")) for i in range(24)]
        S = Sched(nc, esems, dsems)
        k = K(S)

        uid = [0]

        def SB(es, name, shape, dt=F32):
            uid[0] += 1
            return es.enter_context(nc.sbuf_tensor(f"{name}_u{uid[0]}", list(shape), dt))

        pst = es0.enter_context(nc.psum_tensor("pst", [128, 1024], BF16))
        pbig = [es0.enter_context(nc.psum_tensor(f"pb{i}", [128, 512], F32)) for i in range(7)]
        st = {'big': 0, 'q': 0}

        def PB():
            i = st['big'] % 7
            st['big'] += 1
            return pbig[i], f"pb{i}"

        def PQ():
            i = st['q'] % 16
            st['q'] += 1
            return pqb[i // 4][:, (i % 4) * 128:(i % 4) * 128 + 128], f"pq{i}"

        def PQbank():
            return PQ()

        ones = SB(es0, "ones", [128, 128])
        ident = SB(es0, "ident", [128, 128])
        identb = SB(es0, "identb", [128, 128], BF16)
        mST = SB(es0, "mST", [128, 128])
        mIT = SB(es0, "mIT", [128, 128])
        mS = SB(es0, "mS", [128, 128])
        bones_raw = SB(es0, "bones_raw", [128, 128])
        bones = SB(es0, "bones", [128, 128])
        k.memset('pool', ones[:], 1.0, ['ones'])
        k.asel(ident[:], ones[:], [[-1, 128]], ALU.is_equal, 0, 1, ['ones'], ['ident'])
        k.cp('dve', identb[:], ident[:], ['ident'], ['identb'])
        for m_, nm in ((mST, 'mST'), (mIT, 'mIT'), (mS, 'mS'), (bones_raw, 'bones_raw')):
            k.memset('pool', m_[:], 0.0, [nm])
        for b in range(2):
            sl = slice(64 * b, 64 * b + 64)
            k.asel(mST[sl, sl], ones[sl, sl], [[1, 64]], ALU.is_ge, -1, -1, ['ones', 'mST'], ['mST'])
            k.asel(mIT[sl, sl], ones[sl, sl], [[1, 64]], ALU.is_ge, 0, -1, ['ones', 'mIT'], ['mIT'])
            k.asel(mS[sl, sl], ones[sl, sl], [[-1, 64]], ALU.is_ge, -1, 1, ['ones', 'mS'], ['mS'])
            k.cp('pool', bones_raw[sl, sl], ones[sl, sl], ['ones', 'bones_raw'], ['bones_raw'])

        k.cp('dve', R(bones[:]), bones_raw[:], ['bones_raw'], ['bones'])
        nwb = SB(es0, "nwb", [128, D])

        def rms_to_hT(es, xsrc, t0, hT, nwname, tagp, ntile=4, xtl=None, nwt=None, hTn='hT'):
            nwt_ = nwb if nwt is None else nwt
            for i in range(ntile):
                if xtl is None:
                    par = i % 2
                    xt, xtn = xts[par], f'xt{par}'
                else:
                    xt, xtn = xtl[i]
                k.dma('sp', xt[:], xsrc[t0 + 128 * i:t0 + 128 * i + 128, :], [], [xtn])
                k.act(xn[:], xt[:], AF.Square, [xtn], ['xn', 'ss'], accum_out=ss[:])
                k.act(ss[:], ss[:], AF.Sqrt, ['ss'], ['ss'], bias=1e-6, scale=1.0 / D)
                k.recip(rs[:], ss[:], ['ss'], ['rs'])
                k.stt(xn[:], xt[:], rs[:, 0:1], nwt_[:], ALU.mult, ALU.mult, [xtn, 'rs', nwname], ['xn'])
                for kk_ in range(8):
                    k.tr(pst[:, 128 * kk_:128 * kk_ + 128], xn[:, 128 * kk_:128 * kk_ + 128], identb[:],
                         ['xn', 'identb'], ['pst'])
                k.cp('act', hT[:, :, 128 * i:128 * i + 128], pst[:].rearrange("p (k t) -> p k t", t=128),
                     ['pst'], [hTn])

        class Rec:
            def __init__(self):
                self.ops = []

            def op(self, eng, fn, reads=(), writes=()):
                self.ops.append(('op', eng, fn, tuple(reads), tuple(writes)))

            def dma(self, eng, fn, reads=(), writes=()):
                self.ops.append(('dma', eng, fn, tuple(reads), tuple(writes)))

        def record(fn, *a_):
            rec = Rec()
            k.S = rec
            try:
                fn(*a_)
            finally:
                k.S = S
            return rec.ops

        def replay(ops):
            for kind, eng, fn, r, w in ops:
                (S.op if kind == 'op' else S.dma)(eng, fn, r, w)

        def merge(a_, b_):
            o, i, j, na, nb = [], 0, 0, len(a_), len(b_)
            while i < na or j < nb:
                if j >= nb or (i < na and i * nb <= j * na):
                    o.append(a_[i])
                    i += 1
                else:
                    o.append(b_[j])
                    j += 1
            return o

        if 'A1' in phases:
            with contextlib.ExitStack() as es:
                WB = 256
                NBL = T // WB
                cE = {'n': 0}
                cC = {'n': 0}

                def PBE():
                    i = cE['n'] % 3
                    cE['n'] += 1
                    return pbig[i], f"pb{i}"

                def PBC():
                    i = 3 + cC['n'] % 4
                    cC['n'] += 1
                    return pbig[i], f"pb{i}"

                def bc4(m):
                    return m[:].unsqueeze(1).to_broadcast([128, 4, 128])

                k.dma('sp', nwb[:], norm_mix_w.partition_broadcast(128), [], ['nwb'])
                Win = SB(es, "WinR", [128, 8, RW], BF16)
                for kc in range(8):
                    k.dma('pool', Win[:, kc, :], w_in[128 * kc:128 * kc + 128, 0:RW], [], ['Win'])
                xts = [SB(es, f"xt{i}", [128, D]) for i in range(2)]
                ss = SB(es, "ss", [128, 1])
                rs = SB(es, "rs", [128, 1])
                xn = SB(es, "xn", [128, D], BF16)
                hT = SB(es, "hT", [128, 8, WB], BF16)
                T1 = [SB(es, f"T1_{i}", [128, WB]) for i in range(2)]
                carry = SB(es, "carry", [128, 14])
                X = SB(es, "X", [128, 14, WB])
                mu_cm = SB(es, "mu_cm", [128, 14])
                omm_cm = SB(es, "omm_cm", [128, 14])
                prm = SB(es, "prm", [128, 7, 4])
                omka = SB(es, "omka", [128, 4])
                W2Z = SB(es, "W2Z", [128, 512])
                A2Z = SB(es, "A2Z", [128, 512])
                G2 = SB(es, "G2", [128, 512])
                Wraw = [SB(es, f"Wraw{i}", [128, 512]) for i in range(3)]
                rmask = SB(es, "rmask", [128, WB])
                k.dma('sp', mu_cm[:], rwkv_mu.rearrange("(t p) -> p t", p=128), [], ['mu_cm'], allow_slow_non_contiguous=True)
                for i, prm_in in enumerate((rwkv_w0, rwkv_a0, rwkv_k_k, rwkv_k_a, rwkv_r_k, rwkv_lnx_w, rwkv_lnx_b)):
                    k.dma('sp', prm[:, i, :], prm_in.rearrange("(t p) -> p t", p=128), [], ['prm'], allow_slow_non_contiguous=True)
                k.ts('dve', omm_cm[:], mu_cm[:], -1.0, 1.0, ALU.mult, ALU.add, ['mu_cm'], ['omm_cm'])
                k.ts('dve', omka[:], prm[:, 3, :], -1.0, 1.0, ALU.mult, ALU.add, ['prm'], ['omka'])
                k.memset('pool', Wraw[0][:], 0.0, ['Wraw0'])
                k.memset('pool', Wraw[1][:], 0.0, ['Wraw1'])
                k.dma('sp', Wraw[0][0:64, :], rwkv_w2, [], ['Wraw0'])
                k.dma('sp', Wraw[1][64:128, :], rwkv_a2, [], ['Wraw1'])
                k.dma('sp', Wraw[2][:], rwkv_g2, [], ['Wraw2'])
                for i_, (w_, wn_) in enumerate(((W2Z, 'W2Z'), (A2Z, 'A2Z'), (G2, 'G2'))):
                    k.cp('dve', R(w_[:]), Wraw[i_][:], [f'Wraw{i_}'], [wn_])
                k.memset('pool', rmask[:], 1.0, ['rmask'])
                k.memset('pool', rmask[:].rearrange("p (c t) -> p c t", t=64)[:, :, 0:1], 0.0, ['rmask'])
                k.memset('pool', carry[:], 0.0, [f'carry{c_}' for c_ in range(14)])

                TA = SB(es, "TA", [128, WB])
                SGg = SB(es, "SGg", [128, WB])
                tmp = [SB(es, f"tmp{i}", [128, WB]) for i in range(9)]
                SQr = SB(es, "SQr", [128, WB])
                BDn = ('RT', 'AT', 'BT', 'KT', 'BH', 'KH', 'VT')
                BDp = [{n: SB(es, f"BD{p}_{n}", [128, 4, 128]) for n in BDn} for p in range(2)]
                Gtp = [SB(es, f"Gt{p}", [128, WB]) for p in range(2)]
                BStp = [SB(es, f"BSt{p}", [128, WB]) for p in range(2)]
                PCtp = [SB(es, f"PCt{p}", [128, 4]) for p in range(2)]
                for p in range(2):
                    for n in BDn:
                        k.memset('pool', BDp[p][n][:], 0.0, [f'BD{p}_{n}'])
                        k.cp('dve', R(BDp[p][n][:]), BDp[p][n][:], [f'BD{p}_{n}'], [f'BD{p}_{n}'])
                TMn = ('A', 'BH', 'KH', 'V')
                TM = {n: SB(es, f"TM_{n}", [128, 4, 128]) for n in TMn}
                CMn = ('NTa', 'NTb', 'Na', 'Nb', 'ST', 'AKT', 'MRBT', 'MRKT', 'ApT', 'W2', 'Vp', 'U')
                CM = {n: SB(es, f"CM_{n}", [128, 4, 128]) for n in CMn}
                H = [SB(es, f"H{p}", [128, 128]) for p in range(4)]
                for p in range(4):
                    k.memset('pool', H[p][:], 0.0, [f'H{p}'])
                    k.cp('dve', R(H[p][:]), H[p][:], [f'H{p}'], [f'H{p}'])
                YT = SB(es, "YT", [128, WB])
                G1 = SB(es, "G1", [128, WB])
                G2t = SB(es, "G2t", [128, WB])
                SQr2 = SB(es, "SQr2", [128, WB])
                YO = SB(es, "YO", [128, WB], BF16)

                def XR(ct):
                    return [f'X{ct}a', f'X{ct}b']

                def stage_E(b, pr):
                    par = (4 * b + pr) % 2
                    BD = BDp[par]
                    bdn = f'BD{par}_'
                    t0 = WB * b
                    if pr == 0:
                        rms_to_hT(es, x, t0, hT, 'nwb', 'a1', ntile=2)
                        for ct in range(14):
                            pb, pbn = PBE()
                            for kc in range(8):
                                k.mm(pb[:, 0:WB], Win[:, kc, 128 * ct:128 * ct + 128], hT[:, kc, :], ['Win', 'hT'], [pbn],
                                     start=(kc == 0), stop=(kc == 7))
                            t1 = T1[ct % 2]
                            t1n = f'T1_{ct % 2}'
                            k.act(t1[:], pb[:, 0:WB], AF.Identity, [pbn, 'mu_cm'], [t1n], scale=mu_cm[:, ct:ct + 1])
                            xn_ = f'X{ct}'
                            k.stt(X[:, ct, 1:WB], pb[:, 1:WB], omm_cm[:, ct:ct + 1], t1[:, 0:WB - 1], ALU.mult, ALU.add,
                                  [pbn, t1n, 'omm_cm'], [xn_ + 'a'])
                            k.stt(X[:, ct, 0:1], pb[:, 0:1], omm_cm[:, ct:ct + 1], carry[:, ct:ct + 1], ALU.mult, ALU.add,
                                  [pbn, f'carry{ct}', 'omm_cm'], [xn_ + 'b'])
                            k.cp('pool', carry[:, ct:ct + 1], t1[:, WB - 1:WB], [t1n], [f'carry{ct}'])
                        k.act(R(TA[0:64, :]), X[0:64, 12, :], AF.Tanh, XR(12), ['TAa'])
                        k.cp('pool', R(TA[64:128, :]), X[64:128, 12, :], XR(12), ['TAb'])
                        k.act(R(SGg[:]), X[:, 13, :], AF.Sigmoid, XR(13), ['SGg'])
                    cs = slice(128 * pr, 128 * pr + 128)
                    Xr, Xk, Xv = X[:, pr, :], X[:, 4 + pr, :], X[:, 8 + pr, :]
                    rXr, rXk, rXv = XR(pr), XR(4 + pr), XR(8 + pr)
                    SIG, A_, KKN, KP, BV, L, E1, E2, E3 = tmp
                    tn = [f'tmp{i}' for i in range(9)]
                    Gt, Gtn = Gtp[par], f'Gt{par}'
                    BSt, BStn = BStp[par], f'BSt{par}'
                    PCt, PCtn = PCtp[par], f'PCt{par}'

                    def bdwrite(eng, name, in0, in1, op, r):
                        for hh in range(2):
                            ps_ = slice(64 * hh, 64 * hh + 64)
                            o = R(BD[name][ps_, :, 64 * hh:64 * hh + 64])
                            a0 = in0[ps_, :].rearrange("p (c t) -> p c t", t=64)
                            if in1 is None:
                                k.cp(eng, o, a0, r, [bdn + name])
                            else:
                                a1 = in1[ps_, :].rearrange("p (c t) -> p c t", t=64)
                                k.tt(eng, o, a0, a1, op, r, [bdn + name])

                    pb, pbn = PBE()
                    k.mm(pb[:, 0:WB], W2Z[:, cs], TA[:], ['W2Z', 'TAa', 'TAb'], [pbn], r32=True)
                    k.act(SIG[:], pb[:, 0:WB], AF.Sigmoid, [pbn, 'prm'], [tn[0]], bias=prm[:, 0, pr:pr + 1])
                    pb, pbn = PBE()
                    k.mm(pb[:, 0:WB], A2Z[:, cs], TA[:], ['A2Z', 'TAa', 'TAb'], [pbn], r32=True)
                    k.act(A_[:], pb[:, 0:WB], AF.Sigmoid, [pbn, 'prm'], [tn[1]], bias=prm[:, 1, pr:pr + 1])
                    pb, pbn = PBE()
                    k.mm(pb[:, 0:WB], G2[:, cs], SGg[:], ['G2', 'SGg'], [pbn], r32=True)
                    k.cp('act', Gt[:], pb[:, 0:WB], [pbn], [Gtn])
                    k.ts('dve', KKN[:], Xk, prm[:, 2, pr:pr + 1], None, ALU.mult, None, rXk + ['prm'], [tn[2]])
                    k.act(R(SQr[:]), KKN[:], AF.Square, [tn[2]], ['SQr'])
                    pb, pbn = PBE()
                    k.mm(pb[:, 0:WB], bones[:], SQr[:], ['bones', 'SQr'], [pbn], r32=True)
                    k.ts('dve', E1[:], pb[:, 0:WB], 1e-24, None, ALU.max, None, [pbn], [tn[6]])
                    k.act(E1[:], E1[:], AF.Sqrt, [tn[6]], [tn[6]])
                    k.recip(E1[:], E1[:], [tn[6]], [tn[6]])
                    k.tt('dve', KKN[:], KKN[:], E1[:], ALU.mult, [tn[2], tn[6]], [tn[2]])
                    k.ts('dve', KP[:], A_[:], prm[:, 3, pr:pr + 1], omka[:, pr:pr + 1], ALU.mult, ALU.add,
                         [tn[1], 'prm', 'omka'], [tn[3]])
                    k.tt('dve', KP[:], KP[:], Xk, ALU.mult, [tn[3]] + rXk, [tn[3]])
                    k.tt('pool', BV[:], KKN[:], A_[:], ALU.mult, [tn[2], tn[1]], [tn[4]])
                    k.stt(R(SQr[:]), Xr, prm[:, 4, pr:pr + 1], KP[:], ALU.mult, ALU.mult, rXr + ['prm', tn[3]], ['SQr'])
                    pb, pbn = PBE()
                    k.mm(pb[:, 0:WB], bones[:], SQr[:], ['bones', 'SQr'], [pbn], r32=True)
                    k.tt('dve', BSt[:], pb[:, 0:WB], Xv, ALU.mult, [pbn] + rXv, [BStn])
                    k.ts('dve', SIG[:], SIG[:], NEG_E05, None, ALU.mult, None, [tn[0]], [tn[0]])
                    k.scan(L[:], rmask[:], SIG[:], ['rmask', tn[0]], [tn[5]])
                    k.act(E1[:], L[:], AF.Exp, [tn[5]], [tn[6]])
                    bdwrite('dve', 'RT', Xr, E1, ALU.mult, rXr + [tn[6]])
                    k.tt('pool', E2[:], L[:], SIG[:], ALU.subtract, [tn[5], tn[0]], [tn[7]])
                    k.act(E2[:], E2[:], AF.Exp, [tn[7]], [tn[7]])
                    k.ts('dve', E2[:], E2[:], -1.0, None, ALU.mult, None, [tn[7]], [tn[7]])
                    bdwrite('dve', 'AT', KKN, E2, ALU.mult, [tn[2], tn[7]])
                    k.act(E3[:], L[:], AF.Exp, [tn[5]], [tn[8]], scale=-1.0)
                    bdwrite('dve', 'BT', BV, E3, ALU.mult, [tn[4], tn[8]])
                    bdwrite('pool', 'KT', KP, E3, ALU.mult, [tn[3], tn[8]])
                    L3 = L[:].rearrange("p (c t) -> p c t", t=64)
                    k.tt('dve', E1[:].rearrange("p (c t) -> p c t", t=64), L3,
                         L3[:, :, 63:64].to_broadcast([128, 4, 64]), ALU.subtract, [tn[5]], [tn[6]])
                    k.act(E1[:], E1[:], AF.Exp, [tn[6]], [tn[6]], scale=-1.0)
                    bdwrite('dve', 'BH', BV, E1, ALU.mult, [tn[4], tn[6]])
                    bdwrite('pool', 'KH', KP, E1, ALU.mult, [tn[3], tn[6]])
                    k.act(PCt[:], L3[:, :, 63], AF.Exp, [tn[5]], [PCtn])
                    bdwrite('pool', 'VT', Xv, None, None, rXv)

                def stage_C(b, pr):
                    par = (4 * b + pr) % 2
                    BD = BDp[par]
                    bdn = f'BD{par}_'
                    t0 = WB * b
                    Gt, Gtn = Gtp[par], f'Gt{par}'
                    BSt, BStn = BStp[par], f'BSt{par}'
                    PCt, PCtn = PCtp[par], f'PCt{par}'
                    Hp, Hn = H[pr], f'H{pr}'

                    def q4(pb_, i):
                        return pb_[:, 128 * i:128 * i + 128]

                    def v4(pb_):
                        return pb_[:].rearrange("p (i t) -> p i t", t=128)
                    for (ln, rn, mk, mkn, on) in (('BT', 'AT', mST, 'mST', 'NTa'), ('AT', 'BT', mS, 'mS', 'Na'),
                                                  ('KT', 'AT', mST, 'mST', 'AKT'), ('BT', 'RT', mIT, 'mIT', 'MRBT'),
                                                  ('KT', 'RT', mIT, 'mIT', 'MRKT')):
                        pb, pbn = PBC()
                        for c in range(4):
                            k.mm(q4(pb, c), BD[ln][:, c, :], BD[rn][:, c, :], [bdn + ln, bdn + rn], [pbn], r32=True)
                        k.tt('dve', R(CM[on][:]), v4(pb), bc4(mk), ALU.mult, [pbn, mkn], ['CM_' + on])
                    for (src, dst) in (('AT', 'A'), ('BH', 'BH'), ('KH', 'KH'), ('VT', 'V')):
                        pb, pbn = PBC()
                        for c in range(4):
                            k.tr(q4(pb, c), BD[src][:, c, :], ident[:], [bdn + src, 'ident'], [pbn])
                        k.cp('act', R(TM[dst][:]), v4(pb), [pbn], ['TM_' + dst])
                    k.tt('pool', R(CM['ST'][:]), CM['NTa'][:], bc4(ident), ALU.add, ['CM_NTa', 'ident'], ['CM_ST'])
                    curN, curNT = 'Na', 'NTa'
                    for lev in range(1, 6):
                        nxtN = 'Nb' if curN == 'Na' else 'Na'
                        nxtNT = 'NTb' if curNT == 'NTa' else 'NTa'
                        pb, pbn = PBC()
                        for i in range(4):
                            k.mm(q4(pb, i), CM[curNT][:, i, :], CM[curN][:, i, :], ['CM_' + curNT, 'CM_' + curN], [pbn], r32=True)
                        k.cp('act', R(CM[nxtN][:]), v4(pb), [pbn], ['CM_' + nxtN])
                        if lev < 5:
                            pb, pbn = PBC()
                            for i in range(4):
                                k.mm(q4(pb, i), CM[curN][:, i, :], CM[curNT][:, i, :], ['CM_' + curNT, 'CM_' + curN], [pbn], r32=True)
                            k.cp('act', R(CM[nxtNT][:]), v4(pb), [pbn], ['CM_' + nxtNT])
                        pb, pbn = PBC()
                        for i in range(4):
                            k.mm(q4(pb, i), CM[nxtN][:, i, :], CM['ST'][:, i, :], ['CM_' + nxtN, 'CM_ST'], [pbn], r32=True)
                        k.tt('dve', R(CM['ST'][:]), v4(pb), CM['ST'][:], ALU.add, [pbn, 'CM_ST'], ['CM_ST'])
                        curN, curNT = nxtN, nxtNT
                    pb, pbn = PBC()
                    for i in range(4):
                        k.mm(q4(pb, i), TM['A'][:, i, :], CM['ST'][:, i, :], ['TM_A', 'CM_ST'], [pbn], r32=True)
                    k.cp('act', R(CM['ApT'][:]), v4(pb), [pbn], ['CM_ApT'])
                    pb, pbn = PBC()
                    for i in range(4):
                        k.mm(q4(pb, i), CM['AKT'][:, i, :], TM['V'][:, i, :], ['CM_AKT', 'TM_V'], [pbn], r32=True)
                    k.cp('act', R(CM['W2'][:]), v4(pb), [pbn], ['CM_W2'])
                    pb, pbn = PBC()
                    for i in range(4):
                        k.mm(q4(pb, i), CM['ST'][:, i, :], CM['W2'][:, i, :], ['CM_ST', 'CM_W2'], [pbn], r32=True)
                    k.cp('act', R(CM['Vp'][:]), v4(pb), [pbn], ['CM_Vp'])
                    for c in range(4):
                        i = c
                        un = f'CM_U{i}'
                        pb, pbn = PBC()
                        k.mm(q4(pb, 0), CM['ApT'][:, i, :], Hp[:], ['CM_ApT', Hn], [pbn], r32=True)
                        k.tt('dve', R(CM['U'][:, i, :]), q4(pb, 0), CM['Vp'][:, i, :], ALU.add, [pbn, 'CM_Vp'], [un])
                        pb, pbn = PBC()
                        k.mm(q4(pb, 0), Hp[:], BD['RT'][:, c, :], [Hn, bdn + 'RT'], [pbn], r32=True, start=True, stop=False)
                        k.mm(q4(pb, 0), CM['U'][:, i, :], CM['MRBT'][:, i, :], [un, 'CM_MRBT'], [pbn], r32=True,
                             start=False, stop=False)
                        k.mm(q4(pb, 0), TM['V'][:, i, :], CM['MRKT'][:, i, :], ['TM_V', 'CM_MRKT'], [pbn], r32=True,
                             start=False, stop=True)
                        for hh in range(2):
                            ps_ = slice(64 * hh, 64 * hh + 64)
                            k.cp('act', R(YT[ps_, 64 * c:64 * c + 64]), pb[ps_, 64 * hh:64 * hh + 64], [pbn], [f'YT{hh}'])
                        pb, pbn = PBC()
                        k.mm(q4(pb, 0), TM['BH'][:, i, :], CM['U'][:, i, :], ['TM_BH', un], [pbn], r32=True,
                             start=True, stop=False)
                        k.mm(q4(pb, 0), TM['KH'][:, i, :], TM['V'][:, i, :], ['TM_KH', 'TM_V'], [pbn], r32=True,
                             start=False, stop=True)
                        k.stt(R(Hp[:]), Hp[:], PCt[:, c:c + 1], q4(pb, 0), ALU.mult, ALU.add, [Hn, PCtn, pbn], [Hn])
                    rYT = ['YT0', 'YT1']
                    pb, pbn = PBC()
                    k.mm(pb[:, 0:WB], bones[:], YT[:], ['bones'] + rYT, [pbn], r32=True)
                    k.stt(G1[:], pb[:, 0:WB], -1.0 / 64, YT[:], ALU.mult, ALU.add, [pbn] + rYT, ['G1'])
                    k.act(R(SQr2[:]), G1[:], AF.Square, ['G1'], ['SQr2'])
                    pb, pbn = PBC()
                    k.mm(pb[:, 0:WB], bones[:], SQr2[:], ['bones', 'SQr2'], [pbn], r32=True)
                    k.act(G2t[:], pb[:, 0:WB], AF.Sqrt, [pbn], ['G2t'], bias=64e-5, scale=1.0 / 64)
                    k.recip(G2t[:], G2t[:], ['G2t'], ['G2t'])
                    k.tt('dve', G1[:], G1[:], G2t[:], ALU.mult, ['G1', 'G2t'], ['G1'])
                    k.ts('dve', G1[:], G1[:], prm[:, 5, pr:pr + 1], prm[:, 6, pr:pr + 1], ALU.mult, ALU.add,
                         ['G1', 'prm'], ['G1'])
                    k.tt('pool', G1[:], G1[:], BSt[:], ALU.add, ['G1', BStn], ['G1'])
                    k.tt('pool', YO[:], G1[:], Gt[:], ALU.mult, ['G1', Gtn], ['YO'])
                    k.dma('sp', YR[pr, :, t0:t0 + WB], YO[:], ['YO'], ['YR'])

                units = [(b, pr) for b in range(NBL) for pr in range(4)]
                replay(record(stage_E, *units[0]))
                for n, u in enumerate(units):
                    oc = record(stage_C, *u)
                    oe = record(stage_E, *units[n + 1]) if n + 1 < len(units) else []
                    replay(merge(oc, oe))
                S.wait_all('sp')
                S.flush()

        if 'A2' in phases:
            with contextlib.ExitStack() as esAB:
                QT = SB(esAB, "QT", [128, 4, T], BF16)
                KT_ = SB(esAB, "KTf", [128, 4, T], BF16)
                V1 = SB(esAB, "V1", [128, NT, 8, 65], BF16)
                LFs = SB(esAB, "LFs", [128, NT, 8])
                k.memset('pool', V1[:], 1.0, ['V1'])
                with contextlib.ExitStack() as es:
                    k.dma('sp', nwb[:], norm_mix_w.partition_broadcast(128), [], ['nwb'])
                    Wf = SB(es, "Wf", [128, 8, FOXC], BF16)
                    for kc in range(8):
                        k.dma('pool', Wf[:, kc, 0:1024], w_in[128 * kc:128 * kc + 128, RW:RW + 1024], [], ['Wf'])
                        k.dma('pool', Wf[:, kc, 1024:FOXC], w_in[128 * kc:128 * kc + 128, RW + 1024:RW + FOXC], [], ['Wf'])
                    xts = [SB(es, f"xt{i}", [128, D]) for i in range(2)]
                    ss = SB(es, "ss", [128, 1])
                    rs = SB(es, "rs", [128, 1])
                    xn = SB(es, "xn", [128, D], BF16)
                    hT = SB(es, "hT", [128, 8, 512], BF16)
                    qkw = SB(es, "qkw", [128, 2])
                    fbb = SB(es, "fbb", [128, 8])
                    sq = SB(es, "sq", [128, 512])
                    rq = SB(es, "rq", [128, 512])
                    SGt = [SB(es, f"SGt{i}", [128, 512], BF16) for i in range(2)]
                    zt = SB(es, "zt", [128, 8])
                    for hh in range(2):
                        k.dma('sp', qkw[64 * hh:64 * hh + 64, 0:1], fox_q_norm_w.rearrange("(p o) -> p o", o=1), [], ['qkw'])
                        k.dma('sp', qkw[64 * hh:64 * hh + 64, 1:2], fox_k_norm_w.rearrange("(p o) -> p o", o=1), [], ['qkw'])
                    k.dma('sp', fbb[:], fox_f_bias.partition_broadcast(128), [], ['fbb'])
                    for b in range(NB):
                        t0 = 512 * b
                        rms_to_hT(es, x, t0, hT, 'nwb', 'a2')
                        for ct in range(8):
                            pb, pbn = PB()
                            for kc in range(8):
                                k.mm(pb[:], Wf[:, kc, 128 * ct:128 * ct + 128], hT[:, kc, :], ['Wf', 'hT'], [pbn],
                                     start=(kc == 0), stop=(kc == 7))
                            k.act(R(sq[:]), pb[:], AF.Square, [pbn], ['sq'])
                            pb2, pbn2 = PB()
                            k.mm(pb2[:], bones[:], sq[:], ['bones', 'sq'], [pbn2], r32=True)
                            k.act(rq[:], pb2[:], AF.Sqrt, [pbn2], ['rq'], bias=1e-6, scale=1.0 / 64)
                            k.recip(rq[:], rq[:], ['rq'], ['rq'])
                            dst = QT if ct < 4 else KT_
                            dn_ = 'QT' if ct < 4 else 'KTf'
                            k.stt(dst[:, ct % 4, t0:t0 + 512], pb[:], qkw[:, (ct // 4):(ct // 4) + 1], rq[:], ALU.mult, ALU.mult,
                                  [pbn, 'qkw', 'rq'], [dn_])
                        for i in range(4):
                            ti = 4 * b + i
                            tsl = slice(128 * i, 128 * i + 128)
                            pb, pbn = PB()
                            for kc in range(8):
                                k.mm(pb[:], hT[:, kc, tsl], Wf[:, kc, 1024:1536], ['Wf', 'hT'], [pbn], start=(kc == 0), stop=(kc == 7))
                            k.cp('act', V1[:, ti, :, 0:64], pb[:].rearrange("p (h d) -> p h d", d=64), [pbn], ['V1'])
                            pb, pbn = PB()
                            for kc in range(8):
                                k.mm(pb[:], hT[:, kc, tsl], Wf[:, kc, 1536:2048], ['Wf', 'hT'], [pbn], start=(kc == 0), stop=(kc == 7))
                            sg, sgn = SGt[i % 2], f'SGt{i % 2}'
                            k.act(sg[:], pb[:], AF.Sigmoid, [pbn], [sgn])
                            k.dma('sp', SGd[t0 + 128 * i:t0 + 128 * i + 128, :], sg[:], [sgn], ['SGd'])
                            pb, pbn = PB()
                            for kc in range(8):
                                k.mm(pb[:, 0:8], hT[:, kc, tsl], Wf[:, kc, 2048:2056], ['Wf', 'hT'], [pbn], start=(kc == 0), stop=(kc == 7))
                            k.tt('dve', zt[:], pb[:, 0:8], fbb[:], ALU.add, [pbn, 'fbb'], ['zt'])
                            k.act(zt[:], zt[:], AF.Exp, ['zt'], ['zt'], scale=-1.0)
                            k.act(LFs[:, ti, :], zt[:], AF.Ln, ['zt'], ['LFs'], bias=1.0)
                    S.wait_all('sp')
                S.flush()
                with contextlib.ExitStack() as es:
                    tri = SB(es, "tri", [128, 128])
                    cmask = SB(es, "cmask", [128, 128], BF16)
                    NCk = SB(es, "NCk", [128, NT, 8])
                    TOT = SB(es, "TOT", [128, NT, 8])
                    CAR = SB(es, "CAR", [128, NT, 8])
                    NBq = SB(es, "NBq", [128, NT, 8])
                    ownb = SB(es, "ownb", [128, 64])
                    k.asel(tri[:], ones[:], [[1, 128]], ALU.is_ge, 0, -1, ['ones'], ['tri'])
                    k.cp('dve', cmask[:], tri[:], ['tri'], ['cmask'])
                    k.dma('sp', ownb[:], fox_o_norm_w.partition_broadcast(128), [], ['ownb'])
                    LF2 = LFs[:].rearrange("p t h -> p (t h)")
                    nchunk = (NT * 8 + 511) // 512
                    for cc in range(nchunk):
                        c0 = 512 * cc
                        c1 = min(NT * 8, c0 + 512)
                        pb, pbn = PB()
                        k.mm(pb[:, 0:c1 - c0], tri[:], LF2[:, c0:c1], ['tri', 'LFs'], [pbn])
                        k.cp('act', NCk[:].rearrange("p t h -> p (t h)")[:, c0:c1], pb[:, 0:c1 - c0], [pbn], ['NCk'])
                        pb, pbn = PB()
                        k.mm(pb[:, 0:c1 - c0], ones[:], LF2[:, c0:c1], ['ones', 'LFs'], [pbn])
                        k.cp('act', TOT[:].rearrange("p t h -> p (t h)")[:, c0:c1], pb[:, 0:c1 - c0], [pbn], ['TOT'])
                    k.memset('pool', CAR[:, 0, :], 0.0, ['CAR'])
                    for ti in range(1, NT):
                        k.tt('dve', CAR[:, ti, :], CAR[:, ti - 1, :], TOT[:, ti - 1, :], ALU.add, ['CAR', 'TOT'], ['CAR'])
                    k.tt('dve', NCk[:], NCk[:], CAR[:], ALU.add, ['NCk', 'CAR'], ['NCk'])
                    k.stt(NBq[:], TOT[:], 0.5, CAR[:], ALU.mult, ALU.add, ['TOT', 'CAR'], ['NBq'])
                    biasT = [SB(es, f"biasT{i}", [128, NT]) for i in range(2)]
                    PT = [SB(es, f"PT{i}", [128, 128], BF16) for i in range(8)]
                    RL = SB(es, "RL", [128, 8])
                    Ot = SB(es, "Ot", [128, 8, 64])
                    O2 = SB(es, "O2", [128, 8, 64])
                    ssq = SB(es, "ssq", [128, 8])
                    SGl = SB(es, "SGl", [128, 512], BF16)
                    YFt = SB(es, "YFt", [128, 512], BF16)
                    YFo = SB(es, "YFo", [128, 4, 128], BF16)
                    pS = [(pbig[i], f'pb{i}') for i in range(3)]
                    pO = [(pbig[3 + i], f'pb{3 + i}') for i in range(4)]
                    nS = 0
                    nP = 0
                    items = []
                    for qt in range(NT):
                        for h in range(8):
                            for kt0 in range(0, qt + 1, 4):
                                items.append((qt, h, list(range(kt0, min(qt + 1, kt0 + 4)))))

                    def emit_st(it):
                        qt, h, kts = it
                        qsl = slice(128 * qt, 128 * qt + 128)
                        pr, r0 = h // 2, 64 * (h % 2)
                        bt, btn = biasT[h % 2], f'biasT{h % 2}'
                        if kts[0] == 0:
                            k.ts('dve', bt[:, 0:qt + 1], NCk[:, 0:qt + 1, h], NBq[:, qt, h:h + 1], None, ALU.subtract, None,
                                 ['NCk', 'NBq'], [btn])
                        pb, pbn = pS[st['big'] % 3]
                        st['big'] += 1
                        for j, kt in enumerate(kts):
                            k.mm(pb[:, 128 * j:128 * j + 128], KT_[r0:r0 + 64, pr, 128 * kt:128 * kt + 128],
                                 QT[r0:r0 + 64, pr, qsl], ['KTf', 'QT'], [pbn])
                        return pb, pbn

                    def emit_pv(it, pb, pbn):
                        qt, h, kts = it
                        bt, btn = biasT[h % 2], f'biasT{h % 2}'
                        po, pon = pO[2 * (qt % 2) + h // 4]
                        pocol = 65 * (h % 4)
                        for j, kt in enumerate(kts):
                            pt, ptn = PT[st['q'] % 8], f"PT{st['q'] % 8}"
                            st['q'] += 1
                            k.act(pt[:], pb[:, 128 * j:128 * j + 128], AF.Exp, [pbn, btn], [ptn],
                                  bias=bt[:, kt:kt + 1], scale=0.125)
                            if kt == qt:
                                k.tt('pool', pt[:], pt[:], cmask[:], ALU.mult, [ptn, 'cmask'], [ptn])
                            k.mm(po[:, pocol:pocol + 65], pt[:], V1[:, kt, h, :], [ptn, 'V1'], [pon],
                                 start=(kt == 0), stop=(kt == qt))

                    def epilogue(qt):
                        qsl = slice(128 * qt, 128 * qt + 128)
                        for hf in range(2):
                            po, pon = pO[2 * (qt % 2) + hf]
                            po3 = po[:, 0:260].rearrange("p (h d) -> p h d", d=65)
                            k.recip(RL[:, 4 * hf:4 * hf + 4], po3[:, :, 64], [pon], [f'RL{hf}'])
                            k.tt('dve', Ot[:, 4 * hf:4 * hf + 4, :], po3[:, :, 0:64],
                                 RL[:, 4 * hf:4 * hf + 4].unsqueeze(2).to_broadcast([128, 4, 64]), ALU.mult,
                                 [pon, f'RL{hf}'], [f'Ot{hf}'])
                        rOt = ['Ot0', 'Ot1']
                        k.act(O2[:], Ot[:], AF.Square, rOt, ['O2'])
                        S.op('dve', lambda e: e.tensor_reduce(out=ssq[:], in_=O2[:], axis=AX.X, op=ALU.add), ['O2'], ['ssq'])
                        k.act(ssq[:], ssq[:], AF.Sqrt, ['ssq'], ['ssq'], bias=1e-6, scale=1.0 / 64)
                        k.recip(ssq[:], ssq[:], ['ssq'], ['ssq'])
                        k.tt('dve', O2[:], Ot[:], ssq[:].unsqueeze(2).to_broadcast([128, 8, 64]), ALU.mult, rOt + ['ssq'], ['O2'])
                        k.tt('pool', O2[:], O2[:], ownb[:].unsqueeze(1).to_broadcast([128, 8, 64]), ALU.mult, ['O2', 'ownb'], ['O2'])
                        k.dma('sp', SGl[:], SGd[qsl, :], ['SGd'], ['SGl'])
                        k.tt('dve', YFt[:], O2[:].rearrange("p h d -> p (h d)"), SGl[:], ALU.mult, ['O2', 'SGl'], ['YFt'])
                        for c4 in range(4):
                            k.tr(pst[:, 128 * c4:128 * c4 + 128], YFt[:, 128 * c4:128 * c4 + 128], identb[:], ['YFt', 'identb'], ['pst'])
                        k.cp('act', YFo[:], pst[:, 0:512].rearrange("p (c t) -> p c t", t=128), ['pst'], ['YFo'])
                        k.dma('sp', YF[:, :, qsl].rearrange("c p t -> p c t"), YFo[:], ['YFo'], ['YF'])
                    cur = emit_st(items[0])
                    for ii, it in enumerate(items):
                        nxt = emit_st(items[ii + 1]) if ii + 1 < len(items) else None
                        emit_pv(it, *cur)
                        cur = nxt
                        if it[1] == 7 and it[2][-1] == it[0]:
                            epilogue(it[0])
                    S.wait_all('sp')
                S.flush()

        esC = es0.enter_context(contextlib.ExitStack())
        if 'C2' in phases:
            Wup = SB(esC, "Wup", [128, 8, 2 * DFF], BF16)
            Wd = SB(esC, "Wd", [128, 22, D], BF16)
        if 'C1' in phases:
            with contextlib.ExitStack() as es:
                Wo = SB(es, "Wo", [128, 8, D], BF16)
                for kc in range(8):
                    k.dma('pool', Wo[:, kc, :], w_out[128 * kc:128 * kc + 128, :], [], ['Wo'])
                if 'C2' in phases:
                    for kc in range(8):
                        for c0 in range(0, 2 * DFF, 2048):
                            c1 = min(2 * DFF, c0 + 2048)
                            k.dma('pool', Wup[:, kc, c0:c1], ffn_w_up[128 * kc:128 * kc + 128, c0:c1], [], ['Wup'])
                    for ft in range(22):
                        k.dma('pool', Wd[:, ft, :], ffn_w_down[128 * ft:128 * ft + 128, :], [], ['Wd'])
                Yb = [SB(es, f"Yb{i}", [128, 8, 512], BF16) for i in range(2)]
                xts = [SB(es, f"xt{i}", [128, D]) for i in range(2)]
                x1t = [SB(es, f"x1t{i}", [128, D]) for i in range(2)]
                for b in range(NB):
                    t0 = 512 * b
                    yb, ybn = Yb[b % 2], f'Yb{b % 2}'
                    k.dma('sp', yb[:, 0:4, :], YR[:, :, t0:t0 + 512].rearrange("c p t -> p c t"), ['YR'], [ybn])
                    k.dma('sp', yb[:, 4:8, :], YF[:, :, t0:t0 + 512].rearrange("c p t -> p c t"), ['YF'], [ybn])
                    for i in range(4):
                        par = i % 2
                        tsl = slice(128 * i, 128 * i + 128)
                        k.dma('sp', xts[par][:], x[t0 + 128 * i:t0 + 128 * i + 128, :], [], [f'xt{par}'])
                        for hf in range(2):
                            pb, pbn = PB()
                            for kc in range(8):
                                k.mm(pb[:], yb[:, kc, tsl], Wo[:, kc, 512 * hf:512 * hf + 512], [ybn, 'Wo'], [pbn],
                                     start=(kc == 0), stop=(kc == 7))
                            k.tt('dve', x1t[par][:, 512 * hf:512 * hf + 512], pb[:], xts[par][:, 512 * hf:512 * hf + 512], ALU.add,
                                 [pbn, f'xt{par}'], [f'x1t{par}'])
                        k.dma('sp', X1[t0 + 128 * i:t0 + 128 * i + 128, :], x1t[par][:], [f'x1t{par}'], ['X1'])
                S.wait_all('sp')
                S.flush()

        if 'C2' in phases:
            with contextlib.ExitStack() as es:
                NF = 2 * DFF // 128
                if 'C1' not in phases:
                    for kc in range(8):
                        for c0 in range(0, 2 * DFF, 2048):
                            c1 = min(2 * DFF, c0 + 2048)
                            k.dma('pool', Wup[:, kc, c0:c1], ffn_w_up[128 * kc:128 * kc + 128, c0:c1], [], ['Wup'])
                    for ft in range(22):
                        k.dma('pool', Wd[:, ft, :], ffn_w_down[128 * ft:128 * ft + 128, :], [], ['Wd'])
                nw2 = SB(es, "nw2", [128, D])
                nwf = SB(es, "nwf", [128, D])
                k.dma('sp', nw2[:], norm_ffn_w.partition_broadcast(128), [], ['nw2'])
                k.dma('sp', nwf[:], norm_final_w.partition_broadcast(128), [], ['nwf'])
                craw = SB(es, "craw", [128, 128])
                craw2 = SB(es, "craw2", [48, 128])
                cw = SB(es, "cw", [128, 176])
                k.dma('sp', craw[:], ffn_conv_w.rearrange("a (f p) -> (a f) p", p=128)[0:128, :], [], ['craw'])
                k.dma('sp', craw2[0:4, :], ffn_conv_w.rearrange("a (f p) -> (a f) p", p=128)[128:132, :], [], ['craw2'])
                k.dma('sp', craw2[4:48, :], ffn_conv_b.rearrange("(f p) -> f p", p=128), [], ['craw2'])
                pb, pbn = PB()
                k.tr(pb[:, 0:128], craw[:], ident[:], ['craw', 'ident'], [pbn])
                k.tr(pb[:, 128:176], craw2[:], ident[0:48, 0:48], ['craw2', 'ident'], [pbn])
                k.cp('act', cw[:], pb[:, 0:176], [pbn], ['cw'])
                x1sA = [[(SB(es, f"x1s{p}{i}", [128, D]), f'x1s{p}{i}') for i in range(2)] for p in range(2)]
                ss = SB(es, "ss", [128, 1])
                rs = SB(es, "rs", [128, 1])
                xn = SB(es, "xn", [128, D], BF16)
                h2Ts = [SB(es, f"h2T{p}", [128, 8, 256], BF16) for p in range(2)]
                Ub = [SB(es, f"Ub{i}", [128, 258]) for i in range(2)]
                HL = SB(es, "HL", [128, NF, 2])
                k.memset('pool', HL[:], 0.0, [f'HL{f}' for f in range(NF)])
                cva = [SB(es, f"cva{i}", [128, 256]) for i in range(4)]
                cvb = [SB(es, f"cvb{i}", [128, 256]) for i in range(2)]
                gsl = [SB(es, f"gsl{i}", [128, 256]) for i in range(2)]
                GT = SB(es, "GT", [128, 22, 256], BF16)
                ot = [SB(es, "ot", [128, D])] * 2

                pU = [(pbig[i], f'pb{i}') for i in range(3)]
                pD = [(pbig[3 + i], f'pb{3 + i}') for i in range(4)]
                cnt = {'u': 0}
                cur = {}

                def conv_tile(ft, slot):
                    pb, pbn = pU[cnt['u'] % 3]
                    cnt['u'] += 1
                    for kc in range(8):
                        k.mm(pb[:, 0:256], Wup[:, kc, 128 * ft:128 * ft + 128], cur['h2T'][:, kc, :], ['Wup', cur['hTn']], [pbn],
                             start=(kc == 0), stop=(kc == 7))
                    ub, ubn = Ub[slot], f'Ub{slot}'
                    ca, can = cva[slot + 2 * (ft % 2)], f'cva{slot + 2 * (ft % 2)}'
                    cb_, cbn = cvb[slot], f'cvb{slot}'
                    k.cp('act', ub[:, 2:258], pb[:, 0:256], [pbn], [ubn + 'm'])
                    k.act(ca[:], pb[:, 0:256], AF.Identity, [pbn, 'cw'], [can], bias=cw[:, 132 + ft:133 + ft],
                          scale=cw[:, 88 + ft:89 + ft])
                    k.cp('pool', ub[:, 0:2], HL[:, ft, :], [f'HL{ft}'], [ubn + 'h'])
                    ur = [ubn + 'm', ubn + 'h']
                    k.stt(cb_[:], ub[:, 1:257], cw[:, 44 + ft:45 + ft], ca[:], ALU.mult, ALU.add, ur + ['cw', can], [cbn])
                    k.stt(ca[:], ub[:, 0:256], cw[:, ft:ft + 1], cb_[:], ALU.mult, ALU.add, ur + ['cw', cbn], [can])
                    k.cp('pool', HL[:, ft, :], ub[:, 256:258], ur, [f'HL{ft}'])
                    return ca, can

                def down_mm(ft):
                    for i in range(2):
                        for hf in range(2):
                            pd, pdn = pD[2 * i + hf]
                            k.mm(pd[:], GT[:, ft, 128 * i:128 * i + 128], Wd[:, ft, 512 * hf:512 * hf + 512], [f'GT{ft}', 'Wd'], [pdn],
                                 start=(ft == 0), stop=(ft == 21))

                DLY = 2
                NB2 = T // 256

                def prep(b2):
                    p = b2 % 2
                    rms_to_hT(es, X1, 256 * b2, h2Ts[p], 'nw2', 'c2', ntile=2, xtl=x1sA[p], nwt=nw2, hTn=f'h2T{p}')

                prep(0)
                for b2 in range(NB2):
                    t0 = 256 * b2
                    cur['h2T'], cur['hTn'] = h2Ts[b2 % 2], f'h2T{b2 % 2}'
                    x1s = x1sA[b2 % 2]
                    pend = None

                    def finish(pn):
                        ft_, ga, gan, va, van = pn
                        g_, gn_ = gsl[ft_ % 2], f'gsl{ft_ % 2}'
                        k.act(g_[:], ga[:], AF.Silu, [gan], [gn_])
                        k.tt('pool', GT[:, ft_, :], g_[:], va[:], ALU.mult, [gn_, van], [f'GT{ft_}'])

                    for ft in range(22):
                        ga, gan = conv_tile(ft, 0)
                        va, van = conv_tile(22 + ft, 1)
                        if pend is not None:
                            finish(pend)
                        pend = (ft, ga, gan, va, van)
                        if ft >= DLY + 1:
                            down_mm(ft - DLY - 1)
                        if ft == 10 and b2 + 1 < NB2:
                            prep(b2 + 1)
                    finish(pend)
                    for ft in range(22 - DLY - 1, 22):
                        down_mm(ft)
                    for i in range(2):
                        xs_, xsn = x1s[i]
                        for hf in range(2):
                            pd, pdn = pD[2 * i + hf]
                            k.tt('dve', xs_[:, 512 * hf:512 * hf + 512], pd[:], xs_[:, 512 * hf:512 * hf + 512], ALU.add,
                                 [pdn, xsn], [xsn])
                        k.act(xn[:], xs_[:], AF.Square, [xsn], ['xn', 'ss'], accum_out=ss[:])
                        k.act(ss[:], ss[:], AF.Sqrt, ['ss'], ['ss'], bias=1e-6, scale=1.0 / D)
                        k.recip(rs[:], ss[:], ['ss'], ['rs'])
                        k.stt(ot[i][:], xs_[:], rs[:, 0:1], nwf[:], ALU.mult, ALU.mult, [xsn, 'rs', 'nwf'], ['ot'])
                        k.dma('sp', out[t0 + 128 * i:t0 + 128 * i + 128, :], ot[i][:], ['ot'], ['out'])
                S.wait_all('sp')
                S.flush()

        S.wait_all('sp')
        S.flush()
    return nc


_IN_NAMES = ["x", "norm_mix_w", "w_in", "rwkv_mu", "rwkv_w0", "rwkv_w2", "rwkv_a0", "rwkv_a2", "rwkv_g2",
             "rwkv_k_k", "rwkv_k_a", "rwkv_r_k", "rwkv_lnx_w", "rwkv_lnx_b", "fox_f_bias", "fox_q_norm_w",
             "fox_k_norm_w", "fox_o_norm_w", "w_out", "norm_ffn_w", "ffn_w_up", "ffn_conv_w", "ffn_conv_b",
             "ffn_w_down", "norm_final_w"]

_SHAPES = {"norm_mix_w": (1, D), "w_in": (D, RW + FOXC), "rwkv_mu": (RW,), "rwkv_w0": (512,), "rwkv_w2": (64, 512),
           "rwkv_a0": (512,), "rwkv_a2": (64, 512), "rwkv_g2": (128, 512), "rwkv_k_k": (512,), "rwkv_k_a": (512,),
           "rwkv_r_k": (512,), "rwkv_lnx_w": (512,), "rwkv_lnx_b": (512,), "fox_f_bias": (1, 8),
           "fox_q_norm_w": (64,), "fox_k_norm_w": (64,), "fox_o_norm_w": (1, 64), "w_out": (D, D),
           "norm_ffn_w": (1, D), "ffn_w_up": (D, 2 * DFF), "ffn_conv_w": (3, 2 * DFF), "ffn_conv_b": (2 * DFF,),
           "ffn_w_down": (DFF, D), "norm_final_w": (1, D)}


def make_in_maps(inputs, T, ncores):
    shared = {n: np.ascontiguousarray(np.asarray(inputs[n], dtype=np.float32).reshape(_SHAPES[n])) for n in _SHAPES}
    xs = np.asarray(inputs["x"], dtype=np.float32)
    maps = []
    for c in range(ncores):
        m = dict(shared)
        m["x"] = np.ascontiguousarray(xs[c, :T])
        maps.append(m)
    return maps


def kernel(**inputs):
    T = inputs["x"].shape[1]
    B = inputs["x"].shape[0]
    nc = build(T=T)
    res = run_bass_kernel_spmd(nc, make_in_maps(inputs, T, B), core_ids=list(range(B)))
    return np.stack([r["out"] for r in res.results], axis=0).astype(np.float32)
```

```python
import numpy as np
import concourse.bass as bass
import concourse.mybir as mybir
from concourse.bass_utils import run_bass_kernel_spmd

F32 = mybir.dt.float32
BF16 = mybir.dt.bfloat16
AF = mybir.ActivationFunctionType
ALU = mybir.AluOpType
AX = mybir.AxisListType

ENGS = ('pe', 'act', 'dve', 'pool', 'sp')


class Sched:
    def __init__(self, nc, esems, dsems):
        self.nc = nc
        self.esem = dict(zip(ENGS, esems))
        self.dsems = dsems
        self.cnt = {e: 0 for e in ENGS}
        self.stream = {e: [] for e in ENGS}
        self.seen = {e: {} for e in ENGS}
        self.dcount = [0] * len(dsems)
        self.dn = {'sp': 0, 'pool': 0, 'act': 0}
        nq = len(dsems) // 3
        self.dq = {'sp': list(range(0, nq)), 'pool': list(range(nq, 2 * nq)), 'act': list(range(2 * nq, 3 * nq))}
        self.res = {}

    def _deps(self, reads, writes):
        deps = {}
        def add(tok):
            if tok is None:
                return
            k, v = tok
            if deps.get(k, 0) < v:
                deps[k] = v
        for r in reads:
            st = self.res.get(r)
            if st:
                add(st[0])
        for w in writes:
            st = self.res.get(w)
            if st:
                add(st[0])
                for k, v in st[1].items():
                    add((k, v))
        return deps

    def _commit(self, tok, reads, writes):
        for r in reads:
            st = self.res.setdefault(r, [None, {}])
            if st[1].get(tok[0], 0) < tok[1]:
                st[1][tok[0]] = tok[1]
        for w in writes:
            self.res[w] = [tok, {}]

    def op(self, eng, fn, reads=(), writes=()):
        deps = self._deps(reads, writes)
        waits = []
        seen = self.seen[eng]
        for k, v in deps.items():
            if k == 'pe' and eng == 'pe':
                continue
            if seen.get(k, 0) >= v:
                continue
            seen[k] = v
            waits.append((k, v))
        self.cnt[eng] += 1
        tok = (eng, self.cnt[eng])
        self.stream[eng].append((waits, fn, (eng, 1)))
        self._commit(tok, reads, writes)

    def dma(self, eng, fn, reads=(), writes=()):
        deps = self._deps(reads, writes)
        q = self.dq[eng]
        k = q[self.dn[eng] % len(q)]
        self.dn[eng] += 1
        prev = 16 * self.dcount[k]
        self.dcount[k] += 1
        key = ('d', k)
        if prev > 0:
            if deps.get(key, 0) < prev:
                deps[key] = prev
        waits = []
        seen = self.seen[eng]
        for kk, v in deps.items():
            if seen.get(kk, 0) >= v:
                continue
            seen[kk] = v
            waits.append((kk, v))
        tok = (key, prev + 16)
        self.stream[eng].append((waits, fn, (key, 16)))
        self._commit(tok, reads, writes)

    def wait_all(self, eng):
        waits = []
        for e in ENGS:
            if self.cnt[e] > 0 and e != eng:
                waits.append((e, self.cnt[e]))
        for k in range(len(self.dsems)):
            if self.dcount[k] > 0:
                waits.append((('d', k), 16 * self.dcount[k]))
        self.stream[eng].append((waits, None, None))

    def _sem(self, key):
        if isinstance(key, tuple):
            return self.dsems[key[1]]
        return self.esem[key]

    def emit(self, eng, engine):
        for waits, fn, inc in self.stream[eng]:
            for k, v in waits:
                engine.wait_ge(self._sem(k), v)
            if fn is None:
                continue
            inst = fn(engine)
            inst.then_inc(self._sem(inc[0]), inc[1])

    def flush(self):
        self.emit_all()
        self.stream = {e: [] for e in ENGS}

    def emit_all(self):
        nc = self.nc
        with nc.Block() as block:
            @block.tensor
            def _(e):
                self.emit('pe', e)

            @block.scalar
            def _(e):
                self.emit('act', e)

            @block.vector
            def _(e):
                self.emit('dve', e)

            @block.gpsimd
            def _(e):
                self.emit('pool', e)

            @block.sync
            def _(e):
                self.emit('sp', e)


def _mk(eng):
    def f(self, fn, reads=(), writes=()):
        return self.op(eng, fn, reads, writes)
    return f


for _e in ('pe', 'act', 'dve', 'pool'):
    setattr(Sched, _e, _mk(_e))

import contextlib

D = 1024
RW = 1792
FOXC = 2056
DFF = 2816
NEG_E05 = -0.6065306597126334
F32R = mybir.dt.float32r


def R(ap):
    return ap.bitcast(F32R)


class K:
    def __init__(self, S):
        self.S = S

    def tt(self, eng, out, in0, in1, op, r, w):
        self.S.op(eng, lambda e: e.tensor_tensor(out=out, in0=in0, in1=in1, op=op), r, w)

    def ts(self, eng, out, in0, s1, s2, op0, op1, r, w):
        if s2 is None:
            self.S.op(eng, lambda e: e.tensor_scalar(out=out, in0=in0, scalar1=s1, scalar2=None, op0=op0), r, w)
        else:
            self.S.op(eng, lambda e: e.tensor_scalar(out=out, in0=in0, scalar1=s1, scalar2=s2, op0=op0, op1=op1), r, w)

    def stt(self, out, in0, scalar, in1, op0, op1, r, w):
        self.S.op('dve', lambda e: e.scalar_tensor_tensor(out=out, in0=in0, scalar=scalar, in1=in1, op0=op0, op1=op1), r, w)

    def act(self, out, in_, func, r, w, bias=None, scale=None, accum_out=None):
        kw = {}
        if bias is not None:
            kw['bias'] = bias
        if scale is not None:
            kw['scale'] = scale
        if accum_out is not None:
            kw['accum_out'] = accum_out
        self.S.op('act', lambda e: e.activation(out=out, in_=in_, func=func, **kw), r, w)

    def cp(self, eng, out, in_, r, w):
        if eng == 'act':
            self.S.op('act', lambda e: e.copy(out=out, in_=in_), r, w)
        else:
            self.S.op(eng, lambda e: e.tensor_copy(out=out, in_=in_), r, w)

    def mm(self, out, lhsT, rhs, r, w, start=True, stop=True, r32=False):
        if r32:
            lhsT, rhs = R(lhsT), R(rhs)
        self.S.op('pe', lambda e: e.matmul(out, lhsT=lhsT, rhs=rhs, start=start, stop=stop), r, w)

    def tr(self, out, in_, ident, r, w):
        self.S.op('pe', lambda e: e.transpose(out, in_, ident), r, w)

    def dma(self, q, out, in_, r, w, **kw):
        self.S.dma(q, lambda e: e.dma_start(out=out, in_=in_, **kw), r, w)

    def memset(self, eng, ap, val, w):
        self.S.op(eng, lambda e: e.memset(ap, val), [], w)

    def recip(self, out, in_, r, w):
        self.S.op('dve', lambda e: e.reciprocal(out=out, in_=in_), r, w)

    def asel(self, out, in_, pattern, op, base, cm, r, w):
        self.S.op('pool', lambda e: e.affine_select(out=out, in_=in_, pattern=pattern, compare_op=op,
                                                    fill=0.0, base=base, channel_multiplier=cm), r, w)

    def scan(self, out, d0, d1, r, w):
        self.S.op('dve', lambda e: e.tensor_tensor_scan(out=out, data0=d0, data1=d1, initial=0.0,
                                                        op0=ALU.mult, op1=ALU.add), r, w)


def build(T=4096, dbg=False, phases=('A1', 'A2', 'B', 'C1', 'C2')):
    nc = bass.Bass("TRN2", target_bir_lowering=False)
    NT = T // 128
    NB = T // 512
    din = {}

    def DI(name, shape):
        din[name] = nc.dram_tensor(name, list(shape), F32, kind="ExternalInput").ap()
        return din[name]

    x = DI("x", [T, D])
    norm_mix_w = DI("norm_mix_w", [1, D])
    w_in = DI("w_in", [D, RW + FOXC])
    rwkv_mu = DI("rwkv_mu", [RW])
    rwkv_w0 = DI("rwkv_w0", [512])
    rwkv_w2 = DI("rwkv_w2", [64, 512])
    rwkv_a0 = DI("rwkv_a0", [512])
    rwkv_a2 = DI("rwkv_a2", [64, 512])
    rwkv_g2 = DI("rwkv_g2", [128, 512])
    rwkv_k_k = DI("rwkv_k_k", [512])
    rwkv_k_a = DI("rwkv_k_a", [512])
    rwkv_r_k = DI("rwkv_r_k", [512])
    rwkv_lnx_w = DI("rwkv_lnx_w", [512])
    rwkv_lnx_b = DI("rwkv_lnx_b", [512])
    fox_f_bias = DI("fox_f_bias", [1, 8])
    fox_q_norm_w = DI("fox_q_norm_w", [64])
    fox_k_norm_w = DI("fox_k_norm_w", [64])
    fox_o_norm_w = DI("fox_o_norm_w", [1, 64])
    w_out = DI("w_out", [D, D])
    norm_ffn_w = DI("norm_ffn_w", [1, D])
    ffn_w_up = DI("ffn_w_up", [D, 2 * DFF])
    ffn_conv_w = DI("ffn_conv_w", [3, 2 * DFF])
    ffn_conv_b = DI("ffn_conv_b", [2 * DFF])
    ffn_w_down = DI("ffn_w_down", [DFF, D])
    norm_final_w = DI("norm_final_w", [1, D])
    out = nc.dram_tensor("out", [T, D], F32, kind="ExternalOutput").ap()

    okind = "ExternalOutput" if dbg else "Internal"
    YR = nc.dram_tensor("yr", [4, 128, T], BF16, kind=okind).ap()
    YF = nc.dram_tensor("yf", [4, 128, T], BF16, kind=okind).ap()
    SGd = nc.dram_tensor("sgd", [T, 512], BF16, kind="Internal").ap()
    X1 = nc.dram_tensor("x1", [T, D], F32, kind=okind).ap()

    with contextlib.ExitStack() as es0:
        esems = [es0.enter_context(nc.semaphore(f"es{i}")) for i in range(5)]
        dsems = [es0.enter_context(nc.semaphore(f"ds{i}")) for i in range(24)]
        S = Sched(nc, esems, dsems)
        k = K(S)

        uid = [0]

        def SB(es, name, shape, dt=F32):
            uid[0] += 1
            return es.enter_context(nc.sbuf_tensor(f"{name}_u{uid[0]}", list(shape), dt))

        pst = es0.enter_context(nc.psum_tensor("pst", [128, 1024], BF16))
        pbig = [es0.enter_context(nc.psum_tensor(f"pb{i}", [128, 512], F32)) for i in range(7)]
        st = {'big': 0, 'q': 0}

        def PB():
            i = st['big'] % 7
            st['big'] += 1
            return pbig[i], f"pb{i}"

        def PQ():
            i = st['q'] % 16
            st['q'] += 1
            return pqb[i // 4][:, (i % 4) * 128:(i % 4) * 128 + 128], f"pq{i}"

        def PQbank():
            return PQ()

        ones = SB(es0, "ones", [128, 128])
        ident = SB(es0, "ident", [128, 128])
        identb = SB(es0, "identb", [128, 128], BF16)
        mST = SB(es0, "mST", [128, 128])
        mIT = SB(es0, "mIT", [128, 128])
        mS = SB(es0, "mS", [128, 128])
        bones_raw = SB(es0, "bones_raw", [128, 128])
        bones = SB(es0, "bones", [128, 128])
        k.memset('pool', ones[:], 1.0, ['ones'])
        k.asel(ident[:], ones[:], [[-1, 128]], ALU.is_equal, 0, 1, ['ones'], ['ident'])
        k.cp('dve', identb[:], ident[:], ['ident'], ['identb'])
        for m_, nm in ((mST, 'mST'), (mIT, 'mIT'), (mS, 'mS'), (bones_raw, 'bones_raw')):
            k.memset('pool', m_[:], 0.0, [nm])
        for b in range(2):
            sl = slice(64 * b, 64 * b + 64)
            k.asel(mST[sl, sl], ones[sl, sl], [[1, 64]], ALU.is_ge, -1, -1, ['ones', 'mST'], ['mST'])
            k.asel(mIT[sl, sl], ones[sl, sl], [[1, 64]], ALU.is_ge, 0, -1, ['ones', 'mIT'], ['mIT'])
            k.asel(mS[sl, sl], ones[sl, sl], [[-1, 64]], ALU.is_ge, -1, 1, ['ones', 'mS'], ['mS'])
            k.cp('pool', bones_raw[sl, sl], ones[sl, sl], ['ones', 'bones_raw'], ['bones_raw'])

        k.cp('dve', R(bones[:]), bones_raw[:], ['bones_raw'], ['bones'])
        nwb = SB(es0, "nwb", [128, D])

        def rms_to_hT(es, xsrc, t0, hT, nwname, tagp, ntile=4, xtl=None, nwt=None, hTn='hT'):
            nwt_ = nwb if nwt is None else nwt
            for i in range(ntile):
                if xtl is None:
                    par = i % 2
                    xt, xtn = xts[par], f'xt{par}'
                else:
                    xt, xtn = xtl[i]
                k.dma('sp', xt[:], xsrc[t0 + 128 * i:t0 + 128 * i + 128, :], [], [xtn])
                k.act(xn[:], xt[:], AF.Square, [xtn], ['xn', 'ss'], accum_out=ss[:])
                k.act(ss[:], ss[:], AF.Sqrt, ['ss'], ['ss'], bias=1e-6, scale=1.0 / D)
                k.recip(rs[:], ss[:], ['ss'], ['rs'])
                k.stt(xn[:], xt[:], rs[:, 0:1], nwt_[:], ALU.mult, ALU.mult, [xtn, 'rs', nwname], ['xn'])
                for kk_ in range(8):
                    k.tr(pst[:, 128 * kk_:128 * kk_ + 128], xn[:, 128 * kk_:128 * kk_ + 128], identb[:],
                         ['xn', 'identb'], ['pst'])
                k.cp('act', hT[:, :, 128 * i:128 * i + 128], pst[:].rearrange("p (k t) -> p k t", t=128),
                     ['pst'], [hTn])

        class Rec:
            def __init__(self):
                self.ops = []

            def op(self, eng, fn, reads=(), writes=()):
                self.ops.append(('op', eng, fn, tuple(reads), tuple(writes)))

            def dma(self, eng, fn, reads=(), writes=()):
                self.ops.append(('dma', eng, fn, tuple(reads), tuple(writes)))

        def record(fn, *a_):
            rec = Rec()
            k.S = rec
            try:
                fn(*a_)
            finally:
                k.S = S
            return rec.ops

        def replay(ops):
            for kind, eng, fn, r, w in ops:
                (S.op if kind == 'op' else S.dma)(eng, fn, r, w)

        def merge(*lists):
            lists = [l for l in lists if len(l) > 0]
            pos = [0] * len(lists)
            o = []
            total = sum(len(l) for l in lists)
            while len(o) < total:
                best, bf = None, None
                for i, l in enumerate(lists):
                    if pos[i] < len(l):
                        f = pos[i] / len(l)
                        if bf is None or f < bf:
                            best, bf = i, f
                o.append(lists[best][pos[best]])
                pos[best] += 1
            return o

        if 'A1' in phases:
            with contextlib.ExitStack() as es:
                WB = 256
                NBL = T // WB
                cE = {'n': 0}
                cC = {'n': 0}

                def PBE():
                    i = cE['n'] % 3
                    cE['n'] += 1
                    return pbig[i], f"pb{i}"

                def PBC():
                    i = 3 + cC['n'] % 4
                    cC['n'] += 1
                    return pbig[i], f"pb{i}"

                def bc4(m):
                    return m[:].unsqueeze(1).to_broadcast([128, 4, 128])

                k.dma('sp', nwb[:], norm_mix_w.partition_broadcast(128), [], ['nwb'])
                Win = SB(es, "WinR", [128, 8, RW], BF16)
                for kc in range(8):
                    k.dma('pool', Win[:, kc, :], w_in[128 * kc:128 * kc + 128, 0:RW], [], ['Win'])
                xts = [SB(es, f"xt{i}", [128, D]) for i in range(2)]
                ss = SB(es, "ss", [128, 1])
                rs = SB(es, "rs", [128, 1])
                xn = SB(es, "xn", [128, D], BF16)
                hT = SB(es, "hT", [128, 8, WB], BF16)
                T1 = [SB(es, f"T1_{i}", [128, WB]) for i in range(2)]
                carry = SB(es, "carry", [128, 14])
                X = SB(es, "X", [128, 14, WB])
                mu_cm = SB(es, "mu_cm", [128, 14])
                omm_cm = SB(es, "omm_cm", [128, 14])
                prm = SB(es, "prm", [128, 7, 4])
                omka = SB(es, "omka", [128, 4])
                W2Z = SB(es, "W2Z", [128, 512])
                A2Z = SB(es, "A2Z", [128, 512])
                G2 = SB(es, "G2", [128, 512])
                Wraw = [SB(es, f"Wraw{i}", [128, 512]) for i in range(3)]
                rmask = SB(es, "rmask", [128, WB])
                k.dma('sp', mu_cm[:], rwkv_mu.rearrange("(t p) -> p t", p=128), [], ['mu_cm'], allow_slow_non_contiguous=True)
                for i, prm_in in enumerate((rwkv_w0, rwkv_a0, rwkv_k_k, rwkv_k_a, rwkv_r_k, rwkv_lnx_w, rwkv_lnx_b)):
                    k.dma('sp', prm[:, i, :], prm_in.rearrange("(t p) -> p t", p=128), [], ['prm'], allow_slow_non_contiguous=True)
                k.ts('dve', omm_cm[:], mu_cm[:], -1.0, 1.0, ALU.mult, ALU.add, ['mu_cm'], ['omm_cm'])
                k.ts('dve', omka[:], prm[:, 3, :], -1.0, 1.0, ALU.mult, ALU.add, ['prm'], ['omka'])
                k.memset('pool', Wraw[0][:], 0.0, ['Wraw0'])
                k.memset('pool', Wraw[1][:], 0.0, ['Wraw1'])
                k.dma('sp', Wraw[0][0:64, :], rwkv_w2, [], ['Wraw0'])
                k.dma('sp', Wraw[1][64:128, :], rwkv_a2, [], ['Wraw1'])
                k.dma('sp', Wraw[2][:], rwkv_g2, [], ['Wraw2'])
                for i_, (w_, wn_) in enumerate(((W2Z, 'W2Z'), (A2Z, 'A2Z'), (G2, 'G2'))):
                    k.cp('dve', R(w_[:]), Wraw[i_][:], [f'Wraw{i_}'], [wn_])
                k.memset('pool', rmask[:], 1.0, ['rmask'])
                k.memset('pool', rmask[:].rearrange("p (c t) -> p c t", t=64)[:, :, 0:1], 0.0, ['rmask'])
                k.memset('pool', carry[:], 0.0, [f'carry{c_}' for c_ in range(14)])

                TA = SB(es, "TA", [128, WB])
                SGg = SB(es, "SGg", [128, WB])
                tmp = [SB(es, f"tmp{i}", [128, WB]) for i in range(9)]
                SQr = SB(es, "SQr", [128, WB])
                BDn = ('RT', 'AT', 'BT', 'KT', 'BH', 'KH', 'VT')
                BDp = [{n: SB(es, f"BD{p}_{n}", [128, 4, 128]) for n in BDn} for p in range(2)]
                Gtp = [SB(es, f"Gt{p}", [128, WB]) for p in range(2)]
                BStp = [SB(es, f"BSt{p}", [128, WB]) for p in range(2)]
                PCtp = [SB(es, f"PCt{p}", [128, 4]) for p in range(2)]
                for p in range(2):
                    for n in BDn:
                        k.memset('pool', BDp[p][n][:], 0.0, [f'BD{p}_{n}'])
                        k.cp('dve', R(BDp[p][n][:]), BDp[p][n][:], [f'BD{p}_{n}'], [f'BD{p}_{n}'])
                TMn = ('A', 'BH', 'KH', 'V')
                TM = {n: SB(es, f"TM_{n}", [128, 4, 128]) for n in TMn}
                CMn = ('NTa', 'NTb', 'Na', 'Nb', 'ST', 'AKT', 'MRBT', 'MRKT', 'ApT', 'W2', 'Vp', 'U')
                CM = {n: SB(es, f"CM_{n}", [128, 4, 128]) for n in CMn}
                H = [SB(es, f"H{p}", [128, 128]) for p in range(4)]
                for p in range(4):
                    k.memset('pool', H[p][:], 0.0, [f'H{p}'])
                    k.cp('dve', R(H[p][:]), H[p][:], [f'H{p}'], [f'H{p}'])
                YT = SB(es, "YT", [128, WB])
                G1 = SB(es, "G1", [128, WB])
                G2t = SB(es, "G2t", [128, WB])
                SQr2 = SB(es, "SQr2", [128, WB])
                YO = SB(es, "YO", [128, WB], BF16)

                def XR(ct):
                    return [f'X{ct}a', f'X{ct}b']

                def stage_E(b, pr):
                    par = (4 * b + pr) % 2
                    BD = BDp[par]
                    bdn = f'BD{par}_'
                    t0 = WB * b
                    if pr == 0:
                        rms_to_hT(es, x, t0, hT, 'nwb', 'a1', ntile=2)
                        for ct in range(14):
                            pb, pbn = PBE()
                            for kc in range(8):
                                k.mm(pb[:, 0:WB], Win[:, kc, 128 * ct:128 * ct + 128], hT[:, kc, :], ['Win', 'hT'], [pbn],
                                     start=(kc == 0), stop=(kc == 7))
                            t1 = T1[ct % 2]
                            t1n = f'T1_{ct % 2}'
                            k.act(t1[:], pb[:, 0:WB], AF.Identity, [pbn, 'mu_cm'], [t1n], scale=mu_cm[:, ct:ct + 1])
                            xn_ = f'X{ct}'
                            k.stt(X[:, ct, 1:WB], pb[:, 1:WB], omm_cm[:, ct:ct + 1], t1[:, 0:WB - 1], ALU.mult, ALU.add,
                                  [pbn, t1n, 'omm_cm'], [xn_ + 'a'])
                            k.stt(X[:, ct, 0:1], pb[:, 0:1], omm_cm[:, ct:ct + 1], carry[:, ct:ct + 1], ALU.mult, ALU.add,
                                  [pbn, f'carry{ct}', 'omm_cm'], [xn_ + 'b'])
                            k.cp('pool', carry[:, ct:ct + 1], t1[:, WB - 1:WB], [t1n], [f'carry{ct}'])
                        k.act(R(TA[0:64, :]), X[0:64, 12, :], AF.Tanh, XR(12), ['TAa'])
                        k.cp('pool', R(TA[64:128, :]), X[64:128, 12, :], XR(12), ['TAb'])
                        k.act(R(SGg[:]), X[:, 13, :], AF.Sigmoid, XR(13), ['SGg'])
                    cs = slice(128 * pr, 128 * pr + 128)
                    Xr, Xk, Xv = X[:, pr, :], X[:, 4 + pr, :], X[:, 8 + pr, :]
                    rXr, rXk, rXv = XR(pr), XR(4 + pr), XR(8 + pr)
                    SIG, A_, KKN, KP, BV, L, E1, E2, E3 = tmp
                    tn = [f'tmp{i}' for i in range(9)]
                    Gt, Gtn = Gtp[par], f'Gt{par}'
                    BSt, BStn = BStp[par], f'BSt{par}'
                    PCt, PCtn = PCtp[par], f'PCt{par}'

                    def bdwrite(eng, name, in0, in1, op, r):
                        for hh in range(2):
                            ps_ = slice(64 * hh, 64 * hh + 64)
                            o = R(BD[name][ps_, :, 64 * hh:64 * hh + 64])
                            a0 = in0[ps_, :].rearrange("p (c t) -> p c t", t=64)
                            if in1 is None:
                                k.cp(eng, o, a0, r, [bdn + name])
                            else:
                                a1 = in1[ps_, :].rearrange("p (c t) -> p c t", t=64)
                                k.tt(eng, o, a0, a1, op, r, [bdn + name])

                    pb, pbn = PBE()
                    k.mm(pb[:, 0:WB], W2Z[:, cs], TA[:], ['W2Z', 'TAa', 'TAb'], [pbn], r32=True)
                    k.act(SIG[:], pb[:, 0:WB], AF.Sigmoid, [pbn, 'prm'], [tn[0]], bias=prm[:, 0, pr:pr + 1])
                    pb, pbn = PBE()
                    k.mm(pb[:, 0:WB], A2Z[:, cs], TA[:], ['A2Z', 'TAa', 'TAb'], [pbn], r32=True)
                    k.act(A_[:], pb[:, 0:WB], AF.Sigmoid, [pbn, 'prm'], [tn[1]], bias=prm[:, 1, pr:pr + 1])
                    pb, pbn = PBE()
                    k.mm(pb[:, 0:WB], G2[:, cs], SGg[:], ['G2', 'SGg'], [pbn], r32=True)
                    k.cp('act', Gt[:], pb[:, 0:WB], [pbn], [Gtn])
                    k.ts('dve', KKN[:], Xk, prm[:, 2, pr:pr + 1], None, ALU.mult, None, rXk + ['prm'], [tn[2]])
                    k.act(R(SQr[:]), KKN[:], AF.Square, [tn[2]], ['SQr'])
                    pb, pbn = PBE()
                    k.mm(pb[:, 0:WB], bones[:], SQr[:], ['bones', 'SQr'], [pbn], r32=True)
                    k.ts('dve', E1[:], pb[:, 0:WB], 1e-24, None, ALU.max, None, [pbn], [tn[6]])
                    k.act(E1[:], E1[:], AF.Sqrt, [tn[6]], [tn[6]])
                    k.recip(E1[:], E1[:], [tn[6]], [tn[6]])
                    k.tt('dve', KKN[:], KKN[:], E1[:], ALU.mult, [tn[2], tn[6]], [tn[2]])
                    k.ts('dve', KP[:], A_[:], prm[:, 3, pr:pr + 1], omka[:, pr:pr + 1], ALU.mult, ALU.add,
                         [tn[1], 'prm', 'omka'], [tn[3]])
                    k.tt('dve', KP[:], KP[:], Xk, ALU.mult, [tn[3]] + rXk, [tn[3]])
                    k.tt('pool', BV[:], KKN[:], A_[:], ALU.mult, [tn[2], tn[1]], [tn[4]])
                    k.stt(R(SQr[:]), Xr, prm[:, 4, pr:pr + 1], KP[:], ALU.mult, ALU.mult, rXr + ['prm', tn[3]], ['SQr'])
                    pb, pbn = PBE()
                    k.mm(pb[:, 0:WB], bones[:], SQr[:], ['bones', 'SQr'], [pbn], r32=True)
                    k.tt('dve', BSt[:], pb[:, 0:WB], Xv, ALU.mult, [pbn] + rXv, [BStn])
                    k.ts('dve', SIG[:], SIG[:], NEG_E05, None, ALU.mult, None, [tn[0]], [tn[0]])
                    k.scan(L[:], rmask[:], SIG[:], ['rmask', tn[0]], [tn[5]])
                    k.act(E1[:], L[:], AF.Exp, [tn[5]], [tn[6]])
                    bdwrite('dve', 'RT', Xr, E1, ALU.mult, rXr + [tn[6]])
                    k.tt('pool', E2[:], L[:], SIG[:], ALU.subtract, [tn[5], tn[0]], [tn[7]])
                    k.act(E2[:], E2[:], AF.Exp, [tn[7]], [tn[7]])
                    k.ts('dve', E2[:], E2[:], -1.0, None, ALU.mult, None, [tn[7]], [tn[7]])
                    bdwrite('dve', 'AT', KKN, E2, ALU.mult, [tn[2], tn[7]])
                    k.act(E3[:], L[:], AF.Exp, [tn[5]], [tn[8]], scale=-1.0)
                    bdwrite('dve', 'BT', BV, E3, ALU.mult, [tn[4], tn[8]])
                    bdwrite('pool', 'KT', KP, E3, ALU.mult, [tn[3], tn[8]])
                    L3 = L[:].rearrange("p (c t) -> p c t", t=64)
                    k.tt('dve', E1[:].rearrange("p (c t) -> p c t", t=64), L3,
                         L3[:, :, 63:64].to_broadcast([128, 4, 64]), ALU.subtract, [tn[5]], [tn[6]])
                    k.act(E1[:], E1[:], AF.Exp, [tn[6]], [tn[6]], scale=-1.0)
                    bdwrite('dve', 'BH', BV, E1, ALU.mult, [tn[4], tn[6]])
                    bdwrite('pool', 'KH', KP, E1, ALU.mult, [tn[3], tn[6]])
                    k.act(PCt[:], L3[:, :, 63], AF.Exp, [tn[5]], [PCtn])
                    bdwrite('pool', 'VT', Xv, None, None, rXv)

                def stage_C(b, pr):
                    par = (4 * b + pr) % 2
                    BD = BDp[par]
                    bdn = f'BD{par}_'
                    t0 = WB * b
                    Gt, Gtn = Gtp[par], f'Gt{par}'
                    BSt, BStn = BStp[par], f'BSt{par}'
                    PCt, PCtn = PCtp[par], f'PCt{par}'
                    Hp, Hn = H[pr], f'H{pr}'

                    def q4(pb_, i):
                        return pb_[:, 128 * i:128 * i + 128]

                    def v4(pb_):
                        return pb_[:].rearrange("p (i t) -> p i t", t=128)
                    for (ln, rn, mk, mkn, on) in (('BT', 'AT', mST, 'mST', 'NTa'), ('AT', 'BT', mS, 'mS', 'Na'),
                                                  ('KT', 'AT', mST, 'mST', 'AKT'), ('BT', 'RT', mIT, 'mIT', 'MRBT'),
                                                  ('KT', 'RT', mIT, 'mIT', 'MRKT')):
                        pb, pbn = PBC()
                        for c in range(4):
                            k.mm(q4(pb, c), BD[ln][:, c, :], BD[rn][:, c, :], [bdn + ln, bdn + rn], [pbn], r32=True)
                        k.tt('dve', R(CM[on][:]), v4(pb), bc4(mk), ALU.mult, [pbn, mkn], ['CM_' + on])
                    for (src, dst) in (('AT', 'A'), ('BH', 'BH'), ('KH', 'KH'), ('VT', 'V')):
                        pb, pbn = PBC()
                        for c in range(4):
                            k.tr(q4(pb, c), BD[src][:, c, :], ident[:], [bdn + src, 'ident'], [pbn])
                        k.cp('act', R(TM[dst][:]), v4(pb), [pbn], ['TM_' + dst])
                    k.tt('pool', R(CM['ST'][:]), CM['NTa'][:], bc4(ident), ALU.add, ['CM_NTa', 'ident'], ['CM_ST'])
                    curN, curNT = 'Na', 'NTa'
                    for lev in range(1, 6):
                        nxtN = 'Nb' if curN == 'Na' else 'Na'
                        nxtNT = 'NTb' if curNT == 'NTa' else 'NTa'
                        pb, pbn = PBC()
                        for i in range(4):
                            k.mm(q4(pb, i), CM[curNT][:, i, :], CM[curN][:, i, :], ['CM_' + curNT, 'CM_' + curN], [pbn], r32=True)
                        k.cp('act', R(CM[nxtN][:]), v4(pb), [pbn], ['CM_' + nxtN])
                        if lev < 5:
                            pb, pbn = PBC()
                            for i in range(4):
                                k.mm(q4(pb, i), CM[curN][:, i, :], CM[curNT][:, i, :], ['CM_' + curNT, 'CM_' + curN], [pbn], r32=True)
                            k.cp('act', R(CM[nxtNT][:]), v4(pb), [pbn], ['CM_' + nxtNT])
                        pb, pbn = PBC()
                        for i in range(4):
                            k.mm(q4(pb, i), CM[nxtN][:, i, :], CM['ST'][:, i, :], ['CM_' + nxtN, 'CM_ST'], [pbn], r32=True)
                        k.tt('dve', R(CM['ST'][:]), v4(pb), CM['ST'][:], ALU.add, [pbn, 'CM_ST'], ['CM_ST'])
                        curN, curNT = nxtN, nxtNT
                    pb, pbn = PBC()
                    for i in range(4):
                        k.mm(q4(pb, i), TM['A'][:, i, :], CM['ST'][:, i, :], ['TM_A', 'CM_ST'], [pbn], r32=True)
                    k.cp('act', R(CM['ApT'][:]), v4(pb), [pbn], ['CM_ApT'])
                    pb, pbn = PBC()
                    for i in range(4):
                        k.mm(q4(pb, i), CM['AKT'][:, i, :], TM['V'][:, i, :], ['CM_AKT', 'TM_V'], [pbn], r32=True)
                    k.cp('act', R(CM['W2'][:]), v4(pb), [pbn], ['CM_W2'])
                    pb, pbn = PBC()
                    for i in range(4):
                        k.mm(q4(pb, i), CM['ST'][:, i, :], CM['W2'][:, i, :], ['CM_ST', 'CM_W2'], [pbn], r32=True)
                    k.cp('act', R(CM['Vp'][:]), v4(pb), [pbn], ['CM_Vp'])
                    for c in range(4):
                        i = c
                        un = f'CM_U{i}'
                        pb, pbn = PBC()
                        k.mm(q4(pb, 0), CM['ApT'][:, i, :], Hp[:], ['CM_ApT', Hn], [pbn], r32=True)
                        k.tt('dve', R(CM['U'][:, i, :]), q4(pb, 0), CM['Vp'][:, i, :], ALU.add, [pbn, 'CM_Vp'], [un])
                        pb, pbn = PBC()
                        k.mm(q4(pb, 0), Hp[:], BD['RT'][:, c, :], [Hn, bdn + 'RT'], [pbn], r32=True, start=True, stop=False)
                        k.mm(q4(pb, 0), CM['U'][:, i, :], CM['MRBT'][:, i, :], [un, 'CM_MRBT'], [pbn], r32=True,
                             start=False, stop=False)
                        k.mm(q4(pb, 0), TM['V'][:, i, :], CM['MRKT'][:, i, :], ['TM_V', 'CM_MRKT'], [pbn], r32=True,
                             start=False, stop=True)
                        for hh in range(2):
                            ps_ = slice(64 * hh, 64 * hh + 64)
                            k.cp('act', R(YT[ps_, 64 * c:64 * c + 64]), pb[ps_, 64 * hh:64 * hh + 64], [pbn], [f'YT{hh}'])
                        pb, pbn = PBC()
                        k.mm(q4(pb, 0), TM['BH'][:, i, :], CM['U'][:, i, :], ['TM_BH', un], [pbn], r32=True,
                             start=True, stop=False)
                        k.mm(q4(pb, 0), TM['KH'][:, i, :], TM['V'][:, i, :], ['TM_KH', 'TM_V'], [pbn], r32=True,
                             start=False, stop=True)
                        k.stt(R(Hp[:]), Hp[:], PCt[:, c:c + 1], q4(pb, 0), ALU.mult, ALU.add, [Hn, PCtn, pbn], [Hn])
                    rYT = ['YT0', 'YT1']
                    pb, pbn = PBC()
                    k.mm(pb[:, 0:WB], bones[:], YT[:], ['bones'] + rYT, [pbn], r32=True)
                    k.stt(G1[:], pb[:, 0:WB], -1.0 / 64, YT[:], ALU.mult, ALU.add, [pbn] + rYT, ['G1'])
                    k.act(R(SQr2[:]), G1[:], AF.Square, ['G1'], ['SQr2'])
                    pb, pbn = PBC()
                    k.mm(pb[:, 0:WB], bones[:], SQr2[:], ['bones', 'SQr2'], [pbn], r32=True)
                    k.act(G2t[:], pb[:, 0:WB], AF.Sqrt, [pbn], ['G2t'], bias=64e-5, scale=1.0 / 64)
                    k.recip(G2t[:], G2t[:], ['G2t'], ['G2t'])
                    k.tt('dve', G1[:], G1[:], G2t[:], ALU.mult, ['G1', 'G2t'], ['G1'])
                    k.ts('dve', G1[:], G1[:], prm[:, 5, pr:pr + 1], prm[:, 6, pr:pr + 1], ALU.mult, ALU.add,
                         ['G1', 'prm'], ['G1'])
                    k.tt('pool', G1[:], G1[:], BSt[:], ALU.add, ['G1', BStn], ['G1'])
                    k.tt('pool', YO[:], G1[:], Gt[:], ALU.mult, ['G1', Gtn], ['YO'])
                    k.dma('sp', YR[pr, :, t0:t0 + WB], YO[:], ['YO'], ['YR'])

                units = [(b, pr) for b in range(NBL) for pr in range(4)]
                replay(record(stage_E, *units[0]))
                for n, u in enumerate(units):
                    oc = record(stage_C, *u)
                    oe = record(stage_E, *units[n + 1]) if n + 1 < len(units) else []
                    replay(merge(oc, oe))
                S.wait_all('sp')
                S.flush()

        if 'A2' in phases:
            with contextlib.ExitStack() as esAB:
                QT = SB(esAB, "QT", [128, 4, T], BF16)
                KT_ = SB(esAB, "KTf", [128, 4, T], BF16)
                V1 = SB(esAB, "V1", [128, NT, 8, 65], BF16)
                LFs = SB(esAB, "LFs", [128, NT, 8])
                k.memset('pool', V1[:], 1.0, ['V1'])
                with contextlib.ExitStack() as es:
                    k.dma('sp', nwb[:], norm_mix_w.partition_broadcast(128), [], ['nwb'])
                    Wf = SB(es, "Wf", [128, 8, FOXC], BF16)
                    for kc in range(8):
                        k.dma('pool', Wf[:, kc, 0:1024], w_in[128 * kc:128 * kc + 128, RW:RW + 1024], [], ['Wf'])
                        k.dma('pool', Wf[:, kc, 1024:FOXC], w_in[128 * kc:128 * kc + 128, RW + 1024:RW + FOXC], [], ['Wf'])
                    xts = [SB(es, f"xt{i}", [128, D]) for i in range(2)]
                    ss = SB(es, "ss", [128, 1])
                    rs = SB(es, "rs", [128, 1])
                    xn = SB(es, "xn", [128, D], BF16)
                    hT = SB(es, "hT", [128, 8, 512], BF16)
                    qkw = SB(es, "qkw", [128, 2])
                    fbb = SB(es, "fbb", [128, 8])
                    sq = SB(es, "sq", [128, 512])
                    rq = SB(es, "rq", [128, 512])
                    SGt = [SB(es, f"SGt{i}", [128, 512], BF16) for i in range(2)]
                    zt = SB(es, "zt", [128, 8])
                    for hh in range(2):
                        k.dma('sp', qkw[64 * hh:64 * hh + 64, 0:1], fox_q_norm_w.rearrange("(p o) -> p o", o=1), [], ['qkw'])
                        k.dma('sp', qkw[64 * hh:64 * hh + 64, 1:2], fox_k_norm_w.rearrange("(p o) -> p o", o=1), [], ['qkw'])
                    k.dma('sp', fbb[:], fox_f_bias.partition_broadcast(128), [], ['fbb'])
                    hTs = [hT, SB(es, "hTb", [128, 8, 512], BF16)]
                    sqs = [sq, SB(es, "sqb", [128, 512])]
                    rqs = [rq, SB(es, "rqb", [128, 512])]
                    cq = {'n': 0}
                    cv = {'n': 0}

                    def PBQ():
                        i = cq['n'] % 3
                        cq['n'] += 1
                        return pbig[i], f"pb{i}"

                    def PBV():
                        i = 3 + cv['n'] % 4
                        cv['n'] += 1
                        return pbig[i], f"pb{i}"

                    def prepH(b):
                        rms_to_hT(es, x, 512 * b, hTs[b % 2], 'nwb', 'a2', hTn=f'hT{b % 2}')

                    def streamQ(b):
                        t0 = 512 * b
                        hT_, hTn_ = hTs[b % 2], f'hT{b % 2}'
                        for ct in range(8):
                            sq_, sqn_ = sqs[ct % 2], f'sq{ct % 2}'
                            rq_, rqn_ = rqs[ct % 2], f'rq{ct % 2}'
                            pb, pbn = PBQ()
                            for kc in range(8):
                                k.mm(pb[:], Wf[:, kc, 128 * ct:128 * ct + 128], hT_[:, kc, :], ['Wf', hTn_], [pbn],
                                     start=(kc == 0), stop=(kc == 7))
                            k.act(R(sq_[:]), pb[:], AF.Square, [pbn], [sqn_])
                            pb2, pbn2 = PBQ()
                            k.mm(pb2[:], bones[:], sq_[:], ['bones', sqn_], [pbn2], r32=True)
                            k.act(rq_[:], pb2[:], AF.Sqrt, [pbn2], [rqn_], bias=1e-6, scale=1.0 / 64)
                            k.recip(rq_[:], rq_[:], [rqn_], [rqn_])
                            dst = QT if ct < 4 else KT_
                            dn_ = 'QT' if ct < 4 else 'KTf'
                            k.stt(dst[:, ct % 4, t0:t0 + 512], pb[:], qkw[:, (ct // 4):(ct // 4) + 1], rq_[:], ALU.mult, ALU.mult,
                                  [pbn, 'qkw', rqn_], [dn_])

                    def streamV(b):
                        t0 = 512 * b
                        hT_, hTn_ = hTs[b % 2], f'hT{b % 2}'
                        for i in range(4):
                            ti = 4 * b + i
                            tsl = slice(128 * i, 128 * i + 128)
                            pb, pbn = PBV()
                            for kc in range(8):
                                k.mm(pb[:], hT_[:, kc, tsl], Wf[:, kc, 1024:1536], ['Wf', hTn_], [pbn], start=(kc == 0), stop=(kc == 7))
                            k.cp('act', V1[:, ti, :, 0:64], pb[:].rearrange("p (h d) -> p h d", d=64), [pbn], ['V1'])
                            pb, pbn = PBV()
                            for kc in range(8):
                                k.mm(pb[:], hT_[:, kc, tsl], Wf[:, kc, 1536:2048], ['Wf', hTn_], [pbn], start=(kc == 0), stop=(kc == 7))
                            sg, sgn = SGt[i % 2], f'SGt{i % 2}'
                            k.act(sg[:], pb[:], AF.Sigmoid, [pbn], [sgn])
                            k.dma('sp', SGd[t0 + 128 * i:t0 + 128 * i + 128, :], sg[:], [sgn], ['SGd'])
                            pb, pbn = PBV()
                            for kc in range(8):
                                k.mm(pb[:, 0:8], hT_[:, kc, tsl], Wf[:, kc, 2048:2056], ['Wf', hTn_], [pbn], start=(kc == 0), stop=(kc == 7))
                            k.tt('dve', zt[:], pb[:, 0:8], fbb[:], ALU.add, [pbn, 'fbb'], ['zt'])
                            k.act(zt[:], zt[:], AF.Exp, ['zt'], ['zt'], scale=-1.0)
                            k.act(LFs[:, ti, :], zt[:], AF.Ln, ['zt'], ['LFs'], bias=1.0)

                    replay(record(prepH, 0))
                    for b in range(NB):
                        ls = [record(streamQ, b), record(streamV, b)]
                        if b + 1 < NB:
                            ls.append(record(prepH, b + 1))
                        replay(merge(*ls))
                    S.wait_all('sp')
                S.flush()
                with contextlib.ExitStack() as es:
                    tri = SB(es, "tri", [128, 128])
                    cmask = SB(es, "cmask", [128, 128], BF16)
                    NCk = SB(es, "NCk", [128, NT, 8])
                    TOT = SB(es, "TOT", [128, NT, 8])
                    CAR = SB(es, "CAR", [128, NT, 8])
                    NBq = SB(es, "NBq", [128, NT, 8])
                    ownb = SB(es, "ownb", [128, 64])
                    k.asel(tri[:], ones[:], [[1, 128]], ALU.is_ge, 0, -1, ['ones'], ['tri'])
                    k.cp('dve', cmask[:], tri[:], ['tri'], ['cmask'])
                    k.dma('sp', ownb[:], fox_o_norm_w.partition_broadcast(128), [], ['ownb'])
                    LF2 = LFs[:].rearrange("p t h -> p (t h)")
                    nchunk = (NT * 8 + 511) // 512
                    for cc in range(nchunk):
                        c0 = 512 * cc
                        c1 = min(NT * 8, c0 + 512)
                        pb, pbn = PB()
                        k.mm(pb[:, 0:c1 - c0], tri[:], LF2[:, c0:c1], ['tri', 'LFs'], [pbn])
                        k.cp('act', NCk[:].rearrange("p t h -> p (t h)")[:, c0:c1], pb[:, 0:c1 - c0], [pbn], ['NCk'])
                        pb, pbn = PB()
                        k.mm(pb[:, 0:c1 - c0], ones[:], LF2[:, c0:c1], ['ones', 'LFs'], [pbn])
                        k.cp('act', TOT[:].rearrange("p t h -> p (t h)")[:, c0:c1], pb[:, 0:c1 - c0], [pbn], ['TOT'])
                    k.memset('pool', CAR[:, 0, :], 0.0, ['CAR'])
                    for ti in range(1, NT):
                        k.tt('dve', CAR[:, ti, :], CAR[:, ti - 1, :], TOT[:, ti - 1, :], ALU.add, ['CAR', 'TOT'], ['CAR'])
                    k.tt('dve', NCk[:], NCk[:], CAR[:], ALU.add, ['NCk', 'CAR'], ['NCk'])
                    k.stt(NBq[:], TOT[:], 0.5, CAR[:], ALU.mult, ALU.add, ['TOT', 'CAR'], ['NBq'])
                    biasT = [SB(es, f"biasT{i}", [128, NT]) for i in range(2)]
                    PT = [SB(es, f"PT{i}", [128, 128], BF16) for i in range(8)]
                    RL = SB(es, "RL", [128, 8])
                    Ot = SB(es, "Ot", [128, 8, 64])
                    O2 = SB(es, "O2", [128, 8, 64])
                    ssq = SB(es, "ssq", [128, 8])
                    SGl = SB(es, "SGl", [128, 512], BF16)
                    YFt = SB(es, "YFt", [128, 512], BF16)
                    YFo = SB(es, "YFo", [128, 4, 128], BF16)
                    pS = [(pbig[i], f'pb{i}') for i in range(3)]
                    pO = [(pbig[3 + i], f'pb{3 + i}') for i in range(4)]
                    nS = 0
                    nP = 0
                    NQB = NT // 2
                    PT2 = [SB(es, f"PT2_{i}", [128, 256], BF16) for i in range(4)]
                    items = []
                    for qb in range(NQB):
                        for h in range(8):
                            kts_all = list(range(0, 2 * qb + 2))
                            for g0 in range(0, len(kts_all), 2):
                                items.append((qb, h, kts_all[g0:g0 + 2]))

                    def emit_st(it):
                        qb, h, kts = it
                        q0 = 256 * qb
                        pr, r0 = h // 2, 64 * (h % 2)
                        bt, btn = biasT[h % 2], f'biasT{h % 2}'
                        if kts[0] == 0:
                            nk = 2 * qb + 2
                            k.ts('dve', bt[:, 0:nk], NCk[:, 0:nk, h], CAR[:, 2 * qb + 1, h:h + 1], None, ALU.subtract, None,
                                 ['NCk', 'CAR'], [btn])
                        pb, pbn = pS[st['big'] % 3]
                        st['big'] += 1
                        for j, kt in enumerate(kts):
                            lo = 128 if kt == 2 * qb + 1 else 0
                            k.mm(pb[:, 256 * j + lo:256 * j + 256], KT_[r0:r0 + 64, pr, 128 * kt:128 * kt + 128],
                                 QT[r0:r0 + 64, pr, q0 + lo:q0 + 256], ['KTf', 'QT'], [pbn])
                        return pb, pbn

                    def emit_pv(it, pb, pbn):
                        qb, h, kts = it
                        bt, btn = biasT[h % 2], f'biasT{h % 2}'
                        pocol = 65 * (h % 4)
                        for j, kt in enumerate(kts):
                            lo = 128 if kt == 2 * qb + 1 else 0
                            pt, ptn = PT2[st['q'] % 4], f"PT2_{st['q'] % 4}"
                            st['q'] += 1
                            k.act(pt[:, lo:256], pb[:, 256 * j + lo:256 * j + 256], AF.Exp, [pbn, btn], [ptn],
                                  bias=bt[:, kt:kt + 1], scale=0.125)
                            for jq in range(2):
                                qt = 2 * qb + jq
                                if kt > qt:
                                    continue
                                if kt == qt:
                                    k.tt('pool', pt[:, 128 * jq:128 * jq + 128], pt[:, 128 * jq:128 * jq + 128], cmask[:], ALU.mult,
                                         [ptn, 'cmask'], [ptn])
                                po, pon = pO[2 * jq + h // 4]
                                k.mm(po[:, pocol:pocol + 65], pt[:, 128 * jq:128 * jq + 128], V1[:, kt, h, :], [ptn, 'V1'], [pon],
                                     start=(kt == 0), stop=(kt == qt))

                    def epilogue(qt, jq):
                        qsl = slice(128 * qt, 128 * qt + 128)
                        for hf in range(2):
                            po, pon = pO[2 * jq + hf]
                            po3 = po[:, 0:260].rearrange("p (h d) -> p h d", d=65)
                            k.recip(RL[:, 4 * hf:4 * hf + 4], po3[:, :, 64], [pon], [f'RL{hf}'])
                            k.tt('dve', Ot[:, 4 * hf:4 * hf + 4, :], po3[:, :, 0:64],
                                 RL[:, 4 * hf:4 * hf + 4].unsqueeze(2).to_broadcast([128, 4, 64]), ALU.mult,
                                 [pon, f'RL{hf}'], [f'Ot{hf}'])
                        rOt = ['Ot0', 'Ot1']
                        k.act(O2[:], Ot[:], AF.Square, rOt, ['O2'])
                        S.op('dve', lambda e: e.tensor_reduce(out=ssq[:], in_=O2[:], axis=AX.X, op=ALU.add), ['O2'], ['ssq'])
                        k.act(ssq[:], ssq[:], AF.Sqrt, ['ssq'], ['ssq'], bias=1e-6, scale=1.0 / 64)
                        k.recip(ssq[:], ssq[:], ['ssq'], ['ssq'])
                        k.tt('dve', O2[:], Ot[:], ssq[:].unsqueeze(2).to_broadcast([128, 8, 64]), ALU.mult, rOt + ['ssq'], ['O2'])
                        k.tt('pool', O2[:], O2[:], ownb[:].unsqueeze(1).to_broadcast([128, 8, 64]), ALU.mult, ['O2', 'ownb'], ['O2'])
                        k.dma('sp', SGl[:], SGd[qsl, :], ['SGd'], ['SGl'])
                        k.tt('dve', YFt[:], O2[:].rearrange("p h d -> p (h d)"), SGl[:], ALU.mult, ['O2', 'SGl'], ['YFt'])
                        for c4 in range(4):
                            k.tr(pst[:, 128 * c4:128 * c4 + 128], YFt[:, 128 * c4:128 * c4 + 128], identb[:], ['YFt', 'identb'], ['pst'])
                        k.cp('act', YFo[:], pst[:, 0:512].rearrange("p (c t) -> p c t", t=128), ['pst'], ['YFo'])
                        k.dma('sp', YF[:, :, qsl].rearrange("c p t -> p c t"), YFo[:], ['YFo'], ['YF'])
                    cur = emit_st(items[0])
                    for ii, it in enumerate(items):
                        nxt = emit_st(items[ii + 1]) if ii + 1 < len(items) else None
                        emit_pv(it, *cur)
                        cur = nxt
                        if it[1] == 7 and it[2][-1] == 2 * it[0] + 1:
                            epilogue(2 * it[0], 0)
                            epilogue(2 * it[0] + 1, 1)
                    S.wait_all('sp')
                S.flush()

        esC = es0.enter_context(contextlib.ExitStack())
        if 'C2' in phases:
            Wup = SB(esC, "Wup", [128, 8, 2 * DFF], BF16)
            Wd = SB(esC, "Wd", [128, 22, D], BF16)
        if 'C1' in phases:
            with contextlib.ExitStack() as es:
                Wo = SB(es, "Wo", [128, 8, D], BF16)
                for kc in range(8):
                    k.dma('pool', Wo[:, kc, :], w_out[128 * kc:128 * kc + 128, :], [], ['Wo'])
                if 'C2' in phases:
                    for kc in range(8):
                        for c0 in range(0, 2 * DFF, 2048):
                            c1 = min(2 * DFF, c0 + 2048)
                            k.dma('pool', Wup[:, kc, c0:c1], ffn_w_up[128 * kc:128 * kc + 128, c0:c1], [], ['Wup'])
                    for ft in range(22):
                        k.dma('pool', Wd[:, ft, :], ffn_w_down[128 * ft:128 * ft + 128, :], [], ['Wd'])
                Yb = [SB(es, f"Yb{i}", [128, 8, 512], BF16) for i in range(2)]
                xts = [SB(es, f"xt{i}", [128, D]) for i in range(2)]
                x1t = [SB(es, f"x1t{i}", [128, D]) for i in range(2)]
                for b in range(NB):
                    t0 = 512 * b
                    yb, ybn = Yb[b % 2], f'Yb{b % 2}'
                    k.dma('sp', yb[:, 0:4, :], YR[:, :, t0:t0 + 512].rearrange("c p t -> p c t"), ['YR'], [ybn])
                    k.dma('sp', yb[:, 4:8, :], YF[:, :, t0:t0 + 512].rearrange("c p t -> p c t"), ['YF'], [ybn])
                    for i in range(4):
                        par = i % 2
                        tsl = slice(128 * i, 128 * i + 128)
                        k.dma('sp', xts[par][:], x[t0 + 128 * i:t0 + 128 * i + 128, :], [], [f'xt{par}'])
                        for hf in range(2):
                            pb, pbn = PB()
                            for kc in range(8):
                                k.mm(pb[:], yb[:, kc, tsl], Wo[:, kc, 512 * hf:512 * hf + 512], [ybn, 'Wo'], [pbn],
                                     start=(kc == 0), stop=(kc == 7))
                            k.tt('dve', x1t[par][:, 512 * hf:512 * hf + 512], pb[:], xts[par][:, 512 * hf:512 * hf + 512], ALU.add,
                                 [pbn, f'xt{par}'], [f'x1t{par}'])
                        k.dma('sp', X1[t0 + 128 * i:t0 + 128 * i + 128, :], x1t[par][:], [f'x1t{par}'], ['X1'])
                S.wait_all('sp')
                S.flush()

        if 'C2' in phases:
            with contextlib.ExitStack() as es:
                NF = 2 * DFF // 128
                if 'C1' not in phases:
                    for kc in range(8):
                        for c0 in range(0, 2 * DFF, 2048):
                            c1 = min(2 * DFF, c0 + 2048)
                            k.dma('pool', Wup[:, kc, c0:c1], ffn_w_up[128 * kc:128 * kc + 128, c0:c1], [], ['Wup'])
                    for ft in range(22):
                        k.dma('pool', Wd[:, ft, :], ffn_w_down[128 * ft:128 * ft + 128, :], [], ['Wd'])
                nw2 = SB(es, "nw2", [128, D])
                nwf = SB(es, "nwf", [128, D])
                k.dma('sp', nw2[:], norm_ffn_w.partition_broadcast(128), [], ['nw2'])
                k.dma('sp', nwf[:], norm_final_w.partition_broadcast(128), [], ['nwf'])
                craw = SB(es, "craw", [128, 128])
                craw2 = SB(es, "craw2", [48, 128])
                cw = SB(es, "cw", [128, 176])
                k.dma('sp', craw[:], ffn_conv_w.rearrange("a (f p) -> (a f) p", p=128)[0:128, :], [], ['craw'])
                k.dma('sp', craw2[0:4, :], ffn_conv_w.rearrange("a (f p) -> (a f) p", p=128)[128:132, :], [], ['craw2'])
                k.dma('sp', craw2[4:48, :], ffn_conv_b.rearrange("(f p) -> f p", p=128), [], ['craw2'])
                pb, pbn = PB()
                k.tr(pb[:, 0:128], craw[:], ident[:], ['craw', 'ident'], [pbn])
                k.tr(pb[:, 128:176], craw2[:], ident[0:48, 0:48], ['craw2', 'ident'], [pbn])
                k.cp('act', cw[:], pb[:, 0:176], [pbn], ['cw'])
                x1sA = [[(SB(es, f"x1s{p}{i}", [128, D]), f'x1s{p}{i}') for i in range(2)] for p in range(2)]
                ss = SB(es, "ss", [128, 1])
                rs = SB(es, "rs", [128, 1])
                xn = SB(es, "xn", [128, D], BF16)
                h2Ts = [SB(es, f"h2T{p}", [128, 8, 256], BF16) for p in range(2)]
                Ub = [SB(es, f"Ub{i}", [128, 258]) for i in range(2)]
                HL = SB(es, "HL", [128, NF, 2])
                k.memset('pool', HL[:], 0.0, [f'HL{f}' for f in range(NF)])
                cva = [SB(es, f"cva{i}", [128, 256]) for i in range(4)]
                cvb = [SB(es, f"cvb{i}", [128, 256]) for i in range(2)]
                gsl = [SB(es, f"gsl{i}", [128, 256]) for i in range(2)]
                GT = SB(es, "GT", [128, 22, 256], BF16)
                ot = [SB(es, "ot", [128, D])] * 2

                pU = [(pbig[i], f'pb{i}') for i in range(3)]
                pD = [(pbig[3 + i], f'pb{3 + i}') for i in range(4)]
                cnt = {'u': 0}
                cur = {}

                def conv_tile(ft, slot):
                    pb, pbn = pU[cnt['u'] % 3]
                    cnt['u'] += 1
                    for kc in range(8):
                        k.mm(pb[:, 0:256], Wup[:, kc, 128 * ft:128 * ft + 128], cur['h2T'][:, kc, :], ['Wup', cur['hTn']], [pbn],
                             start=(kc == 0), stop=(kc == 7))
                    ub, ubn = Ub[slot], f'Ub{slot}'
                    ca, can = cva[slot + 2 * (ft % 2)], f'cva{slot + 2 * (ft % 2)}'
                    cb_, cbn = cvb[slot], f'cvb{slot}'
                    k.cp('act', ub[:, 2:258], pb[:, 0:256], [pbn], [ubn + 'm'])
                    k.act(ca[:], pb[:, 0:256], AF.Identity, [pbn, 'cw'], [can], bias=cw[:, 132 + ft:133 + ft],
                          scale=cw[:, 88 + ft:89 + ft])
                    k.cp('pool', ub[:, 0:2], HL[:, ft, :], [f'HL{ft}'], [ubn + 'h'])
                    ur = [ubn + 'm', ubn + 'h']
                    k.stt(cb_[:], ub[:, 1:257], cw[:, 44 + ft:45 + ft], ca[:], ALU.mult, ALU.add, ur + ['cw', can], [cbn])
                    k.stt(ca[:], ub[:, 0:256], cw[:, ft:ft + 1], cb_[:], ALU.mult, ALU.add, ur + ['cw', cbn], [can])
                    k.cp('pool', HL[:, ft, :], ub[:, 256:258], ur, [f'HL{ft}'])
                    return ca, can

                def down_mm(ft):
                    for i in range(2):
                        for hf in range(2):
                            pd, pdn = pD[2 * i + hf]
                            k.mm(pd[:], GT[:, ft, 128 * i:128 * i + 128], Wd[:, ft, 512 * hf:512 * hf + 512], [f'GT{ft}', 'Wd'], [pdn],
                                 start=(ft == 0), stop=(ft == 21))

                DLY = 2
                NB2 = T // 256

                def prep(b2):
                    p = b2 % 2
                    rms_to_hT(es, X1, 256 * b2, h2Ts[p], 'nw2', 'c2', ntile=2, xtl=x1sA[p], nwt=nw2, hTn=f'h2T{p}')

                prep(0)
                for b2 in range(NB2):
                    t0 = 256 * b2
                    cur['h2T'], cur['hTn'] = h2Ts[b2 % 2], f'h2T{b2 % 2}'
                    x1s = x1sA[b2 % 2]
                    pend = None

                    def finish(pn):
                        ft_, ga, gan, va, van = pn
                        g_, gn_ = gsl[ft_ % 2], f'gsl{ft_ % 2}'
                        k.act(g_[:], ga[:], AF.Silu, [gan], [gn_])
                        k.tt('pool', GT[:, ft_, :], g_[:], va[:], ALU.mult, [gn_, van], [f'GT{ft_}'])

                    for ft in range(22):
                        ga, gan = conv_tile(ft, 0)
                        va, van = conv_tile(22 + ft, 1)
                        if pend is not None:
                            finish(pend)
                        pend = (ft, ga, gan, va, van)
                        if ft >= DLY + 1:
                            down_mm(ft - DLY - 1)
                        if ft == 10 and b2 + 1 < NB2:
                            prep(b2 + 1)
                    finish(pend)
                    for ft in range(22 - DLY - 1, 22):
                        down_mm(ft)
                    for i in range(2):
                        xs_, xsn = x1s[i]
                        for hf in range(2):
                            pd, pdn = pD[2 * i + hf]
                            k.tt('dve', xs_[:, 512 * hf:512 * hf + 512], pd[:], xs_[:, 512 * hf:512 * hf + 512], ALU.add,
                                 [pdn, xsn], [xsn])
                        k.act(xn[:], xs_[:], AF.Square, [xsn], ['xn', 'ss'], accum_out=ss[:])
                        k.act(ss[:], ss[:], AF.Sqrt, ['ss'], ['ss'], bias=1e-6, scale=1.0 / D)
                        k.recip(rs[:], ss[:], ['ss'], ['rs'])
                        k.stt(ot[i][:], xs_[:], rs[:, 0:1], nwf[:], ALU.mult, ALU.mult, [xsn, 'rs', 'nwf'], ['ot'])
                        k.dma('sp', out[t0 + 128 * i:t0 + 128 * i + 128, :], ot[i][:], ['ot'], ['out'])
                S.wait_all('sp')
                S.flush()

        S.wait_all('sp')
        S.flush()
    return nc


_IN_NAMES = ["x", "norm_mix_w", "w_in", "rwkv_mu", "rwkv_w0", "rwkv_w2", "rwkv_a0", "rwkv_a2", "rwkv_g2",
             "rwkv_k_k", "rwkv_k_a", "rwkv_r_k", "rwkv_lnx_w", "rwkv_lnx_b", "fox_f_bias", "fox_q_norm_w",
             "fox_k_norm_w", "fox_o_norm_w", "w_out", "norm_ffn_w", "ffn_w_up", "ffn_conv_w", "ffn_conv_b",
             "ffn_w_down", "norm_final_w"]

_SHAPES = {"norm_mix_w": (1, D), "w_in": (D, RW + FOXC), "rwkv_mu": (RW,), "rwkv_w0": (512,), "rwkv_w2": (64, 512),
           "rwkv_a0": (512,), "rwkv_a2": (64, 512), "rwkv_g2": (128, 512), "rwkv_k_k": (512,), "rwkv_k_a": (512,),
           "rwkv_r_k": (512,), "rwkv_lnx_w": (512,), "rwkv_lnx_b": (512,), "fox_f_bias": (1, 8),
           "fox_q_norm_w": (64,), "fox_k_norm_w": (64,), "fox_o_norm_w": (1, 64), "w_out": (D, D),
           "norm_ffn_w": (1, D), "ffn_w_up": (D, 2 * DFF), "ffn_conv_w": (3, 2 * DFF), "ffn_conv_b": (2 * DFF,),
           "ffn_w_down": (DFF, D), "norm_final_w": (1, D)}


def make_in_maps(inputs, T, ncores):
    shared = {n: np.ascontiguousarray(np.asarray(inputs[n], dtype=np.float32).reshape(_SHAPES[n])) for n in _SHAPES}
    xs = np.asarray(inputs["x"], dtype=np.float32)
    maps = []
    for c in range(ncores):
        m = dict(shared)
        m["x"] = np.ascontiguousarray(xs[c, :T])
        maps.append(m)
    return maps


def kernel(**inputs):
    T = inputs["x"].shape[1]
    B = inputs["x"].shape[0]
    nc = build(T=T)
    res = run_bass_kernel_spmd(nc, make_in_maps(inputs, T, B), core_ids=list(range(B)))
    return np.stack([r["out"] for r in res.results], axis=0).astype(np.float32)
```

```python
import numpy as np
import concourse.bass as bass
import concourse.mybir as mybir
from concourse.bass_utils import run_bass_kernel_spmd

F32 = mybir.dt.float32
BF16 = mybir.dt.bfloat16
AF = mybir.ActivationFunctionType
ALU = mybir.AluOpType
AX = mybir.AxisListType

ENGS = ('pe', 'act', 'dve', 'pool', 'sp')


class Sched:
    def __init__(self, nc, esems, dsems):
        self.nc = nc
        self.esem = dict(zip(ENGS, esems))
        self.dsems = dsems
        self.cnt = {e: 0 for e in ENGS}
        self.stream = {e: [] for e in ENGS}
        self.seen = {e: {} for e in ENGS}
        self.dcount = [0] * len(dsems)
        self.dn = {'sp': 0, 'pool': 0, 'act': 0}
        nq = len(dsems) // 3
        self.dq = {'sp': list(range(0, nq)), 'pool': list(range(nq, 2 * nq)), 'act': list(range(2 * nq, 3 * nq))}
        self.res = {}

    def _deps(self, reads, writes):
        deps = {}
        def add(tok):
            if tok is None:
                return
            k, v = tok
            if deps.get(k, 0) < v:
                deps[k] = v
        for r in reads:
            st = self.res.get(r)
            if st:
                add(st[0])
        for w in writes:
            st = self.res.get(w)
            if st:
                add(st[0])
                for k, v in st[1].items():
                    add((k, v))
        return deps

    def _commit(self, tok, reads, writes):
        for r in reads:
            st = self.res.setdefault(r, [None, {}])
            if st[1].get(tok[0], 0) < tok[1]:
                st[1][tok[0]] = tok[1]
        for w in writes:
            self.res[w] = [tok, {}]

    def op(self, eng, fn, reads=(), writes=()):
        deps = self._deps(reads, writes)
        waits = []
        seen = self.seen[eng]
        for k, v in deps.items():
            if k == 'pe' and eng == 'pe':
                continue
            if seen.get(k, 0) >= v:
                continue
            seen[k] = v
            waits.append((k, v))
        self.cnt[eng] += 1
        tok = (eng, self.cnt[eng])
        self.stream[eng].append((waits, fn, (eng, 1)))
        self._commit(tok, reads, writes)

    def dma(self, eng, fn, reads=(), writes=()):
        deps = self._deps(reads, writes)
        q = self.dq[eng]
        k = q[self.dn[eng] % len(q)]
        self.dn[eng] += 1
        prev = 16 * self.dcount[k]
        self.dcount[k] += 1
        key = ('d', k)
        if prev > 0:
            if deps.get(key, 0) < prev:
                deps[key] = prev
        waits = []
        seen = self.seen[eng]
        for kk, v in deps.items():
            if seen.get(kk, 0) >= v:
                continue
            seen[kk] = v
            waits.append((kk, v))
        tok = (key, prev + 16)
        self.stream[eng].append((waits, fn, (key, 16)))
        self._commit(tok, reads, writes)

    def wait_all(self, eng):
        waits = []
        for e in ENGS:
            if self.cnt[e] > 0 and e != eng:
                waits.append((e, self.cnt[e]))
        for k in range(len(self.dsems)):
            if self.dcount[k] > 0:
                waits.append((('d', k), 16 * self.dcount[k]))
        self.stream[eng].append((waits, None, None))

    def _sem(self, key):
        if isinstance(key, tuple):
            return self.dsems[key[1]]
        return self.esem[key]

    def emit(self, eng, engine):
        for waits, fn, inc in self.stream[eng]:
            for k, v in waits:
                engine.wait_ge(self._sem(k), v)
            if fn is None:
                continue
            inst = fn(engine)
            inst.then_inc(self._sem(inc[0]), inc[1])

    def flush(self):
        self.emit_all()
        self.stream = {e: [] for e in ENGS}

    def emit_all(self):
        nc = self.nc
        with nc.Block() as block:
            @block.tensor
            def _(e):
                self.emit('pe', e)

            @block.scalar
            def _(e):
                self.emit('act', e)

            @block.vector
            def _(e):
                self.emit('dve', e)

            @block.gpsimd
            def _(e):
                self.emit('pool', e)

            @block.sync
            def _(e):
                self.emit('sp', e)


def _mk(eng):
    def f(self, fn, reads=(), writes=()):
        return self.op(eng, fn, reads, writes)
    return f


for _e in ('pe', 'act', 'dve', 'pool'):
    setattr(Sched, _e, _mk(_e))

import contextlib

D = 1024
RW = 1792
FOXC = 2056
DFF = 2816
NEG_E05 = -0.6065306597126334
F32R = mybir.dt.float32r


def R(ap):
    return ap.bitcast(F32R)


class K:
    def __init__(self, S):
        self.S = S

    def tt(self, eng, out, in0, in1, op, r, w):
        self.S.op(eng, lambda e: e.tensor_tensor(out=out, in0=in0, in1=in1, op=op), r, w)

    def ts(self, eng, out, in0, s1, s2, op0, op1, r, w):
        if s2 is None:
            self.S.op(eng, lambda e: e.tensor_scalar(out=out, in0=in0, scalar1=s1, scalar2=None, op0=op0), r, w)
        else:
            self.S.op(eng, lambda e: e.tensor_scalar(out=out, in0=in0, scalar1=s1, scalar2=s2, op0=op0, op1=op1), r, w)

    def stt(self, out, in0, scalar, in1, op0, op1, r, w):
        self.S.op('dve', lambda e: e.scalar_tensor_tensor(out=out, in0=in0, scalar=scalar, in1=in1, op0=op0, op1=op1), r, w)

    def act(self, out, in_, func, r, w, bias=None, scale=None, accum_out=None):
        kw = {}
        if bias is not None:
            kw['bias'] = bias
        if scale is not None:
            kw['scale'] = scale
        if accum_out is not None:
            kw['accum_out'] = accum_out
        self.S.op('act', lambda e: e.activation(out=out, in_=in_, func=func, **kw), r, w)

    def cp(self, eng, out, in_, r, w):
        if eng == 'act':
            self.S.op('act', lambda e: e.copy(out=out, in_=in_), r, w)
        else:
            self.S.op(eng, lambda e: e.tensor_copy(out=out, in_=in_), r, w)

    def mm(self, out, lhsT, rhs, r, w, start=True, stop=True, r32=False):
        if r32:
            lhsT, rhs = R(lhsT), R(rhs)
        self.S.op('pe', lambda e: e.matmul(out, lhsT=lhsT, rhs=rhs, start=start, stop=stop), r, w)

    def tr(self, out, in_, ident, r, w):
        self.S.op('pe', lambda e: e.transpose(out, in_, ident), r, w)

    def dma(self, q, out, in_, r, w, **kw):
        self.S.dma(q, lambda e: e.dma_start(out=out, in_=in_, **kw), r, w)

    def memset(self, eng, ap, val, w):
        self.S.op(eng, lambda e: e.memset(ap, val), [], w)

    def rsqrt(self, out, in_, r, w, scale=1.0, eps=0.0):
        self.act(out, in_, AF.Ln, r, w, bias=eps, scale=scale)
        self.act(out, out, AF.Exp, w, w, scale=-0.5)

    def sigmoid(self, out, in_, r, w, nbias=None, scale=1.0, tmp=None, tmpn='sgtmp'):
        t = out if tmp is None else tmp
        tw = w if tmp is None else [tmpn]
        if nbias is None:
            self.act(t, in_, AF.Exp, r, tw, scale=-scale)
        else:
            self.act(t, in_, AF.Exp, r, tw, bias=nbias, scale=-scale)
        self.act(t, t, AF.Ln, tw, tw, bias=1.0)
        self.act(out, t, AF.Exp, tw, w, scale=-1.0)

    def recip(self, out, in_, r, w):
        self.S.op('dve', lambda e: e.reciprocal(out=out, in_=in_), r, w)

    def asel(self, out, in_, pattern, op, base, cm, r, w):
        self.S.op('pool', lambda e: e.affine_select(out=out, in_=in_, pattern=pattern, compare_op=op,
                                                    fill=0.0, base=base, channel_multiplier=cm), r, w)

    def scan(self, out, d0, d1, r, w):
        self.S.op('dve', lambda e: e.tensor_tensor_scan(out=out, data0=d0, data1=d1, initial=0.0,
                                                        op0=ALU.mult, op1=ALU.add), r, w)


def build(T=4096, dbg=False, phases=('A1', 'A2', 'B', 'C1', 'C2')):
    nc = bass.Bass("TRN2", target_bir_lowering=False)
    NT = T // 128
    NB = T // 512
    din = {}

    def DI(name, shape):
        din[name] = nc.dram_tensor(name, list(shape), F32, kind="ExternalInput").ap()
        return din[name]

    x = DI("x", [T, D])
    norm_mix_w = DI("norm_mix_w", [1, D])
    w_in = DI("w_in", [D, RW + FOXC])
    rwkv_mu = DI("rwkv_mu", [RW])
    rwkv_w0 = DI("rwkv_w0", [512])
    rwkv_w2 = DI("rwkv_w2", [64, 512])
    rwkv_a0 = DI("rwkv_a0", [512])
    rwkv_a2 = DI("rwkv_a2", [64, 512])
    rwkv_g2 = DI("rwkv_g2", [128, 512])
    rwkv_k_k = DI("rwkv_k_k", [512])
    rwkv_k_a = DI("rwkv_k_a", [512])
    rwkv_r_k = DI("rwkv_r_k", [512])
    rwkv_lnx_w = DI("rwkv_lnx_w", [512])
    rwkv_lnx_b = DI("rwkv_lnx_b", [512])
    fox_f_bias = DI("fox_f_bias", [1, 8])
    fox_q_norm_w = DI("fox_q_norm_w", [64])
    fox_k_norm_w = DI("fox_k_norm_w", [64])
    fox_o_norm_w = DI("fox_o_norm_w", [1, 64])
    w_out = DI("w_out", [D, D])
    norm_ffn_w = DI("norm_ffn_w", [1, D])
    ffn_w_up = DI("ffn_w_up", [D, 2 * DFF])
    ffn_conv_w = DI("ffn_conv_w", [3, 2 * DFF])
    ffn_conv_b = DI("ffn_conv_b", [2 * DFF])
    ffn_w_down = DI("ffn_w_down", [DFF, D])
    norm_final_w = DI("norm_final_w", [1, D])
    out = nc.dram_tensor("out", [T, D], F32, kind="ExternalOutput").ap()

    okind = "ExternalOutput" if dbg else "Internal"
    YR = nc.dram_tensor("yr", [4, 128, T], BF16, kind=okind).ap()
    YF = nc.dram_tensor("yf", [4, 128, T], BF16, kind=okind).ap()
    SGd = nc.dram_tensor("sgd", [T, 512], BF16, kind="Internal").ap()
    X1 = nc.dram_tensor("x1", [T, D], F32, kind=okind).ap()

    with contextlib.ExitStack() as es0:
        esems = [es0.enter_context(nc.semaphore(f"es{i}")) for i in range(5)]
        dsems = [es0.enter_context(nc.semaphore(f"ds{i}")) for i in range(24)]
        S = Sched(nc, esems, dsems)
        k = K(S)

        uid = [0]

        def SB(es, name, shape, dt=F32):
            uid[0] += 1
            return es.enter_context(nc.sbuf_tensor(f"{name}_u{uid[0]}", list(shape), dt))

        pst = es0.enter_context(nc.psum_tensor("pst", [128, 1024], BF16))
        pbig = [es0.enter_context(nc.psum_tensor(f"pb{i}", [128, 512], F32)) for i in range(7)]
        st = {'big': 0, 'q': 0}

        def PB():
            i = st['big'] % 7
            st['big'] += 1
            return pbig[i], f"pb{i}"

        def PQ():
            i = st['q'] % 16
            st['q'] += 1
            return pqb[i // 4][:, (i % 4) * 128:(i % 4) * 128 + 128], f"pq{i}"

        def PQbank():
            return PQ()

        ones = SB(es0, "ones", [128, 128])
        ident = SB(es0, "ident", [128, 128])
        identb = SB(es0, "identb", [128, 128], BF16)
        mST = SB(es0, "mST", [128, 128])
        mIT = SB(es0, "mIT", [128, 128])
        mS = SB(es0, "mS", [128, 128])
        bones_raw = SB(es0, "bones_raw", [128, 128])
        bones = SB(es0, "bones", [128, 128])
        k.memset('pool', ones[:], 1.0, ['ones'])
        k.asel(ident[:], ones[:], [[-1, 128]], ALU.is_equal, 0, 1, ['ones'], ['ident'])
        k.cp('dve', identb[:], ident[:], ['ident'], ['identb'])
        for m_, nm in ((mST, 'mST'), (mIT, 'mIT'), (mS, 'mS'), (bones_raw, 'bones_raw')):
            k.memset('pool', m_[:], 0.0, [nm])
        for b in range(2):
            sl = slice(64 * b, 64 * b + 64)
            k.asel(mST[sl, sl], ones[sl, sl], [[1, 64]], ALU.is_ge, -1, -1, ['ones', 'mST'], ['mST'])
            k.asel(mIT[sl, sl], ones[sl, sl], [[1, 64]], ALU.is_ge, 0, -1, ['ones', 'mIT'], ['mIT'])
            k.asel(mS[sl, sl], ones[sl, sl], [[-1, 64]], ALU.is_ge, -1, 1, ['ones', 'mS'], ['mS'])
            k.cp('pool', bones_raw[sl, sl], ones[sl, sl], ['ones', 'bones_raw'], ['bones_raw'])

        k.cp('dve', R(bones[:]), bones_raw[:], ['bones_raw'], ['bones'])
        nwb = SB(es0, "nwb", [128, D])

        def rms_to_hT(es, xsrc, t0, hT, nwname, tagp, ntile=4, xtl=None, nwt=None, hTn='hT'):
            nwt_ = nwb if nwt is None else nwt
            for i in range(ntile):
                if xtl is None:
                    par = i % 2
                    xt, xtn = xts[par], f'xt{par}'
                else:
                    xt, xtn = xtl[i]
                k.dma('sp', xt[:], xsrc[t0 + 128 * i:t0 + 128 * i + 128, :], [], [xtn])
                k.act(xn[:], xt[:], AF.Square, [xtn], ['xn', 'ss'], accum_out=ss[:])
                k.rsqrt(rs[:], ss[:], ['ss'], ['rs'], scale=1.0 / D, eps=1e-6)
                k.stt(xn[:], xt[:], rs[:, 0:1], nwt_[:], ALU.mult, ALU.mult, [xtn, 'rs', nwname], ['xn'])
                for kk_ in range(8):
                    k.tr(pst[:, 128 * kk_:128 * kk_ + 128], xn[:, 128 * kk_:128 * kk_ + 128], identb[:],
                         ['xn', 'identb'], ['pst'])
                k.cp('act', hT[:, :, 128 * i:128 * i + 128], pst[:].rearrange("p (k t) -> p k t", t=128),
                     ['pst'], [hTn])

        class Rec:
            def __init__(self):
                self.ops = []

            def op(self, eng, fn, reads=(), writes=()):
                self.ops.append(('op', eng, fn, tuple(reads), tuple(writes)))

            def dma(self, eng, fn, reads=(), writes=()):
                self.ops.append(('dma', eng, fn, tuple(reads), tuple(writes)))

        def record(fn, *a_):
            rec = Rec()
            k.S = rec
            try:
                fn(*a_)
            finally:
                k.S = S
            return rec.ops

        def replay(ops):
            for kind, eng, fn, r, w in ops:
                (S.op if kind == 'op' else S.dma)(eng, fn, r, w)

        def merge(*lists):
            lists = [l for l in lists if len(l) > 0]
            pos = [0] * len(lists)
            o = []
            total = sum(len(l) for l in lists)
            while len(o) < total:
                best, bf = None, None
                for i, l in enumerate(lists):
                    if pos[i] < len(l):
                        f = pos[i] / len(l)
                        if bf is None or f < bf:
                            best, bf = i, f
                o.append(lists[best][pos[best]])
                pos[best] += 1
            return o

        if 'A1' in phases:
            with contextlib.ExitStack() as es:
                WB = 256
                NBL = T // WB
                cE = {'n': 0}
                cC = {'n': 0}

                def PBE():
                    i = cE['n'] % 3
                    cE['n'] += 1
                    return pbig[i], f"pb{i}"

                def PBC():
                    i = 3 + cC['n'] % 4
                    cC['n'] += 1
                    return pbig[i], f"pb{i}"

                def bc4(m):
                    return m[:].unsqueeze(1).to_broadcast([128, 4, 128])

                k.dma('sp', nwb[:], norm_mix_w.partition_broadcast(128), [], ['nwb'])
                Win = SB(es, "WinR", [128, 8, RW], BF16)
                for kc in range(8):
                    k.dma('pool', Win[:, kc, :], w_in[128 * kc:128 * kc + 128, 0:RW], [], ['Win'])
                xts = [SB(es, f"xt{i}", [128, D]) for i in range(2)]
                ss = SB(es, "ss", [128, 1])
                rs = SB(es, "rs", [128, 1])
                xn = SB(es, "xn", [128, D], BF16)
                hT = SB(es, "hT", [128, 8, WB], BF16)
                T1 = [SB(es, f"T1_{i}", [128, WB]) for i in range(2)]
                carry = SB(es, "carry", [128, 14])
                X = SB(es, "X", [128, 14, WB])
                mu_cm = SB(es, "mu_cm", [128, 14])
                omm_cm = SB(es, "omm_cm", [128, 14])
                prm = SB(es, "prm", [128, 7, 4])
                omka = SB(es, "omka", [128, 4])
                nprm = SB(es, "nprm", [128, 2, 4])
                W2Z = SB(es, "W2Z", [128, 512])
                A2Z = SB(es, "A2Z", [128, 512])
                G2 = SB(es, "G2", [128, 512])
                Wraw = [SB(es, f"Wraw{i}", [128, 512]) for i in range(3)]
                rmask = SB(es, "rmask", [128, WB])
                k.dma('sp', mu_cm[:], rwkv_mu.rearrange("(t p) -> p t", p=128), [], ['mu_cm'], allow_slow_non_contiguous=True)
                for i, prm_in in enumerate((rwkv_w0, rwkv_a0, rwkv_k_k, rwkv_k_a, rwkv_r_k, rwkv_lnx_w, rwkv_lnx_b)):
                    k.dma('sp', prm[:, i, :], prm_in.rearrange("(t p) -> p t", p=128), [], ['prm'], allow_slow_non_contiguous=True)
                k.ts('dve', omm_cm[:], mu_cm[:], -1.0, 1.0, ALU.mult, ALU.add, ['mu_cm'], ['omm_cm'])
                k.ts('dve', omka[:], prm[:, 3, :], -1.0, 1.0, ALU.mult, ALU.add, ['prm'], ['omka'])
                k.ts('dve', nprm[:], prm[:, 0:2, :], -1.0, None, ALU.mult, None, ['prm'], ['nprm'])
                k.memset('pool', Wraw[0][:], 0.0, ['Wraw0'])
                k.memset('pool', Wraw[1][:], 0.0, ['Wraw1'])
                k.dma('sp', Wraw[0][0:64, :], rwkv_w2, [], ['Wraw0'])
                k.dma('sp', Wraw[1][64:128, :], rwkv_a2, [], ['Wraw1'])
                k.dma('sp', Wraw[2][:], rwkv_g2, [], ['Wraw2'])
                for i_, (w_, wn_) in enumerate(((W2Z, 'W2Z'), (A2Z, 'A2Z'), (G2, 'G2'))):
                    k.cp('dve', R(w_[:]), Wraw[i_][:], [f'Wraw{i_}'], [wn_])
                k.memset('pool', rmask[:], 1.0, ['rmask'])
                k.memset('pool', rmask[:].rearrange("p (c t) -> p c t", t=64)[:, :, 0:1], 0.0, ['rmask'])
                k.memset('pool', carry[:], 0.0, [f'carry{c_}' for c_ in range(14)])

                TA = SB(es, "TA", [128, WB])
                SGg = SB(es, "SGg", [128, WB])
                tmp = [SB(es, f"tmp{i}", [128, WB]) for i in range(9)]
                SQr = SB(es, "SQr", [128, WB])
                BDn = ('RT', 'AT', 'BT', 'KT', 'BH', 'KH', 'VT')
                BDp = [{n: SB(es, f"BD{p}_{n}", [128, 4, 128]) for n in BDn} for p in range(2)]
                Gtp = [SB(es, f"Gt{p}", [128, WB]) for p in range(2)]
                BStp = [SB(es, f"BSt{p}", [128, WB]) for p in range(2)]
                PCtp = [SB(es, f"PCt{p}", [128, 4]) for p in range(2)]
                for p in range(2):
                    for n in BDn:
                        k.memset('pool', BDp[p][n][:], 0.0, [f'BD{p}_{n}'])
                        k.cp('dve', R(BDp[p][n][:]), BDp[p][n][:], [f'BD{p}_{n}'], [f'BD{p}_{n}'])
                TMn = ('A', 'BH', 'KH', 'V')
                TM = {n: SB(es, f"TM_{n}", [128, 4, 128]) for n in TMn}
                CMn = ('NTa', 'NTb', 'Na', 'Nb', 'ST', 'AKT', 'MRBT', 'MRKT', 'ApT', 'W2', 'Vp', 'U')
                CM = {n: SB(es, f"CM_{n}", [128, 4, 128]) for n in CMn}
                H = [SB(es, f"H{p}", [128, 128]) for p in range(4)]
                for p in range(4):
                    k.memset('pool', H[p][:], 0.0, [f'H{p}'])
                    k.cp('dve', R(H[p][:]), H[p][:], [f'H{p}'], [f'H{p}'])
                YT = SB(es, "YT", [128, WB])
                G1 = SB(es, "G1", [128, WB])
                G2t = SB(es, "G2t", [128, WB])
                SQr2 = SB(es, "SQr2", [128, WB])
                YO = SB(es, "YO", [128, WB], BF16)

                def XR(ct):
                    return [f'X{ct}a', f'X{ct}b']

                def stage_E(b, pr):
                    par = (4 * b + pr) % 2
                    BD = BDp[par]
                    bdn = f'BD{par}_'
                    t0 = WB * b
                    if pr == 0:
                        rms_to_hT(es, x, t0, hT, 'nwb', 'a1', ntile=2)
                        for ct in range(14):
                            pb, pbn = PBE()
                            for kc in range(8):
                                k.mm(pb[:, 0:WB], Win[:, kc, 128 * ct:128 * ct + 128], hT[:, kc, :], ['Win', 'hT'], [pbn],
                                     start=(kc == 0), stop=(kc == 7))
                            t1 = T1[ct % 2]
                            t1n = f'T1_{ct % 2}'
                            k.act(t1[:], pb[:, 0:WB], AF.Identity, [pbn, 'mu_cm'], [t1n], scale=mu_cm[:, ct:ct + 1])
                            xn_ = f'X{ct}'
                            k.stt(X[:, ct, 1:WB], pb[:, 1:WB], omm_cm[:, ct:ct + 1], t1[:, 0:WB - 1], ALU.mult, ALU.add,
                                  [pbn, t1n, 'omm_cm'], [xn_ + 'a'])
                            k.stt(X[:, ct, 0:1], pb[:, 0:1], omm_cm[:, ct:ct + 1], carry[:, ct:ct + 1], ALU.mult, ALU.add,
                                  [pbn, f'carry{ct}', 'omm_cm'], [xn_ + 'b'])
                            k.cp('pool', carry[:, ct:ct + 1], t1[:, WB - 1:WB], [t1n], [f'carry{ct}'])
                        k.sigmoid(T1[0][0:64, :], X[0:64, 12, :], XR(12), ['T1_0'], scale=2.0)
                        k.ts('dve', R(TA[0:64, :]), T1[0][0:64, :], 2.0, -1.0, ALU.mult, ALU.add, ['T1_0'], ['TAa'])
                        k.cp('pool', R(TA[64:128, :]), X[64:128, 12, :], XR(12), ['TAb'])
                        k.sigmoid(R(SGg[:]), X[:, 13, :], XR(13), ['SGg'], tmp=T1[1][:], tmpn='T1_1')
                    cs = slice(128 * pr, 128 * pr + 128)
                    Xr, Xk, Xv = X[:, pr, :], X[:, 4 + pr, :], X[:, 8 + pr, :]
                    rXr, rXk, rXv = XR(pr), XR(4 + pr), XR(8 + pr)
                    SIG, A_, KKN, KP, BV, L, E1, E2, E3 = tmp
                    tn = [f'tmp{i}' for i in range(9)]
                    Gt, Gtn = Gtp[par], f'Gt{par}'
                    BSt, BStn = BStp[par], f'BSt{par}'
                    PCt, PCtn = PCtp[par], f'PCt{par}'

                    def bdwrite(eng, name, in0, in1, op, r):
                        for hh in range(2):
                            ps_ = slice(64 * hh, 64 * hh + 64)
                            o = R(BD[name][ps_, :, 64 * hh:64 * hh + 64])
                            a0 = in0[ps_, :].rearrange("p (c t) -> p c t", t=64)
                            if in1 is None:
                                k.cp(eng, o, a0, r, [bdn + name])
                            else:
                                a1 = in1[ps_, :].rearrange("p (c t) -> p c t", t=64)
                                k.tt(eng, o, a0, a1, op, r, [bdn + name])

                    pb, pbn = PBE()
                    k.mm(pb[:, 0:WB], W2Z[:, cs], TA[:], ['W2Z', 'TAa', 'TAb'], [pbn], r32=True)
                    k.sigmoid(SIG[:], pb[:, 0:WB], [pbn, 'nprm'], [tn[0]], nbias=nprm[:, 0, pr:pr + 1])
                    pb, pbn = PBE()
                    k.mm(pb[:, 0:WB], A2Z[:, cs], TA[:], ['A2Z', 'TAa', 'TAb'], [pbn], r32=True)
                    k.sigmoid(A_[:], pb[:, 0:WB], [pbn, 'nprm'], [tn[1]], nbias=nprm[:, 1, pr:pr + 1])
                    pb, pbn = PBE()
                    k.mm(pb[:, 0:WB], G2[:, cs], SGg[:], ['G2', 'SGg'], [pbn], r32=True)
                    k.cp('act', Gt[:], pb[:, 0:WB], [pbn], [Gtn])
                    k.ts('dve', KKN[:], Xk, prm[:, 2, pr:pr + 1], None, ALU.mult, None, rXk + ['prm'], [tn[2]])
                    k.act(R(SQr[:]), KKN[:], AF.Square, [tn[2]], ['SQr'])
                    pb, pbn = PBE()
                    k.mm(pb[:, 0:WB], bones[:], SQr[:], ['bones', 'SQr'], [pbn], r32=True)
                    k.ts('dve', E1[:], pb[:, 0:WB], 1e-19, None, ALU.max, None, [pbn], [tn[6]])
                    k.rsqrt(E1[:], E1[:], [tn[6]], [tn[6]])
                    k.tt('dve', KKN[:], KKN[:], E1[:], ALU.mult, [tn[2], tn[6]], [tn[2]])
                    k.ts('dve', KP[:], A_[:], prm[:, 3, pr:pr + 1], omka[:, pr:pr + 1], ALU.mult, ALU.add,
                         [tn[1], 'prm', 'omka'], [tn[3]])
                    k.tt('dve', KP[:], KP[:], Xk, ALU.mult, [tn[3]] + rXk, [tn[3]])
                    k.tt('pool', BV[:], KKN[:], A_[:], ALU.mult, [tn[2], tn[1]], [tn[4]])
                    k.stt(R(SQr[:]), Xr, prm[:, 4, pr:pr + 1], KP[:], ALU.mult, ALU.mult, rXr + ['prm', tn[3]], ['SQr'])
                    pb, pbn = PBE()
                    k.mm(pb[:, 0:WB], bones[:], SQr[:], ['bones', 'SQr'], [pbn], r32=True)
                    k.tt('dve', BSt[:], pb[:, 0:WB], Xv, ALU.mult, [pbn] + rXv, [BStn])
                    k.ts('dve', SIG[:], SIG[:], NEG_E05, None, ALU.mult, None, [tn[0]], [tn[0]])
                    k.scan(L[:], rmask[:], SIG[:], ['rmask', tn[0]], [tn[5]])
                    k.act(E1[:], L[:], AF.Exp, [tn[5]], [tn[6]])
                    bdwrite('dve', 'RT', Xr, E1, ALU.mult, rXr + [tn[6]])
                    k.tt('pool', E2[:], L[:], SIG[:], ALU.subtract, [tn[5], tn[0]], [tn[7]])
                    k.act(E2[:], E2[:], AF.Exp, [tn[7]], [tn[7]])
                    k.ts('dve', E2[:], E2[:], -1.0, None, ALU.mult, None, [tn[7]], [tn[7]])
                    bdwrite('dve', 'AT', KKN, E2, ALU.mult, [tn[2], tn[7]])
                    k.act(E3[:], L[:], AF.Exp, [tn[5]], [tn[8]], scale=-1.0)
                    bdwrite('dve', 'BT', BV, E3, ALU.mult, [tn[4], tn[8]])
                    bdwrite('pool', 'KT', KP, E3, ALU.mult, [tn[3], tn[8]])
                    L3 = L[:].rearrange("p (c t) -> p c t", t=64)
                    k.tt('dve', E1[:].rearrange("p (c t) -> p c t", t=64), L3,
                         L3[:, :, 63:64].to_broadcast([128, 4, 64]), ALU.subtract, [tn[5]], [tn[6]])
                    k.act(E1[:], E1[:], AF.Exp, [tn[6]], [tn[6]], scale=-1.0)
                    bdwrite('dve', 'BH', BV, E1, ALU.mult, [tn[4], tn[6]])
                    bdwrite('pool', 'KH', KP, E1, ALU.mult, [tn[3], tn[6]])
                    k.act(PCt[:], L3[:, :, 63], AF.Exp, [tn[5]], [PCtn])
                    bdwrite('pool', 'VT', Xv, None, None, rXv)

                def stage_C(b, pr):
                    par = (4 * b + pr) % 2
                    BD = BDp[par]
                    bdn = f'BD{par}_'
                    t0 = WB * b
                    Gt, Gtn = Gtp[par], f'Gt{par}'
                    BSt, BStn = BStp[par], f'BSt{par}'
                    PCt, PCtn = PCtp[par], f'PCt{par}'
                    Hp, Hn = H[pr], f'H{pr}'

                    def q4(pb_, i):
                        return pb_[:, 128 * i:128 * i + 128]

                    def v4(pb_):
                        return pb_[:].rearrange("p (i t) -> p i t", t=128)
                    for (ln, rn, mk, mkn, on) in (('BT', 'AT', mST, 'mST', 'NTa'), ('AT', 'BT', mS, 'mS', 'Na'),
                                                  ('KT', 'AT', mST, 'mST', 'AKT'), ('BT', 'RT', mIT, 'mIT', 'MRBT'),
                                                  ('KT', 'RT', mIT, 'mIT', 'MRKT')):
                        pb, pbn = PBC()
                        for c in range(4):
                            k.mm(q4(pb, c), BD[ln][:, c, :], BD[rn][:, c, :], [bdn + ln, bdn + rn], [pbn], r32=True)
                        k.tt('dve', R(CM[on][:]), v4(pb), bc4(mk), ALU.mult, [pbn, mkn], ['CM_' + on])
                    for (src, dst) in (('AT', 'A'), ('BH', 'BH'), ('KH', 'KH'), ('VT', 'V')):
                        pb, pbn = PBC()
                        for c in range(4):
                            k.tr(q4(pb, c), BD[src][:, c, :], ident[:], [bdn + src, 'ident'], [pbn])
                        k.cp('act', R(TM[dst][:]), v4(pb), [pbn], ['TM_' + dst])
                    k.tt('pool', R(CM['ST'][:]), CM['NTa'][:], bc4(ident), ALU.add, ['CM_NTa', 'ident'], ['CM_ST'])
                    curN, curNT = 'Na', 'NTa'
                    for lev in range(1, 6):
                        nxtN = 'Nb' if curN == 'Na' else 'Na'
                        nxtNT = 'NTb' if curNT == 'NTa' else 'NTa'
                        pb, pbn = PBC()
                        for i in range(4):
                            k.mm(q4(pb, i), CM[curNT][:, i, :], CM[curN][:, i, :], ['CM_' + curNT, 'CM_' + curN], [pbn], r32=True)
                        k.cp('act', R(CM[nxtN][:]), v4(pb), [pbn], ['CM_' + nxtN])
                        if lev < 5:
                            pb, pbn = PBC()
                            for i in range(4):
                                k.mm(q4(pb, i), CM[curN][:, i, :], CM[curNT][:, i, :], ['CM_' + curNT, 'CM_' + curN], [pbn], r32=True)
                            k.cp('act', R(CM[nxtNT][:]), v4(pb), [pbn], ['CM_' + nxtNT])
                        pb, pbn = PBC()
                        for i in range(4):
                            k.mm(q4(pb, i), CM[nxtN][:, i, :], CM['ST'][:, i, :], ['CM_' + nxtN, 'CM_ST'], [pbn], r32=True)
                        k.tt('dve', R(CM['ST'][:]), v4(pb), CM['ST'][:], ALU.add, [pbn, 'CM_ST'], ['CM_ST'])
                        curN, curNT = nxtN, nxtNT
                    pb, pbn = PBC()
                    for i in range(4):
                        k.mm(q4(pb, i), TM['A'][:, i, :], CM['ST'][:, i, :], ['TM_A', 'CM_ST'], [pbn], r32=True)
                    k.cp('act', R(CM['ApT'][:]), v4(pb), [pbn], ['CM_ApT'])
                    pb, pbn = PBC()
                    for i in range(4):
                        k.mm(q4(pb, i), CM['AKT'][:, i, :], TM['V'][:, i, :], ['CM_AKT', 'TM_V'], [pbn], r32=True)
                    k.cp('act', R(CM['W2'][:]), v4(pb), [pbn], ['CM_W2'])
                    pb, pbn = PBC()
                    for i in range(4):
                        k.mm(q4(pb, i), CM['ST'][:, i, :], CM['W2'][:, i, :], ['CM_ST', 'CM_W2'], [pbn], r32=True)
                    k.cp('act', R(CM['Vp'][:]), v4(pb), [pbn], ['CM_Vp'])
                    for c in range(4):
                        i = c
                        un = f'CM_U{i}'
                        pb, pbn = PBC()
                        k.mm(q4(pb, 0), CM['ApT'][:, i, :], Hp[:], ['CM_ApT', Hn], [pbn], r32=True)
                        k.tt('dve', R(CM['U'][:, i, :]), q4(pb, 0), CM['Vp'][:, i, :], ALU.add, [pbn, 'CM_Vp'], [un])
                        pb, pbn = PBC()
                        k.mm(q4(pb, 0), Hp[:], BD['RT'][:, c, :], [Hn, bdn + 'RT'], [pbn], r32=True, start=True, stop=False)
                        k.mm(q4(pb, 0), CM['U'][:, i, :], CM['MRBT'][:, i, :], [un, 'CM_MRBT'], [pbn], r32=True,
                             start=False, stop=False)
                        k.mm(q4(pb, 0), TM['V'][:, i, :], CM['MRKT'][:, i, :], ['TM_V', 'CM_MRKT'], [pbn], r32=True,
                             start=False, stop=True)
                        for hh in range(2):
                            ps_ = slice(64 * hh, 64 * hh + 64)
                            k.cp('act', R(YT[ps_, 64 * c:64 * c + 64]), pb[ps_, 64 * hh:64 * hh + 64], [pbn], [f'YT{hh}'])
                        pb, pbn = PBC()
                        k.mm(q4(pb, 0), TM['BH'][:, i, :], CM['U'][:, i, :], ['TM_BH', un], [pbn], r32=True,
                             start=True, stop=False)
                        k.mm(q4(pb, 0), TM['KH'][:, i, :], TM['V'][:, i, :], ['TM_KH', 'TM_V'], [pbn], r32=True,
                             start=False, stop=True)
                        k.stt(R(Hp[:]), Hp[:], PCt[:, c:c + 1], q4(pb, 0), ALU.mult, ALU.add, [Hn, PCtn, pbn], [Hn])
                    rYT = ['YT0', 'YT1']
                    pb, pbn = PBC()
                    k.mm(pb[:, 0:WB], bones[:], YT[:], ['bones'] + rYT, [pbn], r32=True)
                    k.stt(G1[:], pb[:, 0:WB], -1.0 / 64, YT[:], ALU.mult, ALU.add, [pbn] + rYT, ['G1'])
                    k.act(R(SQr2[:]), G1[:], AF.Square, ['G1'], ['SQr2'])
                    pb, pbn = PBC()
                    k.mm(pb[:, 0:WB], bones[:], SQr2[:], ['bones', 'SQr2'], [pbn], r32=True)
                    k.rsqrt(G2t[:], pb[:, 0:WB], [pbn], ['G2t'], scale=1.0 / 64, eps=64e-5)
                    k.tt('dve', G1[:], G1[:], G2t[:], ALU.mult, ['G1', 'G2t'], ['G1'])
                    k.ts('dve', G1[:], G1[:], prm[:, 5, pr:pr + 1], prm[:, 6, pr:pr + 1], ALU.mult, ALU.add,
                         ['G1', 'prm'], ['G1'])
                    k.tt('pool', G1[:], G1[:], BSt[:], ALU.add, ['G1', BStn], ['G1'])
                    k.tt('pool', YO[:], G1[:], Gt[:], ALU.mult, ['G1', Gtn], ['YO'])
                    k.dma('sp', YR[pr, :, t0:t0 + WB], YO[:], ['YO'], ['YR'])

                units = [(b, pr) for b in range(NBL) for pr in range(4)]
                replay(record(stage_E, *units[0]))
                for n, u in enumerate(units):
                    oc = record(stage_C, *u)
                    oe = record(stage_E, *units[n + 1]) if n + 1 < len(units) else []
                    replay(merge(oc, oe))
                S.wait_all('sp')
                S.flush()

        if 'A2' in phases:
            with contextlib.ExitStack() as esAB:
                QT = SB(esAB, "QT", [128, 4, T], BF16)
                KT_ = SB(esAB, "KTf", [128, 4, T], BF16)
                V1 = SB(esAB, "V1", [128, NT, 8, 65], BF16)
                LFs = SB(esAB, "LFs", [128, NT, 8])
                k.memset('pool', V1[:], 1.0, ['V1'])
                with contextlib.ExitStack() as es:
                    k.dma('sp', nwb[:], norm_mix_w.partition_broadcast(128), [], ['nwb'])
                    Wf = SB(es, "Wf", [128, 8, FOXC], BF16)
                    for kc in range(8):
                        k.dma('pool', Wf[:, kc, 0:1024], w_in[128 * kc:128 * kc + 128, RW:RW + 1024], [], ['Wf'])
                        k.dma('pool', Wf[:, kc, 1024:FOXC], w_in[128 * kc:128 * kc + 128, RW + 1024:RW + FOXC], [], ['Wf'])
                    xts = [SB(es, f"xt{i}", [128, D]) for i in range(2)]
                    ss = SB(es, "ss", [128, 1])
                    rs = SB(es, "rs", [128, 1])
                    xn = SB(es, "xn", [128, D], BF16)
                    hT = SB(es, "hT", [128, 8, 512], BF16)
                    qkw = SB(es, "qkw", [128, 2])
                    fbb = SB(es, "fbb", [128, 8])
                    sq = SB(es, "sq", [128, 512])
                    rq = SB(es, "rq", [128, 512])
                    SGt = [SB(es, f"SGt{i}", [128, 512], BF16) for i in range(2)]
                    zt = SB(es, "zt", [128, 8])
                    sgtmp = SB(es, "sgtmp", [128, 512])
                    for hh in range(2):
                        k.dma('sp', qkw[64 * hh:64 * hh + 64, 0:1], fox_q_norm_w.rearrange("(p o) -> p o", o=1), [], ['qkw'])
                        k.dma('sp', qkw[64 * hh:64 * hh + 64, 1:2], fox_k_norm_w.rearrange("(p o) -> p o", o=1), [], ['qkw'])
                    k.dma('sp', fbb[:], fox_f_bias.partition_broadcast(128), [], ['fbb'])
                    hTs = [hT, SB(es, "hTb", [128, 8, 512], BF16)]
                    sqs = [sq, SB(es, "sqb", [128, 512])]
                    rqs = [rq, SB(es, "rqb", [128, 512])]
                    cq = {'n': 0}
                    cv = {'n': 0}

                    def PBQ():
                        i = cq['n'] % 3
                        cq['n'] += 1
                        return pbig[i], f"pb{i}"

                    def PBV():
                        i = 3 + cv['n'] % 4
                        cv['n'] += 1
                        return pbig[i], f"pb{i}"

                    def prepH(b):
                        rms_to_hT(es, x, 512 * b, hTs[b % 2], 'nwb', 'a2', hTn=f'hT{b % 2}')

                    def streamQ(b):
                        t0 = 512 * b
                        hT_, hTn_ = hTs[b % 2], f'hT{b % 2}'
                        for ct in range(8):
                            sq_, sqn_ = sqs[ct % 2], f'sq{ct % 2}'
                            rq_, rqn_ = rqs[ct % 2], f'rq{ct % 2}'
                            pb, pbn = PBQ()
                            for kc in range(8):
                                k.mm(pb[:], Wf[:, kc, 128 * ct:128 * ct + 128], hT_[:, kc, :], ['Wf', hTn_], [pbn],
                                     start=(kc == 0), stop=(kc == 7))
                            k.act(R(sq_[:]), pb[:], AF.Square, [pbn], [sqn_])
                            pb2, pbn2 = PBQ()
                            k.mm(pb2[:], bones[:], sq_[:], ['bones', sqn_], [pbn2], r32=True)
                            k.rsqrt(rq_[:], pb2[:], [pbn2], [rqn_], scale=1.0 / 64, eps=1e-6)
                            dst = QT if ct < 4 else KT_
                            dn_ = 'QT' if ct < 4 else 'KTf'
                            k.stt(dst[:, ct % 4, t0:t0 + 512], pb[:], qkw[:, (ct // 4):(ct // 4) + 1], rq_[:], ALU.mult, ALU.mult,
                                  [pbn, 'qkw', rqn_], [dn_])

                    def streamV(b):
                        t0 = 512 * b
                        hT_, hTn_ = hTs[b % 2], f'hT{b % 2}'
                        for i in range(4):
                            ti = 4 * b + i
                            tsl = slice(128 * i, 128 * i + 128)
                            pb, pbn = PBV()
                            for kc in range(8):
                                k.mm(pb[:], hT_[:, kc, tsl], Wf[:, kc, 1024:1536], ['Wf', hTn_], [pbn], start=(kc == 0), stop=(kc == 7))
                            k.cp('act', V1[:, ti, :, 0:64], pb[:].rearrange("p (h d) -> p h d", d=64), [pbn], ['V1'])
                            pb, pbn = PBV()
                            for kc in range(8):
                                k.mm(pb[:], hT_[:, kc, tsl], Wf[:, kc, 1536:2048], ['Wf', hTn_], [pbn], start=(kc == 0), stop=(kc == 7))
                            sg, sgn = SGt[i % 2], f'SGt{i % 2}'
                            k.sigmoid(sg[:], pb[:], [pbn], [sgn], tmp=sgtmp[:])
                            k.dma('sp', SGd[t0 + 128 * i:t0 + 128 * i + 128, :], sg[:], [sgn], ['SGd'])
                            pb, pbn = PBV()
                            for kc in range(8):
                                k.mm(pb[:, 0:8], hT_[:, kc, tsl], Wf[:, kc, 2048:2056], ['Wf', hTn_], [pbn], start=(kc == 0), stop=(kc == 7))
                            k.tt('dve', zt[:], pb[:, 0:8], fbb[:], ALU.add, [pbn, 'fbb'], ['zt'])
                            k.act(zt[:], zt[:], AF.Exp, ['zt'], ['zt'], scale=-1.0)
                            k.act(LFs[:, ti, :], zt[:], AF.Ln, ['zt'], ['LFs'], bias=1.0)

                    replay(record(prepH, 0))
                    for b in range(NB):
                        ls = [record(streamQ, b), record(streamV, b)]
                        if b + 1 < NB:
                            ls.append(record(prepH, b + 1))
                        replay(merge(*ls))
                    S.wait_all('sp')
                S.flush()
                with contextlib.ExitStack() as es:
                    tri = SB(es, "tri", [128, 128])
                    cmask = SB(es, "cmask", [128, 128], BF16)
                    NCk = SB(es, "NCk", [128, NT, 8])
                    TOT = SB(es, "TOT", [128, NT, 8])
                    CAR = SB(es, "CAR", [128, NT, 8])
                    NBq = SB(es, "NBq", [128, NT, 8])
                    ownb = SB(es, "ownb", [128, 64])
                    k.asel(tri[:], ones[:], [[1, 128]], ALU.is_ge, 0, -1, ['ones'], ['tri'])
                    k.cp('dve', cmask[:], tri[:], ['tri'], ['cmask'])
                    k.dma('sp', ownb[:], fox_o_norm_w.partition_broadcast(128), [], ['ownb'])
                    LF2 = LFs[:].rearrange("p t h -> p (t h)")
                    nchunk = (NT * 8 + 511) // 512
                    for cc in range(nchunk):
                        c0 = 512 * cc
                        c1 = min(NT * 8, c0 + 512)
                        pb, pbn = PB()
                        k.mm(pb[:, 0:c1 - c0], tri[:], LF2[:, c0:c1], ['tri', 'LFs'], [pbn])
                        k.cp('act', NCk[:].rearrange("p t h -> p (t h)")[:, c0:c1], pb[:, 0:c1 - c0], [pbn], ['NCk'])
                        pb, pbn = PB()
                        k.mm(pb[:, 0:c1 - c0], ones[:], LF2[:, c0:c1], ['ones', 'LFs'], [pbn])
                        k.cp('act', TOT[:].rearrange("p t h -> p (t h)")[:, c0:c1], pb[:, 0:c1 - c0], [pbn], ['TOT'])
                    k.memset('pool', CAR[:, 0, :], 0.0, ['CAR'])
                    for ti in range(1, NT):
                        k.tt('dve', CAR[:, ti, :], CAR[:, ti - 1, :], TOT[:, ti - 1, :], ALU.add, ['CAR', 'TOT'], ['CAR'])
                    k.tt('dve', NCk[:], NCk[:], CAR[:], ALU.add, ['NCk', 'CAR'], ['NCk'])
                    k.stt(NBq[:], TOT[:], 0.5, CAR[:], ALU.mult, ALU.add, ['TOT', 'CAR'], ['NBq'])
                    biasT = [SB(es, f"biasT{i}", [128, NT]) for i in range(2)]
                    PT = [SB(es, f"PT{i}", [128, 128], BF16) for i in range(8)]
                    RL = SB(es, "RL", [128, 8])
                    Ot = SB(es, "Ot", [128, 8, 64])
                    O2 = SB(es, "O2", [128, 8, 64])
                    ssq = SB(es, "ssq", [128, 8])
                    SGl = SB(es, "SGl", [128, 512], BF16)
                    YFt = SB(es, "YFt", [128, 512], BF16)
                    YFo = SB(es, "YFo", [128, 4, 128], BF16)
                    pS = [(pbig[i], f'pb{i}') for i in range(3)]
                    pO = [(pbig[3 + i], f'pb{3 + i}') for i in range(4)]
                    nS = 0
                    nP = 0
                    NQB = NT // 2
                    PT2 = [SB(es, f"PT2_{i}", [128, 256], BF16) for i in range(4)]
                    items = []
                    for qb in range(NQB):
                        for h in range(8):
                            kts_all = list(range(0, 2 * qb + 2))
                            for g0 in range(0, len(kts_all), 2):
                                items.append((qb, h, kts_all[g0:g0 + 2]))

                    def emit_st(it):
                        qb, h, kts = it
                        q0 = 256 * qb
                        pr, r0 = h // 2, 64 * (h % 2)
                        bt, btn = biasT[h % 2], f'biasT{h % 2}'
                        if kts[0] == 0:
                            nk = 2 * qb + 2
                            k.ts('dve', bt[:, 0:nk], NCk[:, 0:nk, h], CAR[:, 2 * qb + 1, h:h + 1], None, ALU.subtract, None,
                                 ['NCk', 'CAR'], [btn])
                        pb, pbn = pS[st['big'] % 3]
                        st['big'] += 1
                        for j, kt in enumerate(kts):
                            lo = 128 if kt == 2 * qb + 1 else 0
                            k.mm(pb[:, 256 * j + lo:256 * j + 256], KT_[r0:r0 + 64, pr, 128 * kt:128 * kt + 128],
                                 QT[r0:r0 + 64, pr, q0 + lo:q0 + 256], ['KTf', 'QT'], [pbn])
                        return pb, pbn

                    def emit_pv(it, pb, pbn):
                        qb, h, kts = it
                        bt, btn = biasT[h % 2], f'biasT{h % 2}'
                        pocol = 65 * (h % 4)
                        for j, kt in enumerate(kts):
                            lo = 128 if kt == 2 * qb + 1 else 0
                            pt, ptn = PT2[st['q'] % 4], f"PT2_{st['q'] % 4}"
                            st['q'] += 1
                            k.act(pt[:, lo:256], pb[:, 256 * j + lo:256 * j + 256], AF.Exp, [pbn, btn], [ptn],
                                  bias=bt[:, kt:kt + 1], scale=0.125)
                            for jq in range(2):
                                qt = 2 * qb + jq
                                if kt > qt:
                                    continue
                                if kt == qt:
                                    k.tt('pool', pt[:, 128 * jq:128 * jq + 128], pt[:, 128 * jq:128 * jq + 128], cmask[:], ALU.mult,
                                         [ptn, 'cmask'], [ptn])
                                po, pon = pO[2 * jq + h // 4]
                                k.mm(po[:, pocol:pocol + 65], pt[:, 128 * jq:128 * jq + 128], V1[:, kt, h, :], [ptn, 'V1'], [pon],
                                     start=(kt == 0), stop=(kt == qt))

                    def epilogue(qt, jq):
                        qsl = slice(128 * qt, 128 * qt + 128)
                        for hf in range(2):
                            po, pon = pO[2 * jq + hf]
                            po3 = po[:, 0:260].rearrange("p (h d) -> p h d", d=65)
                            k.recip(RL[:, 4 * hf:4 * hf + 4], po3[:, :, 64], [pon], [f'RL{hf}'])
                            k.tt('dve', Ot[:, 4 * hf:4 * hf + 4, :], po3[:, :, 0:64],
                                 RL[:, 4 * hf:4 * hf + 4].unsqueeze(2).to_broadcast([128, 4, 64]), ALU.mult,
                                 [pon, f'RL{hf}'], [f'Ot{hf}'])
                        rOt = ['Ot0', 'Ot1']
                        k.act(O2[:], Ot[:], AF.Square, rOt, ['O2'])
                        S.op('dve', lambda e: e.tensor_reduce(out=ssq[:], in_=O2[:], axis=AX.X, op=ALU.add), ['O2'], ['ssq'])
                        k.rsqrt(ssq[:], ssq[:], ['ssq'], ['ssq'], scale=1.0 / 64, eps=1e-6)
                        k.tt('dve', O2[:], Ot[:], ssq[:].unsqueeze(2).to_broadcast([128, 8, 64]), ALU.mult, rOt + ['ssq'], ['O2'])
                        k.tt('pool', O2[:], O2[:], ownb[:].unsqueeze(1).to_broadcast([128, 8, 64]), ALU.mult, ['O2', 'ownb'], ['O2'])
                        k.dma('sp', SGl[:], SGd[qsl, :], ['SGd'], ['SGl'])
                        k.tt('dve', YFt[:], O2[:].rearrange("p h d -> p (h d)"), SGl[:], ALU.mult, ['O2', 'SGl'], ['YFt'])
                        for c4 in range(4):
                            k.tr(pst[:, 128 * c4:128 * c4 + 128], YFt[:, 128 * c4:128 * c4 + 128], identb[:], ['YFt', 'identb'], ['pst'])
                        k.cp('act', YFo[:], pst[:, 0:512].rearrange("p (c t) -> p c t", t=128), ['pst'], ['YFo'])
                        k.dma('sp', YF[:, :, qsl].rearrange("c p t -> p c t"), YFo[:], ['YFo'], ['YF'])
                    cur = emit_st(items[0])
                    for ii, it in enumerate(items):
                        nxt = emit_st(items[ii + 1]) if ii + 1 < len(items) else None
                        emit_pv(it, *cur)
                        cur = nxt
                        if it[1] == 7 and it[2][-1] == 2 * it[0] + 1:
                            epilogue(2 * it[0], 0)
                            epilogue(2 * it[0] + 1, 1)
                    S.wait_all('sp')
                S.flush()

        esC = es0.enter_context(contextlib.ExitStack())
        if 'C2' in phases:
            Wup = SB(esC, "Wup", [128, 8, 2 * DFF], BF16)
            Wd = SB(esC, "Wd", [128, 22, D], BF16)
        if 'C1' in phases:
            with contextlib.ExitStack() as es:
                Wo = SB(es, "Wo", [128, 8, D], BF16)
                for kc in range(8):
                    k.dma('pool', Wo[:, kc, :], w_out[128 * kc:128 * kc + 128, :], [], ['Wo'])
                if 'C2' in phases:
                    for kc in range(8):
                        for c0 in range(0, 2 * DFF, 2048):
                            c1 = min(2 * DFF, c0 + 2048)
                            k.dma('pool', Wup[:, kc, c0:c1], ffn_w_up[128 * kc:128 * kc + 128, c0:c1], [], ['Wup'])
                    for ft in range(22):
                        k.dma('pool', Wd[:, ft, :], ffn_w_down[128 * ft:128 * ft + 128, :], [], ['Wd'])
                Yb = [SB(es, f"Yb{i}", [128, 8, 512], BF16) for i in range(2)]
                xts = [SB(es, f"xt{i}", [128, D]) for i in range(2)]
                x1t = [SB(es, f"x1t{i}", [128, D]) for i in range(2)]
                for b in range(NB):
                    t0 = 512 * b
                    yb, ybn = Yb[b % 2], f'Yb{b % 2}'
                    k.dma('sp', yb[:, 0:4, :], YR[:, :, t0:t0 + 512].rearrange("c p t -> p c t"), ['YR'], [ybn])
                    k.dma('sp', yb[:, 4:8, :], YF[:, :, t0:t0 + 512].rearrange("c p t -> p c t"), ['YF'], [ybn])
                    for i in range(4):
                        par = i % 2
                        tsl = slice(128 * i, 128 * i + 128)
                        k.dma('sp', xts[par][:], x[t0 + 128 * i:t0 + 128 * i + 128, :], [], [f'xt{par}'])
                        for hf in range(2):
                            pb, pbn = PB()
                            for kc in range(8):
                                k.mm(pb[:], yb[:, kc, tsl], Wo[:, kc, 512 * hf:512 * hf + 512], [ybn, 'Wo'], [pbn],
                                     start=(kc == 0), stop=(kc == 7))
                            k.tt('dve', x1t[par][:, 512 * hf:512 * hf + 512], pb[:], xts[par][:, 512 * hf:512 * hf + 512], ALU.add,
                                 [pbn, f'xt{par}'], [f'x1t{par}'])
                        k.dma('sp', X1[t0 + 128 * i:t0 + 128 * i + 128, :], x1t[par][:], [f'x1t{par}'], ['X1'])
                S.wait_all('sp')
                S.flush()

        if 'C2' in phases:
            with contextlib.ExitStack() as es:
                NF = 2 * DFF // 128
                if 'C1' not in phases:
                    for kc in range(8):
                        for c0 in range(0, 2 * DFF, 2048):
                            c1 = min(2 * DFF, c0 + 2048)
                            k.dma('pool', Wup[:, kc, c0:c1], ffn_w_up[128 * kc:128 * kc + 128, c0:c1], [], ['Wup'])
                    for ft in range(22):
                        k.dma('pool', Wd[:, ft, :], ffn_w_down[128 * ft:128 * ft + 128, :], [], ['Wd'])
                nw2 = SB(es, "nw2", [128, D])
                nwf = SB(es, "nwf", [128, D])
                k.dma('sp', nw2[:], norm_ffn_w.partition_broadcast(128), [], ['nw2'])
                k.dma('sp', nwf[:], norm_final_w.partition_broadcast(128), [], ['nwf'])
                craw = SB(es, "craw", [128, 128])
                craw2 = SB(es, "craw2", [48, 128])
                cw = SB(es, "cw", [128, 176])
                k.dma('sp', craw[:], ffn_conv_w.rearrange("a (f p) -> (a f) p", p=128)[0:128, :], [], ['craw'])
                k.dma('sp', craw2[0:4, :], ffn_conv_w.rearrange("a (f p) -> (a f) p", p=128)[128:132, :], [], ['craw2'])
                k.dma('sp', craw2[4:48, :], ffn_conv_b.rearrange("(f p) -> f p", p=128), [], ['craw2'])
                pb, pbn = PB()
                k.tr(pb[:, 0:128], craw[:], ident[:], ['craw', 'ident'], [pbn])
                k.tr(pb[:, 128:176], craw2[:], ident[0:48, 0:48], ['craw2', 'ident'], [pbn])
                k.cp('act', cw[:], pb[:, 0:176], [pbn], ['cw'])
                x1sA = [[(SB(es, f"x1s{p}{i}", [128, D]), f'x1s{p}{i}') for i in range(2)] for p in range(2)]
                ss = SB(es, "ss", [128, 1])
                rs = SB(es, "rs", [128, 1])
                xn = SB(es, "xn", [128, D], BF16)
                h2Ts = [SB(es, f"h2T{p}", [128, 8, 256], BF16) for p in range(2)]
                Ub = [SB(es, f"Ub{i}", [128, 258]) for i in range(2)]
                HL = SB(es, "HL", [128, NF, 2])
                k.memset('pool', HL[:], 0.0, [f'HL{f}' for f in range(NF)])
                cva = [SB(es, f"cva{i}", [128, 256]) for i in range(4)]
                cvb = [SB(es, f"cvb{i}", [128, 256]) for i in range(2)]
                gsl = [SB(es, f"gsl{i}", [128, 256]) for i in range(2)]
                GT = SB(es, "GT", [128, 22, 256], BF16)
                ot = [SB(es, "ot", [128, D])] * 2

                pU = [(pbig[i], f'pb{i}') for i in range(3)]
                pD = [(pbig[3 + i], f'pb{3 + i}') for i in range(4)]
                cnt = {'u': 0}
                cur = {}

                def conv_tile(ft, slot):
                    pb, pbn = pU[cnt['u'] % 3]
                    cnt['u'] += 1
                    for kc in range(8):
                        k.mm(pb[:, 0:256], Wup[:, kc, 128 * ft:128 * ft + 128], cur['h2T'][:, kc, :], ['Wup', cur['hTn']], [pbn],
                             start=(kc == 0), stop=(kc == 7))
                    ub, ubn = Ub[slot], f'Ub{slot}'
                    ca, can = cva[slot + 2 * (ft % 2)], f'cva{slot + 2 * (ft % 2)}'
                    cb_, cbn = cvb[slot], f'cvb{slot}'
                    k.cp('act', ub[:, 2:258], pb[:, 0:256], [pbn], [ubn + 'm'])
                    k.act(ca[:], pb[:, 0:256], AF.Identity, [pbn, 'cw'], [can], bias=cw[:, 132 + ft:133 + ft],
                          scale=cw[:, 88 + ft:89 + ft])
                    k.cp('pool', ub[:, 0:2], HL[:, ft, :], [f'HL{ft}'], [ubn + 'h'])
                    ur = [ubn + 'm', ubn + 'h']
                    k.stt(cb_[:], ub[:, 1:257], cw[:, 44 + ft:45 + ft], ca[:], ALU.mult, ALU.add, ur + ['cw', can], [cbn])
                    k.stt(ca[:], ub[:, 0:256], cw[:, ft:ft + 1], cb_[:], ALU.mult, ALU.add, ur + ['cw', cbn], [can])
                    k.cp('pool', HL[:, ft, :], ub[:, 256:258], ur, [f'HL{ft}'])
                    return ca, can

                def down_mm(ft):
                    for i in range(2):
                        for hf in range(2):
                            pd, pdn = pD[2 * i + hf]
                            k.mm(pd[:], GT[:, ft, 128 * i:128 * i + 128], Wd[:, ft, 512 * hf:512 * hf + 512], [f'GT{ft}', 'Wd'], [pdn],
                                 start=(ft == 0), stop=(ft == 21))

                DLY = 2
                NB2 = T // 256

                def prep(b2):
                    p = b2 % 2
                    rms_to_hT(es, X1, 256 * b2, h2Ts[p], 'nw2', 'c2', ntile=2, xtl=x1sA[p], nwt=nw2, hTn=f'h2T{p}')

                prep(0)
                for b2 in range(NB2):
                    t0 = 256 * b2
                    cur['h2T'], cur['hTn'] = h2Ts[b2 % 2], f'h2T{b2 % 2}'
                    x1s = x1sA[b2 % 2]
                    pend = None

                    def finish(pn):
                        ft_, ga, gan, va, van = pn
                        g_, gn_ = gsl[ft_ % 2], f'gsl{ft_ % 2}'
                        k.act(g_[:], ga[:], AF.Silu, [gan], [gn_])
                        k.tt('pool', GT[:, ft_, :], g_[:], va[:], ALU.mult, [gn_, van], [f'GT{ft_}'])

                    for ft in range(22):
                        ga, gan = conv_tile(ft, 0)
                        va, van = conv_tile(22 + ft, 1)
                        if pend is not None:
                            finish(pend)
                        pend = (ft, ga, gan, va, van)
                        if ft >= DLY + 1:
                            down_mm(ft - DLY - 1)
                        if ft == 10 and b2 + 1 < NB2:
                            prep(b2 + 1)
                    finish(pend)
                    for ft in range(22 - DLY - 1, 22):
                        down_mm(ft)
                    for i in range(2):
                        xs_, xsn = x1s[i]
                        for hf in range(2):
                            pd, pdn = pD[2 * i + hf]
                            k.tt('dve', xs_[:, 512 * hf:512 * hf + 512], pd[:], xs_[:, 512 * hf:512 * hf + 512], ALU.add,
                                 [pdn, xsn], [xsn])
                        k.act(xn[:], xs_[:], AF.Square, [xsn], ['xn', 'ss'], accum_out=ss[:])
                        k.act(ss[:], ss[:], AF.Sqrt, ['ss'], ['ss'], bias=1e-6, scale=1.0 / D)
                        k.recip(rs[:], ss[:], ['ss'], ['rs'])
                        k.stt(ot[i][:], xs_[:], rs[:, 0:1], nwf[:], ALU.mult, ALU.mult, [xsn, 'rs', 'nwf'], ['ot'])
                        k.dma('sp', out[t0 + 128 * i:t0 + 128 * i + 128, :], ot[i][:], ['ot'], ['out'])
                S.wait_all('sp')
                S.flush()

        S.wait_all('sp')
        S.flush()
    return nc


_IN_NAMES = ["x", "norm_mix_w", "w_in", "rwkv_mu", "rwkv_w0", "rwkv_w2", "rwkv_a0", "rwkv_a2", "rwkv_g2",
             "rwkv_k_k", "rwkv_k_a", "rwkv_r_k", "rwkv_lnx_w", "rwkv_lnx_b", "fox_f_bias", "fox_q_norm_w",
             "fox_k_norm_w", "fox_o_norm_w", "w_out", "norm_ffn_w", "ffn_w_up", "ffn_conv_w", "ffn_conv_b",
             "ffn_w_down", "norm_final_w"]

_SHAPES = {"norm_mix_w": (1, D), "w_in": (D, RW + FOXC), "rwkv_mu": (RW,), "rwkv_w0": (512,), "rwkv_w2": (64, 512),
           "rwkv_a0": (512,), "rwkv_a2": (64, 512), "rwkv_g2": (128, 512), "rwkv_k_k": (512,), "rwkv_k_a": (512,),
           "rwkv_r_k": (512,), "rwkv_lnx_w": (512,), "rwkv_lnx_b": (512,), "fox_f_bias": (1, 8),
           "fox_q_norm_w": (64,), "fox_k_norm_w": (64,), "fox_o_norm_w": (1, 64), "w_out": (D, D),
           "norm_ffn_w": (1, D), "ffn_w_up": (D, 2 * DFF), "ffn_conv_w": (3, 2 * DFF), "ffn_conv_b": (2 * DFF,),
           "ffn_w_down": (DFF, D), "norm_final_w": (1, D)}


def make_in_maps(inputs, T, ncores):
    shared = {n: np.ascontiguousarray(np.asarray(inputs[n], dtype=np.float32).reshape(_SHAPES[n])) for n in _SHAPES}
    xs = np.asarray(inputs["x"], dtype=np.float32)
    maps = []
    for c in range(ncores):
        m = dict(shared)
        m["x"] = np.ascontiguousarray(xs[c, :T])
        maps.append(m)
    return maps


def kernel(**inputs):
    T = inputs["x"].shape[1]
    B = inputs["x"].shape[0]
    nc = build(T=T)
    res = run_bass_kernel_spmd(nc, make_in_maps(inputs, T, B), core_ids=list(range(B)))
    return np.stack([r["out"] for r in res.results], axis=0).astype(np.float32)
```

```python
import numpy as np
import concourse.bass as bass
import concourse.mybir as mybir
from concourse.bass_utils import run_bass_kernel_spmd

F32 = mybir.dt.float32
BF16 = mybir.dt.bfloat16
AF = mybir.ActivationFunctionType
ALU = mybir.AluOpType
AX = mybir.AxisListType

ENGS = ('pe', 'act', 'dve', 'pool', 'sp')


class Sched:
    def __init__(self, nc, esems, dsems):
        self.nc = nc
        self.esem = dict(zip(ENGS, esems))
        self.dsems = dsems
        self.cnt = {e: 0 for e in ENGS}
        self.stream = {e: [] for e in ENGS}
        self.seen = {e: {} for e in ENGS}
        self.dcount = [0] * len(dsems)
        self.dn = {'sp': 0, 'pool': 0, 'act': 0}
        nq = len(dsems) // 3
        self.dq = {'sp': list(range(0, nq)), 'pool': list(range(nq, 2 * nq)), 'act': list(range(2 * nq, 3 * nq))}
        self.res = {}

    def _deps(self, reads, writes):
        deps = {}
        def add(tok):
            if tok is None:
                return
            k, v = tok
            if deps.get(k, 0) < v:
                deps[k] = v
        for r in reads:
            st = self.res.get(r)
            if st:
                add(st[0])
        for w in writes:
            st = self.res.get(w)
            if st:
                add(st[0])
                for k, v in st[1].items():
                    add((k, v))
        return deps

    def _commit(self, tok, reads, writes):
        for r in reads:
            st = self.res.setdefault(r, [None, {}])
            if st[1].get(tok[0], 0) < tok[1]:
                st[1][tok[0]] = tok[1]
        for w in writes:
            self.res[w] = [tok, {}]

    def op(self, eng, fn, reads=(), writes=()):
        deps = self._deps(reads, writes)
        waits = []
        seen = self.seen[eng]
        for k, v in deps.items():
            if k == 'pe' and eng == 'pe':
                continue
            if seen.get(k, 0) >= v:
                continue
            seen[k] = v
            waits.append((k, v))
        self.cnt[eng] += 1
        tok = (eng, self.cnt[eng])
        self.stream[eng].append((waits, fn, (eng, 1)))
        self._commit(tok, reads, writes)

    def dma(self, eng, fn, reads=(), writes=()):
        deps = self._deps(reads, writes)
        q = self.dq[eng]
        k = q[self.dn[eng] % len(q)]
        self.dn[eng] += 1
        prev = 16 * self.dcount[k]
        self.dcount[k] += 1
        key = ('d', k)
        if prev > 0:
            if deps.get(key, 0) < prev:
                deps[key] = prev
        waits = []
        seen = self.seen[eng]
        for kk, v in deps.items():
            if seen.get(kk, 0) >= v:
                continue
            seen[kk] = v
            waits.append((kk, v))
        tok = (key, prev + 16)
        self.stream[eng].append((waits, fn, (key, 16)))
        self._commit(tok, reads, writes)

    def wait_all(self, eng):
        waits = []
        for e in ENGS:
            if self.cnt[e] > 0 and e != eng:
                waits.append((e, self.cnt[e]))
        for k in range(len(self.dsems)):
            if self.dcount[k] > 0:
                waits.append((('d', k), 16 * self.dcount[k]))
        self.stream[eng].append((waits, None, None))

    def _sem(self, key):
        if isinstance(key, tuple):
            return self.dsems[key[1]]
        return self.esem[key]

    def emit(self, eng, engine):
        for waits, fn, inc in self.stream[eng]:
            for k, v in waits:
                engine.wait_ge(self._sem(k), v)
            if fn is None:
                continue
            inst = fn(engine)
            inst.then_inc(self._sem(inc[0]), inc[1])

    def flush(self):
        self.emit_all()
        self.stream = {e: [] for e in ENGS}

    def emit_all(self):
        nc = self.nc
        with nc.Block() as block:
            @block.tensor
            def _(e):
                self.emit('pe', e)

            @block.scalar
            def _(e):
                self.emit('act', e)

            @block.vector
            def _(e):
                self.emit('dve', e)

            @block.gpsimd
            def _(e):
                self.emit('pool', e)

            @block.sync
            def _(e):
                self.emit('sp', e)


def _mk(eng):
    def f(self, fn, reads=(), writes=()):
        return self.op(eng, fn, reads, writes)
    return f


for _e in ('pe', 'act', 'dve', 'pool'):
    setattr(Sched, _e, _mk(_e))

import contextlib

D = 1024
RW = 1792
FOXC = 2056
DFF = 2816
NEG_E05 = -0.6065306597126334
F32R = mybir.dt.float32r


def R(ap):
    return ap.bitcast(F32R)


class K:
    def __init__(self, S):
        self.S = S

    def tt(self, eng, out, in0, in1, op, r, w):
        self.S.op(eng, lambda e: e.tensor_tensor(out=out, in0=in0, in1=in1, op=op), r, w)

    def ts(self, eng, out, in0, s1, s2, op0, op1, r, w):
        if s2 is None:
            self.S.op(eng, lambda e: e.tensor_scalar(out=out, in0=in0, scalar1=s1, scalar2=None, op0=op0), r, w)
        else:
            self.S.op(eng, lambda e: e.tensor_scalar(out=out, in0=in0, scalar1=s1, scalar2=s2, op0=op0, op1=op1), r, w)

    def stt(self, out, in0, scalar, in1, op0, op1, r, w):
        self.S.op('dve', lambda e: e.scalar_tensor_tensor(out=out, in0=in0, scalar=scalar, in1=in1, op0=op0, op1=op1), r, w)

    def act(self, out, in_, func, r, w, bias=None, scale=None, accum_out=None):
        kw = {}
        if bias is not None:
            kw['bias'] = bias
        if scale is not None:
            kw['scale'] = scale
        if accum_out is not None:
            kw['accum_out'] = accum_out
        self.S.op('act', lambda e: e.activation(out=out, in_=in_, func=func, **kw), r, w)

    def cp(self, eng, out, in_, r, w):
        if eng == 'act':
            self.S.op('act', lambda e: e.copy(out=out, in_=in_), r, w)
        else:
            self.S.op(eng, lambda e: e.tensor_copy(out=out, in_=in_), r, w)

    def mm(self, out, lhsT, rhs, r, w, start=True, stop=True, r32=False):
        if r32:
            lhsT, rhs = R(lhsT), R(rhs)
        self.S.op('pe', lambda e: e.matmul(out, lhsT=lhsT, rhs=rhs, start=start, stop=stop), r, w)

    def tr(self, out, in_, ident, r, w):
        self.S.op('pe', lambda e: e.transpose(out, in_, ident), r, w)

    def dma(self, q, out, in_, r, w, **kw):
        self.S.dma(q, lambda e: e.dma_start(out=out, in_=in_, **kw), r, w)

    def memset(self, eng, ap, val, w):
        self.S.op(eng, lambda e: e.memset(ap, val), [], w)

    def rsqrt(self, out, in_, r, w, scale=1.0, eps=0.0):
        self.act(out, in_, AF.Ln, r, w, bias=eps, scale=scale)
        self.act(out, out, AF.Exp, w, w, scale=-0.5)

    def sigmoid(self, out, in_, r, w, nbias=None, scale=1.0, tmp=None, tmpn='sgtmp'):
        t = out if tmp is None else tmp
        tw = w if tmp is None else [tmpn]
        if nbias is None:
            self.act(t, in_, AF.Exp, r, tw, scale=-scale)
        else:
            self.act(t, in_, AF.Exp, r, tw, bias=nbias, scale=-scale)
        self.act(t, t, AF.Ln, tw, tw, bias=1.0)
        self.act(out, t, AF.Exp, tw, w, scale=-1.0)

    def recip(self, out, in_, r, w):
        self.S.op('dve', lambda e: e.reciprocal(out=out, in_=in_), r, w)

    def asel(self, out, in_, pattern, op, base, cm, r, w):
        self.S.op('pool', lambda e: e.affine_select(out=out, in_=in_, pattern=pattern, compare_op=op,
                                                    fill=0.0, base=base, channel_multiplier=cm), r, w)

    def scan(self, out, d0, d1, r, w):
        self.S.op('dve', lambda e: e.tensor_tensor_scan(out=out, data0=d0, data1=d1, initial=0.0,
                                                        op0=ALU.mult, op1=ALU.add), r, w)


def build(T=4096, dbg=False, phases=('A1', 'A2', 'B', 'C1', 'C2')):
    nc = bass.Bass("TRN2", target_bir_lowering=False)
    NT = T // 128
    NB = T // 512
    din = {}

    def DI(name, shape):
        din[name] = nc.dram_tensor(name, list(shape), F32, kind="ExternalInput").ap()
        return din[name]

    x = DI("x", [T, D])
    norm_mix_w = DI("norm_mix_w", [1, D])
    w_in = DI("w_in", [D, RW + FOXC])
    rwkv_mu = DI("rwkv_mu", [RW])
    rwkv_w0 = DI("rwkv_w0", [512])
    rwkv_w2 = DI("rwkv_w2", [64, 512])
    rwkv_a0 = DI("rwkv_a0", [512])
    rwkv_a2 = DI("rwkv_a2", [64, 512])
    rwkv_g2 = DI("rwkv_g2", [128, 512])
    rwkv_k_k = DI("rwkv_k_k", [512])
    rwkv_k_a = DI("rwkv_k_a", [512])
    rwkv_r_k = DI("rwkv_r_k", [512])
    rwkv_lnx_w = DI("rwkv_lnx_w", [512])
    rwkv_lnx_b = DI("rwkv_lnx_b", [512])
    fox_f_bias = DI("fox_f_bias", [1, 8])
    fox_q_norm_w = DI("fox_q_norm_w", [64])
    fox_k_norm_w = DI("fox_k_norm_w", [64])
    fox_o_norm_w = DI("fox_o_norm_w", [1, 64])
    w_out = DI("w_out", [D, D])
    norm_ffn_w = DI("norm_ffn_w", [1, D])
    ffn_w_up = DI("ffn_w_up", [D, 2 * DFF])
    ffn_conv_w = DI("ffn_conv_w", [3, 2 * DFF])
    ffn_conv_b = DI("ffn_conv_b", [2 * DFF])
    ffn_w_down = DI("ffn_w_down", [DFF, D])
    norm_final_w = DI("norm_final_w", [1, D])
    out = nc.dram_tensor("out", [T, D], F32, kind="ExternalOutput").ap()

    okind = "ExternalOutput" if dbg else "Internal"
    YR = nc.dram_tensor("yr", [4, 128, T], BF16, kind=okind).ap()
    YF = nc.dram_tensor("yf", [4, 128, T], BF16, kind=okind).ap()
    SGd = nc.dram_tensor("sgd", [T, 512], BF16, kind="Internal").ap()
    X1 = nc.dram_tensor("x1", [T, D], F32, kind=okind).ap()

    with contextlib.ExitStack() as es0:
        esems = [es0.enter_context(nc.semaphore(f"es{i}")) for i in range(5)]
        dsems = [es0.enter_context(nc.semaphore(f"ds{i}")) for i in range(24)]
        S = Sched(nc, esems, dsems)
        k = K(S)

        uid = [0]

        def SB(es, name, shape, dt=F32):
            uid[0] += 1
            return es.enter_context(nc.sbuf_tensor(f"{name}_u{uid[0]}", list(shape), dt))

        pst = es0.enter_context(nc.psum_tensor("pst", [128, 1024], BF16))
        pbig = [es0.enter_context(nc.psum_tensor(f"pb{i}", [128, 512], F32)) for i in range(7)]
        st = {'big': 0, 'q': 0}

        def PB():
            i = st['big'] % 7
            st['big'] += 1
            return pbig[i], f"pb{i}"

        def PQ():
            i = st['q'] % 16
            st['q'] += 1
            return pqb[i // 4][:, (i % 4) * 128:(i % 4) * 128 + 128], f"pq{i}"

        def PQbank():
            return PQ()

        ones = SB(es0, "ones", [128, 128])
        ident = SB(es0, "ident", [128, 128])
        identb = SB(es0, "identb", [128, 128], BF16)
        mST = SB(es0, "mST", [128, 128])
        mIT = SB(es0, "mIT", [128, 128])
        mS = SB(es0, "mS", [128, 128])
        bones_raw = SB(es0, "bones_raw", [128, 128])
        bones = SB(es0, "bones", [128, 128])
        k.memset('pool', ones[:], 1.0, ['ones'])
        k.asel(ident[:], ones[:], [[-1, 128]], ALU.is_equal, 0, 1, ['ones'], ['ident'])
        k.cp('dve', identb[:], ident[:], ['ident'], ['identb'])
        for m_, nm in ((mST, 'mST'), (mIT, 'mIT'), (mS, 'mS'), (bones_raw, 'bones_raw')):
            k.memset('pool', m_[:], 0.0, [nm])
        for b in range(2):
            sl = slice(64 * b, 64 * b + 64)
            k.asel(mST[sl, sl], ones[sl, sl], [[1, 64]], ALU.is_ge, -1, -1, ['ones', 'mST'], ['mST'])
            k.asel(mIT[sl, sl], ones[sl, sl], [[1, 64]], ALU.is_ge, 0, -1, ['ones', 'mIT'], ['mIT'])
            k.asel(mS[sl, sl], ones[sl, sl], [[-1, 64]], ALU.is_ge, -1, 1, ['ones', 'mS'], ['mS'])
            k.cp('pool', bones_raw[sl, sl], ones[sl, sl], ['ones', 'bones_raw'], ['bones_raw'])

        k.cp('dve', R(bones[:]), bones_raw[:], ['bones_raw'], ['bones'])
        nwb = SB(es0, "nwb", [128, D])

        def rms_to_hT(es, xsrc, t0, hT, nwname, tagp, ntile=4, xtl=None, nwt=None, hTn='hT'):
            nwt_ = nwb if nwt is None else nwt
            for i in range(ntile):
                if xtl is None:
                    par = i % 2
                    xt, xtn = xts[par], f'xt{par}'
                else:
                    xt, xtn = xtl[i]
                k.dma('sp', xt[:], xsrc[t0 + 128 * i:t0 + 128 * i + 128, :], [], [xtn])
                k.act(xn[:], xt[:], AF.Square, [xtn], ['xn', 'ss'], accum_out=ss[:])
                k.rsqrt(rs[:], ss[:], ['ss'], ['rs'], scale=1.0 / D, eps=1e-6)
                k.stt(xn[:], xt[:], rs[:, 0:1], nwt_[:], ALU.mult, ALU.mult, [xtn, 'rs', nwname], ['xn'])
                for kk_ in range(8):
                    k.tr(pst[:, 128 * kk_:128 * kk_ + 128], xn[:, 128 * kk_:128 * kk_ + 128], identb[:],
                         ['xn', 'identb'], ['pst'])
                k.cp('act', hT[:, :, 128 * i:128 * i + 128], pst[:].rearrange("p (k t) -> p k t", t=128),
                     ['pst'], [hTn])

        class Rec:
            def __init__(self):
                self.ops = []

            def op(self, eng, fn, reads=(), writes=()):
                self.ops.append(('op', eng, fn, tuple(reads), tuple(writes)))

            def dma(self, eng, fn, reads=(), writes=()):
                self.ops.append(('dma', eng, fn, tuple(reads), tuple(writes)))

        def record(fn, *a_):
            rec = Rec()
            k.S = rec
            try:
                fn(*a_)
            finally:
                k.S = S
            return rec.ops

        def replay(ops):
            for kind, eng, fn, r, w in ops:
                (S.op if kind == 'op' else S.dma)(eng, fn, r, w)

        def merge(*lists):
            lists = [l for l in lists if len(l) > 0]
            pos = [0] * len(lists)
            o = []
            total = sum(len(l) for l in lists)
            while len(o) < total:
                best, bf = None, None
                for i, l in enumerate(lists):
                    if pos[i] < len(l):
                        f = pos[i] / len(l)
                        if bf is None or f < bf:
                            best, bf = i, f
                o.append(lists[best][pos[best]])
                pos[best] += 1
            return o

        if 'A1' in phases:
            with contextlib.ExitStack() as es:
                WB = 256
                NBL = T // WB
                cE = {'n': 0}
                cC = {'n': 0}

                def PBE():
                    i = cE['n'] % 3
                    cE['n'] += 1
                    return pbig[i], f"pb{i}"

                def PBC():
                    i = 3 + cC['n'] % 4
                    cC['n'] += 1
                    return pbig[i], f"pb{i}"

                def bc4(m):
                    return m[:].unsqueeze(1).to_broadcast([128, 4, 128])

                k.dma('sp', nwb[:], norm_mix_w.partition_broadcast(128), [], ['nwb'])
                Win = SB(es, "WinR", [128, 8, RW], BF16)
                for kc in range(8):
                    k.dma('pool', Win[:, kc, :], w_in[128 * kc:128 * kc + 128, 0:RW], [], ['Win'])
                xts = [SB(es, f"xt{i}", [128, D]) for i in range(2)]
                ss = SB(es, "ss", [128, 1])
                rs = SB(es, "rs", [128, 1])
                xn = SB(es, "xn", [128, D], BF16)
                hT = SB(es, "hT", [128, 8, WB], BF16)
                T1 = [SB(es, f"T1_{i}", [128, WB]) for i in range(2)]
                carry = SB(es, "carry", [128, 14])
                X = SB(es, "X", [128, 14, WB])
                mu_cm = SB(es, "mu_cm", [128, 14])
                omm_cm = SB(es, "omm_cm", [128, 14])
                prm = SB(es, "prm", [128, 7, 4])
                omka = SB(es, "omka", [128, 4])
                nprm = SB(es, "nprm", [128, 2, 4])
                W2Z = SB(es, "W2Z", [128, 512])
                A2Z = SB(es, "A2Z", [128, 512])
                G2 = SB(es, "G2", [128, 512])
                Wraw = [SB(es, f"Wraw{i}", [128, 512]) for i in range(3)]
                rmask = SB(es, "rmask", [128, WB])
                k.dma('sp', mu_cm[:], rwkv_mu.rearrange("(t p) -> p t", p=128), [], ['mu_cm'], allow_slow_non_contiguous=True)
                for i, prm_in in enumerate((rwkv_w0, rwkv_a0, rwkv_k_k, rwkv_k_a, rwkv_r_k, rwkv_lnx_w, rwkv_lnx_b)):
                    k.dma('sp', prm[:, i, :], prm_in.rearrange("(t p) -> p t", p=128), [], ['prm'], allow_slow_non_contiguous=True)
                k.ts('dve', omm_cm[:], mu_cm[:], -1.0, 1.0, ALU.mult, ALU.add, ['mu_cm'], ['omm_cm'])
                k.ts('dve', omka[:], prm[:, 3, :], -1.0, 1.0, ALU.mult, ALU.add, ['prm'], ['omka'])
                k.ts('dve', nprm[:], prm[:, 0:2, :], -1.0, None, ALU.mult, None, ['prm'], ['nprm'])
                k.memset('pool', Wraw[0][:], 0.0, ['Wraw0'])
                k.memset('pool', Wraw[1][:], 0.0, ['Wraw1'])
                k.dma('sp', Wraw[0][0:64, :], rwkv_w2, [], ['Wraw0'])
                k.dma('sp', Wraw[1][64:128, :], rwkv_a2, [], ['Wraw1'])
                k.dma('sp', Wraw[2][:], rwkv_g2, [], ['Wraw2'])
                for i_, (w_, wn_) in enumerate(((W2Z, 'W2Z'), (A2Z, 'A2Z'), (G2, 'G2'))):
                    k.cp('dve', R(w_[:]), Wraw[i_][:], [f'Wraw{i_}'], [wn_])
                k.memset('pool', rmask[:], 1.0, ['rmask'])
                k.memset('pool', rmask[:].rearrange("p (c t) -> p c t", t=64)[:, :, 0:1], 0.0, ['rmask'])
                k.memset('pool', carry[:], 0.0, [f'carry{c_}' for c_ in range(14)])

                TA = SB(es, "TA", [128, WB])
                SGg = SB(es, "SGg", [128, WB])
                tmp = [SB(es, f"tmp{i}", [128, WB]) for i in range(9)]
                SQr = SB(es, "SQr", [128, WB])
                BDn = ('AT', 'BT', 'KT', 'BH', 'KH', 'VT')
                BDp = [{n: SB(es, f"BD{p}_{n}", [128, 4, 128]) for n in BDn} for p in range(2)]
                RT3 = [SB(es, f"RT{p}", [128, 4, 128]) for p in range(3)]
                Gtp = [SB(es, f"Gt{p}", [128, WB]) for p in range(3)]
                BStp = [SB(es, f"BSt{p}", [128, WB]) for p in range(3)]
                PCtp = [SB(es, f"PCt{p}", [128, 4]) for p in range(3)]
                for p in range(2):
                    for n in BDn:
                        k.memset('pool', BDp[p][n][:], 0.0, [f'BD{p}_{n}'])
                        k.cp('dve', R(BDp[p][n][:]), BDp[p][n][:], [f'BD{p}_{n}'], [f'BD{p}_{n}'])
                for p in range(3):
                    k.memset('pool', RT3[p][:], 0.0, [f'RT{p}'])
                    k.cp('dve', R(RT3[p][:]), RT3[p][:], [f'RT{p}'], [f'RT{p}'])
                TMs = [{n: SB(es, f"TM{p}_{n}", [128, 4, 128]) for n in ('BH', 'KH', 'V')} for p in range(2)]
                TMA = SB(es, "TM_A", [128, 4, 128])
                CMn = ('NTa', 'NTb', 'Na', 'Nb', 'ST', 'AKT', 'W2', 'U')
                CM = {n: SB(es, f"CM_{n}", [128, 4, 128]) for n in CMn}
                CMs = [{n: SB(es, f"CM{p}_{n}", [128, 4, 128]) for n in ('MRBT', 'MRKT', 'ApT', 'Vp')} for p in range(2)]
                H = [SB(es, f"H{p}", [128, 128]) for p in range(4)]
                for p in range(4):
                    k.memset('pool', H[p][:], 0.0, [f'H{p}'])
                    k.cp('dve', R(H[p][:]), H[p][:], [f'H{p}'], [f'H{p}'])
                YT = SB(es, "YT", [128, WB])
                G1 = SB(es, "G1", [128, WB])
                G2t = SB(es, "G2t", [128, WB])
                SQr2 = SB(es, "SQr2", [128, WB])
                YO = SB(es, "YO", [128, WB], BF16)

                def XR(ct):
                    return [f'X{ct}a', f'X{ct}b']

                def stage_E(b, pr):
                    n_ = 4 * b + pr
                    par = n_ % 2
                    p3 = n_ % 3
                    BD = dict(BDp[par])
                    BD['RT'] = RT3[p3]
                    bdn = f'BD{par}_'
                    t0 = WB * b
                    if pr == 0:
                        rms_to_hT(es, x, t0, hT, 'nwb', 'a1', ntile=2)
                        for ct in range(14):
                            pb, pbn = PBE()
                            for kc in range(8):
                                k.mm(pb[:, 0:WB], Win[:, kc, 128 * ct:128 * ct + 128], hT[:, kc, :], ['Win', 'hT'], [pbn],
                                     start=(kc == 0), stop=(kc == 7))
                            t1 = T1[ct % 2]
                            t1n = f'T1_{ct % 2}'
                            k.act(t1[:], pb[:, 0:WB], AF.Identity, [pbn, 'mu_cm'], [t1n], scale=mu_cm[:, ct:ct + 1])
                            xn_ = f'X{ct}'
                            k.stt(X[:, ct, 1:WB], pb[:, 1:WB], omm_cm[:, ct:ct + 1], t1[:, 0:WB - 1], ALU.mult, ALU.add,
                                  [pbn, t1n, 'omm_cm'], [xn_ + 'a'])
                            k.stt(X[:, ct, 0:1], pb[:, 0:1], omm_cm[:, ct:ct + 1], carry[:, ct:ct + 1], ALU.mult, ALU.add,
                                  [pbn, f'carry{ct}', 'omm_cm'], [xn_ + 'b'])
                            k.cp('pool', carry[:, ct:ct + 1], t1[:, WB - 1:WB], [t1n], [f'carry{ct}'])
                        k.sigmoid(T1[0][0:64, :], X[0:64, 12, :], XR(12), ['T1_0'], scale=2.0)
                        k.ts('dve', R(TA[0:64, :]), T1[0][0:64, :], 2.0, -1.0, ALU.mult, ALU.add, ['T1_0'], ['TAa'])
                        k.cp('pool', R(TA[64:128, :]), X[64:128, 12, :], XR(12), ['TAb'])
                        k.sigmoid(R(SGg[:]), X[:, 13, :], XR(13), ['SGg'], tmp=T1[1][:], tmpn='T1_1')
                    cs = slice(128 * pr, 128 * pr + 128)
                    Xr, Xk, Xv = X[:, pr, :], X[:, 4 + pr, :], X[:, 8 + pr, :]
                    rXr, rXk, rXv = XR(pr), XR(4 + pr), XR(8 + pr)
                    SIG, A_, KKN, KP, BV, L, E1, E2, E3 = tmp
                    tn = [f'tmp{i}' for i in range(9)]
                    Gt, Gtn = Gtp[p3], f'Gt{p3}'
                    BSt, BStn = BStp[p3], f'BSt{p3}'
                    PCt, PCtn = PCtp[p3], f'PCt{p3}'

                    def bdwrite(eng, name, in0, in1, op, r):
                        rn_ = f'RT{p3}' if name == 'RT' else bdn + name
                        for hh in range(2):
                            ps_ = slice(64 * hh, 64 * hh + 64)
                            o = R(BD[name][ps_, :, 64 * hh:64 * hh + 64])
                            a0 = in0[ps_, :].rearrange("p (c t) -> p c t", t=64)
                            if in1 is None:
                                k.cp(eng, o, a0, r, [rn_])
                            else:
                                a1 = in1[ps_, :].rearrange("p (c t) -> p c t", t=64)
                                k.tt(eng, o, a0, a1, op, r, [rn_])

                    pb, pbn = PBE()
                    k.mm(pb[:, 0:WB], W2Z[:, cs], TA[:], ['W2Z', 'TAa', 'TAb'], [pbn], r32=True)
                    k.sigmoid(SIG[:], pb[:, 0:WB], [pbn, 'nprm'], [tn[0]], nbias=nprm[:, 0, pr:pr + 1])
                    pb, pbn = PBE()
                    k.mm(pb[:, 0:WB], A2Z[:, cs], TA[:], ['A2Z', 'TAa', 'TAb'], [pbn], r32=True)
                    k.sigmoid(A_[:], pb[:, 0:WB], [pbn, 'nprm'], [tn[1]], nbias=nprm[:, 1, pr:pr + 1])
                    pb, pbn = PBE()
                    k.mm(pb[:, 0:WB], G2[:, cs], SGg[:], ['G2', 'SGg'], [pbn], r32=True)
                    k.cp('act', Gt[:], pb[:, 0:WB], [pbn], [Gtn])
                    k.ts('dve', KKN[:], Xk, prm[:, 2, pr:pr + 1], None, ALU.mult, None, rXk + ['prm'], [tn[2]])
                    k.act(R(SQr[:]), KKN[:], AF.Square, [tn[2]], ['SQr'])
                    pb, pbn = PBE()
                    k.mm(pb[:, 0:WB], bones[:], SQr[:], ['bones', 'SQr'], [pbn], r32=True)
                    k.ts('dve', E1[:], pb[:, 0:WB], 1e-19, None, ALU.max, None, [pbn], [tn[6]])
                    k.rsqrt(E1[:], E1[:], [tn[6]], [tn[6]])
                    k.tt('dve', KKN[:], KKN[:], E1[:], ALU.mult, [tn[2], tn[6]], [tn[2]])
                    k.ts('dve', KP[:], A_[:], prm[:, 3, pr:pr + 1], omka[:, pr:pr + 1], ALU.mult, ALU.add,
                         [tn[1], 'prm', 'omka'], [tn[3]])
                    k.tt('dve', KP[:], KP[:], Xk, ALU.mult, [tn[3]] + rXk, [tn[3]])
                    k.tt('pool', BV[:], KKN[:], A_[:], ALU.mult, [tn[2], tn[1]], [tn[4]])
                    k.stt(R(SQr[:]), Xr, prm[:, 4, pr:pr + 1], KP[:], ALU.mult, ALU.mult, rXr + ['prm', tn[3]], ['SQr'])
                    pb, pbn = PBE()
                    k.mm(pb[:, 0:WB], bones[:], SQr[:], ['bones', 'SQr'], [pbn], r32=True)
                    k.tt('dve', BSt[:], pb[:, 0:WB], Xv, ALU.mult, [pbn] + rXv, [BStn])
                    k.ts('dve', SIG[:], SIG[:], NEG_E05, None, ALU.mult, None, [tn[0]], [tn[0]])
                    k.scan(L[:], rmask[:], SIG[:], ['rmask', tn[0]], [tn[5]])
                    k.act(E1[:], L[:], AF.Exp, [tn[5]], [tn[6]])
                    bdwrite('dve', 'RT', Xr, E1, ALU.mult, rXr + [tn[6]])
                    k.tt('pool', E2[:], L[:], SIG[:], ALU.subtract, [tn[5], tn[0]], [tn[7]])
                    k.act(E2[:], E2[:], AF.Exp, [tn[7]], [tn[7]])
                    k.ts('dve', E2[:], E2[:], -1.0, None, ALU.mult, None, [tn[7]], [tn[7]])
                    bdwrite('dve', 'AT', KKN, E2, ALU.mult, [tn[2], tn[7]])
                    k.act(E3[:], L[:], AF.Exp, [tn[5]], [tn[8]], scale=-1.0)
                    bdwrite('dve', 'BT', BV, E3, ALU.mult, [tn[4], tn[8]])
                    bdwrite('pool', 'KT', KP, E3, ALU.mult, [tn[3], tn[8]])
                    L3 = L[:].rearrange("p (c t) -> p c t", t=64)
                    k.tt('dve', E1[:].rearrange("p (c t) -> p c t", t=64), L3,
                         L3[:, :, 63:64].to_broadcast([128, 4, 64]), ALU.subtract, [tn[5]], [tn[6]])
                    k.act(E1[:], E1[:], AF.Exp, [tn[6]], [tn[6]], scale=-1.0)
                    bdwrite('dve', 'BH', BV, E1, ALU.mult, [tn[4], tn[6]])
                    bdwrite('pool', 'KH', KP, E1, ALU.mult, [tn[3], tn[6]])
                    k.act(PCt[:], L3[:, :, 63], AF.Exp, [tn[5]], [PCtn])
                    bdwrite('pool', 'VT', Xv, None, None, rXv)

                cC2 = {'n': 0}

                def PBC1():
                    i = 3 + cC['n'] % 2
                    cC['n'] += 1
                    return pbig[i], f"pb{i}"

                def PBC2():
                    i = 5 + cC2['n'] % 2
                    cC2['n'] += 1
                    return pbig[i], f"pb{i}"

                def q4(pb_, i):
                    return pb_[:, 128 * i:128 * i + 128]

                def v4(pb_):
                    return pb_[:].rearrange("p (i t) -> p i t", t=128)

                def stage_C1(b, pr):
                    n_ = 4 * b + pr
                    par = n_ % 2
                    p3 = n_ % 3
                    BD = dict(BDp[par])
                    BD['RT'] = RT3[p3]
                    bdn = f'BD{par}_'

                    def bn(nm):
                        return f'RT{p3}' if nm == 'RT' else bdn + nm
                    CMd, cmd = CMs[par], f'CM{par}_'
                    TMd, tmd = TMs[par], f'TM{par}_'
                    for (ln, rn, mk, mkn, on) in (('BT', 'AT', mST, 'mST', 'NTa'), ('AT', 'BT', mS, 'mS', 'Na'),
                                                  ('KT', 'AT', mST, 'mST', 'AKT'), ('BT', 'RT', mIT, 'mIT', 'MRBT'),
                                                  ('KT', 'RT', mIT, 'mIT', 'MRKT')):
                        pb, pbn = PBC1()
                        for c in range(4):
                            k.mm(q4(pb, c), BD[ln][:, c, :], BD[rn][:, c, :], [bn(ln), bn(rn)], [pbn], r32=True)
                        if on in CMd:
                            k.tt('dve', R(CMd[on][:]), v4(pb), bc4(mk), ALU.mult, [pbn, mkn], [cmd + on])
                        else:
                            k.tt('dve', R(CM[on][:]), v4(pb), bc4(mk), ALU.mult, [pbn, mkn], ['CM_' + on])
                    for (src, dst) in (('AT', 'A'), ('BH', 'BH'), ('KH', 'KH'), ('VT', 'V')):
                        pb, pbn = PBC1()
                        for c in range(4):
                            k.tr(q4(pb, c), BD[src][:, c, :], ident[:], [bn(src), 'ident'], [pbn])
                        if dst == 'A':
                            k.cp('act', R(TMA[:]), v4(pb), [pbn], ['TM_A'])
                        else:
                            k.cp('act', R(TMd[dst][:]), v4(pb), [pbn], [tmd + dst])
                    k.tt('pool', R(CM['ST'][:]), CM['NTa'][:], bc4(ident), ALU.add, ['CM_NTa', 'ident'], ['CM_ST'])
                    curN, curNT = 'Na', 'NTa'
                    for lev in range(1, 6):
                        nxtN = 'Nb' if curN == 'Na' else 'Na'
                        nxtNT = 'NTb' if curNT == 'NTa' else 'NTa'
                        pb, pbn = PBC1()
                        for i in range(4):
                            k.mm(q4(pb, i), CM[curNT][:, i, :], CM[curN][:, i, :], ['CM_' + curNT, 'CM_' + curN], [pbn], r32=True)
                        k.cp('act', R(CM[nxtN][:]), v4(pb), [pbn], ['CM_' + nxtN])
                        if lev < 5:
                            pb, pbn = PBC1()
                            for i in range(4):
                                k.mm(q4(pb, i), CM[curN][:, i, :], CM[curNT][:, i, :], ['CM_' + curNT, 'CM_' + curN], [pbn], r32=True)
                            k.cp('act', R(CM[nxtNT][:]), v4(pb), [pbn], ['CM_' + nxtNT])
                        pb, pbn = PBC1()
                        for i in range(4):
                            k.mm(q4(pb, i), CM[nxtN][:, i, :], CM['ST'][:, i, :], ['CM_' + nxtN, 'CM_ST'], [pbn], r32=True)
                        k.tt('dve', R(CM['ST'][:]), v4(pb), CM['ST'][:], ALU.add, [pbn, 'CM_ST'], ['CM_ST'])
                        curN, curNT = nxtN, nxtNT
                    pb, pbn = PBC1()
                    for i in range(4):
                        k.mm(q4(pb, i), TMA[:, i, :], CM['ST'][:, i, :], ['TM_A', 'CM_ST'], [pbn], r32=True)
                    k.cp('act', R(CMd['ApT'][:]), v4(pb), [pbn], [cmd + 'ApT'])
                    pb, pbn = PBC1()
                    for i in range(4):
                        k.mm(q4(pb, i), CM['AKT'][:, i, :], TMd['V'][:, i, :], ['CM_AKT', tmd + 'V'], [pbn], r32=True)
                    k.cp('act', R(CM['W2'][:]), v4(pb), [pbn], ['CM_W2'])
                    pb, pbn = PBC1()
                    for i in range(4):
                        k.mm(q4(pb, i), CM['ST'][:, i, :], CM['W2'][:, i, :], ['CM_ST', 'CM_W2'], [pbn], r32=True)
                    k.cp('act', R(CMd['Vp'][:]), v4(pb), [pbn], [cmd + 'Vp'])

                def stage_C2(b, pr):
                    n_ = 4 * b + pr
                    par = n_ % 2
                    p3 = n_ % 3
                    t0 = WB * b
                    RT_, rtn = RT3[p3], f'RT{p3}'
                    CMd, cmd = CMs[par], f'CM{par}_'
                    TMd, tmd = TMs[par], f'TM{par}_'
                    Gt, Gtn = Gtp[p3], f'Gt{p3}'
                    BSt, BStn = BStp[p3], f'BSt{p3}'
                    PCt, PCtn = PCtp[p3], f'PCt{p3}'
                    Hp, Hn = H[pr], f'H{pr}'
                    for c in range(4):
                        i = c
                        un = f'CM_U{i}'
                        pb, pbn = PBC2()
                        k.mm(q4(pb, 0), CMd['ApT'][:, i, :], Hp[:], [cmd + 'ApT', Hn], [pbn], r32=True)
                        k.tt('dve', R(CM['U'][:, i, :]), q4(pb, 0), CMd['Vp'][:, i, :], ALU.add, [pbn, cmd + 'Vp'], [un])
                        pb, pbn = PBC2()
                        k.mm(q4(pb, 0), Hp[:], RT_[:, c, :], [Hn, rtn], [pbn], r32=True, start=True, stop=False)
                        k.mm(q4(pb, 0), CM['U'][:, i, :], CMd['MRBT'][:, i, :], [un, cmd + 'MRBT'], [pbn], r32=True,
                             start=False, stop=False)
                        k.mm(q4(pb, 0), TMd['V'][:, i, :], CMd['MRKT'][:, i, :], [tmd + 'V', cmd + 'MRKT'], [pbn], r32=True,
                             start=False, stop=True)
                        for hh in range(2):
                            ps_ = slice(64 * hh, 64 * hh + 64)
                            k.cp('act', R(YT[ps_, 64 * c:64 * c + 64]), pb[ps_, 64 * hh:64 * hh + 64], [pbn], [f'YT{hh}'])
                        pb, pbn = PBC2()
                        k.mm(q4(pb, 0), TMd['BH'][:, i, :], CM['U'][:, i, :], [tmd + 'BH', un], [pbn], r32=True,
                             start=True, stop=False)
                        k.mm(q4(pb, 0), TMd['KH'][:, i, :], TMd['V'][:, i, :], [tmd + 'KH', tmd + 'V'], [pbn], r32=True,
                             start=False, stop=True)
                        k.stt(R(Hp[:]), Hp[:], PCt[:, c:c + 1], q4(pb, 0), ALU.mult, ALU.add, [Hn, PCtn, pbn], [Hn])
                    rYT = ['YT0', 'YT1']
                    pb, pbn = PBC2()
                    k.mm(pb[:, 0:WB], bones[:], YT[:], ['bones'] + rYT, [pbn], r32=True)
                    k.stt(G1[:], pb[:, 0:WB], -1.0 / 64, YT[:], ALU.mult, ALU.add, [pbn] + rYT, ['G1'])
                    k.act(R(SQr2[:]), G1[:], AF.Square, ['G1'], ['SQr2'])
                    pb, pbn = PBC2()
                    k.mm(pb[:, 0:WB], bones[:], SQr2[:], ['bones', 'SQr2'], [pbn], r32=True)
                    k.rsqrt(G2t[:], pb[:, 0:WB], [pbn], ['G2t'], scale=1.0 / 64, eps=64e-5)
                    k.tt('dve', G1[:], G1[:], G2t[:], ALU.mult, ['G1', 'G2t'], ['G1'])
                    k.ts('dve', G1[:], G1[:], prm[:, 5, pr:pr + 1], prm[:, 6, pr:pr + 1], ALU.mult, ALU.add,
                         ['G1', 'prm'], ['G1'])
                    k.tt('pool', G1[:], G1[:], BSt[:], ALU.add, ['G1', BStn], ['G1'])
                    k.tt('pool', YO[:], G1[:], Gt[:], ALU.mult, ['G1', Gtn], ['YO'])
                    k.dma('sp', YR[pr, :, t0:t0 + WB], YO[:], ['YO'], ['YR'])

                units = [(b, pr) for b in range(NBL) for pr in range(4)]
                NU = len(units)
                replay(record(stage_E, *units[0]))
                for n in range(NU + 1):
                    ls = []
                    if n >= 1:
                        ls.append(record(stage_C2, *units[n - 1]))
                    if n < NU:
                        ls.append(record(stage_C1, *units[n]))
                    if n + 1 < NU:
                        ls.append(record(stage_E, *units[n + 1]))
                    replay(merge(*ls))
                S.wait_all('sp')
                S.flush()

        if 'A2' in phases:
            with contextlib.ExitStack() as esAB:
                QT = SB(esAB, "QT", [128, 4, T], BF16)
                KT_ = SB(esAB, "KTf", [128, 4, T], BF16)
                V1 = SB(esAB, "V1", [128, NT, 8, 65], BF16)
                LFs = SB(esAB, "LFs", [128, NT, 8])
                k.memset('pool', V1[:], 1.0, ['V1'])
                with contextlib.ExitStack() as es:
                    k.dma('sp', nwb[:], norm_mix_w.partition_broadcast(128), [], ['nwb'])
                    Wf = SB(es, "Wf", [128, 8, FOXC], BF16)
                    for kc in range(8):
                        k.dma('pool', Wf[:, kc, 0:1024], w_in[128 * kc:128 * kc + 128, RW:RW + 1024], [], ['Wf'])
                        k.dma('pool', Wf[:, kc, 1024:FOXC], w_in[128 * kc:128 * kc + 128, RW + 1024:RW + FOXC], [], ['Wf'])
                    xts = [SB(es, f"xt{i}", [128, D]) for i in range(2)]
                    ss = SB(es, "ss", [128, 1])
                    rs = SB(es, "rs", [128, 1])
                    xn = SB(es, "xn", [128, D], BF16)
                    hT = SB(es, "hT", [128, 8, 512], BF16)
                    qkw = SB(es, "qkw", [128, 2])
                    fbb = SB(es, "fbb", [128, 8])
                    sq = SB(es, "sq", [128, 512])
                    rq = SB(es, "rq", [128, 512])
                    SGt = [SB(es, f"SGt{i}", [128, 512], BF16) for i in range(2)]
                    zt = SB(es, "zt", [128, 8])
                    sgtmp = SB(es, "sgtmp", [128, 512])
                    for hh in range(2):
                        k.dma('sp', qkw[64 * hh:64 * hh + 64, 0:1], fox_q_norm_w.rearrange("(p o) -> p o", o=1), [], ['qkw'])
                        k.dma('sp', qkw[64 * hh:64 * hh + 64, 1:2], fox_k_norm_w.rearrange("(p o) -> p o", o=1), [], ['qkw'])
                    k.dma('sp', fbb[:], fox_f_bias.partition_broadcast(128), [], ['fbb'])
                    hTs = [hT, SB(es, "hTb", [128, 8, 512], BF16)]
                    sqs = [sq, SB(es, "sqb", [128, 512])]
                    rqs = [rq, SB(es, "rqb", [128, 512])]
                    cq = {'n': 0}
                    cv = {'n': 0}

                    def PBQ():
                        i = cq['n'] % 3
                        cq['n'] += 1
                        return pbig[i], f"pb{i}"

                    def PBV():
                        i = 3 + cv['n'] % 4
                        cv['n'] += 1
                        return pbig[i], f"pb{i}"

                    def prepH(b):
                        rms_to_hT(es, x, 512 * b, hTs[b % 2], 'nwb', 'a2', hTn=f'hT{b % 2}')

                    def streamQ(b):
                        t0 = 512 * b
                        hT_, hTn_ = hTs[b % 2], f'hT{b % 2}'
                        for ct in range(8):
                            sq_, sqn_ = sqs[ct % 2], f'sq{ct % 2}'
                            rq_, rqn_ = rqs[ct % 2], f'rq{ct % 2}'
                            pb, pbn = PBQ()
                            for kc in range(8):
                                k.mm(pb[:], Wf[:, kc, 128 * ct:128 * ct + 128], hT_[:, kc, :], ['Wf', hTn_], [pbn],
                                     start=(kc == 0), stop=(kc == 7))
                            k.act(R(sq_[:]), pb[:], AF.Square, [pbn], [sqn_])
                            pb2, pbn2 = PBQ()
                            k.mm(pb2[:], bones[:], sq_[:], ['bones', sqn_], [pbn2], r32=True)
                            k.rsqrt(rq_[:], pb2[:], [pbn2], [rqn_], scale=1.0 / 64, eps=1e-6)
                            dst = QT if ct < 4 else KT_
                            dn_ = 'QT' if ct < 4 else 'KTf'
                            k.stt(dst[:, ct % 4, t0:t0 + 512], pb[:], qkw[:, (ct // 4):(ct // 4) + 1], rq_[:], ALU.mult, ALU.mult,
                                  [pbn, 'qkw', rqn_], [dn_])

                    def streamV(b):
                        t0 = 512 * b
                        hT_, hTn_ = hTs[b % 2], f'hT{b % 2}'
                        for i in range(4):
                            ti = 4 * b + i
                            tsl = slice(128 * i, 128 * i + 128)
                            pb, pbn = PBV()
                            for kc in range(8):
                                k.mm(pb[:], hT_[:, kc, tsl], Wf[:, kc, 1024:1536], ['Wf', hTn_], [pbn], start=(kc == 0), stop=(kc == 7))
                            k.cp('act', V1[:, ti, :, 0:64], pb[:].rearrange("p (h d) -> p h d", d=64), [pbn], ['V1'])
                            pb, pbn = PBV()
                            for kc in range(8):
                                k.mm(pb[:], hT_[:, kc, tsl], Wf[:, kc, 1536:2048], ['Wf', hTn_], [pbn], start=(kc == 0), stop=(kc == 7))
                            sg, sgn = SGt[i % 2], f'SGt{i % 2}'
                            k.sigmoid(sg[:], pb[:], [pbn], [sgn], tmp=sgtmp[:])
                            k.dma('sp', SGd[t0 + 128 * i:t0 + 128 * i + 128, :], sg[:], [sgn], ['SGd'])
                            pb, pbn = PBV()
                            for kc in range(8):
                                k.mm(pb[:, 0:8], hT_[:, kc, tsl], Wf[:, kc, 2048:2056], ['Wf', hTn_], [pbn], start=(kc == 0), stop=(kc == 7))
                            k.tt('dve', zt[:], pb[:, 0:8], fbb[:], ALU.add, [pbn, 'fbb'], ['zt'])
                            k.act(zt[:], zt[:], AF.Exp, ['zt'], ['zt'], scale=-1.0)
                            k.act(LFs[:, ti, :], zt[:], AF.Ln, ['zt'], ['LFs'], bias=1.0)

                    replay(record(prepH, 0))
                    for b in range(NB):
                        ls = [record(streamQ, b), record(streamV, b)]
                        if b + 1 < NB:
                            ls.append(record(prepH, b + 1))
                        replay(merge(*ls))
                    S.wait_all('sp')
                S.flush()
                with contextlib.ExitStack() as es:
                    tri = SB(es, "tri", [128, 128])
                    cmask = SB(es, "cmask", [128, 128], BF16)
                    NCk = SB(es, "NCk", [128, NT, 8])
                    TOT = SB(es, "TOT", [128, NT, 8])
                    CAR = SB(es, "CAR", [128, NT, 8])
                    NBq = SB(es, "NBq", [128, NT, 8])
                    ownb = SB(es, "ownb", [128, 64])
                    k.asel(tri[:], ones[:], [[1, 128]], ALU.is_ge, 0, -1, ['ones'], ['tri'])
                    k.cp('dve', cmask[:], tri[:], ['tri'], ['cmask'])
                    k.dma('sp', ownb[:], fox_o_norm_w.partition_broadcast(128), [], ['ownb'])
                    LF2 = LFs[:].rearrange("p t h -> p (t h)")
                    nchunk = (NT * 8 + 511) // 512
                    for cc in range(nchunk):
                        c0 = 512 * cc
                        c1 = min(NT * 8, c0 + 512)
                        pb, pbn = PB()
                        k.mm(pb[:, 0:c1 - c0], tri[:], LF2[:, c0:c1], ['tri', 'LFs'], [pbn])
                        k.cp('act', NCk[:].rearrange("p t h -> p (t h)")[:, c0:c1], pb[:, 0:c1 - c0], [pbn], ['NCk'])
                        pb, pbn = PB()
                        k.mm(pb[:, 0:c1 - c0], ones[:], LF2[:, c0:c1], ['ones', 'LFs'], [pbn])
                        k.cp('act', TOT[:].rearrange("p t h -> p (t h)")[:, c0:c1], pb[:, 0:c1 - c0], [pbn], ['TOT'])
                    k.memset('pool', CAR[:, 0, :], 0.0, ['CAR'])
                    for ti in range(1, NT):
                        k.tt('dve', CAR[:, ti, :], CAR[:, ti - 1, :], TOT[:, ti - 1, :], ALU.add, ['CAR', 'TOT'], ['CAR'])
                    k.tt('dve', NCk[:], NCk[:], CAR[:], ALU.add, ['NCk', 'CAR'], ['NCk'])
                    k.stt(NBq[:], TOT[:], 0.5, CAR[:], ALU.mult, ALU.add, ['TOT', 'CAR'], ['NBq'])
                    biasT = [SB(es, f"biasT{i}", [128, NT]) for i in range(2)]
                    PT = [SB(es, f"PT{i}", [128, 128], BF16) for i in range(8)]
                    RL = SB(es, "RL", [128, 8])
                    Ot = SB(es, "Ot", [128, 8, 64])
                    O2 = SB(es, "O2", [128, 8, 64])
                    ssq = SB(es, "ssq", [128, 8])
                    SGl = SB(es, "SGl", [128, 512], BF16)
                    YFt = SB(es, "YFt", [128, 512], BF16)
                    YFo = SB(es, "YFo", [128, 4, 128], BF16)
                    pS = [(pbig[i], f'pb{i}') for i in range(3)]
                    pO = [(pbig[3 + i], f'pb{3 + i}') for i in range(4)]
                    nS = 0
                    nP = 0
                    NQB = NT // 2
                    PT2 = [SB(es, f"PT2_{i}", [128, 256], BF16) for i in range(4)]
                    items = []
                    for qb in range(NQB):
                        for h in range(8):
                            kts_all = list(range(0, 2 * qb + 2))
                            for g0 in range(0, len(kts_all), 2):
                                items.append((qb, h, kts_all[g0:g0 + 2]))

                    def emit_st(it):
                        qb, h, kts = it
                        q0 = 256 * qb
                        pr, r0 = h // 2, 64 * (h % 2)
                        bt, btn = biasT[h % 2], f'biasT{h % 2}'
                        if kts[0] == 0:
                            nk = 2 * qb + 2
                            k.ts('dve', bt[:, 0:nk], NCk[:, 0:nk, h], CAR[:, 2 * qb + 1, h:h + 1], None, ALU.subtract, None,
                                 ['NCk', 'CAR'], [btn])
                        pb, pbn = pS[st['big'] % 3]
                        st['big'] += 1
                        for j, kt in enumerate(kts):
                            lo = 128 if kt == 2 * qb + 1 else 0
                            k.mm(pb[:, 256 * j + lo:256 * j + 256], KT_[r0:r0 + 64, pr, 128 * kt:128 * kt + 128],
                                 QT[r0:r0 + 64, pr, q0 + lo:q0 + 256], ['KTf', 'QT'], [pbn])
                        return pb, pbn

                    def emit_pv(it, pb, pbn):
                        qb, h, kts = it
                        bt, btn = biasT[h % 2], f'biasT{h % 2}'
                        pocol = 65 * (h % 4)
                        for j, kt in enumerate(kts):
                            lo = 128 if kt == 2 * qb + 1 else 0
                            pt, ptn = PT2[st['q'] % 4], f"PT2_{st['q'] % 4}"
                            st['q'] += 1
                            k.act(pt[:, lo:256], pb[:, 256 * j + lo:256 * j + 256], AF.Exp, [pbn, btn], [ptn],
                                  bias=bt[:, kt:kt + 1], scale=0.125)
                            for jq in range(2):
                                qt = 2 * qb + jq
                                if kt > qt:
                                    continue
                                if kt == qt:
                                    k.tt('pool', pt[:, 128 * jq:128 * jq + 128], pt[:, 128 * jq:128 * jq + 128], cmask[:], ALU.mult,
                                         [ptn, 'cmask'], [ptn])
                                po, pon = pO[2 * jq + h // 4]
                                k.mm(po[:, pocol:pocol + 65], pt[:, 128 * jq:128 * jq + 128], V1[:, kt, h, :], [ptn, 'V1'], [pon],
                                     start=(kt == 0), stop=(kt == qt))

                    def epilogue(qt, jq):
                        qsl = slice(128 * qt, 128 * qt + 128)
                        for hf in range(2):
                            po, pon = pO[2 * jq + hf]
                            po3 = po[:, 0:260].rearrange("p (h d) -> p h d", d=65)
                            k.recip(RL[:, 4 * hf:4 * hf + 4], po3[:, :, 64], [pon], [f'RL{hf}'])
                            k.tt('dve', Ot[:, 4 * hf:4 * hf + 4, :], po3[:, :, 0:64],
                                 RL[:, 4 * hf:4 * hf + 4].unsqueeze(2).to_broadcast([128, 4, 64]), ALU.mult,
                                 [pon, f'RL{hf}'], [f'Ot{hf}'])
                        rOt = ['Ot0', 'Ot1']
                        k.act(O2[:], Ot[:], AF.Square, rOt, ['O2'])
                        S.op('dve', lambda e: e.tensor_reduce(out=ssq[:], in_=O2[:], axis=AX.X, op=ALU.add), ['O2'], ['ssq'])
                        k.rsqrt(ssq[:], ssq[:], ['ssq'], ['ssq'], scale=1.0 / 64, eps=1e-6)
                        k.tt('dve', O2[:], Ot[:], ssq[:].unsqueeze(2).to_broadcast([128, 8, 64]), ALU.mult, rOt + ['ssq'], ['O2'])
                        k.tt('pool', O2[:], O2[:], ownb[:].unsqueeze(1).to_broadcast([128, 8, 64]), ALU.mult, ['O2', 'ownb'], ['O2'])
                        k.dma('sp', SGl[:], SGd[qsl, :], ['SGd'], ['SGl'])
                        k.tt('dve', YFt[:], O2[:].rearrange("p h d -> p (h d)"), SGl[:], ALU.mult, ['O2', 'SGl'], ['YFt'])
                        for c4 in range(4):
                            k.tr(pst[:, 128 * c4:128 * c4 + 128], YFt[:, 128 * c4:128 * c4 + 128], identb[:], ['YFt', 'identb'], ['pst'])
                        k.cp('act', YFo[:], pst[:, 0:512].rearrange("p (c t) -> p c t", t=128), ['pst'], ['YFo'])
                        k.dma('sp', YF[:, :, qsl].rearrange("c p t -> p c t"), YFo[:], ['YFo'], ['YF'])
                    cur = emit_st(items[0])
                    for ii, it in enumerate(items):
                        nxt = emit_st(items[ii + 1]) if ii + 1 < len(items) else None
                        emit_pv(it, *cur)
                        cur = nxt
                        if it[1] == 7 and it[2][-1] == 2 * it[0] + 1:
                            epilogue(2 * it[0], 0)
                            epilogue(2 * it[0] + 1, 1)
                    S.wait_all('sp')
                S.flush()

        esC = es0.enter_context(contextlib.ExitStack())
        if 'C2' in phases:
            Wup = SB(esC, "Wup", [128, 8, 2 * DFF], BF16)
            Wd = SB(esC, "Wd", [128, 22, D], BF16)
        if 'C1' in phases:
            with contextlib.ExitStack() as es:
                Wo = SB(es, "Wo", [128, 8, D], BF16)
                for kc in range(8):
                    k.dma('pool', Wo[:, kc, :], w_out[128 * kc:128 * kc + 128, :], [], ['Wo'])
                if 'C2' in phases:
                    for kc in range(8):
                        for c0 in range(0, 2 * DFF, 2048):
                            c1 = min(2 * DFF, c0 + 2048)
                            k.dma('pool', Wup[:, kc, c0:c1], ffn_w_up[128 * kc:128 * kc + 128, c0:c1], [], ['Wup'])
                    for ft in range(22):
                        k.dma('pool', Wd[:, ft, :], ffn_w_down[128 * ft:128 * ft + 128, :], [], ['Wd'])
                Yb = [SB(es, f"Yb{i}", [128, 8, 512], BF16) for i in range(2)]
                xts = [SB(es, f"xt{i}", [128, D]) for i in range(2)]
                x1t = [SB(es, f"x1t{i}", [128, D]) for i in range(2)]
                for b in range(NB):
                    t0 = 512 * b
                    yb, ybn = Yb[b % 2], f'Yb{b % 2}'
                    k.dma('sp', yb[:, 0:4, :], YR[:, :, t0:t0 + 512].rearrange("c p t -> p c t"), ['YR'], [ybn])
                    k.dma('sp', yb[:, 4:8, :], YF[:, :, t0:t0 + 512].rearrange("c p t -> p c t"), ['YF'], [ybn])
                    for i in range(4):
                        par = i % 2
                        tsl = slice(128 * i, 128 * i + 128)
                        k.dma('sp', xts[par][:], x[t0 + 128 * i:t0 + 128 * i + 128, :], [], [f'xt{par}'])
                        for hf in range(2):
                            pb, pbn = PB()
                            for kc in range(8):
                                k.mm(pb[:], yb[:, kc, tsl], Wo[:, kc, 512 * hf:512 * hf + 512], [ybn, 'Wo'], [pbn],
                                     start=(kc == 0), stop=(kc == 7))
                            k.tt('dve', x1t[par][:, 512 * hf:512 * hf + 512], pb[:], xts[par][:, 512 * hf:512 * hf + 512], ALU.add,
                                 [pbn, f'xt{par}'], [f'x1t{par}'])
                        k.dma('sp', X1[t0 + 128 * i:t0 + 128 * i + 128, :], x1t[par][:], [f'x1t{par}'], ['X1'])
                S.wait_all('sp')
                S.flush()

        if 'C2' in phases:
            with contextlib.ExitStack() as es:
                NF = 2 * DFF // 128
                if 'C1' not in phases:
                    for kc in range(8):
                        for c0 in range(0, 2 * DFF, 2048):
                            c1 = min(2 * DFF, c0 + 2048)
                            k.dma('pool', Wup[:, kc, c0:c1], ffn_w_up[128 * kc:128 * kc + 128, c0:c1], [], ['Wup'])
                    for ft in range(22):
                        k.dma('pool', Wd[:, ft, :], ffn_w_down[128 * ft:128 * ft + 128, :], [], ['Wd'])
                nw2 = SB(es, "nw2", [128, D])
                nwf = SB(es, "nwf", [128, D])
                k.dma('sp', nw2[:], norm_ffn_w.partition_broadcast(128), [], ['nw2'])
                k.dma('sp', nwf[:], norm_final_w.partition_broadcast(128), [], ['nwf'])
                craw = SB(es, "craw", [128, 128])
                craw2 = SB(es, "craw2", [48, 128])
                cw = SB(es, "cw", [128, 176])
                k.dma('sp', craw[:], ffn_conv_w.rearrange("a (f p) -> (a f) p", p=128)[0:128, :], [], ['craw'])
                k.dma('sp', craw2[0:4, :], ffn_conv_w.rearrange("a (f p) -> (a f) p", p=128)[128:132, :], [], ['craw2'])
                k.dma('sp', craw2[4:48, :], ffn_conv_b.rearrange("(f p) -> f p", p=128), [], ['craw2'])
                pb, pbn = PB()
                k.tr(pb[:, 0:128], craw[:], ident[:], ['craw', 'ident'], [pbn])
                k.tr(pb[:, 128:176], craw2[:], ident[0:48, 0:48], ['craw2', 'ident'], [pbn])
                k.cp('act', cw[:], pb[:, 0:176], [pbn], ['cw'])
                x1sA = [[(SB(es, f"x1s{p}{i}", [128, D]), f'x1s{p}{i}') for i in range(2)] for p in range(2)]
                ss = SB(es, "ss", [128, 1])
                rs = SB(es, "rs", [128, 1])
                xn = SB(es, "xn", [128, D], BF16)
                h2Ts = [SB(es, f"h2T{p}", [128, 8, 256], BF16) for p in range(2)]
                Ub = [SB(es, f"Ub{i}", [128, 258]) for i in range(2)]
                HL = SB(es, "HL", [128, NF, 2])
                k.memset('pool', HL[:], 0.0, [f'HL{f}' for f in range(NF)])
                cva = [SB(es, f"cva{i}", [128, 256]) for i in range(4)]
                cvb = [SB(es, f"cvb{i}", [128, 256]) for i in range(2)]
                gsl = [SB(es, f"gsl{i}", [128, 256]) for i in range(2)]
                GT = SB(es, "GT", [128, 22, 256], BF16)
                ot = [SB(es, "ot", [128, D])] * 2

                pU = [(pbig[i], f'pb{i}') for i in range(3)]
                pD = [(pbig[3 + i], f'pb{3 + i}') for i in range(4)]
                cnt = {'u': 0}
                cur = {}

                def conv_tile(ft, slot):
                    pb, pbn = pU[cnt['u'] % 3]
                    cnt['u'] += 1
                    for kc in range(8):
                        k.mm(pb[:, 0:256], Wup[:, kc, 128 * ft:128 * ft + 128], cur['h2T'][:, kc, :], ['Wup', cur['hTn']], [pbn],
                             start=(kc == 0), stop=(kc == 7))
                    ub, ubn = Ub[slot], f'Ub{slot}'
                    ca, can = cva[slot + 2 * (ft % 2)], f'cva{slot + 2 * (ft % 2)}'
                    cb_, cbn = cvb[slot], f'cvb{slot}'
                    k.cp('act', ub[:, 2:258], pb[:, 0:256], [pbn], [ubn + 'm'])
                    k.act(ca[:], pb[:, 0:256], AF.Identity, [pbn, 'cw'], [can], bias=cw[:, 132 + ft:133 + ft],
                          scale=cw[:, 88 + ft:89 + ft])
                    k.cp('pool', ub[:, 0:2], HL[:, ft, :], [f'HL{ft}'], [ubn + 'h'])
                    ur = [ubn + 'm', ubn + 'h']
                    k.stt(cb_[:], ub[:, 1:257], cw[:, 44 + ft:45 + ft], ca[:], ALU.mult, ALU.add, ur + ['cw', can], [cbn])
                    k.stt(ca[:], ub[:, 0:256], cw[:, ft:ft + 1], cb_[:], ALU.mult, ALU.add, ur + ['cw', cbn], [can])
                    k.cp('pool', HL[:, ft, :], ub[:, 256:258], ur, [f'HL{ft}'])
                    return ca, can

                def down_mm(ft):
                    for i in range(2):
                        for hf in range(2):
                            pd, pdn = pD[2 * i + hf]
                            k.mm(pd[:], GT[:, ft, 128 * i:128 * i + 128], Wd[:, ft, 512 * hf:512 * hf + 512], [f'GT{ft}', 'Wd'], [pdn],
                                 start=(ft == 0), stop=(ft == 21))

                DLY = 2
                NB2 = T // 256

                def prep(b2):
                    p = b2 % 2
                    rms_to_hT(es, X1, 256 * b2, h2Ts[p], 'nw2', 'c2', ntile=2, xtl=x1sA[p], nwt=nw2, hTn=f'h2T{p}')

                prep(0)
                for b2 in range(NB2):
                    t0 = 256 * b2
                    cur['h2T'], cur['hTn'] = h2Ts[b2 % 2], f'h2T{b2 % 2}'
                    x1s = x1sA[b2 % 2]
                    pend = None

                    def finish(pn):
                        ft_, ga, gan, va, van = pn
                        g_, gn_ = gsl[ft_ % 2], f'gsl{ft_ % 2}'
                        k.act(g_[:], ga[:], AF.Silu, [gan], [gn_])
                        k.tt('pool', GT[:, ft_, :], g_[:], va[:], ALU.mult, [gn_, van], [f'GT{ft_}'])

                    for ft in range(22):
                        ga, gan = conv_tile(ft, 0)
                        va, van = conv_tile(22 + ft, 1)
                        if pend is not None:
                            finish(pend)
                        pend = (ft, ga, gan, va, van)
                        if ft >= DLY + 1:
                            down_mm(ft - DLY - 1)
                        if ft == 10 and b2 + 1 < NB2:
                            prep(b2 + 1)
                    finish(pend)
                    for ft in range(22 - DLY - 1, 22):
                        down_mm(ft)
                    for i in range(2):
                        xs_, xsn = x1s[i]
                        for hf in range(2):
                            pd, pdn = pD[2 * i + hf]
                            k.tt('dve', xs_[:, 512 * hf:512 * hf + 512], pd[:], xs_[:, 512 * hf:512 * hf + 512], ALU.add,
                                 [pdn, xsn], [xsn])
                        k.act(xn[:], xs_[:], AF.Square, [xsn], ['xn', 'ss'], accum_out=ss[:])
                        k.act(ss[:], ss[:], AF.Sqrt, ['ss'], ['ss'], bias=1e-6, scale=1.0 / D)
                        k.recip(rs[:], ss[:], ['ss'], ['rs'])
                        k.stt(ot[i][:], xs_[:], rs[:, 0:1], nwf[:], ALU.mult, ALU.mult, [xsn, 'rs', 'nwf'], ['ot'])
                        k.dma('sp', out[t0 + 128 * i:t0 + 128 * i + 128, :], ot[i][:], ['ot'], ['out'])
                S.wait_all('sp')
                S.flush()

        S.wait_all('sp')
        S.flush()
    return nc


_IN_NAMES = ["x", "norm_mix_w", "w_in", "rwkv_mu", "rwkv_w0", "rwkv_w2", "rwkv_a0", "rwkv_a2", "rwkv_g2",
             "rwkv_k_k", "rwkv_k_a", "rwkv_r_k", "rwkv_lnx_w", "rwkv_lnx_b", "fox_f_bias", "fox_q_norm_w",
             "fox_k_norm_w", "fox_o_norm_w", "w_out", "norm_ffn_w", "ffn_w_up", "ffn_conv_w", "ffn_conv_b",
             "ffn_w_down", "norm_final_w"]

_SHAPES = {"norm_mix_w": (1, D), "w_in": (D, RW + FOXC), "rwkv_mu": (RW,), "rwkv_w0": (512,), "rwkv_w2": (64, 512),
           "rwkv_a0": (512,), "rwkv_a2": (64, 512), "rwkv_g2": (128, 512), "rwkv_k_k": (512,), "rwkv_k_a": (512,),
           "rwkv_r_k": (512,), "rwkv_lnx_w": (512,), "rwkv_lnx_b": (512,), "fox_f_bias": (1, 8),
           "fox_q_norm_w": (64,), "fox_k_norm_w": (64,), "fox_o_norm_w": (1, 64), "w_out": (D, D),
           "norm_ffn_w": (1, D), "ffn_w_up": (D, 2 * DFF), "ffn_conv_w": (3, 2 * DFF), "ffn_conv_b": (2 * DFF,),
           "ffn_w_down": (DFF, D), "norm_final_w": (1, D)}


def make_in_maps(inputs, T, ncores):
    shared = {n: np.ascontiguousarray(np.asarray(inputs[n], dtype=np.float32).reshape(_SHAPES[n])) for n in _SHAPES}
    xs = np.asarray(inputs["x"], dtype=np.float32)
    maps = []
    for c in range(ncores):
        m = dict(shared)
        m["x"] = np.ascontiguousarray(xs[c, :T])
        maps.append(m)
    return maps


def kernel(**inputs):
    T = inputs["x"].shape[1]
    B = inputs["x"].shape[0]
    nc = build(T=T)
    res = run_bass_kernel_spmd(nc, make_in_maps(inputs, T, B), core_ids=list(range(B)))
    return np.stack([r["out"] for r in res.results], axis=0).astype(np.float32)
```
